# Optimizing a Trainium2 kernel written in Bass

```python
import math
import jax
import jax.numpy as jnp
from jax import lax
import numpy as np

D_MODEL = 1024
BATCH = 16
SEQ = 4096
DEPTH = 4

HEAD_DIM = 64
GRID_W = 64
EPS = 1e-6
NEG_INF = -1e30

NA_HEADS = 4
NA_KH_MAX = 8
NA_KW = 16

DIL_HEADS = 8
DIL_PATTERNS = ((128, 1), (512, 4), (2048, 16))
DIL_BLK = 128

MLA_HEADS = 4
MLA_Q_LORA = 384
MLA_KV_LORA = 256
MLA_NOPE = 64
MLA_ROPE = 32
MLA_V = 64
MLA_BLK = 128
ROPE_THETA = 10000.0

T5_BUCKETS = 32
T5_MAX_DIST = 1024

D_FF = 2816
CONV_W = 3

D_A = NA_HEADS * HEAD_DIM
D_B = DIL_HEADS * HEAD_DIM
D_C = MLA_HEADS * MLA_V
D_MIX = D_A + D_B + D_C
D_IN = 3 * D_A + 3 * D_B + MLA_Q_LORA + MLA_KV_LORA + MLA_ROPE
N_MOD = 6

kernel_name = "hybrid_na_dilated_mla_encoder"


def rms_norm(x, g):
    xf = x.astype(jnp.float32)
    y = xf * lax.rsqrt(jnp.mean(xf * xf, axis=-1, keepdims=True) + EPS)
    return (y * g.astype(jnp.float32)).astype(x.dtype)


def rope_tables(seq_len):
    inv_freq = jnp.asarray(ROPE_THETA ** (-np.arange(0, MLA_ROPE, 2, dtype=np.float32) / MLA_ROPE), jnp.float32)
    ang = jnp.arange(seq_len, dtype=jnp.float32)[:, None] * inv_freq[None, :]
    return jnp.cos(ang), jnp.sin(ang)


def apply_rope(x, cos, sin):
    half = x.shape[-1] // 2
    x1, x2 = x[..., :half], x[..., half:]
    c = cos[None, :, None, :].astype(x.dtype)
    s = sin[None, :, None, :].astype(x.dtype)
    return jnp.concatenate([x1 * c - x2 * s, x1 * s + x2 * c], axis=-1)


def neighbourhood_attention(q, k, v, rpb):
    B, S, H, Dh = q.shape
    rows = S // GRID_W
    kh = min(NA_KH_MAX, rows)
    qg = q.reshape(B, rows, GRID_W, H, Dh)
    kg = k.reshape(B, rows, GRID_W, H, Dh)
    vg = v.reshape(B, rows, GRID_W, H, Dh)
    col = np.arange(GRID_W)
    c_start = np.clip(col - NA_KW // 2, 0, GRID_W - NA_KW)
    c_idx = c_start[:, None] + np.arange(NA_KW)[None, :]
    d_col = c_idx - col[:, None] + (NA_KW - 1)
    row = np.arange(rows)
    r_start = np.clip(row - kh // 2, 0, rows - kh)
    d_row = r_start[:, None] + np.arange(kh)[None, :] - row[:, None] + (NA_KH_MAX - 1)
    scale = Dh ** -0.5

    def one_row(args):
        r, r0, dr = args
        q_r = lax.dynamic_index_in_dim(qg, r, axis=1, keepdims=False)
        k_r = lax.dynamic_slice_in_dim(kg, r0, kh, axis=1)[:, :, c_idx]
        v_r = lax.dynamic_slice_in_dim(vg, r0, kh, axis=1)[:, :, c_idx]
        bias = rpb[:, dr[:, None, None], d_col[None, :, :]]
        s = jnp.einsum("bwhd,biwjhd->bhwij", q_r, k_r).astype(jnp.float32) * scale
        s = s + jnp.transpose(bias, (0, 2, 1, 3)).astype(jnp.float32)[None]
        p = jax.nn.softmax(s.reshape(B, H, GRID_W, kh * NA_KW), axis=-1)
        p = p.reshape(B, H, GRID_W, kh, NA_KW).astype(v.dtype)
        return jnp.einsum("bhwij,biwjhd->bwhd", p, v_r)

    out = lax.map(one_row, (jnp.arange(rows, dtype=jnp.int32),
                            jnp.asarray(r_start, jnp.int32),
                            jnp.asarray(d_row, jnp.int32)))
    return jnp.transpose(out, (1, 0, 2, 3, 4)).reshape(B, S, H, Dh)


def t5_bucket(rel):
    nb = T5_BUCKETS // 2
    max_exact = nb // 2
    n = np.abs(rel)
    large = max_exact + (np.log(np.maximum(n, 1) / max_exact)
                         / math.log(T5_MAX_DIST / max_exact) * (nb - max_exact)).astype(np.int64)
    large = np.minimum(large, nb - 1)
    return (np.where(rel > 0, nb, 0) + np.where(n < max_exact, n, large)).astype(np.int32)


def dilated_branch(q, k, v, t5_table, window, dilation):
    B, S, H, Dh = q.shape
    d = dilation
    half = window // (2 * d)
    L = S // d
    nblk = -(-L // DIL_BLK)
    Lp = nblk * DIL_BLK
    kb_len = DIL_BLK + 2 * half

    def residue_major(t):
        return jnp.transpose(t.reshape(B, L, d, H, Dh), (0, 2, 1, 3, 4))

    qr = jnp.pad(residue_major(q), ((0, 0), (0, 0), (0, Lp - L), (0, 0), (0, 0)))
    qr = qr.reshape(B, d, nblk, DIL_BLK, H, Dh)
    kv_pad = ((0, 0), (0, 0), (half, Lp - L + half), (0, 0), (0, 0))
    k_idx = np.arange(nblk)[:, None] * DIL_BLK + np.arange(kb_len)[None, :]
    kr = jnp.take(jnp.pad(residue_major(k), kv_pad), k_idx, axis=2)
    vr = jnp.take(jnp.pad(residue_major(v), kv_pad), k_idx, axis=2)

    a = np.arange(DIL_BLK)[:, None]
    j = np.arange(kb_len)[None, :]
    rel_m = j - half - a
    m_key = np.arange(nblk)[:, None, None] * DIL_BLK + (j - half)[None]
    valid = (np.abs(rel_m) <= half)[None] & (m_key >= 0) & (m_key < L)
    bias = jnp.transpose(t5_table[t5_bucket(rel_m * d)], (2, 0, 1)).astype(jnp.float32)

    s = jnp.einsum("brnqhd,brnkhd->brnhqk", qr, kr).astype(jnp.float32) * (Dh ** -0.5)
    s = jnp.where(valid[None, None, :, None], s + bias, NEG_INF)
    lse = jax.nn.logsumexp(s, axis=-1)
    p = jnp.exp(s - lse[..., None]).astype(v.dtype)
    o = jnp.einsum("brnhqk,brnkhd->brnqhd", p, vr)
    o = o.reshape(B, d, Lp, H, Dh)[:, :, :L]
    o = jnp.transpose(o, (0, 2, 1, 3, 4)).reshape(B, S, H, Dh)
    lse = jnp.transpose(lse, (0, 1, 2, 4, 3)).reshape(B, d, Lp, H)[:, :, :L]
    lse = jnp.transpose(lse, (0, 2, 1, 3)).reshape(B, S, H)
    return o, lse


def dilated_attention(q, k, v, t5_table):
    outs, lses = [], []
    for window, dilation in DIL_PATTERNS:
        o, lse = dilated_branch(q, k, v, t5_table, window, dilation)
        outs.append(o)
        lses.append(lse)
    w = jax.nn.softmax(jnp.stack(lses, axis=0), axis=0)
    mixed = jnp.sum(w[..., None] * jnp.stack(outs, axis=0).astype(jnp.float32), axis=0)
    return mixed.astype(q.dtype)


def latent_attention(c_q, c_kv, k_rope_raw, g_q, g_kv, w_uq, w_ukv, cos, sin):
    B, S, _ = c_q.shape
    q = (rms_norm(c_q, g_q) @ w_uq).reshape(B, S, MLA_HEADS, MLA_NOPE + MLA_ROPE)
    q_nope, q_rope = q[..., :MLA_NOPE], apply_rope(q[..., MLA_NOPE:], cos, sin)
    kv = (rms_norm(c_kv, g_kv) @ w_ukv).reshape(B, S, MLA_HEADS, MLA_NOPE + MLA_V)
    k_nope, v = kv[..., :MLA_NOPE], kv[..., MLA_NOPE:]
    k_rope = apply_rope(k_rope_raw[:, :, None, :], cos, sin)[:, :, 0]
    nb = S // MLA_BLK
    scale = (MLA_NOPE + MLA_ROPE) ** -0.5

    def to_blocks(t):
        return jnp.moveaxis(t.reshape(B, nb, MLA_BLK, MLA_HEADS, t.shape[-1]), 1, 0)

    def one_block(args):
        qn, qr = args
        s = jnp.einsum("bqhd,bkhd->bhqk", qn, k_nope) + jnp.einsum("bqhd,bkd->bhqk", qr, k_rope)
        p = jax.nn.softmax(s.astype(jnp.float32) * scale, axis=-1).astype(v.dtype)
        return jnp.einsum("bhqk,bkhd->bqhd", p, v)

    o = lax.map(one_block, (to_blocks(q_nope), to_blocks(q_rope)))
    return jnp.moveaxis(o, 0, 1).reshape(B, S, MLA_HEADS * MLA_V)


def hybrid_mixer(h, w_in, rpb, t5_table, g_q, g_kv, w_uq, w_ukv, w_out, cos, sin):
    B, S, _ = h.shape
    z = h @ w_in
    bounds = [int(b) for b in np.cumsum([D_A, D_A, D_A, D_B, D_B, D_B, MLA_Q_LORA, MLA_KV_LORA])]
    q_a, k_a, v_a, q_b, k_b, v_b, c_q, c_kv, k_rope_raw = jnp.split(z, bounds, axis=-1)

    def heads(t, n):
        return t.reshape(B, S, n, HEAD_DIM)

    o_a = neighbourhood_attention(heads(q_a, NA_HEADS), heads(k_a, NA_HEADS),
                                  heads(v_a, NA_HEADS), rpb).reshape(B, S, D_A)
    o_b = dilated_attention(heads(q_b, DIL_HEADS), heads(k_b, DIL_HEADS),
                            heads(v_b, DIL_HEADS), t5_table).reshape(B, S, D_B)
    o_c = latent_attention(c_q, c_kv, k_rope_raw, g_q, g_kv, w_uq, w_ukv, cos, sin)
    return jnp.concatenate([o_a, o_b, o_c], axis=-1) @ w_out


def conv_ffn(h, w_up, conv_w, conv_b, w_down):
    S = h.shape[1]
    a, g = jnp.split(h @ w_up, 2, axis=-1)
    pad = CONV_W // 2
    gp = jnp.pad(g, ((0, 0), (pad, pad), (0, 0)))
    gc = conv_b
    for i in range(CONV_W):
        gc = gc + gp[:, i:i + S] * conv_w[i]
    return (jax.nn.gelu(gc) * a) @ w_down


def setup_inputs(seed: int = 0) -> dict:
    key = jax.random.key(seed)
    ks = jax.random.split(key, 20)

    def nrm(k, shape, s):
        return jax.random.normal(k, shape, jnp.float32) * s

    def gain(k, shape):
        return 1.0 + 0.05 * jax.random.normal(k, shape, jnp.float32)

    return {
        "x": nrm(ks[0], (BATCH, SEQ, D_MODEL), 1.0),
        "c": nrm(ks[1], (BATCH, D_MODEL), 1.0),
        "w_ada": nrm(ks[2], (DEPTH, D_MODEL, N_MOD * D_MODEL), 0.5 * D_MODEL ** -0.5),
        "b_ada": nrm(ks[3], (DEPTH, N_MOD * D_MODEL), 0.02),
        "g_pre_mix": gain(ks[4], (DEPTH, D_MODEL)),
        "g_post_mix": gain(ks[5], (DEPTH, D_MODEL)),
        "g_pre_ffn": gain(ks[6], (DEPTH, D_MODEL)),
        "g_post_ffn": gain(ks[7], (DEPTH, D_MODEL)),
        "w_in": nrm(ks[8], (DEPTH, D_MODEL, D_IN), D_MODEL ** -0.5),
        "na_rpb": nrm(ks[9], (DEPTH, NA_HEADS, 2 * NA_KH_MAX - 1, 2 * NA_KW - 1), 0.1),
        "t5_table": nrm(ks[10], (T5_BUCKETS, DIL_HEADS), 0.1),
        "mla_g_q": gain(ks[11], (DEPTH, MLA_Q_LORA)),
        "mla_g_kv": gain(ks[12], (DEPTH, MLA_KV_LORA)),
        "w_uq": nrm(ks[13], (DEPTH, MLA_Q_LORA, MLA_HEADS * (MLA_NOPE + MLA_ROPE)), MLA_Q_LORA ** -0.5),
        "w_ukv": nrm(ks[14], (DEPTH, MLA_KV_LORA, MLA_HEADS * (MLA_NOPE + MLA_V)), MLA_KV_LORA ** -0.5),
        "w_out": nrm(ks[15], (DEPTH, D_MIX, D_MODEL), D_MIX ** -0.5),
        "w_up": nrm(ks[16], (DEPTH, D_MODEL, 2 * D_FF), D_MODEL ** -0.5),
        "conv_w": nrm(ks[17], (DEPTH, CONV_W, D_FF), CONV_W ** -0.5),
        "conv_b": nrm(ks[18], (DEPTH, D_FF), 0.02),
        "w_down": nrm(ks[19], (DEPTH, D_FF, D_MODEL), D_FF ** -0.5),
    }


def reference(x, c, w_ada, b_ada, g_pre_mix, g_post_mix, g_pre_ffn, g_post_ffn, w_in, na_rpb,
              t5_table, mla_g_q, mla_g_kv, w_uq, w_ukv, w_out, w_up, conv_w, conv_b, w_down):
    S = x.shape[1]
    cos, sin = rope_tables(S)
    c_act = jax.nn.silu(c)
    for l in range(DEPTH):
        mod = c_act @ w_ada[l] + b_ada[l]
        sh1, sc1, g1, sh2, sc2, g2 = [m[:, None, :] for m in jnp.split(mod, N_MOD, axis=-1)]
        h = rms_norm(x, g_pre_mix[l]) * (1.0 + sc1) + sh1
        y = hybrid_mixer(h, w_in[l], na_rpb[l], t5_table, mla_g_q[l], mla_g_kv[l],
                         w_uq[l], w_ukv[l], w_out[l], cos, sin)
        x = x + g1 * rms_norm(y, g_post_mix[l])
        h = rms_norm(x, g_pre_ffn[l]) * (1.0 + sc2) + sh2
        y = conv_ffn(h, w_up[l], conv_w[l], conv_b[l], w_down[l])
        x = x + g2 * rms_norm(y, g_post_ffn[l])
    return x
```

```python
import math
from contextlib import ExitStack

import numpy as np
import ml_dtypes
import concourse.bass as bass
import concourse.mybir as mybir
from concourse.bass_utils import run_bass_kernel_spmd

F32 = mybir.dt.float32
BF16 = mybir.dt.bfloat16
AF = mybir.ActivationFunctionType
ALU = mybir.AluOpType

NCORES = 8
S = 4096
DM = 1024
DFF = 2816
NCH = 22
D_IN = 2976
EPS = 1e-6
NEG = -30000.0


class Buf:
    __slots__ = ("lw", "rd")

    def __init__(self):
        self.lw = []
        self.rd = {}


class Prog:
    def __init__(self, nc, ctx, n_dma_sems=40):
        self.nc = nc
        self.E = {"pe": nc.tensor, "act": nc.scalar, "dve": nc.vector, "pool": nc.gpsimd, "sp": nc.sync}
        self.psem = {k: ctx.enter_context(nc.semaphore("ps_" + k)) for k in self.E}
        self.pcnt = {k: 0 for k in self.E}
        self.seen = {k: {} for k in self.E}
        self.dsem = [ctx.enter_context(nc.semaphore("ds%d" % i)) for i in range(n_dma_sems)]
        self.dcnt = [0] * n_dma_sems
        self.dnext = 0
        self.nins = 0
        self.dead = False

    def _wait(self, eng, deps):
        if self.dead:
            return
        need = {}
        for d in deps:
            if d is None:
                continue
            s, v = d
            if need.get(s, 0) < v:
                need[s] = v
        seen = self.seen[eng]
        for s, v in need.items():
            if seen.get(s, 0) < v:
                self.E[eng].wait_ge(s, v)
                seen[s] = v
                self.nins += 1

    @staticmethod
    def _deps(reads, writes):
        deps = []
        for b in reads:
            deps.extend(b.lw)
        for b in writes:
            deps.extend(b.lw)
            deps.extend(b.rd.values())
        return deps

    def op(self, eng, fns, reads=(), writes=()):
        if self.dead:
            return None
        self._wait(eng, self._deps(reads, writes))
        if callable(fns):
            fns = [fns]
        ins = None
        e = self.E[eng]
        for f in fns:
            ins = f(e)
            self.nins += 1
        self.pcnt[eng] += 1
        ins.then_inc(self.psem[eng], 1)
        tok = (self.psem[eng], self.pcnt[eng])
        for b in reads:
            b.rd[eng] = tok
        for b in writes:
            b.lw = [tok]
            b.rd = {}
        return tok

    def dma(self, eng, out, in_, reads=(), writes=(), **kw):
        return self.dmas(eng, [(out, in_)], reads, writes, **kw)

    def dmas(self, eng, pairs, reads=(), writes=(), **kw):
        if self.dead:
            return []
        deps = self._deps(reads, writes)
        idx = []
        for _ in pairs:
            i = self.dnext
            self.dnext = (i + 1) % len(self.dsem)
            idx.append(i)
            if self.dcnt[i] > 0:
                deps.append((self.dsem[i], 16 * self.dcnt[i]))
        assert len(set(idx)) == len(idx)
        self._wait(eng, deps)
        toks = []
        for i, (out, in_) in zip(idx, pairs):
            self.dcnt[i] += 1
            self.E[eng].dma_start(out=out, in_=in_, **kw).then_inc(self.dsem[i], 16)
            self.nins += 1
            toks.append((self.dsem[i], 16 * self.dcnt[i]))
        for b in reads:
            for i, tok in zip(idx, toks):
                b.rd[("dma", i)] = tok
        for b in writes:
            b.lw = list(toks)
            b.rd = {}
        return toks

    def barrier(self):
        deps = [(self.psem[k], self.pcnt[k]) for k in self.E if self.pcnt[k] > 0]
        deps += [(s, 16 * c) for s, c in zip(self.dsem, self.dcnt) if c > 0]
        for k in self.E:
            self._wait(k, deps)


def _r_start(r):
    return min(max(r - 4, 0), 56)


def _c_start(c):
    return min(max(c - 8, 0), 48)


def _na_plan():
    pats = {}
    plan = []
    for J in range(16):
        klo = _r_start(4 * J)
        khi = _r_start(4 * J + 3) + 7
        units = []
        for i in range(klo // 2, khi // 2 + 1):
            key = []
            for a in range(2):
                for ap in range(4):
                    r = 4 * J + ap
                    kr = 2 * i + a
                    key.append(_r_start(r) <= kr <= _r_start(r) + 7)
            key = tuple(key)
            if all(key):
                pid = -1
            else:
                if key not in pats:
                    pats[key] = len(pats)
                pid = pats[key]
            di0 = 4 * J - 2 * i + 6
            assert 0 <= di0 <= 10
            units.append((i, di0, pid))
        plan.append(units)
    npat = len(pats)
    pens = np.zeros((2, npat, 4, 64), np.float32)
    for key, pid in pats.items():
        for a in range(2):
            for ap in range(4):
                if not key[a * 4 + ap]:
                    pens[a, pid, ap, :] = NEG
    return plan, pens.reshape(2, npat * 256)


NA_PLAN, NA_PENS = _na_plan()
NPAT = NA_PENS.shape[1] // 256


def _t5_bucket(rel):
    nb = 16
    max_exact = 8
    n = np.abs(rel)
    large = max_exact + (np.log(np.maximum(n, 1) / max_exact) / math.log(1024 / max_exact) * (nb - max_exact)).astype(np.int64)
    large = np.minimum(large, nb - 1)
    return (np.where(rel > 0, nb, 0) + np.where(n < max_exact, n, large)).astype(np.int32)


def _host_tables(na_rpb, t5_table):
    p = np.arange(128)
    a = p // 64
    kc = p % 64
    di = np.arange(14)
    c = np.arange(64)
    drow = a[:, None] - (di[None, :] - 6) + 7
    dcol = kc[:, None] - c[None, :] + 15
    ok = ((drow >= 0) & (drow <= 14))[:, :, None] & ((dcol >= 0) & (dcol <= 30))[:, None, :]
    drc = np.clip(drow, 0, 14)
    dcc = np.clip(dcol, 0, 30)
    G = na_rpb[:, :, drc[:, :, None], dcc[:, None, :]]
    G = np.where(ok[None, None], G, np.float32(0.0)).astype(np.float32)
    G = np.ascontiguousarray(np.transpose(G, (0, 2, 1, 3, 4))).reshape(na_rpb.shape[0], 128, 4 * 14 * 64)
    cs = np.array([_c_start(x) for x in range(64)])
    cm = ((kc[:, None] >= cs[None, :]) & (kc[:, None] < cs[None, :] + 16)).astype(np.float32)
    cm = np.ascontiguousarray(np.broadcast_to(cm[:, None, :], (128, 14, 64))).reshape(128, 14 * 64)
    q = np.arange(128)
    pc = np.arange(3)
    rel = 128 * (pc[None, :, None] - 1) + p[:, None, None] - q[None, None, :]
    valid = (np.abs(rel) <= 64)
    T = np.zeros((128, 8, 3, 3, 128), np.float32)
    for pi, d in enumerate((1, 4, 16)):
        b = _t5_bucket(rel * d)
        vals = t5_table[b]
        vals = np.where(valid[..., None], vals, np.float32(0.0))
        T[:, :, pi] = np.transpose(vals, (0, 3, 1, 2))
    T = T.reshape(128, 8 * 3 * 384)
    tm = valid.astype(np.float32).reshape(128, 384)
    return G, cm, T, tm


def _rope_tables():
    inv_freq = (10000.0 ** (-np.arange(0, 32, 2, dtype=np.float32) / 32)).astype(np.float32)
    ang = np.arange(S, dtype=np.float32)[:, None] * inv_freq[None, :]
    cos = np.cos(ang).astype(np.float32)
    sin = np.sin(ang).astype(np.float32)
    idx = (np.arange(128) % 32) % 16
    return np.ascontiguousarray(cos[:, idx].T), np.ascontiguousarray(sin[:, idx].T)


class _Stop(Exception):
    pass


def build(NL=4, dbg=()):
    nc = bass.Bass("TRN2", target_bir_lowering=False)
    stop_at = -1
    for dflag in dbg:
        if dflag.startswith("stage="):
            stop_at = int(dflag[6:])

    PP = []

    def ckpt(n):
        if n == stop_at and not PP[0].dead:
            PP[0].barrier()
            PP[0].dead = True
    D = {}

    def din(name, shape, dt=F32):
        D[name] = nc.dram_tensor(name, list(shape), dt, kind="ExternalInput").ap()

    def dscr(name, shape, dt):
        kind = "ExternalOutput" if name in dbg else "Internal"
        D[name] = nc.dram_tensor(name, list(shape), dt, kind=kind).ap()

    din("x", [2, S, DM])
    din("cT", [128, 8, 2])
    din("w_ada", [4, 128, 8, 6144])
    din("b_ada", [4, 6144])
    din("gpre_mixT", [4, 128, 8])
    din("gpre_ffnT", [4, 128, 8])
    din("gpost_mix", [4, DM])
    din("gpost_ffn", [4, DM])
    din("w_in", [4, 128, 8, D_IN])
    din("w_uq", [4, 128, 3, 384])
    din("w_ukv", [4, 128, 2, 512])
    din("gqT", [4, 128, 3])
    din("gkvT", [4, 128, 2])
    din("w_out", [4, 128, 8, DM])
    din("w_up", [4, NCH, 128, 8, 256])
    din("w_down", [4, 128, NCH, DM])
    din("convwT", [4, 128, NCH, 3])
    din("convbT", [4, 128, NCH])
    din("naG", [4, 128, 4 * 14 * 64])
    din("nacm", [128, 14 * 64])
    din("pens", [2, NPAT * 256])
    din("A2", [2, 128])
    din("t5G", [128, 8 * 3 * 384])
    din("t5m", [128, 384])
    din("cosT", [128, S])
    din("sinT", [128, S])
    D["out"] = nc.dram_tensor("out", [2, S, DM], F32, kind="ExternalOutput").ap()
    dscr("modD", [4, 2, 6144], F32)
    dscr("xs", [2, S, DM], F32)
    dscr("zT", [2, 12, 128, S], BF16)
    dscr("vtok", [2, S, 768], BF16)
    dscr("qn", [2, 2, 128, S], BF16)
    dscr("qr", [2, 128, S], BF16)
    dscr("kn", [2, 2, 128, S], BF16)
    dscr("kr", [2, 32, S], BF16)
    dscr("vc", [2, S, 256], BF16)
    dscr("OT", [2, DM, S], BF16)

    with ExitStack() as gctx:
        P = Prog(nc, gctx)
        PP.append(P)

        uid = [0]

        def SB(ctx, name, shape, dt):
            uid[0] += 1
            return ctx.enter_context(nc.sbuf_tensor("%s_u%d" % (name, uid[0]), list(shape), dt))

        def PS(ctx, name, shape, dt=F32):
            uid[0] += 1
            return ctx.enter_context(nc.psum_tensor("%s_u%d" % (name, uid[0]), list(shape), dt))

        zcol = SB(gctx, "zcol", [128, 1], F32)
        bZc = Buf()
        ident = SB(gctx, "ident", [128, 128], BF16)
        identf = SB(gctx, "identf", [128, 128], F32)
        bId = Buf()
        P.op("pool", lambda e: e.memset(zcol[:], 0.0), writes=[bZc])
        P.op("pool", lambda e: e.memset(identf[:], 1.0), writes=[bId])
        P.op("pool", lambda e: e.affine_select(out=identf[:], in_=identf[:], pattern=[[-1, 128]],
                                                compare_op=ALU.is_equal, fill=0.0, base=0, channel_multiplier=1),
             reads=[bId], writes=[bId])
        P.op("pool", lambda e: e.tensor_copy(out=ident[:], in_=identf[:]), reads=[bId], writes=[bId])

        def rstd_from_ss(ss_ap, out_ap, n, bss, bout, tmp_ap, btmp):
            P.op("act", lambda e: e.activation(out=tmp_ap, in_=ss_ap, func=AF.Ln, scale=1.0 / n, bias=EPS),
                 reads=[bss], writes=[btmp])
            P.op("act", lambda e: e.activation(out=out_ap, in_=tmp_ap, func=AF.Exp, scale=-0.5),
                 reads=[btmp], writes=[bout])

        try:
            with ExitStack() as c:
                cT = SB(c, "cT_sb", [128, 8, 2], F32)
                cact = SB(c, "cact", [128, 8, 2], F32)
                brow = SB(c, "brow", [2, 6144], F32)
                mrow = SB(c, "mrow", [2, 6144], F32)
                wt = [SB(c, "wada%d" % i, [128, 8, 512], F32) for i in range(2)]
                psM = [PS(c, "psM%d" % i, [2, 512]) for i in range(2)]
                bc, bb, bm = Buf(), Buf(), Buf()
                bw = [Buf(), Buf()]
                bp = [Buf(), Buf()]
                P.dma("sp", cT[:], D["cT"], writes=[bc])
                P.op("act", lambda e: e.activation(out=cact[:], in_=cT[:], func=AF.Silu), reads=[bc], writes=[bc])
                it = 0
                for l in range(NL):
                    P.dma("sp", brow[:], D["b_ada"][l:l + 1, :].partition_broadcast(2), writes=[bb])
                    for nb in range(12):
                        j = it % 2
                        it += 1
                        P.dma("sp", wt[j][:], D["w_ada"][l, :, :, nb * 512:(nb + 1) * 512], writes=[bw[j]])
                        P.op("pe", [(lambda e, k=k, j=j: e.matmul(psM[j][:], lhsT=cact[:, k, :], rhs=wt[j][:, k, :],
                                                                  start=(k == 0), stop=(k == 7))) for k in range(8)],
                             reads=[bc, bw[j]], writes=[bp[j]])
                        P.op("dve", lambda e, j=j, nb=nb: e.tensor_tensor(out=mrow[:, nb * 512:(nb + 1) * 512], in0=psM[j][:],
                                                                          in1=brow[:, nb * 512:(nb + 1) * 512], op=ALU.add),
                             reads=[bp[j], bb], writes=[bm])
                    P.dma("sp", D["modD"][l], mrow[:], reads=[bm])
            P.barrier()
            ckpt(1)

            def load_vec8(ctx_name, dst, l, s, j, bdst):
                src = D["modD"][l, s, j * 1024:(j + 1) * 1024].rearrange("(k p) -> p k", p=128)
                P.dma("sp", dst, src, writes=[bdst], allow_slow_non_contiguous=True)

            for l in range(NL):
                x_src = D["x"] if l == 0 else D["xs"]
                x_dst = D["out"] if l == NL - 1 else D["xs"]

                with ExitStack() as c:
                    w_in = SB(c, "w_in", [128, 8, D_IN], BF16)
                    w_rot = SB(c, "w_rot", [128, 8, 32], BF16)
                    w_uq = SB(c, "w_uq", [128, 3, 384], BF16)
                    w_uqr = SB(c, "w_uqr", [128, 3, 128], BF16)
                    w_ukv = SB(c, "w_ukv", [128, 2, 512], BF16)
                    gq = SB(c, "gq", [128, 3], F32)
                    gkv = SB(c, "gkv", [128, 2], F32)
                    gpre = SB(c, "gpre", [128, 8], F32)
                    scA = [SB(c, "scA%d" % s, [128, 8], F32) for s in range(2)]
                    biA = [SB(c, "biA%d" % s, [128, 8], F32) for s in range(2)]
                    xg = [SB(c, "xgA%d" % i, [128, 4, DM], F32) for i in range(2)]
                    cosg = [SB(c, "cosg%d" % i, [128, 512], F32) for i in range(2)]
                    sing = [SB(c, "sing%d" % i, [128, 512], F32) for i in range(2)]
                    junk = SB(c, "junkA", [128, DM], BF16)
                    ss = SB(c, "ssA", [128, 4], F32)
                    lnv = SB(c, "lnvA", [128, 4], F32)
                    rstd = SB(c, "rstdA", [128, 4], F32)
                    ssq = SB(c, "ssq", [128, 4], F32)
                    sskv = SB(c, "sskv", [128, 4], F32)
                    lnq = SB(c, "lnq", [128, 4], F32)
                    lnk = SB(c, "lnk", [128, 4], F32)
                    rsq = SB(c, "rsq", [128, 4], F32)
                    rskv = SB(c, "rskv", [128, 4], F32)
                    xn = SB(c, "xnA", [128, 4, DM], BF16)
                    hT = SB(c, "hT", [128, 8, 512], BF16)
                    zst = [SB(c, "zst%d" % i, [128, 12, 512], BF16) for i in range(2)]
                    vst = [SB(c, "vst%d" % i, [128, 4, 768], BF16) for i in range(2)]
                    cst = SB(c, "cst", [128, 4, 640], F32)
                    cqn = SB(c, "cqn", [128, 4, 384], BF16)
                    ckvn = SB(c, "ckvn", [128, 4, 256], BF16)
                    cqT = SB(c, "cqT", [128, 3, 512], BF16)
                    ckvT = SB(c, "ckvT", [128, 2, 512], BF16)
                    qnst = [SB(c, "qnst%d" % i, [128, 2, 512], BF16) for i in range(2)]
                    knst = [SB(c, "knst%d" % i, [128, 2, 512], BF16) for i in range(2)]
                    qrst = [SB(c, "qrst%d" % i, [128, 512], BF16) for i in range(2)]
                    krst = [SB(c, "krst%d" % i, [32, 512], BF16) for i in range(2)]
                    vcst = [SB(c, "vcst%d" % i, [128, 4, 256], BF16) for i in range(2)]
                    t1 = SB(c, "t1A", [128, 512], F32)
                    t2 = SB(c, "t2A", [128, 512], F32)
                    psT = [PS(c, "psTA%d" % i, [128, 512], BF16) for i in range(2)]
                    psZ = [PS(c, "psZA%d" % i, [128, 512]) for i in range(2)]
                    psW = [PS(c, "psWA%d" % i, [128, 1024]) for i in range(2)]

                    bW, bWr, bUq, bUqr, bUkv, bG = Buf(), Buf(), Buf(), Buf(), Buf(), Buf()
                    bSc = [Buf(), Buf()]
                    bXg = [Buf(), Buf()]
                    bCS = [Buf(), Buf()]
                    bJ, bSS, bLn, bRs, bXn, bHT = Buf(), Buf(), Buf(), Buf(), Buf(), Buf()
                    bSq, bLq, bRq = Buf(), Buf(), Buf()
                    bZst = [Buf(), Buf()]
                    bVst = [Buf(), Buf()]
                    bCst, bCqn, bCkvn, bCqT, bCkvT = Buf(), Buf(), Buf(), Buf(), Buf()
                    bQn = [Buf(), Buf()]
                    bKn = [Buf(), Buf()]
                    bQr = [Buf(), Buf()]
                    bKr = [Buf(), Buf()]
                    bVc = [Buf(), Buf()]
                    bT1, bT2 = Buf(), Buf()
                    bPT = [Buf(), Buf()]
                    bPZ = [Buf(), Buf()]
                    bPW = [Buf(), Buf()]

                    P.dmas("pool", [(w_in[:, k, :], D["w_in"][l, :, k, :]) for k in range(8)], writes=[bW])
                    P.dma("pool", w_uq[:], D["w_uq"][l], writes=[bUq])
                    P.dma("pool", w_ukv[:], D["w_ukv"][l], writes=[bUkv])
                    P.dmas("sp", [(gq[:], D["gqT"][l]), (gkv[:], D["gkvT"][l]), (gpre[:], D["gpre_mixT"][l])], writes=[bG])
                    P.op("act", lambda e: e.mul(out=w_rot[:, :, 0:16], in_=w_in[:, :, 2960:2976], mul=-1.0), reads=[bW], writes=[bWr])
                    P.op("act", lambda e: e.copy(out=w_rot[:, :, 16:32], in_=w_in[:, :, 2944:2960]), reads=[bW], writes=[bWr])
                    for k in range(3):
                        src = w_uq[:, k, 256:384].rearrange("p (h t j) -> p h t j", t=2, j=16)
                        dst = w_uqr[:, k, :].rearrange("p (h t j) -> p h t j", t=2, j=16)
                        P.op("act", lambda e, src=src, dst=dst: e.mul(out=dst[:, :, 0, :], in_=src[:, :, 1, :], mul=-1.0),
                             reads=[bUq], writes=[bUqr])
                        P.op("act", lambda e, src=src, dst=dst: e.copy(out=dst[:, :, 1, :], in_=src[:, :, 0, :]),
                             reads=[bUq], writes=[bUqr])
                    for s in range(2):
                        load_vec8(c, biA[s][:], l, s, 0, bSc[s])
                        load_vec8(c, scA[s][:], l, s, 1, bSc[s])
                        P.op("dve", lambda e, s=s: e.scalar_tensor_tensor(out=scA[s][:], in0=scA[s][:], scalar=1.0, in1=gpre[:],
                                                                          op0=ALU.add, op1=ALU.mult),
                             reads=[bSc[s], bG], writes=[bSc[s]])

                    def loadA(g):
                        s, t0 = g // 8, (g % 8) * 512
                        j = g % 2
                        P.dma("sp", xg[j][:], x_src[s, t0:t0 + 512, :].rearrange("(t p) d -> p t d", p=128), writes=[bXg[j]])
                        P.dmas("sp", [(cosg[j][:], D["cosT"][:, t0:t0 + 512]), (sing[j][:], D["sinT"][:, t0:t0 + 512])], writes=[bCS[j]])

                    evac_i = [0]

                    def evac(out_ap, in_ap, reads, writes):
                        evac_i[0] += 1
                        if evac_i[0] % 2:
                            P.op("dve", lambda e: e.tensor_copy(out=out_ap, in_=in_ap), reads=reads, writes=writes)
                        else:
                            P.op("act", lambda e: e.copy(out=out_ap, in_=in_ap), reads=reads, writes=writes)

                    NG = 16
                    ckpt(2)
                    loadA(0)
                    for g in range(NG):
                        s, t0 = g // 8, (g % 8) * 512
                        j = g % 2
                        if g + 1 < NG:
                            loadA(g + 1)
                        X = xg[j]
                        for t in range(4):
                            P.op("act", lambda e, t=t: e.activation(out=junk[:], in_=X[:, t, :], func=AF.Square, accum_out=ss[:, t:t + 1]),
                                 reads=[bXg[j]], writes=[bJ, bSS])
                        rstd_from_ss(ss[:], rstd[:], DM, bSS, bRs, lnv[:], bLn)
                        for t in range(4):
                            P.op("dve", lambda e, t=t: e.tensor_scalar(out=xn[:, t, :], in0=X[:, t, :], scalar1=rstd[:, t:t + 1], scalar2=None,
                                                                       op0=ALU.mult),
                                 reads=[bXg[j], bRs], writes=[bXn])
                        for k in range(8):
                            pj = k % 2
                            P.op("pe", [(lambda e, t=t, k=k, pj=pj: e.transpose(psT[pj][:, t * 128:(t + 1) * 128], xn[:, t, k * 128:(k + 1) * 128], ident[:]))
                                        for t in range(4)], reads=[bXn, bId], writes=[bPT[pj]])
                            P.op("act", lambda e, k=k, pj=pj: e.activation(out=hT[:, k, :], in_=psT[pj][:], func=AF.Identity,
                                                                           scale=scA[s][:, k:k + 1], bias=biA[s][:, k:k + 1]),
                                 reads=[bPT[pj], bSc[s]], writes=[bHT])
                        ckpt(3)
                        cbs = [0, 128, 256, 384] + [768 + 128 * i for i in range(8)]
                        for ci, cb in enumerate(cbs):
                            pj = ci % 2
                            P.op("pe", [(lambda e, k=k, cb=cb, pj=pj: e.matmul(psZ[pj][:], lhsT=w_in[:, k, cb:cb + 128], rhs=hT[:, k, :],
                                                                              start=(k == 0), stop=(k == 7))) for k in range(8)],
                                 reads=[bW, bHT], writes=[bPZ[pj]])
                            evac(zst[j][:, ci, :], psZ[pj][:], [bPZ[pj]], [bZst[j]])
                        P.dma("sp", D["zT"][s].rearrange("c p t -> p c t")[:, :, t0:t0 + 512], zst[j][:], reads=[bZst[j]])
                        ckpt(4)
                        for t in range(4):
                            pj = t % 2
                            fns = []
                            for k in range(8):
                                fns.append(lambda e, k=k, t=t, pj=pj: e.matmul(psW[pj][:, 0:256], lhsT=hT[:, k, t * 128:(t + 1) * 128],
                                                                               rhs=w_in[:, k, 512:768], start=(k == 0), stop=(k == 7)))
                            for k in range(8):
                                fns.append(lambda e, k=k, t=t, pj=pj: e.matmul(psW[pj][:, 512:1024], lhsT=hT[:, k, t * 128:(t + 1) * 128],
                                                                               rhs=w_in[:, k, 1792:2304], start=(k == 0), stop=(k == 7)))
                            P.op("pe", fns, reads=[bW, bHT], writes=[bPW[pj]])
                            evac(vst[j][:, t, 0:256], psW[pj][:, 0:256], [bPW[pj]], [bVst[j]])
                            evac(vst[j][:, t, 256:768], psW[pj][:, 512:1024], [bPW[pj]], [bVst[j]])
                        P.dma("sp", D["vtok"][s, t0:t0 + 512, :].rearrange("(t p) c -> p t c", p=128), vst[j][:], reads=[bVst[j]])
                        ckpt(5)
                        for t in range(4):
                            pj = t % 2
                            fns = []
                            for k in range(8):
                                fns.append(lambda e, k=k, t=t, pj=pj: e.matmul(psW[pj][:, 0:384], lhsT=hT[:, k, t * 128:(t + 1) * 128],
                                                                               rhs=w_in[:, k, 2304:2688], start=(k == 0), stop=(k == 7)))
                            for k in range(8):
                                fns.append(lambda e, k=k, t=t, pj=pj: e.matmul(psW[pj][:, 512:768], lhsT=hT[:, k, t * 128:(t + 1) * 128],
                                                                               rhs=w_in[:, k, 2688:2944], start=(k == 0), stop=(k == 7)))
                            if "noMm" not in dbg:
                                P.op("pe", fns, reads=[bW, bHT], writes=[bPW[pj]])
                            if "noCp" not in dbg:
                                P.op("dve", lambda e, t=t, pj=pj: e.tensor_copy(out=cst[:, t, 0:384], in_=psW[pj][:, 0:384]), reads=[bPW[pj]], writes=[bCst])
                                P.op("dve", lambda e, t=t, pj=pj: e.tensor_copy(out=cst[:, t, 384:640], in_=psW[pj][:, 512:768]), reads=[bPW[pj]], writes=[bCst])
                            if "noSq" not in dbg:
                              P.op("act", lambda e, t=t, pj=pj: e.activation(out=junk[:, 0:384], in_=cst[:, t, 0:384], func=AF.Square,
                                                                           accum_out=ssq[:, t:t + 1]), reads=[bCst], writes=[bJ, bSq])
                            if "noSq" not in dbg:
                              P.op("act", lambda e, t=t, pj=pj: e.activation(out=junk[:, 384:640], in_=cst[:, t, 384:640], func=AF.Square,
                                                                           accum_out=sskv[:, t:t + 1]), reads=[bCst], writes=[bJ, bSq])
                        ckpt(51)
                        rstd_from_ss(ssq[:], rsq[:], 384, bSq, bRq, lnq[:], bLq)
                        rstd_from_ss(sskv[:], rskv[:], 256, bSq, bRq, lnk[:], bLq)
                        for t in range(4):
                            P.op("dve", lambda e, t=t: e.tensor_scalar(out=cqn[:, t, :], in0=cst[:, t, 0:384], scalar1=rsq[:, t:t + 1], scalar2=None,
                                                                       op0=ALU.mult), reads=[bCst, bRq], writes=[bCqn])
                            P.op("dve", lambda e, t=t: e.tensor_scalar(out=ckvn[:, t, :], in0=cst[:, t, 384:640], scalar1=rskv[:, t:t + 1], scalar2=None,
                                                                       op0=ALU.mult), reads=[bCst, bRq], writes=[bCkvn])
                        ckpt(53)
                        for kk in range(3):
                            pj = kk % 2
                            P.op("pe", [(lambda e, t=t, kk=kk, pj=pj: e.transpose(psT[pj][:, t * 128:(t + 1) * 128], cqn[:, t, kk * 128:(kk + 1) * 128], ident[:]))
                                        for t in range(4)], reads=[bCqn, bId], writes=[bPT[pj]])
                            P.op("act", lambda e, kk=kk, pj=pj: e.activation(out=cqT[:, kk, :], in_=psT[pj][:], func=AF.Identity, scale=gq[:, kk:kk + 1], bias=zcol[:, 0:1]),
                                 reads=[bPT[pj], bG, bZc], writes=[bCqT])
                        ckpt(54)
                        for kk in range(2):
                            pj = (kk + 1) % 2
                            P.op("pe", [(lambda e, t=t, kk=kk, pj=pj: e.transpose(psT[pj][:, t * 128:(t + 1) * 128], ckvn[:, t, kk * 128:(kk + 1) * 128], ident[:]))
                                        for t in range(4)], reads=[bCkvn, bId], writes=[bPT[pj]])
                            P.op("act", lambda e, kk=kk, pj=pj: e.activation(out=ckvT[:, kk, :], in_=psT[pj][:], func=AF.Identity, scale=gkv[:, kk:kk + 1], bias=zcol[:, 0:1]),
                                 reads=[bPT[pj], bG, bZc], writes=[bCkvT])
                        ckpt(6)
                        for jj in range(2):
                            P.op("pe", [(lambda e, kk=kk, jj=jj: e.matmul(psZ[0][:], lhsT=w_uq[:, kk, jj * 128:(jj + 1) * 128], rhs=cqT[:, kk, :],
                                                                          start=(kk == 0), stop=(kk == 2))) for kk in range(3)],
                                 reads=[bUq, bCqT], writes=[bPZ[0]])
                            evac(qnst[j][:, jj, :], psZ[0][:], [bPZ[0]], [bQn[j]])
                            P.op("pe", [(lambda e, kk=kk, jj=jj: e.matmul(psZ[1][:], lhsT=w_ukv[:, kk, jj * 128:(jj + 1) * 128], rhs=ckvT[:, kk, :],
                                                                          start=(kk == 0), stop=(kk == 1))) for kk in range(2)],
                                 reads=[bUkv, bCkvT], writes=[bPZ[1]])
                            evac(knst[j][:, jj, :], psZ[1][:], [bPZ[1]], [bKn[j]])
                        P.op("pe", [(lambda e, kk=kk: e.matmul(psZ[0][:], lhsT=w_uq[:, kk, 256:384], rhs=cqT[:, kk, :], start=(kk == 0), stop=(kk == 2)))
                                    for kk in range(3)], reads=[bUq, bCqT], writes=[bPZ[0]])
                        P.op("pe", [(lambda e, kk=kk: e.matmul(psZ[1][:], lhsT=w_uqr[:, kk, :], rhs=cqT[:, kk, :], start=(kk == 0), stop=(kk == 2)))
                                    for kk in range(3)], reads=[bUqr, bCqT], writes=[bPZ[1]])
                        P.op("dve", lambda e: e.tensor_tensor(out=t1[:], in0=psZ[0][:], in1=cosg[j][:], op=ALU.mult), reads=[bPZ[0], bCS[j]], writes=[bT1])
                        P.op("dve", lambda e: e.tensor_tensor(out=t2[:], in0=psZ[1][:], in1=sing[j][:], op=ALU.mult), reads=[bPZ[1], bCS[j]], writes=[bT2])
                        P.op("pool", lambda e: e.tensor_tensor(out=qrst[j][:], in0=t1[:], in1=t2[:], op=ALU.add), reads=[bT1, bT2], writes=[bQr[j]])
                        ckpt(7)
                        P.op("pe", [(lambda e, k=k: e.matmul(psZ[0][0:32, :], lhsT=w_in[:, k, 2944:2976], rhs=hT[:, k, :], start=(k == 0), stop=(k == 7)))
                                    for k in range(8)], reads=[bW, bHT], writes=[bPZ[0]])
                        P.op("pe", [(lambda e, k=k: e.matmul(psZ[1][0:32, :], lhsT=w_rot[:, k, :], rhs=hT[:, k, :], start=(k == 0), stop=(k == 7)))
                                    for k in range(8)], reads=[bWr, bHT], writes=[bPZ[1]])
                        P.op("dve", lambda e: e.tensor_tensor(out=t1[0:32, :], in0=psZ[0][0:32, :], in1=cosg[j][0:32, :], op=ALU.mult),
                             reads=[bPZ[0], bCS[j]], writes=[bT1])
                        P.op("dve", lambda e: e.tensor_tensor(out=t2[0:32, :], in0=psZ[1][0:32, :], in1=sing[j][0:32, :], op=ALU.mult),
                             reads=[bPZ[1], bCS[j]], writes=[bT2])
                        P.op("pool", lambda e: e.tensor_tensor(out=krst[j][:], in0=t1[0:32, :], in1=t2[0:32, :], op=ALU.add),
                             reads=[bT1, bT2], writes=[bKr[j]])
                        ckpt(8)
                        for t in range(4):
                            pj = t % 2
                            P.op("pe", [(lambda e, kk=kk, t=t, pj=pj: e.matmul(psW[pj][:, 0:256], lhsT=ckvT[:, kk, t * 128:(t + 1) * 128],
                                                                               rhs=w_ukv[:, kk, 256:512], start=(kk == 0), stop=(kk == 1))) for kk in range(2)],
                                 reads=[bUkv, bCkvT], writes=[bPW[pj]])
                            evac(vcst[j][:, t, :], psW[pj][:, 0:256], [bPW[pj]], [bVc[j]])
                        P.dma("sp", D["qn"][s].rearrange("j p t -> p j t")[:, :, t0:t0 + 512], qnst[j][:], reads=[bQn[j]])
                        P.dma("sp", D["kn"][s].rearrange("j p t -> p j t")[:, :, t0:t0 + 512], knst[j][:], reads=[bKn[j]])
                        P.dma("sp", D["qr"][s, :, t0:t0 + 512], qrst[j][:], reads=[bQr[j]])
                        P.dma("sp", D["kr"][s, :, t0:t0 + 512], krst[j][:], reads=[bKr[j]])
                        P.dma("sp", D["vc"][s, t0:t0 + 512, :].rearrange("(t p) c -> p t c", p=128), vcst[j][:], reads=[bVc[j]])
                        ckpt(9)
                P.barrier()
                if "stopA" in dbg:
                    break

                SC_AB = 0.125
                SC_C = 96.0 ** -0.5
                with ExitStack() as c:
                    naX = SB(c, "naX", [128, 4 * 14 * 64], BF16)
                    t5X = SB(c, "t5X", [128, 8 * 3 * 384], BF16)
                    pens = SB(c, "pens", [2, NPAT * 256], BF16)
                    A2 = SB(c, "A2", [2, 128], BF16)
                    psS = [PS(c, "psS%d" % i, [128, 512]) for i in range(2)]
                    psO = [PS(c, "psO%d" % i, [128, 512]) for i in range(2)]
                    E = [SB(c, "E%d" % i, [128, 384], BF16) for i in range(3)]
                    bE = [Buf() for _ in range(3)]
                    bPS = [Buf(), Buf()]
                    bPO = [Buf(), Buf()]
                    bNaX, bT5X, bPen = Buf(), Buf(), Buf()
                    P.dmas("pool", [(pens[:], D["pens"]), (A2[:], D["A2"])], writes=[bPen])
                    with ExitStack() as c2:
                        stg = SB(c2, "tstg", [128, 8 * 3 * 384], F32)
                        msk = SB(c2, "tmsk", [128, 14 * 64], F32)
                        bStg, bMsk = Buf(), Buf()
                        P.dma("sp", stg[:, 0:3584], D["naG"][l], writes=[bStg])
                        P.dma("sp", msk[:], D["nacm"], writes=[bMsk])
                        P.op("act", lambda e: e.activation(out=stg[:, 0:3584], in_=stg[:, 0:3584], func=AF.Exp), reads=[bStg], writes=[bStg])
                        for h in range(4):
                            P.op("dve", lambda e, h=h: e.tensor_tensor(out=naX[:, h * 896:(h + 1) * 896], in0=stg[:, h * 896:(h + 1) * 896], in1=msk[:],
                                                                       op=ALU.mult), reads=[bStg, bMsk], writes=[bNaX])
                        P.dma("sp", stg[:], D["t5G"], writes=[bStg])
                        P.dma("sp", msk[:, 0:384], D["t5m"], writes=[bMsk])
                        for hp in range(24):
                            P.op("act", lambda e, hp=hp: e.activation(out=stg[:, hp * 384:(hp + 1) * 384], in_=stg[:, hp * 384:(hp + 1) * 384], func=AF.Exp),
                                 reads=[bStg], writes=[bStg])
                            P.op("dve", lambda e, hp=hp: e.tensor_tensor(out=t5X[:, hp * 384:(hp + 1) * 384], in0=stg[:, hp * 384:(hp + 1) * 384],
                                                                         in1=msk[:, 0:384], op=ALU.mult), reads=[bStg, bMsk], writes=[bT5X])
                        P.barrier()
                    ucnt = [0]

                    with ExitStack() as c2:
                        qT = [SB(c2, "naq%d" % i, [128, S], BF16) for i in range(2)]
                        kT = [SB(c2, "nak%d" % i, [128, S], BF16) for i in range(2)]
                        va = [SB(c2, "nav%d" % i, [128, 32, 2, 128], BF16) for i in range(2)]
                        ot = [SB(c2, "naot%d" % i, [128, S], BF16) for i in range(2)]
                        rec = SB(c2, "narec", [64, 256], F32)
                        bQ = [Buf(), Buf()]
                        bV = [Buf(), Buf()]
                        bOt = [Buf(), Buf()]
                        bRec = Buf()
                        for i in range(2):
                            P.op("pool", lambda e, i=i: e.memset(va[i][:, :, :, 64:128], 1.0), writes=[bV[i]])

                        items = [(s, hp) for s in range(2) for hp in range(2)]

                        def loadNA(n):
                            s, hp = items[n]
                            i = n % 2
                            P.dmas("sp", [(qT[i][:], D["zT"][s, hp]), (kT[i][:], D["zT"][s, 2 + hp])], writes=[bQ[i]])
                            src = D["vtok"][s].rearrange("(n p) c -> p n c", p=128)
                            P.dmas("sp", [(va[i][:, n0:n0 + 16, hh, 0:64], src[:, n0:n0 + 16, hp * 128 + hh * 64:hp * 128 + hh * 64 + 64])
                                          for n0 in range(0, 32, 16) for hh in range(2)], writes=[bV[i]])

                        loadNA(0)
                        for n, (s, hp) in enumerate(items):
                            i = n % 2
                            if n + 1 < len(items):
                                loadNA(n + 1)
                            for hh in range(2):
                                h = 2 * hp + hh
                                pr = slice(hh * 64, hh * 64 + 64)
                                for J in range(16):
                                    units = NA_PLAN[J]
                                    oj = J % 2
                                    for ui, (ci, di0, pid) in enumerate(units):
                                        u = ucnt[0]
                                        ucnt[0] += 1
                                        sj, ej = u % 2, u % 3
                                        fns = [lambda e, ci=ci, sj=sj: e.matmul(psS[sj][:, 0:256], lhsT=kT[i][pr, ci * 128:(ci + 1) * 128],
                                                                                rhs=qT[i][pr, J * 256:(J + 1) * 256], start=True, stop=(pid < 0))]
                                        if pid >= 0:
                                            fns.append(lambda e, pid=pid, sj=sj: e.matmul(psS[sj][:, 0:256], lhsT=A2[:, :], rhs=pens[:, pid * 256:(pid + 1) * 256],
                                                                                          start=False, stop=True))
                                        P.op("pe", fns, reads=[bQ[i], bPen], writes=[bPS[sj]])
                                        P.op("act", lambda e, sj=sj, ej=ej: e.activation(out=E[ej][:, 0:256], in_=psS[sj][:, 0:256], func=AF.Exp, scale=SC_AB),
                                             reads=[bPS[sj]], writes=[bE[ej]])
                                        xo = h * 896 + di0 * 64
                                        P.op("dve", lambda e, ej=ej, xo=xo: e.tensor_tensor(out=E[ej][:, 0:256], in0=E[ej][:, 0:256], in1=naX[:, xo:xo + 256],
                                                                                            op=ALU.mult), reads=[bE[ej], bNaX], writes=[bE[ej]])
                                        P.op("pe", lambda e, ci=ci, ej=ej, ui=ui: e.matmul(psO[oj][:, 0:256], lhsT=va[i][:, ci, hh, :], rhs=E[ej][:, 0:256],
                                                                                           start=(ui == 0), stop=(ui == len(units) - 1)),
                                             reads=[bV[i], bE[ej]], writes=[bPO[oj]])
                                    P.op("dve", lambda e: e.reciprocal(out=rec[:], in_=psO[oj][64:128, 0:256]), reads=[bPO[oj]], writes=[bRec])
                                    P.op("dve", lambda e: e.tensor_tensor(out=ot[i][pr, J * 256:(J + 1) * 256], in0=psO[oj][0:64, 0:256], in1=rec[:], op=ALU.mult),
                                         reads=[bPO[oj], bRec], writes=[bOt[i]])
                            P.dma("sp", D["OT"][s, hp * 128:(hp + 1) * 128, :], ot[i][:], reads=[bOt[i]])
                        P.barrier()

                    with ExitStack() as c2:
                        qT = [SB(c2, "dq%d" % i, [128, S], BF16) for i in range(2)]
                        kT = [SB(c2, "dk%d" % i, [128, S], BF16) for i in range(2)]
                        qo = [SB(c2, "dqo%d" % i, [128, S], BF16) for i in range(2)]
                        ko = [SB(c2, "dko%d" % i, [128, S], BF16) for i in range(2)]
                        va = [SB(c2, "dv%d" % i, [128, 32, 2, 128], BF16) for i in range(2)]
                        oacc = [SB(c2, "oacc%d" % i, [128, S], F32) for i in range(2)]
                        ot = [SB(c2, "dot%d" % i, [128, S], BF16) for i in range(2)]
                        rec = SB(c2, "drec", [64, 1024], F32)
                        bQ = [Buf(), Buf()]
                        bQo = [Buf(), Buf()]
                        bV = [Buf(), Buf()]
                        bAcc = [Buf(), Buf()]
                        bOt = [Buf(), Buf()]
                        bRec = Buf()
                        for i in range(2):
                            P.op("pool", lambda e, i=i: e.memset(va[i][:, :, :, 64:128], 1.0), writes=[bV[i]])
                        items = [(s, hp) for s in range(2) for hp in range(4)]
                        pats = (1, 4, 16)
                        vcount = [0]
                        ocount = [0]

                        def loadQK(n):
                            s, hp = items[n]
                            i = n % 2
                            P.dmas("sp", [(qT[i][:], D["zT"][s, 4 + hp]), (kT[i][:], D["zT"][s, 8 + hp])], writes=[bQ[i]])

                        def loadV(n, pi):
                            s, hp = items[n]
                            d = pats[pi]
                            vi = (n * 3 + pi) % 2
                            nchunk = 32 // d
                            src = D["vtok"][s].rearrange("(n p r) c -> p r n c", p=128, r=d)
                            pairs = []
                            for r in range(d):
                                for n0 in range(0, nchunk, 16):
                                    n1 = min(n0 + 16, nchunk)
                                    for hh in range(2):
                                        cb0 = 256 + hp * 128 + hh * 64
                                        pairs.append((va[vi][:, r * nchunk + n0:r * nchunk + n1, hh, 0:64], src[:, r, n0:n1, cb0:cb0 + 64]))
                            P.dmas("sp", pairs, writes=[bV[vi]])

                        loadQK(0)
                        loadV(0, 0)
                        for n, (s, hp) in enumerate(items):
                            i = n % 2
                            if n + 1 < len(items):
                                loadQK(n + 1)
                            for pi, d in enumerate(pats):
                                vi = (n * 3 + pi) % 2
                                if pi < 2:
                                    loadV(n, pi + 1)
                                elif n + 1 < len(items):
                                    loadV(n + 1, 0)
                                nchunk = 32 // d
                                if d == 1:
                                    Q, K, bQQ = qT[i], kT[i], bQ[i]
                                else:
                                    oi = ocount[0] % 2
                                    ocount[0] += 1
                                    Q, K, bQQ = qo[oi], ko[oi], bQo[oi]
                                    P.op("pool", lambda e, Q=Q, d=d: e.tensor_copy(out=Q[:].rearrange("p (r m) -> p r m", r=d),
                                                                                   in_=qT[i][:].rearrange("p (m r) -> p r m", r=d)),
                                         reads=[bQ[i]], writes=[bQQ])
                                    P.op("pool", lambda e, K=K, d=d: e.tensor_copy(out=K[:].rearrange("p (r m) -> p r m", r=d),
                                                                                   in_=kT[i][:].rearrange("p (m r) -> p r m", r=d)),
                                         reads=[bQ[i]], writes=[bQQ])
                                for hh in range(2):
                                    h = 2 * hp + hh
                                    pr = slice(hh * 64, hh * 64 + 64)
                                    accv = oacc[hh][:].rearrange("p (m r) -> p r m", r=d)
                                    for r in range(d):
                                        for jb in range(nchunk):
                                            u = ucnt[0]
                                            ucnt[0] += 1
                                            sj, ej, oj = u % 2, u % 3, u % 2
                                            base = (r * nchunk + jb) * 128
                                            pcs = [pc for pc in range(3) if 0 <= jb + pc - 1 < nchunk]
                                            c0, c1 = pcs[0] * 128, (pcs[-1] + 1) * 128
                                            P.op("pe", [(lambda e, pc=pc, sj=sj: e.matmul(
                                                psS[sj][:, pc * 128:(pc + 1) * 128],
                                                lhsT=K[pr, base + (pc - 1) * 128:base + pc * 128],
                                                rhs=Q[pr, base:base + 128], start=True, stop=True)) for pc in pcs],
                                                reads=[bQQ], writes=[bPS[sj]])
                                            P.op("act", lambda e, sj=sj, ej=ej, c0=c0, c1=c1: e.activation(out=E[ej][:, c0:c1], in_=psS[sj][:, c0:c1],
                                                                                                            func=AF.Exp, scale=SC_AB),
                                                 reads=[bPS[sj]], writes=[bE[ej]])
                                            xo = (h * 3 + pi) * 384
                                            P.op("dve", lambda e, ej=ej, c0=c0, c1=c1, xo=xo: e.tensor_tensor(out=E[ej][:, c0:c1], in0=E[ej][:, c0:c1],
                                                                                                               in1=t5X[:, xo + c0:xo + c1], op=ALU.mult),
                                                 reads=[bE[ej], bT5X], writes=[bE[ej]])
                                            P.op("pe", [(lambda e, pc=pc, ej=ej, oj=oj, q=q: e.matmul(
                                                psO[oj][:, 0:128], lhsT=va[vi][:, r * nchunk + jb + pc - 1, hh, :],
                                                rhs=E[ej][:, pc * 128:(pc + 1) * 128], start=(q == 0), stop=(q == len(pcs) - 1)))
                                                for q, pc in enumerate(pcs)], reads=[bV[vi], bE[ej]], writes=[bPO[oj]])
                                            dst = accv[:, r, jb * 128:(jb + 1) * 128]
                                            if pi == 0:
                                                P.op("dve", lambda e, dst=dst, oj=oj: e.tensor_copy(out=dst, in_=psO[oj][:, 0:128]),
                                                     reads=[bPO[oj]], writes=[bAcc[hh]])
                                            else:
                                                P.op("dve", lambda e, dst=dst, oj=oj: e.tensor_tensor(out=dst, in0=psO[oj][:, 0:128], in1=dst, op=ALU.add),
                                                     reads=[bPO[oj]], writes=[bAcc[hh]])
                            for hh in range(2):
                                pr = slice(hh * 64, hh * 64 + 64)
                                for q4 in range(4):
                                    cs = slice(q4 * 1024, (q4 + 1) * 1024)
                                    P.op("dve", lambda e, cs=cs, hh=hh: e.reciprocal(out=rec[:], in_=oacc[hh][64:128, cs]), reads=[bAcc[hh]], writes=[bRec])
                                    P.op("dve", lambda e, cs=cs, hh=hh, pr=pr: e.tensor_tensor(out=ot[i][pr, cs], in0=oacc[hh][0:64, cs], in1=rec[:], op=ALU.mult),
                                         reads=[bAcc[hh], bRec], writes=[bOt[i]])
                            P.dma("sp", D["OT"][s, 256 + hp * 128:256 + (hp + 1) * 128, :], ot[i][:], reads=[bOt[i]])
                        P.barrier()
                P.barrier()

                with ExitStack() as c:
                    psS = [PS(c, "psSc%d" % i, [128, 1024]) for i in range(2)]
                    psO = [PS(c, "psOc%d" % i, [128, 1024]) for i in range(2)]
                    E = [SB(c, "Ec%d" % i, [128, 1024], BF16) for i in range(3)]
                    qT = [SB(c, "cq%d" % i, [96, S], BF16) for i in range(2)]
                    kT = [SB(c, "ck%d" % i, [96, S], BF16) for i in range(2)]
                    va = [SB(c, "cv%d" % i, [128, 32, 128], BF16) for i in range(2)]
                    ot = [SB(c, "cot%d" % i, [128, S], BF16) for i in range(2)]
                    rec = SB(c, "crec", [64, 1024], F32)
                    bE = [Buf() for _ in range(3)]
                    bPS = [Buf(), Buf()]
                    bPO = [Buf(), Buf()]
                    bQ = [Buf(), Buf()]
                    bV = [Buf(), Buf()]
                    bOt = [Buf(), Buf()]
                    bRec = Buf()
                    for i in range(2):
                        P.op("pool", lambda e, i=i: e.memset(va[i][:, :, 64:128], 1.0), writes=[bV[i]])
                    items = [(s, h) for s in range(2) for h in range(4)]

                    def loadC(n):
                        s, h = items[n]
                        i = n % 2
                        P.dmas("sp", [(qT[i][0:64, :], D["qn"][s, h // 2, (h % 2) * 64:(h % 2) * 64 + 64, :]),
                                      (qT[i][64:96, :], D["qr"][s, h * 32:(h + 1) * 32, :]),
                                      (kT[i][0:64, :], D["kn"][s, h // 2, (h % 2) * 64:(h % 2) * 64 + 64, :]),
                                      (kT[i][64:96, :], D["kr"][s, :, :])], writes=[bQ[i]])
                        src = D["vc"][s].rearrange("(n p) c -> p n c", p=128)
                        P.dmas("sp", [(va[i][:, n0:n0 + 8, 0:64], src[:, n0:n0 + 8, h * 64:(h + 1) * 64]) for n0 in range(0, 32, 8)], writes=[bV[i]])

                    u = 0
                    loadC(0)
                    for n, (s, h) in enumerate(items):
                        i = n % 2
                        oi = (n // 2) % 2
                        if n + 1 < len(items):
                            loadC(n + 1)
                        pr = slice((h % 2) * 64, (h % 2) * 64 + 64)
                        for qb in range(4):
                            oj = (n * 4 + qb) % 2
                            for kc in range(32):
                                sj, ej = u % 2, u % 3
                                u += 1
                                P.op("pe", [(lambda e, hf=hf, kc=kc, sj=sj: e.matmul(psS[sj][:, hf * 512:(hf + 1) * 512], lhsT=kT[i][0:96, kc * 128:(kc + 1) * 128],
                                                                                     rhs=qT[i][0:96, qb * 1024 + hf * 512:qb * 1024 + (hf + 1) * 512],
                                                                                     start=True, stop=True)) for hf in range(2)],
                                     reads=[bQ[i]], writes=[bPS[sj]])
                                P.op("act", lambda e, sj=sj, ej=ej: e.activation(out=E[ej][:], in_=psS[sj][:], func=AF.Exp, scale=SC_C),
                                     reads=[bPS[sj]], writes=[bE[ej]])
                                P.op("pe", [(lambda e, hf=hf, kc=kc, ej=ej: e.matmul(psO[oj][:, hf * 512:(hf + 1) * 512], lhsT=va[i][:, kc, :],
                                                                                     rhs=E[ej][:, hf * 512:(hf + 1) * 512], start=(kc == 0), stop=(kc == 31)))
                                            for hf in range(2)], reads=[bV[i], bE[ej]], writes=[bPO[oj]])
                            cs = slice(qb * 1024, (qb + 1) * 1024)
                            P.op("dve", lambda e, oj=oj: e.reciprocal(out=rec[:], in_=psO[oj][64:128, :]), reads=[bPO[oj]], writes=[bRec])
                            P.op("dve", lambda e, oj=oj, cs=cs: e.tensor_tensor(out=ot[oi][pr, cs], in0=psO[oj][0:64, :], in1=rec[:], op=ALU.mult),
                                 reads=[bPO[oj], bRec], writes=[bOt[oi]])
                        if h % 2 == 1:
                            P.dma("sp", D["OT"][s, 768 + (h // 2) * 128:768 + (h // 2 + 1) * 128, :], ot[oi][:], reads=[bOt[oi]])
                P.barrier()
                if "stopB" in dbg:
                    break

                with ExitStack() as c:
                    w_out = SB(c, "w_out", [128, 8, DM], BF16)
                    w_dn = SB(c, "w_dn", [128, NCH, DM], BF16)
                    wu = [SB(c, "wu%d" % i, [128, 8, 256], BF16) for i in range(3)]
                    cw = SB(c, "convw", [128, NCH, 3], F32)
                    cb = SB(c, "convb", [128, NCH], F32)
                    gpre = SB(c, "gpreF", [128, 8], F32)
                    scF = [SB(c, "scF%d" % s, [128, 8], F32) for s in range(2)]
                    biF = [SB(c, "biF%d" % s, [128, 8], F32) for s in range(2)]
                    gate1 = SB(c, "gate1", [128, DM], F32)
                    gate2 = SB(c, "gate2", [128, DM], F32)
                    gtmp = SB(c, "gtmp", [128, DM], F32)
                    xg = [SB(c, "xgC%d" % i, [128, 4, DM], F32) for i in range(2)]
                    otg = [SB(c, "otg%d" % i, [128, 8, 512], BF16) for i in range(2)]
                    h2T = [SB(c, "h2T%d" % i, [128, 8, 512], BF16) for i in range(2)]
                    halo = SB(c, "halo", [128, 8, 2], BF16)
                    edge = [SB(c, "edge%d" % i, [128, 8, 2], BF16) for i in range(3)]
                    actT = SB(c, "actT", [128, NCH, 512], BF16)
                    xn2 = SB(c, "xn2", [128, 4, DM], BF16)
                    junk = SB(c, "junkC", [128, DM], BF16)
                    tmp = SB(c, "tmpC", [128, DM], F32)
                    gcb = [SB(c, "gc%d" % i, [128, 512], F32) for i in range(2)]
                    geb = [SB(c, "ge%d" % i, [128, 512], F32) for i in range(2)]
                    ssv = SB(c, "ssC", [128, 16], F32)
                    lnc = SB(c, "lnC", [128, 16], F32)
                    rsc = SB(c, "rsC", [128, 16], F32)
                    psY = PS(c, "psY", [128, 1024])
                    psT = PS(c, "psTC", [128, 512], BF16)
                    psA = [PS(c, "psA%d" % i, [128, 512]) for i in range(2)]
                    psG = [PS(c, "psG%d" % i, [128, 512]) for i in range(2)]
                    psH = PS(c, "psH", [128, 2])
                    bWo, bWd, bCw, bGp = Buf(), Buf(), Buf(), Buf()
                    bWu = [Buf() for _ in range(3)]
                    bScF = [Buf(), Buf()]
                    bGate, bGt = Buf(), Buf()
                    bXg = [Buf(), Buf()]
                    bOtg = [Buf(), Buf()]
                    bH2 = [Buf(), Buf()]
                    bEdge = [Buf(), Buf(), Buf()]
                    bHalo, bAct, bXn2, bJ, bTmp = Buf(), Buf(), Buf(), Buf(), Buf()
                    bGc = [Buf(), Buf()]
                    bGe = [Buf(), Buf()]
                    bSs, bLn, bRs = Buf(), Buf(), Buf()
                    bPY, bPT, bPH = Buf(), Buf(), Buf()
                    bPA = [Buf(), Buf()]
                    bPG = [Buf(), Buf()]

                    P.dmas("pool", [(w_out[:, k, :], D["w_out"][l, :, k, :]) for k in range(8)], writes=[bWo])
                    P.dmas("pool", [(w_dn[:, c0:c0 + 2, :], D["w_down"][l, :, c0:c0 + 2, :]) for c0 in range(0, NCH, 2)], writes=[bWd])
                    P.dmas("sp", [(cw[:], D["convwT"][l]), (cb[:], D["convbT"][l])], writes=[bCw])
                    P.dma("sp", gpre[:], D["gpre_ffnT"][l], writes=[bGp])
                    for s in range(2):
                        load_vec8(c, biF[s][:], l, s, 3, bScF[s])
                        load_vec8(c, scF[s][:], l, s, 4, bScF[s])
                        P.op("dve", lambda e, s=s: e.scalar_tensor_tensor(out=scF[s][:], in0=scF[s][:], scalar=1.0, in1=gpre[:], op0=ALU.add, op1=ALU.mult),
                             reads=[bScF[s], bGp], writes=[bScF[s]])

                    def load_gates(s):
                        for gt, jm, gp in ((gate1, 2, "gpost_mix"), (gate2, 5, "gpost_ffn")):
                            P.dma("sp", gt[:], D["modD"][l, s:s + 1, jm * 1024:(jm + 1) * 1024].partition_broadcast(128), writes=[bGate])
                            P.dma("sp", gtmp[:], D[gp][l:l + 1, :].partition_broadcast(128), writes=[bGt])
                            P.op("dve", lambda e, gt=gt: e.tensor_tensor(out=gt[:], in0=gt[:], in1=gtmp[:], op=ALU.mult), reads=[bGate, bGt], writes=[bGate])

                    def loadC1(g):
                        s, t0 = g // 8, (g % 8) * 512
                        j = g % 2
                        P.dma("sp", xg[j][:], x_src[s, t0:t0 + 512, :].rearrange("(t p) d -> p t d", p=128), writes=[bXg[j]])
                        P.dma("sp", otg[j][:], D["OT"][s, :, t0:t0 + 512].rearrange("(k p) t -> p k t", p=128), writes=[bOtg[j]])

                    def norm_resid(X, t, gate, col, bX):
                        P.op("act", lambda e: e.activation(out=junk[:], in_=psY[:], func=AF.Square, accum_out=ssv[:, col:col + 1]),
                             reads=[bPY], writes=[bJ, bSs])
                        P.op("act", lambda e: e.activation(out=lnc[:, col:col + 1], in_=ssv[:, col:col + 1], func=AF.Ln, scale=1.0 / DM, bias=EPS),
                             reads=[bSs], writes=[bLn])
                        P.op("act", lambda e: e.activation(out=rsc[:, col:col + 1], in_=lnc[:, col:col + 1], func=AF.Exp, scale=-0.5),
                             reads=[bLn], writes=[bRs])
                        P.op("dve", lambda e: e.scalar_tensor_tensor(out=tmp[:], in0=psY[:], scalar=rsc[:, col:col + 1], in1=gate[:], op0=ALU.mult, op1=ALU.mult),
                             reads=[bPY, bRs, bGate], writes=[bTmp])
                        P.op("pool", lambda e: e.tensor_tensor(out=X[:, t, :], in0=X[:, t, :], in1=tmp[:], op=ALU.add), reads=[bTmp, bX], writes=[bX])

                    def stage1(g):
                        s, t0 = g // 8, (g % 8) * 512
                        j = g % 2
                        X = xg[j]
                        for t in range(4):
                            fns = []
                            for hf in range(2):
                                for k in range(8):
                                    fns.append(lambda e, k=k, hf=hf, t=t: e.matmul(psY[:, hf * 512:(hf + 1) * 512], lhsT=otg[j][:, k, t * 128:(t + 1) * 128],
                                                                                   rhs=w_out[:, k, hf * 512:(hf + 1) * 512], start=(k == 0), stop=(k == 7)))
                            P.op("pe", fns, reads=[bOtg[j], bWo], writes=[bPY])
                            norm_resid(X, t, gate1, t, bXg[j])
                            P.op("act", lambda e, t=t: e.activation(out=junk[:], in_=X[:, t, :], func=AF.Square, accum_out=ssv[:, 4 + t:5 + t]),
                                 reads=[bXg[j]], writes=[bJ, bSs])
                        P.op("act", lambda e: e.activation(out=lnc[:, 4:8], in_=ssv[:, 4:8], func=AF.Ln, scale=1.0 / DM, bias=EPS), reads=[bSs], writes=[bLn])
                        P.op("act", lambda e: e.activation(out=rsc[:, 4:8], in_=lnc[:, 4:8], func=AF.Exp, scale=-0.5), reads=[bLn], writes=[bRs])
                        for t in range(4):
                            P.op("dve", lambda e, t=t: e.tensor_scalar(out=xn2[:, t, :], in0=X[:, t, :], scalar1=rsc[:, 4 + t:5 + t], scalar2=None, op0=ALU.mult),
                                 reads=[bXg[j], bRs], writes=[bXn2])
                        for k in range(8):
                            P.op("pe", [(lambda e, t=t, k=k: e.transpose(psT[:, t * 128:(t + 1) * 128], xn2[:, t, k * 128:(k + 1) * 128], ident[:]))
                                        for t in range(4)], reads=[bXn2, bId], writes=[bPT])
                            P.op("act", lambda e, k=k: e.activation(out=h2T[j][:, k, :], in_=psT[:], func=AF.Identity,
                                                                    scale=scF[s][:, k:k + 1], bias=biF[s][:, k:k + 1]),
                                 reads=[bPT, bScF[s]], writes=[bH2[j]])
                        ej3 = g % 3
                        P.op("pool", lambda e: e.tensor_copy(out=edge[ej3][:, :, 0:1], in_=h2T[j][:, :, 0:1]), reads=[bH2[j]], writes=[bEdge[ej3]])
                        P.op("pool", lambda e: e.tensor_copy(out=edge[ej3][:, :, 1:2], in_=h2T[j][:, :, 511:512]), reads=[bH2[j]], writes=[bEdge[ej3]])

                    wcount = [0]

                    def load_wu(ci):
                        wi = wcount[0] % 3
                        wcount[0] += 1
                        P.dma("pool", wu[wi][:], D["w_up"][l, ci], writes=[bWu[wi]])
                        return wi

                    def stage2(g, have_next):
                        s, t0 = g // 8, (g % 8) * 512
                        j = g % 2
                        X = xg[j]
                        if t0 > 0:
                            P.op("pool", lambda e: e.tensor_copy(out=halo[:, :, 0:1], in_=edge[(g - 1) % 3][:, :, 1:2]), reads=[bEdge[(g - 1) % 3]], writes=[bHalo])
                        else:
                            P.op("pool", lambda e: e.memset(halo[:, :, 0:1], 0.0), writes=[bHalo])
                        if t0 + 512 < S:
                            assert have_next
                            P.op("pool", lambda e: e.tensor_copy(out=halo[:, :, 1:2], in_=edge[(g + 1) % 3][:, :, 0:1]), reads=[bEdge[(g + 1) % 3]], writes=[bHalo])
                        else:
                            P.op("pool", lambda e: e.memset(halo[:, :, 1:2], 0.0), writes=[bHalo])
                        pend = [load_wu(0), load_wu(1)]
                        for ci in range(NCH):
                            wi = pend.pop(0)
                            if ci + 2 < NCH:
                                pend.append(load_wu(ci + 2))
                            pj = ci % 2
                            W = wu[wi]
                            P.op("pe", [(lambda e, k=k, W=W, pj=pj: e.matmul(psA[pj][:], lhsT=W[:, k, 0:128], rhs=h2T[j][:, k, :], start=(k == 0), stop=(k == 7)))
                                        for k in range(8)], reads=[bWu[wi], bH2[j]], writes=[bPA[pj]])
                            P.op("pe", [(lambda e, k=k, W=W, pj=pj: e.matmul(psG[pj][:], lhsT=W[:, k, 128:256], rhs=h2T[j][:, k, :], start=(k == 0), stop=(k == 7)))
                                        for k in range(8)], reads=[bWu[wi], bH2[j]], writes=[bPG[pj]])
                            P.op("pe", [(lambda e, k=k, W=W: e.matmul(psH[:], lhsT=W[:, k, 128:256], rhs=halo[:, k, :], start=(k == 0), stop=(k == 7)))
                                        for k in range(8)], reads=[bWu[wi], bHalo], writes=[bPH])
                            gc, ge = gcb[pj], geb[pj]
                            P.op("act", lambda e, ci=ci, pj=pj, gc=gc: e.activation(out=gc[:], in_=psG[pj][:], func=AF.Identity, scale=cw[:, ci, 1:2], bias=cb[:, ci:ci + 1]),
                                 reads=[bPG[pj], bCw], writes=[bGc[pj]])
                            P.op("dve", lambda e, ci=ci, pj=pj, gc=gc: e.scalar_tensor_tensor(out=gc[:, 1:512], in0=psG[pj][:, 0:511], scalar=cw[:, ci, 0:1],
                                                                                              in1=gc[:, 1:512], op0=ALU.mult, op1=ALU.add),
                                 reads=[bPG[pj], bCw, bGc[pj]], writes=[bGc[pj]])
                            P.op("dve", lambda e, ci=ci, pj=pj, gc=gc: e.scalar_tensor_tensor(out=gc[:, 0:511], in0=psG[pj][:, 1:512], scalar=cw[:, ci, 2:3],
                                                                                              in1=gc[:, 0:511], op0=ALU.mult, op1=ALU.add),
                                 reads=[bPG[pj], bCw, bGc[pj]], writes=[bGc[pj]])
                            P.op("dve", lambda e, ci=ci, gc=gc: e.scalar_tensor_tensor(out=gc[:, 0:1], in0=psH[:, 0:1], scalar=cw[:, ci, 0:1], in1=gc[:, 0:1],
                                                                                       op0=ALU.mult, op1=ALU.add), reads=[bPH, bCw, bGc[pj]], writes=[bGc[pj]])
                            P.op("dve", lambda e, ci=ci, gc=gc: e.scalar_tensor_tensor(out=gc[:, 511:512], in0=psH[:, 1:2], scalar=cw[:, ci, 2:3], in1=gc[:, 511:512],
                                                                                       op0=ALU.mult, op1=ALU.add), reads=[bPH, bCw, bGc[pj]], writes=[bGc[pj]])
                            P.op("act", lambda e, gc=gc, ge=ge: e.activation(out=ge[:], in_=gc[:], func=AF.Gelu_apprx_tanh), reads=[bGc[pj]], writes=[bGe[pj]])
                            P.op("dve", lambda e, ci=ci, pj=pj, ge=ge: e.tensor_tensor(out=actT[:, ci, :], in0=psA[pj][:], in1=ge[:], op=ALU.mult),
                                 reads=[bPA[pj], bGe[pj]], writes=[bAct])
                        for t in range(4):
                            fns = []
                            for hf in range(2):
                                for ci in range(NCH):
                                    fns.append(lambda e, ci=ci, hf=hf, t=t: e.matmul(psY[:, hf * 512:(hf + 1) * 512], lhsT=actT[:, ci, t * 128:(t + 1) * 128],
                                                                                     rhs=w_dn[:, ci, hf * 512:(hf + 1) * 512], start=(ci == 0), stop=(ci == NCH - 1)))
                            P.op("pe", fns, reads=[bAct, bWd], writes=[bPY])
                            norm_resid(X, t, gate2, 8 + t, bXg[j])
                        P.dma("sp", x_dst[s, t0:t0 + 512, :].rearrange("(t p) d -> p t d", p=128), X[:], reads=[bXg[j]])

                    NG = 16
                    load_gates(0)
                    loadC1(0)
                    stage1(0)
                    for g in range(NG):
                        s = g // 8
                        nxt_same_seq = (g + 1 < NG) and ((g + 1) // 8 == s)
                        if nxt_same_seq:
                            loadC1(g + 1)
                            stage1(g + 1)
                        stage2(g, nxt_same_seq)
                        if g + 1 < NG and not nxt_same_seq:
                            load_gates((g + 1) // 8)
                            loadC1(g + 1)
                            stage1(g + 1)
                P.barrier()

        except _Stop:
            pass
        P.barrier()
    return nc


def _prep_shared(inp):
    f = lambda a: np.ascontiguousarray(np.asarray(a, dtype=np.float32))
    L = 4
    sh = {}
    sh["w_ada"] = f(np.asarray(inp["w_ada"]).reshape(L, 8, 128, 6144).transpose(0, 2, 1, 3))
    sh["b_ada"] = f(inp["b_ada"])
    sh["gpre_mixT"] = f(np.asarray(inp["g_pre_mix"]).reshape(L, 8, 128).transpose(0, 2, 1))
    sh["gpre_ffnT"] = f(np.asarray(inp["g_pre_ffn"]).reshape(L, 8, 128).transpose(0, 2, 1))
    sh["gpost_mix"] = f(inp["g_post_mix"])
    sh["gpost_ffn"] = f(inp["g_post_ffn"])
    sh["w_in"] = f(np.asarray(inp["w_in"]).reshape(L, 8, 128, D_IN).transpose(0, 2, 1, 3))
    wuq = np.asarray(inp["w_uq"]).reshape(L, 384, 4, 96)
    wuq = np.concatenate([wuq[..., :64].reshape(L, 384, 256), wuq[..., 64:].reshape(L, 384, 128)], axis=-1)
    sh["w_uq"] = f(wuq.reshape(L, 3, 128, 384).transpose(0, 2, 1, 3))
    wukv = np.asarray(inp["w_ukv"]).reshape(L, 256, 4, 128)
    wukv = np.concatenate([wukv[..., :64].reshape(L, 256, 256), wukv[..., 64:].reshape(L, 256, 256)], axis=-1)
    sh["w_ukv"] = f(wukv.reshape(L, 2, 128, 512).transpose(0, 2, 1, 3))
    sh["gqT"] = f(np.asarray(inp["mla_g_q"]).reshape(L, 3, 128).transpose(0, 2, 1))
    sh["gkvT"] = f(np.asarray(inp["mla_g_kv"]).reshape(L, 2, 128).transpose(0, 2, 1))
    sh["w_out"] = f(np.asarray(inp["w_out"]).reshape(L, 8, 128, DM).transpose(0, 2, 1, 3))
    wup = np.asarray(inp["w_up"]).reshape(L, 8, 128, 2, NCH, 128)
    sh["w_up"] = f(wup.transpose(0, 4, 2, 1, 3, 5).reshape(L, NCH, 128, 8, 256))
    sh["w_down"] = f(np.asarray(inp["w_down"]).reshape(L, NCH, 128, DM).transpose(0, 2, 1, 3))
    sh["convwT"] = f(np.asarray(inp["conv_w"]).reshape(L, 3, NCH, 128).transpose(0, 3, 2, 1))
    sh["convbT"] = f(np.asarray(inp["conv_b"]).reshape(L, NCH, 128).transpose(0, 2, 1))
    G, cm, T, tm = _host_tables(np.asarray(inp["na_rpb"], np.float32), np.asarray(inp["t5_table"], np.float32))
    sh["naG"] = f(G)
    sh["nacm"] = f(cm)
    sh["t5G"] = f(T)
    sh["t5m"] = f(tm)
    sh["pens"] = f(NA_PENS)
    A2 = np.zeros((2, 128), np.float32)
    A2[0, :64] = 1.0
    A2[1, 64:] = 1.0
    sh["A2"] = A2
    cosT, sinT = _rope_tables()
    sh["cosT"] = f(cosT)
    sh["sinT"] = f(sinT)
    return sh


def _in_maps(inp, ncores):
    sh = _prep_shared(inp)
    x = np.asarray(inp["x"], np.float32)
    cc = np.asarray(inp["c"], np.float32)
    maps = []
    for i in range(ncores):
        m = dict(sh)
        m["x"] = np.ascontiguousarray(x[2 * i:2 * i + 2])
        m["cT"] = np.ascontiguousarray(cc[2 * i:2 * i + 2].reshape(2, 8, 128).transpose(2, 1, 0))
        maps.append(m)
    return maps


_NC_CACHE = {}


def kernel(**inputs):
    if "nc" not in _NC_CACHE:
        _NC_CACHE["nc"] = build()
    nc = _NC_CACHE["nc"]
    maps = _in_maps(inputs, NCORES)
    res = run_bass_kernel_spmd(nc, maps, core_ids=list(range(NCORES)))
    out = np.concatenate([np.asarray(r["out"], dtype=np.float32) for r in res.results], axis=0)
    return out
```

```python
import math
from contextlib import ExitStack

import numpy as np
import ml_dtypes
import concourse.bass as bass
import concourse.mybir as mybir
from concourse.bass_utils import run_bass_kernel_spmd

F32 = mybir.dt.float32
BF16 = mybir.dt.bfloat16
AF = mybir.ActivationFunctionType
ALU = mybir.AluOpType

NCORES = 8
S = 4096
DM = 1024
DFF = 2816
NCH = 22
D_IN = 2976
EPS = 1e-6
NEG = -30000.0


class Buf:
    __slots__ = ("lw", "rd")

    def __init__(self):
        self.lw = []
        self.rd = {}


class Prog:
    def __init__(self, nc, ctx, n_dma_sems=40):
        self.nc = nc
        self.E = {"pe": nc.tensor, "act": nc.scalar, "dve": nc.vector, "pool": nc.gpsimd, "sp": nc.sync}
        self.psem = {k: ctx.enter_context(nc.semaphore("ps_" + k)) for k in self.E}
        self.pcnt = {k: 0 for k in self.E}
        self.seen = {k: {} for k in self.E}
        self.dsem = [ctx.enter_context(nc.semaphore("ds%d" % i)) for i in range(n_dma_sems)]
        self.dcnt = [0] * n_dma_sems
        self.dnext = 0
        self.nins = 0
        self.dead = False

    def _wait(self, eng, deps):
        if self.dead:
            return
        need = {}
        for d in deps:
            if d is None:
                continue
            s, v = d
            if need.get(s, 0) < v:
                need[s] = v
        seen = self.seen[eng]
        for s, v in need.items():
            if seen.get(s, 0) < v:
                self.E[eng].wait_ge(s, v)
                seen[s] = v
                self.nins += 1

    @staticmethod
    def _deps(reads, writes):
        deps = []
        for b in reads:
            deps.extend(b.lw)
        for b in writes:
            deps.extend(b.lw)
            deps.extend(b.rd.values())
        return deps

    def op(self, eng, fns, reads=(), writes=()):
        if self.dead:
            return None
        self._wait(eng, self._deps(reads, writes))
        if callable(fns):
            fns = [fns]
        ins = None
        e = self.E[eng]
        for f in fns:
            ins = f(e)
            self.nins += 1
        self.pcnt[eng] += 1
        ins.then_inc(self.psem[eng], 1)
        tok = (self.psem[eng], self.pcnt[eng])
        for b in reads:
            b.rd[eng] = tok
        for b in writes:
            b.lw = [tok]
            b.rd = {}
        return tok

    def dma(self, eng, out, in_, reads=(), writes=(), **kw):
        return self.dmas(eng, [(out, in_)], reads, writes, **kw)

    def dmas(self, eng, pairs, reads=(), writes=(), **kw):
        if self.dead:
            return []
        deps = self._deps(reads, writes)
        idx = []
        for _ in pairs:
            i = self.dnext
            self.dnext = (i + 1) % len(self.dsem)
            idx.append(i)
            if self.dcnt[i] > 0:
                deps.append((self.dsem[i], 16 * self.dcnt[i]))
        assert len(set(idx)) == len(idx)
        self._wait(eng, deps)
        toks = []
        for i, (out, in_) in zip(idx, pairs):
            self.dcnt[i] += 1
            self.E[eng].dma_start(out=out, in_=in_, **kw).then_inc(self.dsem[i], 16)
            self.nins += 1
            toks.append((self.dsem[i], 16 * self.dcnt[i]))
        for b in reads:
            for i, tok in zip(idx, toks):
                b.rd[("dma", i)] = tok
        for b in writes:
            b.lw = list(toks)
            b.rd = {}
        return toks

    def barrier(self):
        deps = [(self.psem[k], self.pcnt[k]) for k in self.E if self.pcnt[k] > 0]
        deps += [(s, 16 * c) for s, c in zip(self.dsem, self.dcnt) if c > 0]
        for k in self.E:
            self._wait(k, deps)


def _r_start(r):
    return min(max(r - 4, 0), 56)


def _c_start(c):
    return min(max(c - 8, 0), 48)


def _na_plan():
    pats = {}
    plan = []
    for J in range(16):
        klo = _r_start(4 * J)
        khi = _r_start(4 * J + 3) + 7
        units = []
        for i in range(klo // 2, khi // 2 + 1):
            key = []
            for a in range(2):
                for ap in range(4):
                    r = 4 * J + ap
                    kr = 2 * i + a
                    key.append(_r_start(r) <= kr <= _r_start(r) + 7)
            key = tuple(key)
            if all(key):
                pid = -1
            else:
                if key not in pats:
                    pats[key] = len(pats)
                pid = pats[key]
            di0 = 4 * J - 2 * i + 6
            assert 0 <= di0 <= 10
            units.append((i, di0, pid))
        plan.append(units)
    npat = len(pats)
    pens = np.zeros((2, npat, 4, 64), np.float32)
    for key, pid in pats.items():
        for a in range(2):
            for ap in range(4):
                if not key[a * 4 + ap]:
                    pens[a, pid, ap, :] = NEG
    return plan, pens.reshape(2, npat * 256)


NA_PLAN, NA_PENS = _na_plan()
NPAT = NA_PENS.shape[1] // 256


def _t5_bucket(rel):
    nb = 16
    max_exact = 8
    n = np.abs(rel)
    large = max_exact + (np.log(np.maximum(n, 1) / max_exact) / math.log(1024 / max_exact) * (nb - max_exact)).astype(np.int64)
    large = np.minimum(large, nb - 1)
    return (np.where(rel > 0, nb, 0) + np.where(n < max_exact, n, large)).astype(np.int32)


def _host_tables(na_rpb, t5_table):
    p = np.arange(128)
    a = p // 64
    kc = p % 64
    di = np.arange(14)
    c = np.arange(64)
    drow = a[:, None] - (di[None, :] - 6) + 7
    dcol = kc[:, None] - c[None, :] + 15
    ok = ((drow >= 0) & (drow <= 14))[:, :, None] & ((dcol >= 0) & (dcol <= 30))[:, None, :]
    drc = np.clip(drow, 0, 14)
    dcc = np.clip(dcol, 0, 30)
    G = na_rpb[:, :, drc[:, :, None], dcc[:, None, :]]
    G = np.where(ok[None, None], G, np.float32(0.0)).astype(np.float32)
    G = np.ascontiguousarray(np.transpose(G, (0, 2, 1, 3, 4))).reshape(na_rpb.shape[0], 128, 4 * 14 * 64)
    cs = np.array([_c_start(x) for x in range(64)])
    cm = ((kc[:, None] >= cs[None, :]) & (kc[:, None] < cs[None, :] + 16)).astype(np.float32)
    cm = np.ascontiguousarray(np.broadcast_to(cm[:, None, :], (128, 14, 64))).reshape(128, 14 * 64)
    q = np.arange(128)
    pc = np.arange(3)
    rel = 128 * (pc[None, :, None] - 1) + p[:, None, None] - q[None, None, :]
    valid = (np.abs(rel) <= 64)
    T = np.zeros((128, 8, 3, 3, 128), np.float32)
    for pi, d in enumerate((1, 4, 16)):
        b = _t5_bucket(rel * d)
        vals = t5_table[b]
        vals = np.where(valid[..., None], vals, np.float32(0.0))
        T[:, :, pi] = np.transpose(vals, (0, 3, 1, 2))
    T = T.reshape(128, 8 * 3 * 384)
    tm = valid.astype(np.float32).reshape(128, 384)
    return G, cm, T, tm


def _rope_tables():
    inv_freq = (10000.0 ** (-np.arange(0, 32, 2, dtype=np.float32) / 32)).astype(np.float32)
    ang = np.arange(S, dtype=np.float32)[:, None] * inv_freq[None, :]
    cos = np.cos(ang).astype(np.float32)
    sin = np.sin(ang).astype(np.float32)
    idx = (np.arange(128) % 32) % 16
    return np.ascontiguousarray(cos[:, idx].T), np.ascontiguousarray(sin[:, idx].T)


class _Stop(Exception):
    pass


def build(NL=4, dbg=()):
    nc = bass.Bass("TRN2", target_bir_lowering=False)
    stop_at = -1
    for dflag in dbg:
        if dflag.startswith("stage="):
            stop_at = int(dflag[6:])

    PP = []

    def ckpt(n):
        if n == stop_at and not PP[0].dead:
            PP[0].barrier()
            PP[0].dead = True
    D = {}

    def din(name, shape, dt=F32):
        D[name] = nc.dram_tensor(name, list(shape), dt, kind="ExternalInput").ap()

    def dscr(name, shape, dt):
        kind = "ExternalOutput" if name in dbg else "Internal"
        D[name] = nc.dram_tensor(name, list(shape), dt, kind=kind).ap()

    din("x", [2, S, DM])
    din("cT", [128, 8, 2])
    din("w_ada", [4, 128, 8, 6144])
    din("b_ada", [4, 6144])
    din("gpre_mixT", [4, 128, 8])
    din("gpre_ffnT", [4, 128, 8])
    din("gpost_mix", [4, DM])
    din("gpost_ffn", [4, DM])
    din("w_in", [4, 128, 8, D_IN])
    din("w_uq", [4, 128, 3, 384])
    din("w_ukv", [4, 128, 2, 512])
    din("gqT", [4, 128, 3])
    din("gkvT", [4, 128, 2])
    din("w_out", [4, 128, 8, DM])
    din("w_up", [4, NCH, 128, 8, 256])
    din("w_down", [4, 128, NCH, DM])
    din("convwT", [4, 128, NCH, 3])
    din("convbT", [4, 128, NCH])
    din("naG", [4, 128, 4 * 14 * 64])
    din("nacm", [128, 14 * 64])
    din("pens", [2, NPAT * 256])
    din("A2", [2, 128])
    din("t5G", [128, 8 * 3 * 384])
    din("t5m", [128, 384])
    din("cosT", [128, S])
    din("sinT", [128, S])
    D["out"] = nc.dram_tensor("out", [2, S, DM], F32, kind="ExternalOutput").ap()
    dscr("modD", [4, 2, 6144], F32)
    dscr("xs", [2, S, DM], F32)
    dscr("zT", [2, 12, 128, S], BF16)
    dscr("vtok", [2, S, 768], BF16)
    dscr("qn", [2, 2, 128, S], BF16)
    dscr("qr", [2, 128, S], BF16)
    dscr("kn", [2, 2, 128, S], BF16)
    dscr("kr", [2, 32, S], BF16)
    dscr("vc", [2, S, 256], BF16)
    dscr("OT", [2, DM, S], BF16)

    with ExitStack() as gctx:
        P = Prog(nc, gctx)
        PP.append(P)

        uid = [0]

        def SB(ctx, name, shape, dt):
            uid[0] += 1
            return ctx.enter_context(nc.sbuf_tensor("%s_u%d" % (name, uid[0]), list(shape), dt))

        def PS(ctx, name, shape, dt=F32):
            uid[0] += 1
            return ctx.enter_context(nc.psum_tensor("%s_u%d" % (name, uid[0]), list(shape), dt))

        zcol = SB(gctx, "zcol", [128, 1], F32)
        bZc = Buf()
        ident = SB(gctx, "ident", [128, 128], BF16)
        identf = SB(gctx, "identf", [128, 128], F32)
        bId = Buf()
        P.op("pool", lambda e: e.memset(zcol[:], 0.0), writes=[bZc])
        P.op("pool", lambda e: e.memset(identf[:], 1.0), writes=[bId])
        P.op("pool", lambda e: e.affine_select(out=identf[:], in_=identf[:], pattern=[[-1, 128]],
                                                compare_op=ALU.is_equal, fill=0.0, base=0, channel_multiplier=1),
             reads=[bId], writes=[bId])
        P.op("pool", lambda e: e.tensor_copy(out=ident[:], in_=identf[:]), reads=[bId], writes=[bId])

        def rstd_from_ss(ss_ap, out_ap, n, bss, bout, tmp_ap, btmp):
            P.op("act", lambda e: e.activation(out=tmp_ap, in_=ss_ap, func=AF.Ln, scale=1.0 / n, bias=EPS),
                 reads=[bss], writes=[btmp])
            P.op("act", lambda e: e.activation(out=out_ap, in_=tmp_ap, func=AF.Exp, scale=-0.5),
                 reads=[btmp], writes=[bout])

        try:
            with ExitStack() as c:
                cT = SB(c, "cT_sb", [128, 8, 2], F32)
                cact = SB(c, "cact", [128, 8, 2], F32)
                brow = SB(c, "brow", [2, 6144], F32)
                mrow = SB(c, "mrow", [2, 6144], F32)
                wt = [SB(c, "wada%d" % i, [128, 8, 512], F32) for i in range(2)]
                psM = [PS(c, "psM%d" % i, [2, 512]) for i in range(2)]
                bc, bb, bm = Buf(), Buf(), Buf()
                bw = [Buf(), Buf()]
                bp = [Buf(), Buf()]
                P.dma("sp", cT[:], D["cT"], writes=[bc])
                P.op("act", lambda e: e.activation(out=cact[:], in_=cT[:], func=AF.Silu), reads=[bc], writes=[bc])
                it = 0
                for l in range(NL):
                    P.dma("sp", brow[:], D["b_ada"][l:l + 1, :].partition_broadcast(2), writes=[bb])
                    for nb in range(12):
                        j = it % 2
                        it += 1
                        P.dma("sp", wt[j][:], D["w_ada"][l, :, :, nb * 512:(nb + 1) * 512], writes=[bw[j]])
                        P.op("pe", [(lambda e, k=k, j=j: e.matmul(psM[j][:], lhsT=cact[:, k, :], rhs=wt[j][:, k, :],
                                                                  start=(k == 0), stop=(k == 7))) for k in range(8)],
                             reads=[bc, bw[j]], writes=[bp[j]])
                        P.op("dve", lambda e, j=j, nb=nb: e.tensor_tensor(out=mrow[:, nb * 512:(nb + 1) * 512], in0=psM[j][:],
                                                                          in1=brow[:, nb * 512:(nb + 1) * 512], op=ALU.add),
                             reads=[bp[j], bb], writes=[bm])
                    P.dma("sp", D["modD"][l], mrow[:], reads=[bm])
            P.barrier()
            ckpt(1)

            def load_vec8(ctx_name, dst, l, s, j, bdst):
                src = D["modD"][l, s, j * 1024:(j + 1) * 1024].rearrange("(k p) -> p k", p=128)
                P.dma("sp", dst, src, writes=[bdst], allow_slow_non_contiguous=True)

            for l in range(NL):
                x_src = D["x"] if l == 0 else D["xs"]
                x_dst = D["out"] if l == NL - 1 else D["xs"]

                with ExitStack() as c:
                    w_in = SB(c, "w_in", [128, 8, D_IN], BF16)
                    w_rot = SB(c, "w_rot", [128, 8, 32], BF16)
                    w_uq = SB(c, "w_uq", [128, 3, 384], BF16)
                    w_uqr = SB(c, "w_uqr", [128, 3, 128], BF16)
                    w_ukv = SB(c, "w_ukv", [128, 2, 512], BF16)
                    gq = SB(c, "gq", [128, 3], F32)
                    gkv = SB(c, "gkv", [128, 2], F32)
                    gpre = SB(c, "gpre", [128, 8], F32)
                    scA = [SB(c, "scA%d" % s, [128, 8], F32) for s in range(2)]
                    biA = [SB(c, "biA%d" % s, [128, 8], F32) for s in range(2)]
                    xg = [SB(c, "xgA%d" % i, [128, 4, DM], F32) for i in range(2)]
                    cosg = [SB(c, "cosg%d" % i, [128, 512], F32) for i in range(2)]
                    sing = [SB(c, "sing%d" % i, [128, 512], F32) for i in range(2)]
                    junk = SB(c, "junkA", [128, DM], BF16)
                    ss = SB(c, "ssA", [128, 4], F32)
                    lnv = SB(c, "lnvA", [128, 4], F32)
                    rstd = SB(c, "rstdA", [128, 4], F32)
                    ssq = SB(c, "ssq", [128, 4], F32)
                    sskv = SB(c, "sskv", [128, 4], F32)
                    lnq = SB(c, "lnq", [128, 4], F32)
                    lnk = SB(c, "lnk", [128, 4], F32)
                    rsq = SB(c, "rsq", [128, 4], F32)
                    rskv = SB(c, "rskv", [128, 4], F32)
                    xn = SB(c, "xnA", [128, 4, DM], BF16)
                    hT = SB(c, "hT", [128, 8, 512], BF16)
                    zst = [SB(c, "zst%d" % i, [128, 12, 512], BF16) for i in range(2)]
                    vst = [SB(c, "vst%d" % i, [128, 4, 768], BF16) for i in range(2)]
                    cst = SB(c, "cst", [128, 4, 640], F32)
                    cqn = SB(c, "cqn", [128, 4, 384], BF16)
                    ckvn = SB(c, "ckvn", [128, 4, 256], BF16)
                    cqT = SB(c, "cqT", [128, 3, 512], BF16)
                    ckvT = SB(c, "ckvT", [128, 2, 512], BF16)
                    qnst = [SB(c, "qnst%d" % i, [128, 2, 512], BF16) for i in range(2)]
                    knst = [SB(c, "knst%d" % i, [128, 2, 512], BF16) for i in range(2)]
                    qrst = [SB(c, "qrst%d" % i, [128, 512], BF16) for i in range(2)]
                    krst = [SB(c, "krst%d" % i, [32, 512], BF16) for i in range(2)]
                    vcst = [SB(c, "vcst%d" % i, [128, 4, 256], BF16) for i in range(2)]
                    t1 = SB(c, "t1A", [128, 512], F32)
                    t2 = SB(c, "t2A", [128, 512], F32)
                    psT = [PS(c, "psTA%d" % i, [128, 512], BF16) for i in range(2)]
                    psZ = [PS(c, "psZA%d" % i, [128, 512]) for i in range(2)]
                    psW = [PS(c, "psWA%d" % i, [128, 1024]) for i in range(2)]

                    bW, bWr, bUq, bUqr, bUkv, bG = Buf(), Buf(), Buf(), Buf(), Buf(), Buf()
                    bSc = [Buf(), Buf()]
                    bXg = [Buf(), Buf()]
                    bCS = [Buf(), Buf()]
                    bJ, bSS, bLn, bRs, bXn, bHT = Buf(), Buf(), Buf(), Buf(), Buf(), Buf()
                    bSq, bLq, bRq = Buf(), Buf(), Buf()
                    bZst = [Buf(), Buf()]
                    bVst = [Buf(), Buf()]
                    bCst, bCqn, bCkvn, bCqT, bCkvT = Buf(), Buf(), Buf(), Buf(), Buf()
                    bQn = [Buf(), Buf()]
                    bKn = [Buf(), Buf()]
                    bQr = [Buf(), Buf()]
                    bKr = [Buf(), Buf()]
                    bVc = [Buf(), Buf()]
                    bT1, bT2 = Buf(), Buf()
                    bPT = [Buf(), Buf()]
                    bPZ = [Buf(), Buf()]
                    bPW = [Buf(), Buf()]

                    P.dmas("pool", [(w_in[:, k, :], D["w_in"][l, :, k, :]) for k in range(8)], writes=[bW])
                    P.dma("pool", w_uq[:], D["w_uq"][l], writes=[bUq])
                    P.dma("pool", w_ukv[:], D["w_ukv"][l], writes=[bUkv])
                    P.dmas("sp", [(gq[:], D["gqT"][l]), (gkv[:], D["gkvT"][l]), (gpre[:], D["gpre_mixT"][l])], writes=[bG])
                    P.op("act", lambda e: e.mul(out=w_rot[:, :, 0:16], in_=w_in[:, :, 2960:2976], mul=-1.0), reads=[bW], writes=[bWr])
                    P.op("act", lambda e: e.copy(out=w_rot[:, :, 16:32], in_=w_in[:, :, 2944:2960]), reads=[bW], writes=[bWr])
                    for k in range(3):
                        src = w_uq[:, k, 256:384].rearrange("p (h t j) -> p h t j", t=2, j=16)
                        dst = w_uqr[:, k, :].rearrange("p (h t j) -> p h t j", t=2, j=16)
                        P.op("act", lambda e, src=src, dst=dst: e.mul(out=dst[:, :, 0, :], in_=src[:, :, 1, :], mul=-1.0),
                             reads=[bUq], writes=[bUqr])
                        P.op("act", lambda e, src=src, dst=dst: e.copy(out=dst[:, :, 1, :], in_=src[:, :, 0, :]),
                             reads=[bUq], writes=[bUqr])
                    for s in range(2):
                        load_vec8(c, biA[s][:], l, s, 0, bSc[s])
                        load_vec8(c, scA[s][:], l, s, 1, bSc[s])
                        P.op("dve", lambda e, s=s: e.scalar_tensor_tensor(out=scA[s][:], in0=scA[s][:], scalar=1.0, in1=gpre[:],
                                                                          op0=ALU.add, op1=ALU.mult),
                             reads=[bSc[s], bG], writes=[bSc[s]])

                    def loadA(g):
                        s, t0 = g // 8, (g % 8) * 512
                        j = g % 2
                        P.dma("sp", xg[j][:], x_src[s, t0:t0 + 512, :].rearrange("(t p) d -> p t d", p=128), writes=[bXg[j]])
                        P.dmas("sp", [(cosg[j][:], D["cosT"][:, t0:t0 + 512]), (sing[j][:], D["sinT"][:, t0:t0 + 512])], writes=[bCS[j]])

                    evac_i = [0]

                    def evac(out_ap, in_ap, reads, writes):
                        evac_i[0] += 1
                        if evac_i[0] % 2:
                            P.op("dve", lambda e: e.tensor_copy(out=out_ap, in_=in_ap), reads=reads, writes=writes)
                        else:
                            P.op("act", lambda e: e.copy(out=out_ap, in_=in_ap), reads=reads, writes=writes)

                    NG = 16
                    ckpt(2)
                    loadA(0)
                    for g in range(NG):
                        s, t0 = g // 8, (g % 8) * 512
                        j = g % 2
                        if g + 1 < NG:
                            loadA(g + 1)
                        X = xg[j]
                        for t in range(4):
                            P.op("act", lambda e, t=t: e.activation(out=junk[:], in_=X[:, t, :], func=AF.Square, accum_out=ss[:, t:t + 1]),
                                 reads=[bXg[j]], writes=[bJ, bSS])
                        rstd_from_ss(ss[:], rstd[:], DM, bSS, bRs, lnv[:], bLn)
                        for t in range(4):
                            P.op("dve", lambda e, t=t: e.tensor_scalar(out=xn[:, t, :], in0=X[:, t, :], scalar1=rstd[:, t:t + 1], scalar2=None,
                                                                       op0=ALU.mult),
                                 reads=[bXg[j], bRs], writes=[bXn])
                        for k in range(8):
                            pj = k % 2
                            P.op("pe", [(lambda e, t=t, k=k, pj=pj: e.transpose(psT[pj][:, t * 128:(t + 1) * 128], xn[:, t, k * 128:(k + 1) * 128], ident[:]))
                                        for t in range(4)], reads=[bXn, bId], writes=[bPT[pj]])
                            P.op("act", lambda e, k=k, pj=pj: e.activation(out=hT[:, k, :], in_=psT[pj][:], func=AF.Identity,
                                                                           scale=scA[s][:, k:k + 1], bias=biA[s][:, k:k + 1]),
                                 reads=[bPT[pj], bSc[s]], writes=[bHT])
                        ckpt(3)
                        cbs = [0, 128, 256, 384] + [768 + 128 * i for i in range(8)]
                        for ci, cb in enumerate(cbs):
                            pj = ci % 2
                            P.op("pe", [(lambda e, k=k, cb=cb, pj=pj: e.matmul(psZ[pj][:], lhsT=w_in[:, k, cb:cb + 128], rhs=hT[:, k, :],
                                                                              start=(k == 0), stop=(k == 7))) for k in range(8)],
                                 reads=[bW, bHT], writes=[bPZ[pj]])
                            evac(zst[j][:, ci, :], psZ[pj][:], [bPZ[pj]], [bZst[j]])
                        P.dma("sp", D["zT"][s].rearrange("c p t -> p c t")[:, :, t0:t0 + 512], zst[j][:], reads=[bZst[j]])
                        ckpt(4)
                        for t in range(4):
                            pj = t % 2
                            fns = []
                            for k in range(8):
                                fns.append(lambda e, k=k, t=t, pj=pj: e.matmul(psW[pj][:, 0:256], lhsT=hT[:, k, t * 128:(t + 1) * 128],
                                                                               rhs=w_in[:, k, 512:768], start=(k == 0), stop=(k == 7)))
                            for k in range(8):
                                fns.append(lambda e, k=k, t=t, pj=pj: e.matmul(psW[pj][:, 512:1024], lhsT=hT[:, k, t * 128:(t + 1) * 128],
                                                                               rhs=w_in[:, k, 1792:2304], start=(k == 0), stop=(k == 7)))
                            P.op("pe", fns, reads=[bW, bHT], writes=[bPW[pj]])
                            evac(vst[j][:, t, 0:256], psW[pj][:, 0:256], [bPW[pj]], [bVst[j]])
                            evac(vst[j][:, t, 256:768], psW[pj][:, 512:1024], [bPW[pj]], [bVst[j]])
                        P.dma("sp", D["vtok"][s, t0:t0 + 512, :].rearrange("(t p) c -> p t c", p=128), vst[j][:], reads=[bVst[j]])
                        ckpt(5)
                        for t in range(4):
                            pj = t % 2
                            fns = []
                            for k in range(8):
                                fns.append(lambda e, k=k, t=t, pj=pj: e.matmul(psW[pj][:, 0:384], lhsT=hT[:, k, t * 128:(t + 1) * 128],
                                                                               rhs=w_in[:, k, 2304:2688], start=(k == 0), stop=(k == 7)))
                            for k in range(8):
                                fns.append(lambda e, k=k, t=t, pj=pj: e.matmul(psW[pj][:, 512:768], lhsT=hT[:, k, t * 128:(t + 1) * 128],
                                                                               rhs=w_in[:, k, 2688:2944], start=(k == 0), stop=(k == 7)))
                            if "noMm" not in dbg:
                                P.op("pe", fns, reads=[bW, bHT], writes=[bPW[pj]])
                            if "noCp" not in dbg:
                                P.op("dve", lambda e, t=t, pj=pj: e.tensor_copy(out=cst[:, t, 0:384], in_=psW[pj][:, 0:384]), reads=[bPW[pj]], writes=[bCst])
                                P.op("dve", lambda e, t=t, pj=pj: e.tensor_copy(out=cst[:, t, 384:640], in_=psW[pj][:, 512:768]), reads=[bPW[pj]], writes=[bCst])
                            if "noSq" not in dbg:
                              P.op("act", lambda e, t=t, pj=pj: e.activation(out=junk[:, 0:384], in_=cst[:, t, 0:384], func=AF.Square,
                                                                           accum_out=ssq[:, t:t + 1]), reads=[bCst], writes=[bJ, bSq])
                            if "noSq" not in dbg:
                              P.op("act", lambda e, t=t, pj=pj: e.activation(out=junk[:, 384:640], in_=cst[:, t, 384:640], func=AF.Square,
                                                                           accum_out=sskv[:, t:t + 1]), reads=[bCst], writes=[bJ, bSq])
                        ckpt(51)
                        rstd_from_ss(ssq[:], rsq[:], 384, bSq, bRq, lnq[:], bLq)
                        rstd_from_ss(sskv[:], rskv[:], 256, bSq, bRq, lnk[:], bLq)
                        for t in range(4):
                            P.op("dve", lambda e, t=t: e.tensor_scalar(out=cqn[:, t, :], in0=cst[:, t, 0:384], scalar1=rsq[:, t:t + 1], scalar2=None,
                                                                       op0=ALU.mult), reads=[bCst, bRq], writes=[bCqn])
                            P.op("dve", lambda e, t=t: e.tensor_scalar(out=ckvn[:, t, :], in0=cst[:, t, 384:640], scalar1=rskv[:, t:t + 1], scalar2=None,
                                                                       op0=ALU.mult), reads=[bCst, bRq], writes=[bCkvn])
                        ckpt(53)
                        for kk in range(3):
                            pj = kk % 2
                            P.op("pe", [(lambda e, t=t, kk=kk, pj=pj: e.transpose(psT[pj][:, t * 128:(t + 1) * 128], cqn[:, t, kk * 128:(kk + 1) * 128], ident[:]))
                                        for t in range(4)], reads=[bCqn, bId], writes=[bPT[pj]])
                            P.op("act", lambda e, kk=kk, pj=pj: e.activation(out=cqT[:, kk, :], in_=psT[pj][:], func=AF.Identity, scale=gq[:, kk:kk + 1], bias=zcol[:, 0:1]),
                                 reads=[bPT[pj], bG, bZc], writes=[bCqT])
                        ckpt(54)
                        for kk in range(2):
                            pj = (kk + 1) % 2
                            P.op("pe", [(lambda e, t=t, kk=kk, pj=pj: e.transpose(psT[pj][:, t * 128:(t + 1) * 128], ckvn[:, t, kk * 128:(kk + 1) * 128], ident[:]))
                                        for t in range(4)], reads=[bCkvn, bId], writes=[bPT[pj]])
                            P.op("act", lambda e, kk=kk, pj=pj: e.activation(out=ckvT[:, kk, :], in_=psT[pj][:], func=AF.Identity, scale=gkv[:, kk:kk + 1], bias=zcol[:, 0:1]),
                                 reads=[bPT[pj], bG, bZc], writes=[bCkvT])
                        ckpt(6)
                        for jj in range(2):
                            P.op("pe", [(lambda e, kk=kk, jj=jj: e.matmul(psZ[0][:], lhsT=w_uq[:, kk, jj * 128:(jj + 1) * 128], rhs=cqT[:, kk, :],
                                                                          start=(kk == 0), stop=(kk == 2))) for kk in range(3)],
                                 reads=[bUq, bCqT], writes=[bPZ[0]])
                            evac(qnst[j][:, jj, :], psZ[0][:], [bPZ[0]], [bQn[j]])
                            P.op("pe", [(lambda e, kk=kk, jj=jj: e.matmul(psZ[1][:], lhsT=w_ukv[:, kk, jj * 128:(jj + 1) * 128], rhs=ckvT[:, kk, :],
                                                                          start=(kk == 0), stop=(kk == 1))) for kk in range(2)],
                                 reads=[bUkv, bCkvT], writes=[bPZ[1]])
                            evac(knst[j][:, jj, :], psZ[1][:], [bPZ[1]], [bKn[j]])
                        P.op("pe", [(lambda e, kk=kk: e.matmul(psZ[0][:], lhsT=w_uq[:, kk, 256:384], rhs=cqT[:, kk, :], start=(kk == 0), stop=(kk == 2)))
                                    for kk in range(3)], reads=[bUq, bCqT], writes=[bPZ[0]])
                        P.op("pe", [(lambda e, kk=kk: e.matmul(psZ[1][:], lhsT=w_uqr[:, kk, :], rhs=cqT[:, kk, :], start=(kk == 0), stop=(kk == 2)))
                                    for kk in range(3)], reads=[bUqr, bCqT], writes=[bPZ[1]])
                        P.op("dve", lambda e: e.tensor_tensor(out=t1[:], in0=psZ[0][:], in1=cosg[j][:], op=ALU.mult), reads=[bPZ[0], bCS[j]], writes=[bT1])
                        P.op("dve", lambda e: e.tensor_tensor(out=t2[:], in0=psZ[1][:], in1=sing[j][:], op=ALU.mult), reads=[bPZ[1], bCS[j]], writes=[bT2])
                        P.op("pool", lambda e: e.tensor_tensor(out=qrst[j][:], in0=t1[:], in1=t2[:], op=ALU.add), reads=[bT1, bT2], writes=[bQr[j]])
                        ckpt(7)
                        P.op("pe", [(lambda e, k=k: e.matmul(psZ[0][0:32, :], lhsT=w_in[:, k, 2944:2976], rhs=hT[:, k, :], start=(k == 0), stop=(k == 7)))
                                    for k in range(8)], reads=[bW, bHT], writes=[bPZ[0]])
                        P.op("pe", [(lambda e, k=k: e.matmul(psZ[1][0:32, :], lhsT=w_rot[:, k, :], rhs=hT[:, k, :], start=(k == 0), stop=(k == 7)))
                                    for k in range(8)], reads=[bWr, bHT], writes=[bPZ[1]])
                        P.op("dve", lambda e: e.tensor_tensor(out=t1[0:32, :], in0=psZ[0][0:32, :], in1=cosg[j][0:32, :], op=ALU.mult),
                             reads=[bPZ[0], bCS[j]], writes=[bT1])
                        P.op("dve", lambda e: e.tensor_tensor(out=t2[0:32, :], in0=psZ[1][0:32, :], in1=sing[j][0:32, :], op=ALU.mult),
                             reads=[bPZ[1], bCS[j]], writes=[bT2])
                        P.op("pool", lambda e: e.tensor_tensor(out=krst[j][:], in0=t1[0:32, :], in1=t2[0:32, :], op=ALU.add),
                             reads=[bT1, bT2], writes=[bKr[j]])
                        ckpt(8)
                        for t in range(4):
                            pj = t % 2
                            P.op("pe", [(lambda e, kk=kk, t=t, pj=pj: e.matmul(psW[pj][:, 0:256], lhsT=ckvT[:, kk, t * 128:(t + 1) * 128],
                                                                               rhs=w_ukv[:, kk, 256:512], start=(kk == 0), stop=(kk == 1))) for kk in range(2)],
                                 reads=[bUkv, bCkvT], writes=[bPW[pj]])
                            evac(vcst[j][:, t, :], psW[pj][:, 0:256], [bPW[pj]], [bVc[j]])
                        P.dma("sp", D["qn"][s].rearrange("j p t -> p j t")[:, :, t0:t0 + 512], qnst[j][:], reads=[bQn[j]])
                        P.dma("sp", D["kn"][s].rearrange("j p t -> p j t")[:, :, t0:t0 + 512], knst[j][:], reads=[bKn[j]])
                        P.dma("sp", D["qr"][s, :, t0:t0 + 512], qrst[j][:], reads=[bQr[j]])
                        P.dma("sp", D["kr"][s, :, t0:t0 + 512], krst[j][:], reads=[bKr[j]])
                        P.dma("sp", D["vc"][s, t0:t0 + 512, :].rearrange("(t p) c -> p t c", p=128), vcst[j][:], reads=[bVc[j]])
                        ckpt(9)
                P.barrier()
                if "stopA" in dbg:
                    break

                SC_AB = 0.125
                SC_C = 96.0 ** -0.5
                def run_pipeline(units, nb):
                    n = len(units)
                    for i in range(min(nb, n)):
                        if "pre" in units[i]:
                            units[i]["pre"]()
                        units[i]["qk"]()
                    for i in range(n):
                        units[i]["mid"]()
                        units[i]["pv"]()
                        if "post" in units[i]:
                            units[i]["post"]()
                        if i + nb < n:
                            u2 = units[i + nb]
                            if "pre" in u2:
                                u2["pre"]()
                            u2["qk"]()

                with ExitStack() as c:
                    NB = 3
                    naX = SB(c, "naX", [128, 4 * 14 * 64], BF16)
                    t5X = SB(c, "t5X", [128, 8 * 3 * 384], BF16)
                    pens = SB(c, "pens", [2, NPAT * 256], BF16)
                    A2 = SB(c, "A2", [2, 128], BF16)
                    psS = [PS(c, "psS%d" % i, [128, 512]) for i in range(NB)]
                    psO = [PS(c, "psO%d" % i, [128, 512]) for i in range(2)]
                    E = [SB(c, "E%d" % i, [128, 384], BF16) for i in range(NB + 1)]
                    bE = [Buf() for _ in range(NB + 1)]
                    bPS = [Buf() for _ in range(NB)]
                    bPO = [Buf(), Buf()]
                    bNaX, bT5X, bPen = Buf(), Buf(), Buf()
                    P.dmas("pool", [(pens[:], D["pens"]), (A2[:], D["A2"])], writes=[bPen])
                    with ExitStack() as c2:
                        stg = SB(c2, "tstg", [128, 8 * 3 * 384], F32)
                        msk = SB(c2, "tmsk", [128, 14 * 64], F32)
                        bStg, bMsk = Buf(), Buf()
                        P.dma("sp", stg[:, 0:3584], D["naG"][l], writes=[bStg])
                        P.dma("sp", msk[:], D["nacm"], writes=[bMsk])
                        P.op("act", lambda e: e.activation(out=stg[:, 0:3584], in_=stg[:, 0:3584], func=AF.Exp), reads=[bStg], writes=[bStg])
                        for h in range(4):
                            P.op("dve", lambda e, h=h: e.tensor_tensor(out=naX[:, h * 896:(h + 1) * 896], in0=stg[:, h * 896:(h + 1) * 896], in1=msk[:],
                                                                       op=ALU.mult), reads=[bStg, bMsk], writes=[bNaX])
                        P.dma("sp", stg[:], D["t5G"], writes=[bStg])
                        P.dma("sp", msk[:, 0:384], D["t5m"], writes=[bMsk])
                        for hp in range(24):
                            P.op("act", lambda e, hp=hp: e.activation(out=stg[:, hp * 384:(hp + 1) * 384], in_=stg[:, hp * 384:(hp + 1) * 384], func=AF.Exp),
                                 reads=[bStg], writes=[bStg])
                            P.op("dve", lambda e, hp=hp: e.tensor_tensor(out=t5X[:, hp * 384:(hp + 1) * 384], in0=stg[:, hp * 384:(hp + 1) * 384],
                                                                         in1=msk[:, 0:384], op=ALU.mult), reads=[bStg, bMsk], writes=[bT5X])
                        P.barrier()
                    ucnt = [0]
                    bcnt = [0]

                    with ExitStack() as c2:
                        qT = [SB(c2, "naq%d" % i, [128, S], BF16) for i in range(2)]
                        kT = [SB(c2, "nak%d" % i, [128, S], BF16) for i in range(2)]
                        va = [SB(c2, "nav%d" % i, [128, 32, 2, 128], BF16) for i in range(2)]
                        ot = [SB(c2, "naot%d" % i, [128, S], BF16) for i in range(2)]
                        rec = SB(c2, "narec", [64, 256], F32)
                        bQ = [Buf(), Buf()]
                        bV = [Buf(), Buf()]
                        bOt = [Buf(), Buf()]
                        bRec = Buf()
                        for i in range(2):
                            P.op("pool", lambda e, i=i: e.memset(va[i][:, :, :, 64:128], 1.0), writes=[bV[i]])
                        items = [(s, hp) for s in range(2) for hp in range(2)]

                        def loadNA(n):
                            s, hp = items[n]
                            i = n % 2
                            P.dmas("sp", [(qT[i][:], D["zT"][s, hp]), (kT[i][:], D["zT"][s, 2 + hp])], writes=[bQ[i]])
                            src = D["vtok"][s].rearrange("(n p) c -> p n c", p=128)
                            P.dmas("sp", [(va[i][:, n0:n0 + 16, hh, 0:64], src[:, n0:n0 + 16, hp * 128 + hh * 64:hp * 128 + hh * 64 + 64])
                                          for n0 in range(0, 32, 16) for hh in range(2)], writes=[bV[i]])

                        units = []
                        for n, (s, hp) in enumerate(items):
                            i = n % 2
                            for hh in range(2):
                                h = 2 * hp + hh
                                pr = slice(hh * 64, hh * 64 + 64)
                                for J in range(16):
                                    ul = NA_PLAN[J]
                                    oj = bcnt[0] % 2
                                    bcnt[0] += 1
                                    for ui, (ci, di0, pid) in enumerate(ul):
                                        u = ucnt[0]
                                        ucnt[0] += 1
                                        sj, ej = u % NB, u % (NB + 1)
                                        first, last = (ui == 0), (ui == len(ul) - 1)

                                        def qk(i=i, pr=pr, ci=ci, J=J, pid=pid, sj=sj):
                                            fns = [lambda e: e.matmul(psS[sj][:, 0:256], lhsT=kT[i][pr, ci * 128:(ci + 1) * 128],
                                                                      rhs=qT[i][pr, J * 256:(J + 1) * 256], start=True, stop=(pid < 0))]
                                            if pid >= 0:
                                                fns.append(lambda e: e.matmul(psS[sj][:, 0:256], lhsT=A2[:, :], rhs=pens[:, pid * 256:(pid + 1) * 256],
                                                                              start=False, stop=True))
                                            P.op("pe", fns, reads=[bQ[i], bPen], writes=[bPS[sj]])

                                        def mid(h=h, di0=di0, sj=sj, ej=ej):
                                            P.op("act", lambda e: e.activation(out=E[ej][:, 0:256], in_=psS[sj][:, 0:256], func=AF.Exp, scale=SC_AB),
                                                 reads=[bPS[sj]], writes=[bE[ej]])
                                            xo = h * 896 + di0 * 64
                                            P.op("dve", lambda e: e.tensor_tensor(out=E[ej][:, 0:256], in0=E[ej][:, 0:256], in1=naX[:, xo:xo + 256], op=ALU.mult),
                                                 reads=[bE[ej], bNaX], writes=[bE[ej]])

                                        def pv(i=i, ci=ci, hh=hh, ej=ej, oj=oj, first=first, last=last, J=J, pr=pr):
                                            P.op("pe", lambda e: e.matmul(psO[oj][:, 0:256], lhsT=va[i][:, ci, hh, :], rhs=E[ej][:, 0:256], start=first, stop=last),
                                                 reads=[bV[i], bE[ej]], writes=[bPO[oj]])
                                            if last:
                                                P.op("dve", lambda e: e.reciprocal(out=rec[:], in_=psO[oj][64:128, 0:256]), reads=[bPO[oj]], writes=[bRec])
                                                P.op("dve", lambda e: e.tensor_tensor(out=ot[i][pr, J * 256:(J + 1) * 256], in0=psO[oj][0:64, 0:256], in1=rec[:],
                                                                                      op=ALU.mult), reads=[bPO[oj], bRec], writes=[bOt[i]])

                                        units.append(dict(qk=qk, mid=mid, pv=pv))

                            def post(n=n, s=s, hp=hp, i=i):
                                P.dma("sp", D["OT"][s, hp * 128:(hp + 1) * 128, :], ot[i][:], reads=[bOt[i]])
                                if n + 2 < len(items):
                                    loadNA(n + 2)
                            units[-1]["post"] = post
                        loadNA(0)
                        loadNA(1)
                        run_pipeline(units, NB)
                        P.barrier()

                    with ExitStack() as c2:
                        qT = [SB(c2, "dq%d" % i, [128, S], BF16) for i in range(2)]
                        kT = [SB(c2, "dk%d" % i, [128, S], BF16) for i in range(2)]
                        qo = [SB(c2, "dqo%d" % i, [128, S], BF16) for i in range(2)]
                        ko = [SB(c2, "dko%d" % i, [128, S], BF16) for i in range(2)]
                        va = [SB(c2, "dv%d" % i, [128, 32, 2, 128], BF16) for i in range(2)]
                        oacc = [SB(c2, "oacc%d" % i, [128, S], F32) for i in range(2)]
                        ot = [SB(c2, "dot%d" % i, [128, S], BF16) for i in range(2)]
                        rec = SB(c2, "drec", [64, 1024], F32)
                        bQ = [Buf(), Buf()]
                        bQo = [Buf(), Buf()]
                        bV = [Buf(), Buf()]
                        bAcc = [Buf(), Buf()]
                        bOt = [Buf(), Buf()]
                        bRec = Buf()
                        for i in range(2):
                            P.op("pool", lambda e, i=i: e.memset(va[i][:, :, :, 64:128], 1.0), writes=[bV[i]])
                        items = [(s, hp) for s in range(2) for hp in range(4)]
                        pats = (1, 4, 16)
                        NPI = len(items) * 3

                        def loadQK(n):
                            s, hp = items[n]
                            i = n % 2
                            P.dmas("sp", [(qT[i][:], D["zT"][s, 4 + hp]), (kT[i][:], D["zT"][s, 8 + hp])], writes=[bQ[i]])

                        def loadV(pidx):
                            n, pi = pidx // 3, pidx % 3
                            s, hp = items[n]
                            d = pats[pi]
                            vi = pidx % 2
                            nchunk = 32 // d
                            src = D["vtok"][s].rearrange("(n p r) c -> p r n c", p=128, r=d)
                            pairs = []
                            for r in range(d):
                                for n0 in range(0, nchunk, 16):
                                    n1 = min(n0 + 16, nchunk)
                                    for hh in range(2):
                                        cb0 = 256 + hp * 128 + hh * 64
                                        pairs.append((va[vi][:, r * nchunk + n0:r * nchunk + n1, hh, 0:64], src[:, r, n0:n1, cb0:cb0 + 64]))
                            P.dmas("sp", pairs, writes=[bV[vi]])

                        units = []
                        ocount = 0
                        for n, (s, hp) in enumerate(items):
                            i = n % 2
                            for pi, d in enumerate(pats):
                                pidx = n * 3 + pi
                                vi = pidx % 2
                                nchunk = 32 // d
                                pre = None
                                if d == 1:
                                    Q, K, bQQ = qT[i], kT[i], bQ[i]
                                else:
                                    oi = ocount % 2
                                    ocount += 1
                                    Q, K, bQQ = qo[oi], ko[oi], bQo[oi]

                                    def pre(Q=Q, K=K, d=d, i=i, bQQ=bQQ):
                                        P.op("pool", lambda e: e.tensor_copy(out=Q[:].rearrange("p (r m) -> p r m", r=d),
                                                                             in_=qT[i][:].rearrange("p (m r) -> p r m", r=d)),
                                             reads=[bQ[i]], writes=[bQQ])
                                        P.op("pool", lambda e: e.tensor_copy(out=K[:].rearrange("p (r m) -> p r m", r=d),
                                                                             in_=kT[i][:].rearrange("p (m r) -> p r m", r=d)),
                                             reads=[bQ[i]], writes=[bQQ])
                                firstu = True
                                for hh in range(2):
                                    h = 2 * hp + hh
                                    pr = slice(hh * 64, hh * 64 + 64)
                                    accv = oacc[hh][:].rearrange("p (m r) -> p r m", r=d)
                                    for r in range(d):
                                        for jb in range(nchunk):
                                            u = ucnt[0]
                                            ucnt[0] += 1
                                            sj, ej, oj = u % NB, u % (NB + 1), u % 2
                                            base = (r * nchunk + jb) * 128
                                            pcs = [pc for pc in range(3) if 0 <= jb + pc - 1 < nchunk]
                                            c0, c1 = pcs[0] * 128, (pcs[-1] + 1) * 128
                                            xo = (h * 3 + pi) * 384
                                            dst = accv[:, r, jb * 128:(jb + 1) * 128]

                                            def qk(Q=Q, K=K, bQQ=bQQ, pr=pr, base=base, pcs=pcs, sj=sj):
                                                P.op("pe", [(lambda e, pc=pc: e.matmul(
                                                    psS[sj][:, pc * 128:(pc + 1) * 128], lhsT=K[pr, base + (pc - 1) * 128:base + pc * 128],
                                                    rhs=Q[pr, base:base + 128], start=True, stop=True)) for pc in pcs], reads=[bQQ], writes=[bPS[sj]])

                                            def mid(sj=sj, ej=ej, c0=c0, c1=c1, xo=xo):
                                                P.op("act", lambda e: e.activation(out=E[ej][:, c0:c1], in_=psS[sj][:, c0:c1], func=AF.Exp, scale=SC_AB),
                                                     reads=[bPS[sj]], writes=[bE[ej]])
                                                P.op("pool", lambda e: e.tensor_tensor(out=E[ej][:, c0:c1], in0=E[ej][:, c0:c1], in1=t5X[:, xo + c0:xo + c1],
                                                                                       op=ALU.mult), reads=[bE[ej], bT5X], writes=[bE[ej]])

                                            def pv(vi=vi, r=r, nchunk=nchunk, jb=jb, hh=hh, pcs=pcs, ej=ej, oj=oj, dst=dst, pi=pi):
                                                P.op("pe", [(lambda e, pc=pc, q=q: e.matmul(
                                                    psO[oj][:, 0:128], lhsT=va[vi][:, r * nchunk + jb + pc - 1, hh, :],
                                                    rhs=E[ej][:, pc * 128:(pc + 1) * 128], start=(q == 0), stop=(q == len(pcs) - 1)))
                                                    for q, pc in enumerate(pcs)], reads=[bV[vi], bE[ej]], writes=[bPO[oj]])
                                                if pi == 0:
                                                    P.op("dve", lambda e: e.tensor_copy(out=dst, in_=psO[oj][:, 0:128]), reads=[bPO[oj]], writes=[bAcc[hh]])
                                                else:
                                                    P.op("dve", lambda e: e.tensor_tensor(out=dst, in0=psO[oj][:, 0:128], in1=dst, op=ALU.add),
                                                         reads=[bPO[oj]], writes=[bAcc[hh]])

                                            ud = dict(qk=qk, mid=mid, pv=pv)
                                            if firstu and pre is not None:
                                                ud["pre"] = pre
                                            firstu = False
                                            units.append(ud)
                                    if pi == 2:
                                        def fin(hh=hh, pr=pr, i=i):
                                            for q4 in range(4):
                                                cs = slice(q4 * 1024, (q4 + 1) * 1024)
                                                P.op("dve", lambda e: e.reciprocal(out=rec[:], in_=oacc[hh][64:128, cs]), reads=[bAcc[hh]], writes=[bRec])
                                                P.op("dve", lambda e: e.tensor_tensor(out=ot[i][pr, cs], in0=oacc[hh][0:64, cs], in1=rec[:], op=ALU.mult),
                                                     reads=[bAcc[hh], bRec], writes=[bOt[i]])
                                        units[-1]["fin"] = fin

                                def postp(pidx=pidx):
                                    if pidx + 2 < NPI:
                                        loadV(pidx + 2)
                                units[-1]["postp"] = postp

                            def posti(n=n, s=s, hp=hp, i=i):
                                P.dma("sp", D["OT"][s, 256 + hp * 128:256 + (hp + 1) * 128, :], ot[i][:], reads=[bOt[i]])
                                if n + 2 < len(items):
                                    loadQK(n + 2)
                            units[-1]["posti"] = posti
                        for ud in units:
                            hooks = [ud[k] for k in ("fin", "postp", "posti") if k in ud]
                            if hooks:
                                ud["post"] = (lambda hooks=hooks: [hk() for hk in hooks])
                        loadQK(0)
                        loadQK(1)
                        loadV(0)
                        loadV(1)
                        run_pipeline(units, NB)
                        P.barrier()
                P.barrier()

                with ExitStack() as c:
                    psS = [PS(c, "psSc%d" % i, [128, 1024]) for i in range(2)]
                    psO = [PS(c, "psOc%d" % i, [128, 1024]) for i in range(2)]
                    E = [SB(c, "Ec%d" % i, [128, 1024], BF16) for i in range(3)]
                    qT = [SB(c, "cq%d" % i, [96, S], BF16) for i in range(2)]
                    kT = [SB(c, "ck%d" % i, [96, S], BF16) for i in range(2)]
                    va = [SB(c, "cv%d" % i, [128, 32, 128], BF16) for i in range(2)]
                    ot = [SB(c, "cot%d" % i, [128, S], BF16) for i in range(2)]
                    rec = SB(c, "crec", [64, 1024], F32)
                    bE = [Buf() for _ in range(3)]
                    bPS = [Buf(), Buf()]
                    bPO = [Buf(), Buf()]
                    bQ = [Buf(), Buf()]
                    bV = [Buf(), Buf()]
                    bOt = [Buf(), Buf()]
                    bRec = Buf()
                    for i in range(2):
                        P.op("pool", lambda e, i=i: e.memset(va[i][:, :, 64:128], 1.0), writes=[bV[i]])
                    items = [(s, h) for s in range(2) for h in range(4)]

                    def loadC(n):
                        s, h = items[n]
                        i = n % 2
                        P.dmas("sp", [(qT[i][0:64, :], D["qn"][s, h // 2, (h % 2) * 64:(h % 2) * 64 + 64, :]),
                                      (qT[i][64:96, :], D["qr"][s, h * 32:(h + 1) * 32, :]),
                                      (kT[i][0:64, :], D["kn"][s, h // 2, (h % 2) * 64:(h % 2) * 64 + 64, :]),
                                      (kT[i][64:96, :], D["kr"][s, :, :])], writes=[bQ[i]])
                        src = D["vc"][s].rearrange("(n p) c -> p n c", p=128)
                        P.dmas("sp", [(va[i][:, n0:n0 + 8, 0:64], src[:, n0:n0 + 8, h * 64:(h + 1) * 64]) for n0 in range(0, 32, 8)], writes=[bV[i]])

                    units = []
                    u = 0
                    blk = 0
                    for n, (s, h) in enumerate(items):
                        i = n % 2
                        oi = (n // 2) % 2
                        pr = slice((h % 2) * 64, (h % 2) * 64 + 64)
                        for qb in range(4):
                            oj = blk % 2
                            blk += 1
                            cs = slice(qb * 1024, (qb + 1) * 1024)
                            for kc in range(32):
                                sj, ej = u % 2, u % 3
                                u += 1

                                def qk(i=i, kc=kc, qb=qb, sj=sj):
                                    P.op("pe", [(lambda e, hf=hf: e.matmul(psS[sj][:, hf * 512:(hf + 1) * 512], lhsT=kT[i][0:96, kc * 128:(kc + 1) * 128],
                                                                           rhs=qT[i][0:96, qb * 1024 + hf * 512:qb * 1024 + (hf + 1) * 512],
                                                                           start=True, stop=True)) for hf in range(2)],
                                         reads=[bQ[i]], writes=[bPS[sj]])

                                def mid(sj=sj, ej=ej):
                                    P.op("act", lambda e: e.activation(out=E[ej][:], in_=psS[sj][:], func=AF.Exp, scale=SC_C),
                                         reads=[bPS[sj]], writes=[bE[ej]])

                                def pv(i=i, kc=kc, ej=ej, oj=oj, oi=oi, pr=pr, cs=cs):
                                    P.op("pe", [(lambda e, hf=hf: e.matmul(psO[oj][:, hf * 512:(hf + 1) * 512], lhsT=va[i][:, kc, :],
                                                                           rhs=E[ej][:, hf * 512:(hf + 1) * 512], start=(kc == 0), stop=(kc == 31)))
                                                for hf in range(2)], reads=[bV[i], bE[ej]], writes=[bPO[oj]])
                                    if kc == 31:
                                        P.op("dve", lambda e: e.reciprocal(out=rec[:], in_=psO[oj][64:128, :]), reads=[bPO[oj]], writes=[bRec])
                                        P.op("dve", lambda e: e.tensor_tensor(out=ot[oi][pr, cs], in0=psO[oj][0:64, :], in1=rec[:], op=ALU.mult),
                                             reads=[bPO[oj], bRec], writes=[bOt[oi]])

                                units.append(dict(qk=qk, mid=mid, pv=pv))

                        def post(n=n, s=s, h=h, oi=oi):
                            if h % 2 == 1:
                                P.dma("sp", D["OT"][s, 768 + (h // 2) * 128:768 + (h // 2 + 1) * 128, :], ot[oi][:], reads=[bOt[oi]])
                            if n + 2 < len(items):
                                loadC(n + 2)
                        units[-1]["post"] = post
                    loadC(0)
                    loadC(1)
                    run_pipeline(units, 2)
                P.barrier()
                if "stopB" in dbg:
                    break

                with ExitStack() as c:
                    w_out = SB(c, "w_out", [128, 8, DM], BF16)
                    w_dn = SB(c, "w_dn", [128, NCH, DM], BF16)
                    wu = [SB(c, "wu%d" % i, [128, 8, 256], BF16) for i in range(3)]
                    cw = SB(c, "convw", [128, NCH, 3], F32)
                    cb = SB(c, "convb", [128, NCH], F32)
                    gpre = SB(c, "gpreF", [128, 8], F32)
                    scF = [SB(c, "scF%d" % s, [128, 8], F32) for s in range(2)]
                    biF = [SB(c, "biF%d" % s, [128, 8], F32) for s in range(2)]
                    gate1 = SB(c, "gate1", [128, DM], F32)
                    gate2 = SB(c, "gate2", [128, DM], F32)
                    gtmp = SB(c, "gtmp", [128, DM], F32)
                    xg = [SB(c, "xgC%d" % i, [128, 4, DM], F32) for i in range(2)]
                    otg = [SB(c, "otg%d" % i, [128, 8, 512], BF16) for i in range(2)]
                    h2T = [SB(c, "h2T%d" % i, [128, 8, 512], BF16) for i in range(2)]
                    halo = SB(c, "halo", [128, 8, 2], BF16)
                    edge = [SB(c, "edge%d" % i, [128, 8, 2], BF16) for i in range(3)]
                    actT = SB(c, "actT", [128, NCH, 512], BF16)
                    xn2 = SB(c, "xn2", [128, 4, DM], BF16)
                    junk = SB(c, "junkC", [128, DM], BF16)
                    tmp = SB(c, "tmpC", [128, DM], F32)
                    gcb = [SB(c, "gc%d" % i, [128, 512], F32) for i in range(2)]
                    geb = [SB(c, "ge%d" % i, [128, 512], F32) for i in range(2)]
                    ssv = SB(c, "ssC", [128, 16], F32)
                    lnc = SB(c, "lnC", [128, 16], F32)
                    rsc = SB(c, "rsC", [128, 16], F32)
                    psYs = [PS(c, "psY%d" % i, [128, 1024]) for i in range(2)]
                    psT1 = PS(c, "psTC", [128, 512], BF16)
                    psTs = [psT1[:], psT1[:]]
                    psHt = PS(c, "psH", [128, 2])
                    psH = psHt[:]
                    psA1 = PS(c, "psA", [128, 512])
                    psG1 = PS(c, "psG", [128, 512])
                    g_sb = [SB(c, "g_sb%d" % i, [128, 512], F32) for i in range(2)]
                    bGsb = [Buf(), Buf()]
                    bPG1 = Buf()
                    a_sb = [SB(c, "a_sb%d" % i, [128, 512], BF16) for i in range(2)]
                    hsb = SB(c, "hsb", [128, 2], F32)
                    bHsb = Buf()
                    bAsb = [Buf(), Buf()]
                    ycnt = [0]
                    bWo, bWd, bCw, bGp = Buf(), Buf(), Buf(), Buf()
                    bWu = [Buf() for _ in range(3)]
                    bScF = [Buf(), Buf()]
                    bGate, bGt = Buf(), Buf()
                    bXg = [Buf(), Buf()]
                    bOtg = [Buf(), Buf()]
                    bH2 = [Buf(), Buf()]
                    bEdge = [Buf(), Buf(), Buf()]
                    bHalo, bAct, bXn2, bJ, bTmp = Buf(), Buf(), Buf(), Buf(), Buf()
                    bGc = [Buf(), Buf()]
                    bGe = [Buf(), Buf()]
                    bSs, bLn, bRs = Buf(), Buf(), Buf()
                    bPYs = [Buf(), Buf()]
                    bPT1 = Buf()
                    bPTs = [bPT1, bPT1]
                    bPH = Buf()
                    bPA1 = Buf()
                    bPG = [Buf(), Buf()]

                    P.dmas("pool", [(w_out[:, k, :], D["w_out"][l, :, k, :]) for k in range(8)], writes=[bWo])
                    P.dmas("pool", [(w_dn[:, c0:c0 + 2, :], D["w_down"][l, :, c0:c0 + 2, :]) for c0 in range(0, NCH, 2)], writes=[bWd])
                    P.dmas("sp", [(cw[:], D["convwT"][l]), (cb[:], D["convbT"][l])], writes=[bCw])
                    P.dma("sp", gpre[:], D["gpre_ffnT"][l], writes=[bGp])
                    for s in range(2):
                        load_vec8(c, biF[s][:], l, s, 3, bScF[s])
                        load_vec8(c, scF[s][:], l, s, 4, bScF[s])
                        P.op("dve", lambda e, s=s: e.scalar_tensor_tensor(out=scF[s][:], in0=scF[s][:], scalar=1.0, in1=gpre[:], op0=ALU.add, op1=ALU.mult),
                             reads=[bScF[s], bGp], writes=[bScF[s]])

                    def load_gates(s):
                        for gt, jm, gp in ((gate1, 2, "gpost_mix"), (gate2, 5, "gpost_ffn")):
                            P.dma("sp", gt[:], D["modD"][l, s:s + 1, jm * 1024:(jm + 1) * 1024].partition_broadcast(128), writes=[bGate])
                            P.dma("sp", gtmp[:], D[gp][l:l + 1, :].partition_broadcast(128), writes=[bGt])
                            P.op("dve", lambda e, gt=gt: e.tensor_tensor(out=gt[:], in0=gt[:], in1=gtmp[:], op=ALU.mult), reads=[bGate, bGt], writes=[bGate])

                    def loadC1(g):
                        s, t0 = g // 8, (g % 8) * 512
                        j = g % 2
                        P.dma("sp", xg[j][:], x_src[s, t0:t0 + 512, :].rearrange("(t p) d -> p t d", p=128), writes=[bXg[j]])
                        P.dma("sp", otg[j][:], D["OT"][s, :, t0:t0 + 512].rearrange("(k p) t -> p k t", p=128), writes=[bOtg[j]])

                    def norm_resid(X, t, gate, col, bX, psY, bPY):
                        P.op("act", lambda e: e.activation(out=junk[:], in_=psY[:], func=AF.Square, accum_out=ssv[:, col:col + 1]),
                             reads=[bPY], writes=[bJ, bSs])
                        P.op("act", lambda e: e.activation(out=lnc[:, col:col + 1], in_=ssv[:, col:col + 1], func=AF.Ln, scale=1.0 / DM, bias=EPS),
                             reads=[bSs], writes=[bLn])
                        P.op("act", lambda e: e.activation(out=rsc[:, col:col + 1], in_=lnc[:, col:col + 1], func=AF.Exp, scale=-0.5),
                             reads=[bLn], writes=[bRs])
                        P.op("dve", lambda e: e.scalar_tensor_tensor(out=tmp[:], in0=psY[:], scalar=rsc[:, col:col + 1], in1=gate[:], op0=ALU.mult, op1=ALU.mult),
                             reads=[bPY, bRs, bGate], writes=[bTmp])
                        P.op("pool", lambda e: e.tensor_tensor(out=X[:, t, :], in0=X[:, t, :], in1=tmp[:], op=ALU.add), reads=[bTmp, bX], writes=[bX])

                    def stage1(g):
                        s, t0 = g // 8, (g % 8) * 512
                        j = g % 2
                        X = xg[j]
                        for t in range(4):
                            yi = ycnt[0] % 2
                            ycnt[0] += 1
                            psY, bPY = psYs[yi], bPYs[yi]
                            fns = []
                            for hf in range(2):
                                for k in range(8):
                                    fns.append(lambda e, k=k, hf=hf, t=t, psY=psY: e.matmul(psY[:, hf * 512:(hf + 1) * 512], lhsT=otg[j][:, k, t * 128:(t + 1) * 128],
                                                                                   rhs=w_out[:, k, hf * 512:(hf + 1) * 512], start=(k == 0), stop=(k == 7)))
                            P.op("pe", fns, reads=[bOtg[j], bWo], writes=[bPY])
                            norm_resid(X, t, gate1, t, bXg[j], psY, bPY)
                            P.op("act", lambda e, t=t: e.activation(out=junk[:], in_=X[:, t, :], func=AF.Square, accum_out=ssv[:, 4 + t:5 + t]),
                                 reads=[bXg[j]], writes=[bJ, bSs])
                        P.op("act", lambda e: e.activation(out=lnc[:, 4:8], in_=ssv[:, 4:8], func=AF.Ln, scale=1.0 / DM, bias=EPS), reads=[bSs], writes=[bLn])
                        P.op("act", lambda e: e.activation(out=rsc[:, 4:8], in_=lnc[:, 4:8], func=AF.Exp, scale=-0.5), reads=[bLn], writes=[bRs])
                        for t in range(4):
                            P.op("dve", lambda e, t=t: e.tensor_scalar(out=xn2[:, t, :], in0=X[:, t, :], scalar1=rsc[:, 4 + t:5 + t], scalar2=None, op0=ALU.mult),
                                 reads=[bXg[j], bRs], writes=[bXn2])
                        for k in range(8):
                            psT, bPT = psTs[k % 2], bPTs[k % 2]
                            P.op("pe", [(lambda e, t=t, k=k, psT=psT: e.transpose(psT[:, t * 128:(t + 1) * 128], xn2[:, t, k * 128:(k + 1) * 128], ident[:]))
                                        for t in range(4)], reads=[bXn2, bId], writes=[bPT])
                            P.op("act", lambda e, k=k, psT=psT: e.activation(out=h2T[j][:, k, :], in_=psT[:], func=AF.Identity,
                                                                    scale=scF[s][:, k:k + 1], bias=biF[s][:, k:k + 1]),
                                 reads=[bPT, bScF[s]], writes=[bH2[j]])
                        ej3 = g % 3
                        P.op("pool", lambda e: e.tensor_copy(out=edge[ej3][:, :, 0:1], in_=h2T[j][:, :, 0:1]), reads=[bH2[j]], writes=[bEdge[ej3]])
                        P.op("pool", lambda e: e.tensor_copy(out=edge[ej3][:, :, 1:2], in_=h2T[j][:, :, 511:512]), reads=[bH2[j]], writes=[bEdge[ej3]])

                    wcount = [0]

                    def load_wu(ci):
                        wi = wcount[0] % 3
                        wcount[0] += 1
                        P.dma("pool", wu[wi][:], D["w_up"][l, ci], writes=[bWu[wi]])
                        return wi

                    def stage2(g, have_next):
                        s, t0 = g // 8, (g % 8) * 512
                        j = g % 2
                        X = xg[j]
                        if t0 > 0:
                            P.op("pool", lambda e: e.tensor_copy(out=halo[:, :, 0:1], in_=edge[(g - 1) % 3][:, :, 1:2]), reads=[bEdge[(g - 1) % 3]], writes=[bHalo])
                        else:
                            P.op("pool", lambda e: e.memset(halo[:, :, 0:1], 0.0), writes=[bHalo])
                        if t0 + 512 < S:
                            assert have_next
                            P.op("pool", lambda e: e.tensor_copy(out=halo[:, :, 1:2], in_=edge[(g + 1) % 3][:, :, 0:1]), reads=[bEdge[(g + 1) % 3]], writes=[bHalo])
                        else:
                            P.op("pool", lambda e: e.memset(halo[:, :, 1:2], 0.0), writes=[bHalo])
                        def finish_chunk(ci):
                            pj = ci % 2
                            gc, ge = gcb[pj], geb[pj]
                            P.op("act", lambda e: e.activation(out=ge[:], in_=gc[:], func=AF.Gelu_apprx_tanh), reads=[bGc[pj]], writes=[bGe[pj]])
                            P.op("pool", lambda e: e.tensor_tensor(out=actT[:, ci, :], in0=a_sb[pj][:], in1=ge[:], op=ALU.mult),
                                 reads=[bAsb[pj], bGe[pj]], writes=[bAct])

                        pend = [load_wu(0), load_wu(1)]
                        for ci in range(NCH):
                            wi = pend.pop(0)
                            if ci + 2 < NCH:
                                pend.append(load_wu(ci + 2))
                            pj = ci % 2
                            W = wu[wi]
                            P.op("pe", [(lambda e, k=k, W=W, pj=pj: e.matmul(psA1[:], lhsT=W[:, k, 0:128], rhs=h2T[j][:, k, :], start=(k == 0), stop=(k == 7)))
                                        for k in range(8)], reads=[bWu[wi], bH2[j]], writes=[bPA1])
                            P.op("act", lambda e, pj=pj: e.copy(out=a_sb[pj][:], in_=psA1[:]), reads=[bPA1], writes=[bAsb[pj]])
                            P.op("pe", [(lambda e, k=k, W=W: e.matmul(psG1[:], lhsT=W[:, k, 128:256], rhs=h2T[j][:, k, :], start=(k == 0), stop=(k == 7)))
                                        for k in range(8)], reads=[bWu[wi], bH2[j]], writes=[bPG1])
                            P.op("pe", [(lambda e, k=k, W=W: e.matmul(psH, lhsT=W[:, k, 128:256], rhs=halo[:, k, :], start=(k == 0), stop=(k == 7)))
                                        for k in range(8)], reads=[bWu[wi], bHalo], writes=[bPH])
                            gc, ge, gs = gcb[pj], geb[pj], g_sb[pj]
                            P.op("act", lambda e, gs=gs: e.copy(out=gs[:], in_=psG1[:]), reads=[bPG1], writes=[bGsb[pj]])
                            P.op("dve", lambda e, ci=ci, gc=gc, gs=gs: e.tensor_scalar(out=gc[:], in0=gs[:], scalar1=cw[:, ci, 1:2], scalar2=cb[:, ci:ci + 1],
                                                                                       op0=ALU.mult, op1=ALU.add),
                                 reads=[bGsb[pj], bCw], writes=[bGc[pj]])
                            P.op("dve", lambda e, ci=ci, gc=gc, gs=gs: e.scalar_tensor_tensor(out=gc[:, 1:512], in0=gs[:, 0:511], scalar=cw[:, ci, 0:1],
                                                                                              in1=gc[:, 1:512], op0=ALU.mult, op1=ALU.add),
                                 reads=[bGsb[pj], bCw, bGc[pj]], writes=[bGc[pj]])
                            P.op("dve", lambda e, ci=ci, gc=gc, gs=gs: e.scalar_tensor_tensor(out=gc[:, 0:511], in0=gs[:, 1:512], scalar=cw[:, ci, 2:3],
                                                                                              in1=gc[:, 0:511], op0=ALU.mult, op1=ALU.add),
                                 reads=[bGsb[pj], bCw, bGc[pj]], writes=[bGc[pj]])
                            P.op("act", lambda e: e.copy(out=hsb[:], in_=psH), reads=[bPH], writes=[bHsb])
                            P.op("dve", lambda e, ci=ci, gc=gc: e.scalar_tensor_tensor(out=gc[:, 0:1], in0=hsb[:, 0:1], scalar=cw[:, ci, 0:1], in1=gc[:, 0:1],
                                                                                       op0=ALU.mult, op1=ALU.add), reads=[bHsb, bCw, bGc[pj]], writes=[bGc[pj]])
                            P.op("dve", lambda e, ci=ci, gc=gc: e.scalar_tensor_tensor(out=gc[:, 511:512], in0=hsb[:, 1:2], scalar=cw[:, ci, 2:3], in1=gc[:, 511:512],
                                                                                       op0=ALU.mult, op1=ALU.add), reads=[bHsb, bCw, bGc[pj]], writes=[bGc[pj]])
                            if ci >= 1:
                                finish_chunk(ci - 1)
                        finish_chunk(NCH - 1)
                        for t in range(4):
                            yi = ycnt[0] % 2
                            ycnt[0] += 1
                            psY, bPY = psYs[yi], bPYs[yi]
                            fns = []
                            for hf in range(2):
                                for ci in range(NCH):
                                    fns.append(lambda e, ci=ci, hf=hf, t=t, psY=psY: e.matmul(psY[:, hf * 512:(hf + 1) * 512], lhsT=actT[:, ci, t * 128:(t + 1) * 128],
                                                                                     rhs=w_dn[:, ci, hf * 512:(hf + 1) * 512], start=(ci == 0), stop=(ci == NCH - 1)))
                            P.op("pe", fns, reads=[bAct, bWd], writes=[bPY])
                            norm_resid(X, t, gate2, 8 + t, bXg[j], psY, bPY)
                        P.dma("sp", x_dst[s, t0:t0 + 512, :].rearrange("(t p) d -> p t d", p=128), X[:], reads=[bXg[j]])

                    NG = 16
                    load_gates(0)
                    loadC1(0)
                    stage1(0)
                    for g in range(NG):
                        s = g // 8
                        nxt_same_seq = (g + 1 < NG) and ((g + 1) // 8 == s)
                        if nxt_same_seq:
                            loadC1(g + 1)
                            stage1(g + 1)
                        stage2(g, nxt_same_seq)
                        if g + 1 < NG and not nxt_same_seq:
                            load_gates((g + 1) // 8)
                            loadC1(g + 1)
                            stage1(g + 1)
                P.barrier()

        except _Stop:
            pass
        P.barrier()
    return nc


def _prep_shared(inp):
    f = lambda a: np.ascontiguousarray(np.asarray(a, dtype=np.float32))
    L = 4
    sh = {}
    sh["w_ada"] = f(np.asarray(inp["w_ada"]).reshape(L, 8, 128, 6144).transpose(0, 2, 1, 3))
    sh["b_ada"] = f(inp["b_ada"])
    sh["gpre_mixT"] = f(np.asarray(inp["g_pre_mix"]).reshape(L, 8, 128).transpose(0, 2, 1))
    sh["gpre_ffnT"] = f(np.asarray(inp["g_pre_ffn"]).reshape(L, 8, 128).transpose(0, 2, 1))
    sh["gpost_mix"] = f(inp["g_post_mix"])
    sh["gpost_ffn"] = f(inp["g_post_ffn"])
    sh["w_in"] = f(np.asarray(inp["w_in"]).reshape(L, 8, 128, D_IN).transpose(0, 2, 1, 3))
    wuq = np.asarray(inp["w_uq"]).reshape(L, 384, 4, 96)
    wuq = np.concatenate([wuq[..., :64].reshape(L, 384, 256), wuq[..., 64:].reshape(L, 384, 128)], axis=-1)
    sh["w_uq"] = f(wuq.reshape(L, 3, 128, 384).transpose(0, 2, 1, 3))
    wukv = np.asarray(inp["w_ukv"]).reshape(L, 256, 4, 128)
    wukv = np.concatenate([wukv[..., :64].reshape(L, 256, 256), wukv[..., 64:].reshape(L, 256, 256)], axis=-1)
    sh["w_ukv"] = f(wukv.reshape(L, 2, 128, 512).transpose(0, 2, 1, 3))
    sh["gqT"] = f(np.asarray(inp["mla_g_q"]).reshape(L, 3, 128).transpose(0, 2, 1))
    sh["gkvT"] = f(np.asarray(inp["mla_g_kv"]).reshape(L, 2, 128).transpose(0, 2, 1))
    sh["w_out"] = f(np.asarray(inp["w_out"]).reshape(L, 8, 128, DM).transpose(0, 2, 1, 3))
    wup = np.asarray(inp["w_up"]).reshape(L, 8, 128, 2, NCH, 128)
    sh["w_up"] = f(wup.transpose(0, 4, 2, 1, 3, 5).reshape(L, NCH, 128, 8, 256))
    sh["w_down"] = f(np.asarray(inp["w_down"]).reshape(L, NCH, 128, DM).transpose(0, 2, 1, 3))
    sh["convwT"] = f(np.asarray(inp["conv_w"]).reshape(L, 3, NCH, 128).transpose(0, 3, 2, 1))
    sh["convbT"] = f(np.asarray(inp["conv_b"]).reshape(L, NCH, 128).transpose(0, 2, 1))
    G, cm, T, tm = _host_tables(np.asarray(inp["na_rpb"], np.float32), np.asarray(inp["t5_table"], np.float32))
    sh["naG"] = f(G)
    sh["nacm"] = f(cm)
    sh["t5G"] = f(T)
    sh["t5m"] = f(tm)
    sh["pens"] = f(NA_PENS)
    A2 = np.zeros((2, 128), np.float32)
    A2[0, :64] = 1.0
    A2[1, 64:] = 1.0
    sh["A2"] = A2
    cosT, sinT = _rope_tables()
    sh["cosT"] = f(cosT)
    sh["sinT"] = f(sinT)
    return sh


def _in_maps(inp, ncores):
    sh = _prep_shared(inp)
    x = np.asarray(inp["x"], np.float32)
    cc = np.asarray(inp["c"], np.float32)
    maps = []
    for i in range(ncores):
        m = dict(sh)
        m["x"] = np.ascontiguousarray(x[2 * i:2 * i + 2])
        m["cT"] = np.ascontiguousarray(cc[2 * i:2 * i + 2].reshape(2, 8, 128).transpose(2, 1, 0))
        maps.append(m)
    return maps


_NC_CACHE = {}


def kernel(**inputs):
    if "nc" not in _NC_CACHE:
        _NC_CACHE["nc"] = build()
    nc = _NC_CACHE["nc"]
    maps = _in_maps(inputs, NCORES)
    res = run_bass_kernel_spmd(nc, maps, core_ids=list(range(NCORES)))
    out = np.concatenate([np.asarray(r["out"], dtype=np.float32) for r in res.results], axis=0)
    return out
```

```python
import math
from contextlib import ExitStack

import numpy as np
import ml_dtypes
import concourse.bass as bass
import concourse.mybir as mybir
from concourse.bass_utils import run_bass_kernel_spmd

F32 = mybir.dt.float32
BF16 = mybir.dt.bfloat16
AF = mybir.ActivationFunctionType
ALU = mybir.AluOpType

NCORES = 8
S = 4096
DM = 1024
DFF = 2816
NCH = 22
D_IN = 2976
EPS = 1e-6
NEG = -30000.0


class Buf:
    __slots__ = ("lw", "rd")

    def __init__(self):
        self.lw = []
        self.rd = {}


class Prog:
    def __init__(self, nc, ctx, n_dma_sems=40):
        self.nc = nc
        self.E = {"pe": nc.tensor, "act": nc.scalar, "dve": nc.vector, "pool": nc.gpsimd, "sp": nc.sync}
        self.psem = {k: ctx.enter_context(nc.semaphore("ps_" + k)) for k in self.E}
        self.pcnt = {k: 0 for k in self.E}
        self.seen = {k: {} for k in self.E}
        self.dsem = [ctx.enter_context(nc.semaphore("ds%d" % i)) for i in range(n_dma_sems)]
        self.dcnt = [0] * n_dma_sems
        self.dnext = 0
        self.nins = 0
        self.dead = False

    def _wait(self, eng, deps):
        if self.dead:
            return
        need = {}
        for d in deps:
            if d is None:
                continue
            s, v = d
            if need.get(s, 0) < v:
                need[s] = v
        seen = self.seen[eng]
        for s, v in need.items():
            if seen.get(s, 0) < v:
                self.E[eng].wait_ge(s, v)
                seen[s] = v
                self.nins += 1

    @staticmethod
    def _deps(reads, writes):
        deps = []
        for b in reads:
            deps.extend(b.lw)
        for b in writes:
            deps.extend(b.lw)
            deps.extend(b.rd.values())
        return deps

    def op(self, eng, fns, reads=(), writes=()):
        if self.dead:
            return None
        self._wait(eng, self._deps(reads, writes))
        if callable(fns):
            fns = [fns]
        ins = None
        e = self.E[eng]
        for f in fns:
            ins = f(e)
            self.nins += 1
        self.pcnt[eng] += 1
        ins.then_inc(self.psem[eng], 1)
        tok = (self.psem[eng], self.pcnt[eng])
        for b in reads:
            b.rd[eng] = tok
        for b in writes:
            b.lw = [tok]
            b.rd = {}
        return tok

    def dma(self, eng, out, in_, reads=(), writes=(), **kw):
        return self.dmas(eng, [(out, in_)], reads, writes, **kw)

    def dmas(self, eng, pairs, reads=(), writes=(), **kw):
        if self.dead:
            return []
        deps = self._deps(reads, writes)
        idx = []
        for _ in pairs:
            i = self.dnext
            self.dnext = (i + 1) % len(self.dsem)
            idx.append(i)
            if self.dcnt[i] > 0:
                deps.append((self.dsem[i], 16 * self.dcnt[i]))
        assert len(set(idx)) == len(idx)
        self._wait(eng, deps)
        toks = []
        for i, (out, in_) in zip(idx, pairs):
            self.dcnt[i] += 1
            self.E[eng].dma_start(out=out, in_=in_, **kw).then_inc(self.dsem[i], 16)
            self.nins += 1
            toks.append((self.dsem[i], 16 * self.dcnt[i]))
        for b in reads:
            for i, tok in zip(idx, toks):
                b.rd[("dma", i)] = tok
        for b in writes:
            b.lw = list(toks)
            b.rd = {}
        return toks

    def barrier(self):
        deps = [(self.psem[k], self.pcnt[k]) for k in self.E if self.pcnt[k] > 0]
        deps += [(s, 16 * c) for s, c in zip(self.dsem, self.dcnt) if c > 0]
        for k in self.E:
            self._wait(k, deps)


def _r_start(r):
    return min(max(r - 4, 0), 56)


def _c_start(c):
    return min(max(c - 8, 0), 48)


def _na_plan():
    pats = {}
    plan = []
    for J in range(16):
        klo = _r_start(4 * J)
        khi = _r_start(4 * J + 3) + 7
        units = []
        for i in range(klo // 2, khi // 2 + 1):
            key = []
            for a in range(2):
                for ap in range(4):
                    r = 4 * J + ap
                    kr = 2 * i + a
                    key.append(_r_start(r) <= kr <= _r_start(r) + 7)
            key = tuple(key)
            if all(key):
                pid = -1
            else:
                if key not in pats:
                    pats[key] = len(pats)
                pid = pats[key]
            di0 = 4 * J - 2 * i + 6
            assert 0 <= di0 <= 10
            units.append((i, di0, pid))
        plan.append(units)
    npat = len(pats)
    pens = np.zeros((2, npat, 4, 64), np.float32)
    for key, pid in pats.items():
        for a in range(2):
            for ap in range(4):
                if not key[a * 4 + ap]:
                    pens[a, pid, ap, :] = NEG
    return plan, pens.reshape(2, npat * 256)


NA_PLAN, NA_PENS = _na_plan()
NPAT = NA_PENS.shape[1] // 256


def _t5_bucket(rel):
    nb = 16
    max_exact = 8
    n = np.abs(rel)
    large = max_exact + (np.log(np.maximum(n, 1) / max_exact) / math.log(1024 / max_exact) * (nb - max_exact)).astype(np.int64)
    large = np.minimum(large, nb - 1)
    return (np.where(rel > 0, nb, 0) + np.where(n < max_exact, n, large)).astype(np.int32)


def _host_tables(na_rpb, t5_table):
    p = np.arange(128)
    a = p // 64
    kc = p % 64
    di = np.arange(14)
    c = np.arange(64)
    drow = a[:, None] - (di[None, :] - 6) + 7
    dcol = kc[:, None] - c[None, :] + 15
    ok = ((drow >= 0) & (drow <= 14))[:, :, None] & ((dcol >= 0) & (dcol <= 30))[:, None, :]
    drc = np.clip(drow, 0, 14)
    dcc = np.clip(dcol, 0, 30)
    G = na_rpb[:, :, drc[:, :, None], dcc[:, None, :]]
    G = np.where(ok[None, None], G, np.float32(0.0)).astype(np.float32)
    G = np.ascontiguousarray(np.transpose(G, (0, 2, 1, 3, 4))).reshape(na_rpb.shape[0], 128, 4 * 14 * 64)
    cs = np.array([_c_start(x) for x in range(64)])
    cm = ((kc[:, None] >= cs[None, :]) & (kc[:, None] < cs[None, :] + 16)).astype(np.float32)
    cm = np.ascontiguousarray(np.broadcast_to(cm[:, None, :], (128, 14, 64))).reshape(128, 14 * 64)
    q = np.arange(128)
    pc = np.arange(3)
    rel = 128 * (1 - pc[None, :, None]) + p[:, None, None] - q[None, None, :]
    valid = (np.abs(rel) <= 64)
    T = np.zeros((128, 8, 3, 3, 128), np.float32)
    for pi, d in enumerate((1, 4, 16)):
        b = _t5_bucket(rel * d)
        vals = t5_table[b]
        vals = np.where(valid[..., None], vals, np.float32(0.0))
        T[:, :, pi] = np.transpose(vals, (0, 3, 1, 2))
    T = T.reshape(128, 8 * 3 * 384)
    tm = valid.astype(np.float32).reshape(128, 384)
    return G, cm, T, tm


def _rope_tables():
    inv_freq = (10000.0 ** (-np.arange(0, 32, 2, dtype=np.float32) / 32)).astype(np.float32)
    ang = np.arange(S, dtype=np.float32)[:, None] * inv_freq[None, :]
    cos = np.cos(ang).astype(np.float32)
    sin = np.sin(ang).astype(np.float32)
    idx = (np.arange(128) % 32) % 16
    return np.ascontiguousarray(cos[:, idx].T), np.ascontiguousarray(sin[:, idx].T)


class _Stop(Exception):
    pass


def build(NL=4, dbg=()):
    nc = bass.Bass("TRN2", target_bir_lowering=False)
    stop_at = -1
    for dflag in dbg:
        if dflag.startswith("stage="):
            stop_at = int(dflag[6:])

    PP = []

    def ckpt(n):
        if n == stop_at and not PP[0].dead:
            PP[0].barrier()
            PP[0].dead = True
    D = {}

    def din(name, shape, dt=F32):
        D[name] = nc.dram_tensor(name, list(shape), dt, kind="ExternalInput").ap()

    def dscr(name, shape, dt):
        kind = "ExternalOutput" if name in dbg else "Internal"
        D[name] = nc.dram_tensor(name, list(shape), dt, kind=kind).ap()

    din("x", [2, S, DM])
    din("cT", [128, 8, 2])
    din("w_ada", [4, 128, 8, 6144])
    din("b_ada", [4, 6144])
    din("gpre_mixT", [4, 128, 8])
    din("gpre_ffnT", [4, 128, 8])
    din("gpost_mix", [4, DM])
    din("gpost_ffn", [4, DM])
    din("w_in", [4, 128, 8, D_IN])
    din("w_uq", [4, 128, 3, 384])
    din("w_ukv", [4, 128, 2, 512])
    din("gqT", [4, 128, 3])
    din("gkvT", [4, 128, 2])
    din("w_out", [4, 128, 8, DM])
    din("w_up", [4, NCH, 128, 8, 256])
    din("w_down", [4, 128, NCH, DM])
    din("convwT", [4, 128, NCH, 3])
    din("convbT", [4, 128, NCH])
    din("naG", [4, 128, 4 * 14 * 64])
    din("nacm", [128, 14 * 64])
    din("pens", [2, NPAT * 256])
    din("A2", [2, 128])
    din("t5G", [128, 8 * 3 * 384])
    din("t5m", [128, 384])
    din("cosT", [128, S])
    din("sinT", [128, S])
    D["out"] = nc.dram_tensor("out", [2, S, DM], F32, kind="ExternalOutput").ap()
    dscr("modD", [4, 2, 6144], F32)
    dscr("xs", [2, S, DM], F32)
    dscr("zT", [2, 12, 128, S], BF16)
    dscr("vtok", [2, S, 768], BF16)
    dscr("qn", [2, 2, 128, S], BF16)
    dscr("qr", [2, 128, S], BF16)
    dscr("kn", [2, 2, 128, S], BF16)
    dscr("kr", [2, 32, S], BF16)
    dscr("vc", [2, S, 256], BF16)
    dscr("OT", [2, DM, S], BF16)

    with ExitStack() as gctx:
        P = Prog(nc, gctx)
        PP.append(P)

        uid = [0]

        def SB(ctx, name, shape, dt):
            uid[0] += 1
            return ctx.enter_context(nc.sbuf_tensor("%s_u%d" % (name, uid[0]), list(shape), dt))

        def PS(ctx, name, shape, dt=F32):
            uid[0] += 1
            return ctx.enter_context(nc.psum_tensor("%s_u%d" % (name, uid[0]), list(shape), dt))

        zcol = SB(gctx, "zcol", [128, 1], F32)
        bZc = Buf()
        ident = SB(gctx, "ident", [128, 128], BF16)
        identf = SB(gctx, "identf", [128, 128], F32)
        bId = Buf()
        P.op("pool", lambda e: e.memset(zcol[:], 0.0), writes=[bZc])
        P.op("pool", lambda e: e.memset(identf[:], 1.0), writes=[bId])
        P.op("pool", lambda e: e.affine_select(out=identf[:], in_=identf[:], pattern=[[-1, 128]],
                                                compare_op=ALU.is_equal, fill=0.0, base=0, channel_multiplier=1),
             reads=[bId], writes=[bId])
        P.op("pool", lambda e: e.tensor_copy(out=ident[:], in_=identf[:]), reads=[bId], writes=[bId])

        def rstd_from_ss(ss_ap, out_ap, n, bss, bout, tmp_ap, btmp):
            P.op("act", lambda e: e.activation(out=tmp_ap, in_=ss_ap, func=AF.Ln, scale=1.0 / n, bias=EPS),
                 reads=[bss], writes=[btmp])
            P.op("act", lambda e: e.activation(out=out_ap, in_=tmp_ap, func=AF.Exp, scale=-0.5),
                 reads=[btmp], writes=[bout])

        try:
            with ExitStack() as c:
                cT = SB(c, "cT_sb", [128, 8, 2], F32)
                cact = SB(c, "cact", [128, 8, 2], F32)
                brow = SB(c, "brow", [2, 6144], F32)
                mrow = SB(c, "mrow", [2, 6144], F32)
                wt = [SB(c, "wada%d" % i, [128, 8, 512], F32) for i in range(2)]
                psM = [PS(c, "psM%d" % i, [2, 512]) for i in range(2)]
                bc, bb, bm = Buf(), Buf(), Buf()
                bw = [Buf(), Buf()]
                bp = [Buf(), Buf()]
                P.dma("sp", cT[:], D["cT"], writes=[bc])
                P.op("act", lambda e: e.activation(out=cact[:], in_=cT[:], func=AF.Silu), reads=[bc], writes=[bc])
                it = 0
                for l in range(NL):
                    P.dma("sp", brow[:], D["b_ada"][l:l + 1, :].partition_broadcast(2), writes=[bb])
                    for nb in range(12):
                        j = it % 2
                        it += 1
                        P.dma("sp", wt[j][:], D["w_ada"][l, :, :, nb * 512:(nb + 1) * 512], writes=[bw[j]])
                        P.op("pe", [(lambda e, k=k, j=j: e.matmul(psM[j][:], lhsT=cact[:, k, :], rhs=wt[j][:, k, :],
                                                                  start=(k == 0), stop=(k == 7))) for k in range(8)],
                             reads=[bc, bw[j]], writes=[bp[j]])
                        P.op("dve", lambda e, j=j, nb=nb: e.tensor_tensor(out=mrow[:, nb * 512:(nb + 1) * 512], in0=psM[j][:],
                                                                          in1=brow[:, nb * 512:(nb + 1) * 512], op=ALU.add),
                             reads=[bp[j], bb], writes=[bm])
                    P.dma("sp", D["modD"][l], mrow[:], reads=[bm])
            P.barrier()
            ckpt(1)

            def load_vec8(ctx_name, dst, l, s, j, bdst):
                src = D["modD"][l, s, j * 1024:(j + 1) * 1024].rearrange("(k p) -> p k", p=128)
                P.dma("sp", dst, src, writes=[bdst], allow_slow_non_contiguous=True)

            for l in range(NL):
                x_src = D["x"] if l == 0 else D["xs"]
                x_dst = D["out"] if l == NL - 1 else D["xs"]

                with ExitStack() as c:
                    w_in = SB(c, "w_in", [128, 8, D_IN], BF16)
                    w_rot = SB(c, "w_rot", [128, 8, 32], BF16)
                    w_uq = SB(c, "w_uq", [128, 3, 384], BF16)
                    w_uqr = SB(c, "w_uqr", [128, 3, 128], BF16)
                    w_ukv = SB(c, "w_ukv", [128, 2, 512], BF16)
                    gq = SB(c, "gq", [128, 3], F32)
                    gkv = SB(c, "gkv", [128, 2], F32)
                    gpre = SB(c, "gpre", [128, 8], F32)
                    scA = [SB(c, "scA%d" % s, [128, 8], F32) for s in range(2)]
                    biA = [SB(c, "biA%d" % s, [128, 8], F32) for s in range(2)]
                    xg = [SB(c, "xgA%d" % i, [128, 4, DM], F32) for i in range(2)]
                    cosg = [SB(c, "cosg%d" % i, [128, 512], F32) for i in range(2)]
                    sing = [SB(c, "sing%d" % i, [128, 512], F32) for i in range(2)]
                    junk = SB(c, "junkA", [128, DM], BF16)
                    ss = SB(c, "ssA", [128, 4], F32)
                    lnv = SB(c, "lnvA", [128, 4], F32)
                    rstd = SB(c, "rstdA", [128, 4], F32)
                    ssq = SB(c, "ssq", [128, 4], F32)
                    sskv = SB(c, "sskv", [128, 4], F32)
                    lnq = SB(c, "lnq", [128, 4], F32)
                    lnk = SB(c, "lnk", [128, 4], F32)
                    rsq = SB(c, "rsq", [128, 4], F32)
                    rskv = SB(c, "rskv", [128, 4], F32)
                    xn = SB(c, "xnA", [128, 4, DM], BF16)
                    hT = SB(c, "hT", [128, 8, 512], BF16)
                    zst = [SB(c, "zst%d" % i, [128, 12, 512], BF16) for i in range(2)]
                    vst = [SB(c, "vst%d" % i, [128, 4, 768], BF16) for i in range(2)]
                    cst = SB(c, "cst", [128, 4, 640], F32)
                    cqn = SB(c, "cqn", [128, 4, 384], BF16)
                    ckvn = SB(c, "ckvn", [128, 4, 256], BF16)
                    cqT = SB(c, "cqT", [128, 3, 512], BF16)
                    ckvT = SB(c, "ckvT", [128, 2, 512], BF16)
                    qnst = [SB(c, "qnst%d" % i, [128, 2, 512], BF16) for i in range(2)]
                    knst = [SB(c, "knst%d" % i, [128, 2, 512], BF16) for i in range(2)]
                    qrst = [SB(c, "qrst%d" % i, [128, 512], BF16) for i in range(2)]
                    krst = [SB(c, "krst%d" % i, [32, 512], BF16) for i in range(2)]
                    vcst = [SB(c, "vcst%d" % i, [128, 4, 256], BF16) for i in range(2)]
                    t1 = SB(c, "t1A", [128, 512], F32)
                    t2 = SB(c, "t2A", [128, 512], F32)
                    psT = [PS(c, "psTA%d" % i, [128, 512], BF16) for i in range(2)]
                    psZ = [PS(c, "psZA%d" % i, [128, 512]) for i in range(2)]
                    psW = [PS(c, "psWA%d" % i, [128, 1024]) for i in range(2)]

                    bW, bWr, bUq, bUqr, bUkv, bG = Buf(), Buf(), Buf(), Buf(), Buf(), Buf()
                    bSc = [Buf(), Buf()]
                    bXg = [Buf(), Buf()]
                    bCS = [Buf(), Buf()]
                    bJ, bSS, bLn, bRs, bXn, bHT = Buf(), Buf(), Buf(), Buf(), Buf(), Buf()
                    bSq, bLq, bRq = Buf(), Buf(), Buf()
                    bZst = [Buf(), Buf()]
                    bVst = [Buf(), Buf()]
                    bCst, bCqn, bCkvn, bCqT, bCkvT = Buf(), Buf(), Buf(), Buf(), Buf()
                    bQn = [Buf(), Buf()]
                    bKn = [Buf(), Buf()]
                    bQr = [Buf(), Buf()]
                    bKr = [Buf(), Buf()]
                    bVc = [Buf(), Buf()]
                    bT1, bT2 = Buf(), Buf()
                    bPT = [Buf(), Buf()]
                    bPZ = [Buf(), Buf()]
                    bPW = [Buf(), Buf()]

                    P.dmas("pool", [(w_in[:, k, :], D["w_in"][l, :, k, :]) for k in range(8)], writes=[bW])
                    P.dma("pool", w_uq[:], D["w_uq"][l], writes=[bUq])
                    P.dma("pool", w_ukv[:], D["w_ukv"][l], writes=[bUkv])
                    P.dmas("sp", [(gq[:], D["gqT"][l]), (gkv[:], D["gkvT"][l]), (gpre[:], D["gpre_mixT"][l])], writes=[bG])
                    P.op("act", lambda e: e.mul(out=w_rot[:, :, 0:16], in_=w_in[:, :, 2960:2976], mul=-1.0), reads=[bW], writes=[bWr])
                    P.op("act", lambda e: e.copy(out=w_rot[:, :, 16:32], in_=w_in[:, :, 2944:2960]), reads=[bW], writes=[bWr])
                    for k in range(3):
                        src = w_uq[:, k, 256:384].rearrange("p (h t j) -> p h t j", t=2, j=16)
                        dst = w_uqr[:, k, :].rearrange("p (h t j) -> p h t j", t=2, j=16)
                        P.op("act", lambda e, src=src, dst=dst: e.mul(out=dst[:, :, 0, :], in_=src[:, :, 1, :], mul=-1.0),
                             reads=[bUq], writes=[bUqr])
                        P.op("act", lambda e, src=src, dst=dst: e.copy(out=dst[:, :, 1, :], in_=src[:, :, 0, :]),
                             reads=[bUq], writes=[bUqr])
                    for s in range(2):
                        load_vec8(c, biA[s][:], l, s, 0, bSc[s])
                        load_vec8(c, scA[s][:], l, s, 1, bSc[s])
                        P.op("dve", lambda e, s=s: e.scalar_tensor_tensor(out=scA[s][:], in0=scA[s][:], scalar=1.0, in1=gpre[:],
                                                                          op0=ALU.add, op1=ALU.mult),
                             reads=[bSc[s], bG], writes=[bSc[s]])

                    def loadA(g):
                        s, t0 = g // 8, (g % 8) * 512
                        j = g % 2
                        P.dma("sp", xg[j][:], x_src[s, t0:t0 + 512, :].rearrange("(t p) d -> p t d", p=128), writes=[bXg[j]])
                        P.dmas("sp", [(cosg[j][:], D["cosT"][:, t0:t0 + 512]), (sing[j][:], D["sinT"][:, t0:t0 + 512])], writes=[bCS[j]])

                    evac_i = [0]

                    def evac(out_ap, in_ap, reads, writes):
                        evac_i[0] += 1
                        if evac_i[0] % 2:
                            P.op("dve", lambda e: e.tensor_copy(out=out_ap, in_=in_ap), reads=reads, writes=writes)
                        else:
                            P.op("act", lambda e: e.copy(out=out_ap, in_=in_ap), reads=reads, writes=writes)

                    NG = 16
                    ckpt(2)
                    loadA(0)
                    for g in range(NG):
                        s, t0 = g // 8, (g % 8) * 512
                        j = g % 2
                        if g + 1 < NG:
                            loadA(g + 1)
                        X = xg[j]
                        for t in range(4):
                            P.op("act", lambda e, t=t: e.activation(out=junk[:], in_=X[:, t, :], func=AF.Square, accum_out=ss[:, t:t + 1]),
                                 reads=[bXg[j]], writes=[bJ, bSS])
                        rstd_from_ss(ss[:], rstd[:], DM, bSS, bRs, lnv[:], bLn)
                        for t in range(4):
                            P.op("dve", lambda e, t=t: e.tensor_scalar(out=xn[:, t, :], in0=X[:, t, :], scalar1=rstd[:, t:t + 1], scalar2=None,
                                                                       op0=ALU.mult),
                                 reads=[bXg[j], bRs], writes=[bXn])
                        for k in range(8):
                            pj = k % 2
                            P.op("pe", [(lambda e, t=t, k=k, pj=pj: e.transpose(psT[pj][:, t * 128:(t + 1) * 128], xn[:, t, k * 128:(k + 1) * 128], ident[:]))
                                        for t in range(4)], reads=[bXn, bId], writes=[bPT[pj]])
                            P.op("act", lambda e, k=k, pj=pj: e.activation(out=hT[:, k, :], in_=psT[pj][:], func=AF.Identity,
                                                                           scale=scA[s][:, k:k + 1], bias=biA[s][:, k:k + 1]),
                                 reads=[bPT[pj], bSc[s]], writes=[bHT])
                        ckpt(3)
                        cbs = [0, 128, 256, 384] + [768 + 128 * i for i in range(8)]
                        for ci, cb in enumerate(cbs):
                            pj = ci % 2
                            P.op("pe", [(lambda e, k=k, cb=cb, pj=pj: e.matmul(psZ[pj][:], lhsT=w_in[:, k, cb:cb + 128], rhs=hT[:, k, :],
                                                                              start=(k == 0), stop=(k == 7))) for k in range(8)],
                                 reads=[bW, bHT], writes=[bPZ[pj]])
                            evac(zst[j][:, ci, :], psZ[pj][:], [bPZ[pj]], [bZst[j]])
                        P.dma("sp", D["zT"][s].rearrange("c p t -> p c t")[:, :, t0:t0 + 512], zst[j][:], reads=[bZst[j]])
                        ckpt(4)
                        for t in range(4):
                            pj = t % 2
                            fns = []
                            for k in range(8):
                                fns.append(lambda e, k=k, t=t, pj=pj: e.matmul(psW[pj][:, 0:256], lhsT=hT[:, k, t * 128:(t + 1) * 128],
                                                                               rhs=w_in[:, k, 512:768], start=(k == 0), stop=(k == 7)))
                            for k in range(8):
                                fns.append(lambda e, k=k, t=t, pj=pj: e.matmul(psW[pj][:, 512:1024], lhsT=hT[:, k, t * 128:(t + 1) * 128],
                                                                               rhs=w_in[:, k, 1792:2304], start=(k == 0), stop=(k == 7)))
                            P.op("pe", fns, reads=[bW, bHT], writes=[bPW[pj]])
                            evac(vst[j][:, t, 0:256], psW[pj][:, 0:256], [bPW[pj]], [bVst[j]])
                            evac(vst[j][:, t, 256:768], psW[pj][:, 512:1024], [bPW[pj]], [bVst[j]])
                        P.dma("sp", D["vtok"][s, t0:t0 + 512, :].rearrange("(t p) c -> p t c", p=128), vst[j][:], reads=[bVst[j]])
                        ckpt(5)
                        for t in range(4):
                            pj = t % 2
                            fns = []
                            for k in range(8):
                                fns.append(lambda e, k=k, t=t, pj=pj: e.matmul(psW[pj][:, 0:384], lhsT=hT[:, k, t * 128:(t + 1) * 128],
                                                                               rhs=w_in[:, k, 2304:2688], start=(k == 0), stop=(k == 7)))
                            for k in range(8):
                                fns.append(lambda e, k=k, t=t, pj=pj: e.matmul(psW[pj][:, 512:768], lhsT=hT[:, k, t * 128:(t + 1) * 128],
                                                                               rhs=w_in[:, k, 2688:2944], start=(k == 0), stop=(k == 7)))
                            if "noMm" not in dbg:
                                P.op("pe", fns, reads=[bW, bHT], writes=[bPW[pj]])
                            if "noCp" not in dbg:
                                P.op("dve", lambda e, t=t, pj=pj: e.tensor_copy(out=cst[:, t, 0:384], in_=psW[pj][:, 0:384]), reads=[bPW[pj]], writes=[bCst])
                                P.op("dve", lambda e, t=t, pj=pj: e.tensor_copy(out=cst[:, t, 384:640], in_=psW[pj][:, 512:768]), reads=[bPW[pj]], writes=[bCst])
                            if "noSq" not in dbg:
                              P.op("act", lambda e, t=t, pj=pj: e.activation(out=junk[:, 0:384], in_=cst[:, t, 0:384], func=AF.Square,
                                                                           accum_out=ssq[:, t:t + 1]), reads=[bCst], writes=[bJ, bSq])
                            if "noSq" not in dbg:
                              P.op("act", lambda e, t=t, pj=pj: e.activation(out=junk[:, 384:640], in_=cst[:, t, 384:640], func=AF.Square,
                                                                           accum_out=sskv[:, t:t + 1]), reads=[bCst], writes=[bJ, bSq])
                        ckpt(51)
                        rstd_from_ss(ssq[:], rsq[:], 384, bSq, bRq, lnq[:], bLq)
                        rstd_from_ss(sskv[:], rskv[:], 256, bSq, bRq, lnk[:], bLq)
                        for t in range(4):
                            P.op("dve", lambda e, t=t: e.tensor_scalar(out=cqn[:, t, :], in0=cst[:, t, 0:384], scalar1=rsq[:, t:t + 1], scalar2=None,
                                                                       op0=ALU.mult), reads=[bCst, bRq], writes=[bCqn])
                            P.op("dve", lambda e, t=t: e.tensor_scalar(out=ckvn[:, t, :], in0=cst[:, t, 384:640], scalar1=rskv[:, t:t + 1], scalar2=None,
                                                                       op0=ALU.mult), reads=[bCst, bRq], writes=[bCkvn])
                        ckpt(53)
                        for kk in range(3):
                            pj = kk % 2
                            P.op("pe", [(lambda e, t=t, kk=kk, pj=pj: e.transpose(psT[pj][:, t * 128:(t + 1) * 128], cqn[:, t, kk * 128:(kk + 1) * 128], ident[:]))
                                        for t in range(4)], reads=[bCqn, bId], writes=[bPT[pj]])
                            P.op("act", lambda e, kk=kk, pj=pj: e.activation(out=cqT[:, kk, :], in_=psT[pj][:], func=AF.Identity, scale=gq[:, kk:kk + 1], bias=zcol[:, 0:1]),
                                 reads=[bPT[pj], bG, bZc], writes=[bCqT])
                        ckpt(54)
                        for kk in range(2):
                            pj = (kk + 1) % 2
                            P.op("pe", [(lambda e, t=t, kk=kk, pj=pj: e.transpose(psT[pj][:, t * 128:(t + 1) * 128], ckvn[:, t, kk * 128:(kk + 1) * 128], ident[:]))
                                        for t in range(4)], reads=[bCkvn, bId], writes=[bPT[pj]])
                            P.op("act", lambda e, kk=kk, pj=pj: e.activation(out=ckvT[:, kk, :], in_=psT[pj][:], func=AF.Identity, scale=gkv[:, kk:kk + 1], bias=zcol[:, 0:1]),
                                 reads=[bPT[pj], bG, bZc], writes=[bCkvT])
                        ckpt(6)
                        for jj in range(2):
                            P.op("pe", [(lambda e, kk=kk, jj=jj: e.matmul(psZ[0][:], lhsT=w_uq[:, kk, jj * 128:(jj + 1) * 128], rhs=cqT[:, kk, :],
                                                                          start=(kk == 0), stop=(kk == 2))) for kk in range(3)],
                                 reads=[bUq, bCqT], writes=[bPZ[0]])
                            evac(qnst[j][:, jj, :], psZ[0][:], [bPZ[0]], [bQn[j]])
                            P.op("pe", [(lambda e, kk=kk, jj=jj: e.matmul(psZ[1][:], lhsT=w_ukv[:, kk, jj * 128:(jj + 1) * 128], rhs=ckvT[:, kk, :],
                                                                          start=(kk == 0), stop=(kk == 1))) for kk in range(2)],
                                 reads=[bUkv, bCkvT], writes=[bPZ[1]])
                            evac(knst[j][:, jj, :], psZ[1][:], [bPZ[1]], [bKn[j]])
                        P.op("pe", [(lambda e, kk=kk: e.matmul(psZ[0][:], lhsT=w_uq[:, kk, 256:384], rhs=cqT[:, kk, :], start=(kk == 0), stop=(kk == 2)))
                                    for kk in range(3)], reads=[bUq, bCqT], writes=[bPZ[0]])
                        P.op("pe", [(lambda e, kk=kk: e.matmul(psZ[1][:], lhsT=w_uqr[:, kk, :], rhs=cqT[:, kk, :], start=(kk == 0), stop=(kk == 2)))
                                    for kk in range(3)], reads=[bUqr, bCqT], writes=[bPZ[1]])
                        P.op("dve", lambda e: e.tensor_tensor(out=t1[:], in0=psZ[0][:], in1=cosg[j][:], op=ALU.mult), reads=[bPZ[0], bCS[j]], writes=[bT1])
                        P.op("dve", lambda e: e.tensor_tensor(out=t2[:], in0=psZ[1][:], in1=sing[j][:], op=ALU.mult), reads=[bPZ[1], bCS[j]], writes=[bT2])
                        P.op("pool", lambda e: e.tensor_tensor(out=qrst[j][:], in0=t1[:], in1=t2[:], op=ALU.add), reads=[bT1, bT2], writes=[bQr[j]])
                        ckpt(7)
                        P.op("pe", [(lambda e, k=k: e.matmul(psZ[0][0:32, :], lhsT=w_in[:, k, 2944:2976], rhs=hT[:, k, :], start=(k == 0), stop=(k == 7)))
                                    for k in range(8)], reads=[bW, bHT], writes=[bPZ[0]])
                        P.op("pe", [(lambda e, k=k: e.matmul(psZ[1][0:32, :], lhsT=w_rot[:, k, :], rhs=hT[:, k, :], start=(k == 0), stop=(k == 7)))
                                    for k in range(8)], reads=[bWr, bHT], writes=[bPZ[1]])
                        P.op("dve", lambda e: e.tensor_tensor(out=t1[0:32, :], in0=psZ[0][0:32, :], in1=cosg[j][0:32, :], op=ALU.mult),
                             reads=[bPZ[0], bCS[j]], writes=[bT1])
                        P.op("dve", lambda e: e.tensor_tensor(out=t2[0:32, :], in0=psZ[1][0:32, :], in1=sing[j][0:32, :], op=ALU.mult),
                             reads=[bPZ[1], bCS[j]], writes=[bT2])
                        P.op("pool", lambda e: e.tensor_tensor(out=krst[j][:], in0=t1[0:32, :], in1=t2[0:32, :], op=ALU.add),
                             reads=[bT1, bT2], writes=[bKr[j]])
                        ckpt(8)
                        for t in range(4):
                            pj = t % 2
                            P.op("pe", [(lambda e, kk=kk, t=t, pj=pj: e.matmul(psW[pj][:, 0:256], lhsT=ckvT[:, kk, t * 128:(t + 1) * 128],
                                                                               rhs=w_ukv[:, kk, 256:512], start=(kk == 0), stop=(kk == 1))) for kk in range(2)],
                                 reads=[bUkv, bCkvT], writes=[bPW[pj]])
                            evac(vcst[j][:, t, :], psW[pj][:, 0:256], [bPW[pj]], [bVc[j]])
                        P.dma("sp", D["qn"][s].rearrange("j p t -> p j t")[:, :, t0:t0 + 512], qnst[j][:], reads=[bQn[j]])
                        P.dma("sp", D["kn"][s].rearrange("j p t -> p j t")[:, :, t0:t0 + 512], knst[j][:], reads=[bKn[j]])
                        P.dma("sp", D["qr"][s, :, t0:t0 + 512], qrst[j][:], reads=[bQr[j]])
                        P.dma("sp", D["kr"][s, :, t0:t0 + 512], krst[j][:], reads=[bKr[j]])
                        P.dma("sp", D["vc"][s, t0:t0 + 512, :].rearrange("(t p) c -> p t c", p=128), vcst[j][:], reads=[bVc[j]])
                        ckpt(9)
                P.barrier()
                if "stopA" in dbg:
                    break

                SC_AB = 0.125
                SC_C = 96.0 ** -0.5
                def run_pipeline(units, nb):
                    n = len(units)
                    for i in range(min(nb, n)):
                        if "pre" in units[i]:
                            units[i]["pre"]()
                        units[i]["qk"]()
                    for i in range(n):
                        units[i]["mid"]()
                        units[i]["pv"]()
                        if "post" in units[i]:
                            units[i]["post"]()
                        if i + nb < n:
                            u2 = units[i + nb]
                            if "pre" in u2:
                                u2["pre"]()
                            u2["qk"]()

                with ExitStack() as c:
                    NB = 3
                    naX = SB(c, "naX", [128, 4 * 14 * 64], BF16)
                    t5X = SB(c, "t5X", [128, 8 * 3 * 384], BF16)
                    pens = SB(c, "pens", [2, NPAT * 256], BF16)
                    A2 = SB(c, "A2", [2, 128], BF16)
                    psS = [PS(c, "psS%d" % i, [128, 512]) for i in range(NB)]
                    psO = [PS(c, "psO%d" % i, [128, 512]) for i in range(2)]
                    E = [SB(c, "E%d" % i, [128, 384], BF16) for i in range(NB + 1)]
                    bE = [Buf() for _ in range(NB + 1)]
                    bPS = [Buf() for _ in range(NB)]
                    bPO = [Buf(), Buf()]
                    bNaX, bT5X, bPen = Buf(), Buf(), Buf()
                    P.dmas("pool", [(pens[:], D["pens"]), (A2[:], D["A2"])], writes=[bPen])
                    with ExitStack() as c2:
                        stg = SB(c2, "tstg", [128, 8 * 3 * 384], F32)
                        msk = SB(c2, "tmsk", [128, 14 * 64], F32)
                        bStg, bMsk = Buf(), Buf()
                        P.dma("sp", stg[:, 0:3584], D["naG"][l], writes=[bStg])
                        P.dma("sp", msk[:], D["nacm"], writes=[bMsk])
                        P.op("act", lambda e: e.activation(out=stg[:, 0:3584], in_=stg[:, 0:3584], func=AF.Exp), reads=[bStg], writes=[bStg])
                        for h in range(4):
                            P.op("dve", lambda e, h=h: e.tensor_tensor(out=naX[:, h * 896:(h + 1) * 896], in0=stg[:, h * 896:(h + 1) * 896], in1=msk[:],
                                                                       op=ALU.mult), reads=[bStg, bMsk], writes=[bNaX])
                        P.dma("sp", stg[:], D["t5G"], writes=[bStg])
                        P.dma("sp", msk[:, 0:384], D["t5m"], writes=[bMsk])
                        for hp in range(24):
                            P.op("act", lambda e, hp=hp: e.activation(out=stg[:, hp * 384:(hp + 1) * 384], in_=stg[:, hp * 384:(hp + 1) * 384], func=AF.Exp),
                                 reads=[bStg], writes=[bStg])
                            P.op("dve", lambda e, hp=hp: e.tensor_tensor(out=t5X[:, hp * 384:(hp + 1) * 384], in0=stg[:, hp * 384:(hp + 1) * 384],
                                                                         in1=msk[:, 0:384], op=ALU.mult), reads=[bStg, bMsk], writes=[bT5X])
                        P.barrier()
                    ucnt = [0]
                    bcnt = [0]

                    with ExitStack() as c2:
                        qT = [SB(c2, "naq%d" % i, [128, S], BF16) for i in range(2)]
                        kT = [SB(c2, "nak%d" % i, [128, S], BF16) for i in range(2)]
                        va = [SB(c2, "nav%d" % i, [128, 32, 2, 128], BF16) for i in range(2)]
                        ot = [SB(c2, "naot%d" % i, [128, S], BF16) for i in range(2)]
                        rec = SB(c2, "narec", [64, 256], F32)
                        bQ = [Buf(), Buf()]
                        bV = [Buf(), Buf()]
                        bOt = [Buf(), Buf()]
                        bRec = Buf()
                        for i in range(2):
                            P.op("pool", lambda e, i=i: e.memset(va[i][:, :, :, 64:128], 1.0), writes=[bV[i]])
                        items = [(s, hp) for s in range(2) for hp in range(2)]

                        def loadNA(n):
                            s, hp = items[n]
                            i = n % 2
                            P.dmas("sp", [(qT[i][:], D["zT"][s, hp]), (kT[i][:], D["zT"][s, 2 + hp])], writes=[bQ[i]])
                            src = D["vtok"][s].rearrange("(n p) c -> p n c", p=128)
                            P.dmas("sp", [(va[i][:, n0:n0 + 16, hh, 0:64], src[:, n0:n0 + 16, hp * 128 + hh * 64:hp * 128 + hh * 64 + 64])
                                          for n0 in range(0, 32, 16) for hh in range(2)], writes=[bV[i]])

                        units = []
                        for n, (s, hp) in enumerate(items):
                            i = n % 2
                            for hh in range(2):
                                h = 2 * hp + hh
                                pr = slice(hh * 64, hh * 64 + 64)
                                for J in range(16):
                                    ul = NA_PLAN[J]
                                    oj = bcnt[0] % 2
                                    bcnt[0] += 1
                                    for ui, (ci, di0, pid) in enumerate(ul):
                                        u = ucnt[0]
                                        ucnt[0] += 1
                                        sj, ej = u % NB, u % (NB + 1)
                                        first, last = (ui == 0), (ui == len(ul) - 1)

                                        def qk(i=i, pr=pr, ci=ci, J=J, pid=pid, sj=sj):
                                            fns = [lambda e: e.matmul(psS[sj][:, 0:256], lhsT=kT[i][pr, ci * 128:(ci + 1) * 128],
                                                                      rhs=qT[i][pr, J * 256:(J + 1) * 256], start=True, stop=(pid < 0))]
                                            if pid >= 0:
                                                fns.append(lambda e: e.matmul(psS[sj][:, 0:256], lhsT=A2[:, :], rhs=pens[:, pid * 256:(pid + 1) * 256],
                                                                              start=False, stop=True))
                                            P.op("pe", fns, reads=[bQ[i], bPen], writes=[bPS[sj]])

                                        def mid(h=h, di0=di0, sj=sj, ej=ej):
                                            P.op("act", lambda e: e.activation(out=E[ej][:, 0:256], in_=psS[sj][:, 0:256], func=AF.Exp, scale=SC_AB),
                                                 reads=[bPS[sj]], writes=[bE[ej]])
                                            xo = h * 896 + di0 * 64
                                            P.op("dve", lambda e: e.tensor_tensor(out=E[ej][:, 0:256], in0=E[ej][:, 0:256], in1=naX[:, xo:xo + 256], op=ALU.mult),
                                                 reads=[bE[ej], bNaX], writes=[bE[ej]])

                                        def pv(i=i, ci=ci, hh=hh, ej=ej, oj=oj, first=first, last=last, J=J, pr=pr):
                                            P.op("pe", lambda e: e.matmul(psO[oj][:, 0:256], lhsT=va[i][:, ci, hh, :], rhs=E[ej][:, 0:256], start=first, stop=last),
                                                 reads=[bV[i], bE[ej]], writes=[bPO[oj]])
                                            if last:
                                                P.op("dve", lambda e: e.reciprocal(out=rec[:], in_=psO[oj][64:128, 0:256]), reads=[bPO[oj]], writes=[bRec])
                                                P.op("dve", lambda e: e.tensor_tensor(out=ot[i][pr, J * 256:(J + 1) * 256], in0=psO[oj][0:64, 0:256], in1=rec[:],
                                                                                      op=ALU.mult), reads=[bPO[oj], bRec], writes=[bOt[i]])

                                        units.append(dict(qk=qk, mid=mid, pv=pv))

                            def post(n=n, s=s, hp=hp, i=i):
                                P.dma("sp", D["OT"][s, hp * 128:(hp + 1) * 128, :], ot[i][:], reads=[bOt[i]])
                                if n + 2 < len(items):
                                    loadNA(n + 2)
                            units[-1]["post"] = post
                        loadNA(0)
                        loadNA(1)
                        run_pipeline(units, NB)
                        P.barrier()

                    with ExitStack() as c2:
                        qT = [SB(c2, "dq%d" % i, [128, S], BF16) for i in range(2)]
                        kT = [SB(c2, "dk%d" % i, [128, S], BF16) for i in range(2)]
                        qo = [SB(c2, "dqo%d" % i, [128, S], BF16) for i in range(2)]
                        ko = [SB(c2, "dko%d" % i, [128, S], BF16) for i in range(2)]
                        va = [SB(c2, "dv%d" % i, [128, 32, 2, 128], BF16) for i in range(2)]
                        oacc = [SB(c2, "oacc%d" % i, [128, S], F32) for i in range(2)]
                        ot = [SB(c2, "dot%d" % i, [128, S], BF16) for i in range(2)]
                        rec = SB(c2, "drec", [64, 1024], F32)
                        bQ = [Buf(), Buf()]
                        bQo = [Buf(), Buf()]
                        bV = [Buf(), Buf()]
                        bAcc = [Buf(), Buf()]
                        bOt = [Buf(), Buf()]
                        bRec = Buf()
                        for i in range(2):
                            P.op("pool", lambda e, i=i: e.memset(va[i][:, :, :, 64:128], 1.0), writes=[bV[i]])
                        items = [(s, hp) for s in range(2) for hp in range(4)]
                        pats = (1, 4, 16)
                        NPI = len(items) * 3

                        def loadQK(n):
                            s, hp = items[n]
                            i = n % 2
                            P.dmas("sp", [(qT[i][:], D["zT"][s, 4 + hp]), (kT[i][:], D["zT"][s, 8 + hp])], writes=[bQ[i]])

                        def loadV(pidx):
                            n, pi = pidx // 3, pidx % 3
                            s, hp = items[n]
                            d = pats[pi]
                            vi = pidx % 2
                            nchunk = 32 // d
                            src = D["vtok"][s].rearrange("(n p r) c -> p r n c", p=128, r=d)
                            pairs = []
                            for r in range(d):
                                for n0 in range(0, nchunk, 16):
                                    n1 = min(n0 + 16, nchunk)
                                    for hh in range(2):
                                        cb0 = 256 + hp * 128 + hh * 64
                                        pairs.append((va[vi][:, r * nchunk + n0:r * nchunk + n1, hh, 0:64], src[:, r, n0:n1, cb0:cb0 + 64]))
                            P.dmas("sp", pairs, writes=[bV[vi]])

                        units = []
                        ocount = 0
                        for n, (s, hp) in enumerate(items):
                            i = n % 2
                            for pi, d in enumerate(pats):
                                pidx = n * 3 + pi
                                vi = pidx % 2
                                nchunk = 32 // d
                                pre = None
                                if d == 1:
                                    Q, K, bQQ = qT[i], kT[i], bQ[i]
                                else:
                                    oi = ocount % 2
                                    ocount += 1
                                    Q, K, bQQ = qo[oi], ko[oi], bQo[oi]

                                    def pre(Q=Q, K=K, d=d, i=i, bQQ=bQQ):
                                        P.op("act", lambda e: e.copy(out=Q[:].rearrange("p (r m) -> p r m", r=d),
                                                                     in_=qT[i][:].rearrange("p (m r) -> p r m", r=d)),
                                             reads=[bQ[i]], writes=[bQQ])
                                        P.op("act", lambda e: e.copy(out=K[:].rearrange("p (r m) -> p r m", r=d),
                                                                     in_=kT[i][:].rearrange("p (m r) -> p r m", r=d)),
                                             reads=[bQ[i]], writes=[bQQ])
                                firstu = True
                                for hh in range(2):
                                    h = 2 * hp + hh
                                    pr = slice(hh * 64, hh * 64 + 64)
                                    accv = oacc[hh][:].rearrange("p (m r) -> p r m", r=d)
                                    zero_first = (pi == 0)
                                    for r in range(d):
                                        for kc in range(nchunk):
                                            u = ucnt[0]
                                            ucnt[0] += 1
                                            sj, ej, oj = u % NB, u % (NB + 1), u % 2
                                            j0, j1 = max(kc - 1, 0), min(kc + 1, nchunk - 1)
                                            g0 = j0 - (kc - 1)
                                            ncol = (j1 - j0 + 1) * 128
                                            c0, c1 = g0 * 128, g0 * 128 + ncol
                                            qbase = (r * nchunk + j0) * 128
                                            kbase = (r * nchunk + kc) * 128
                                            xo = (h * 3 + pi) * 384
                                            dst = accv[:, r, j0 * 128:(j1 + 1) * 128]
                                            zf = zero_first
                                            zero_first = False

                                            def qk(Q=Q, K=K, bQQ=bQQ, pr=pr, qbase=qbase, kbase=kbase, ncol=ncol, c0=c0, c1=c1, sj=sj):
                                                P.op("pe", lambda e: e.matmul(psS[sj][:, c0:c1], lhsT=K[pr, kbase:kbase + 128], rhs=Q[pr, qbase:qbase + ncol],
                                                                              start=True, stop=True), reads=[bQQ], writes=[bPS[sj]])

                                            def mid(sj=sj, ej=ej, c0=c0, c1=c1, xo=xo):
                                                P.op("act", lambda e: e.activation(out=E[ej][:, c0:c1], in_=psS[sj][:, c0:c1], func=AF.Exp, scale=SC_AB),
                                                     reads=[bPS[sj]], writes=[bE[ej]])
                                                P.op("pool", lambda e: e.tensor_tensor(out=E[ej][:, c0:c1], in0=E[ej][:, c0:c1], in1=t5X[:, xo + c0:xo + c1],
                                                                                       op=ALU.mult), reads=[bE[ej], bT5X], writes=[bE[ej]])

                                            def pv(vi=vi, r=r, nchunk=nchunk, kc=kc, hh=hh, ej=ej, oj=oj, dst=dst, c0=c0, c1=c1, zf=zf):
                                                if zf:
                                                    P.op("pool", lambda e: e.memset(oacc[hh][:], 0.0), writes=[bAcc[hh]])
                                                P.op("pe", lambda e: e.matmul(psO[oj][:, c0:c1], lhsT=va[vi][:, r * nchunk + kc, hh, :], rhs=E[ej][:, c0:c1],
                                                                              start=True, stop=True), reads=[bV[vi], bE[ej]], writes=[bPO[oj]])
                                                P.op("dve", lambda e: e.tensor_tensor(out=dst, in0=psO[oj][:, c0:c1], in1=dst, op=ALU.add),
                                                     reads=[bPO[oj], bAcc[hh]], writes=[bAcc[hh]])

                                            ud = dict(qk=qk, mid=mid, pv=pv)
                                            if firstu and pre is not None:
                                                ud["pre"] = pre
                                            firstu = False
                                            units.append(ud)
                                    if pi == 2:
                                        def fin(hh=hh, pr=pr, i=i):
                                            for q4 in range(4):
                                                cs = slice(q4 * 1024, (q4 + 1) * 1024)
                                                P.op("dve", lambda e: e.reciprocal(out=rec[:], in_=oacc[hh][64:128, cs]), reads=[bAcc[hh]], writes=[bRec])
                                                P.op("dve", lambda e: e.tensor_tensor(out=ot[i][pr, cs], in0=oacc[hh][0:64, cs], in1=rec[:], op=ALU.mult),
                                                     reads=[bAcc[hh], bRec], writes=[bOt[i]])
                                        units[-1]["fin"] = fin

                                def postp(pidx=pidx):
                                    if pidx + 2 < NPI:
                                        loadV(pidx + 2)
                                units[-1]["postp"] = postp

                            def posti(n=n, s=s, hp=hp, i=i):
                                P.dma("sp", D["OT"][s, 256 + hp * 128:256 + (hp + 1) * 128, :], ot[i][:], reads=[bOt[i]])
                                if n + 2 < len(items):
                                    loadQK(n + 2)
                            units[-1]["posti"] = posti
                        for ud in units:
                            hooks = [ud[k] for k in ("fin", "postp", "posti") if k in ud]
                            if hooks:
                                ud["post"] = (lambda hooks=hooks: [hk() for hk in hooks])
                        loadQK(0)
                        loadQK(1)
                        loadV(0)
                        loadV(1)
                        run_pipeline(units, NB)
                        P.barrier()
                P.barrier()

                with ExitStack() as c:
                    psS = [PS(c, "psSc%d" % i, [128, 1024]) for i in range(2)]
                    psO = [PS(c, "psOc%d" % i, [128, 1024]) for i in range(2)]
                    E = [SB(c, "Ec%d" % i, [128, 1024], BF16) for i in range(3)]
                    qT = [SB(c, "cq%d" % i, [96, S], BF16) for i in range(2)]
                    kT = [SB(c, "ck%d" % i, [96, S], BF16) for i in range(2)]
                    va = [SB(c, "cv%d" % i, [128, 32, 128], BF16) for i in range(2)]
                    ot = [SB(c, "cot%d" % i, [128, S], BF16) for i in range(2)]
                    rec = SB(c, "crec", [64, 1024], F32)
                    bE = [Buf() for _ in range(3)]
                    bPS = [Buf(), Buf()]
                    bPO = [Buf(), Buf()]
                    bQ = [Buf(), Buf()]
                    bV = [Buf(), Buf()]
                    bOt = [Buf(), Buf()]
                    bRec = Buf()
                    for i in range(2):
                        P.op("pool", lambda e, i=i: e.memset(va[i][:, :, 64:128], 1.0), writes=[bV[i]])
                    items = [(s, h) for s in range(2) for h in range(4)]

                    def loadC(n):
                        s, h = items[n]
                        i = n % 2
                        P.dmas("sp", [(qT[i][0:64, :], D["qn"][s, h // 2, (h % 2) * 64:(h % 2) * 64 + 64, :]),
                                      (qT[i][64:96, :], D["qr"][s, h * 32:(h + 1) * 32, :]),
                                      (kT[i][0:64, :], D["kn"][s, h // 2, (h % 2) * 64:(h % 2) * 64 + 64, :]),
                                      (kT[i][64:96, :], D["kr"][s, :, :])], writes=[bQ[i]])
                        src = D["vc"][s].rearrange("(n p) c -> p n c", p=128)
                        P.dmas("sp", [(va[i][:, n0:n0 + 8, 0:64], src[:, n0:n0 + 8, h * 64:(h + 1) * 64]) for n0 in range(0, 32, 8)], writes=[bV[i]])

                    units = []
                    u = 0
                    blk = 0
                    for n, (s, h) in enumerate(items):
                        i = n % 2
                        oi = (n // 2) % 2
                        pr = slice((h % 2) * 64, (h % 2) * 64 + 64)
                        for qb in range(4):
                            oj = blk % 2
                            blk += 1
                            cs = slice(qb * 1024, (qb + 1) * 1024)
                            for kc in range(32):
                                sj, ej = u % 2, u % 3
                                u += 1

                                def qk(i=i, kc=kc, qb=qb, sj=sj):
                                    P.op("pe", [(lambda e, hf=hf: e.matmul(psS[sj][:, hf * 512:(hf + 1) * 512], lhsT=kT[i][0:96, kc * 128:(kc + 1) * 128],
                                                                           rhs=qT[i][0:96, qb * 1024 + hf * 512:qb * 1024 + (hf + 1) * 512],
                                                                           start=True, stop=True)) for hf in range(2)],
                                         reads=[bQ[i]], writes=[bPS[sj]])

                                def mid(sj=sj, ej=ej):
                                    P.op("act", lambda e: e.activation(out=E[ej][:], in_=psS[sj][:], func=AF.Exp, scale=SC_C),
                                         reads=[bPS[sj]], writes=[bE[ej]])

                                def pv(i=i, kc=kc, ej=ej, oj=oj, oi=oi, pr=pr, cs=cs):
                                    P.op("pe", [(lambda e, hf=hf: e.matmul(psO[oj][:, hf * 512:(hf + 1) * 512], lhsT=va[i][:, kc, :],
                                                                           rhs=E[ej][:, hf * 512:(hf + 1) * 512], start=(kc == 0), stop=(kc == 31)))
                                                for hf in range(2)], reads=[bV[i], bE[ej]], writes=[bPO[oj]])
                                    if kc == 31:
                                        P.op("dve", lambda e: e.reciprocal(out=rec[:], in_=psO[oj][64:128, :]), reads=[bPO[oj]], writes=[bRec])
                                        P.op("dve", lambda e: e.tensor_tensor(out=ot[oi][pr, cs], in0=psO[oj][0:64, :], in1=rec[:], op=ALU.mult),
                                             reads=[bPO[oj], bRec], writes=[bOt[oi]])

                                units.append(dict(qk=qk, mid=mid, pv=pv))

                        def post(n=n, s=s, h=h, oi=oi):
                            if h % 2 == 1:
                                P.dma("sp", D["OT"][s, 768 + (h // 2) * 128:768 + (h // 2 + 1) * 128, :], ot[oi][:], reads=[bOt[oi]])
                            if n + 2 < len(items):
                                loadC(n + 2)
                        units[-1]["post"] = post
                    loadC(0)
                    loadC(1)
                    run_pipeline(units, 2)
                P.barrier()
                if "stopB" in dbg:
                    break

                with ExitStack() as c:
                    w_out = SB(c, "w_out", [128, 8, DM], BF16)
                    w_dn = SB(c, "w_dn", [128, NCH, DM], BF16)
                    wu = [SB(c, "wu%d" % i, [128, 8, 256], BF16) for i in range(3)]
                    cw = SB(c, "convw", [128, NCH, 3], F32)
                    cb = SB(c, "convb", [128, NCH], F32)
                    gpre = SB(c, "gpreF", [128, 8], F32)
                    scF = [SB(c, "scF%d" % s, [128, 8], F32) for s in range(2)]
                    biF = [SB(c, "biF%d" % s, [128, 8], F32) for s in range(2)]
                    gate1 = SB(c, "gate1", [128, DM], F32)
                    gate2 = SB(c, "gate2", [128, DM], F32)
                    gtmp = SB(c, "gtmp", [128, DM], F32)
                    xg = [SB(c, "xgC%d" % i, [128, 4, DM], F32) for i in range(2)]
                    otg = [SB(c, "otg%d" % i, [128, 8, 512], BF16) for i in range(2)]
                    h2T = [SB(c, "h2T%d" % i, [128, 8, 512], BF16) for i in range(2)]
                    halo = SB(c, "halo", [128, 8, 2], BF16)
                    edge = [SB(c, "edge%d" % i, [128, 8, 2], BF16) for i in range(3)]
                    actT = SB(c, "actT", [128, NCH, 512], BF16)
                    xn2 = SB(c, "xn2", [128, 4, DM], BF16)
                    junk = SB(c, "junkC", [128, DM], BF16)
                    tmp = SB(c, "tmpC", [128, DM], F32)
                    gcb = [SB(c, "gc%d" % i, [128, 512], F32) for i in range(2)]
                    geb = [SB(c, "ge%d" % i, [128, 512], F32) for i in range(2)]
                    ssv = SB(c, "ssC", [128, 16], F32)
                    lnc = SB(c, "lnC", [128, 16], F32)
                    rsc = SB(c, "rsC", [128, 16], F32)
                    psYs = [PS(c, "psY%d" % i, [128, 1024]) for i in range(2)]
                    psT1 = PS(c, "psTC", [128, 512], BF16)
                    psTs = [psT1[:], psT1[:]]
                    psHt = PS(c, "psH", [128, 2])
                    psH = psHt[:]
                    psA1 = PS(c, "psA", [128, 512])
                    psG1 = PS(c, "psG", [128, 512])
                    g_sb = [SB(c, "g_sb%d" % i, [128, 512], F32) for i in range(2)]
                    bGsb = [Buf(), Buf()]
                    bPG1 = Buf()
                    a_sb = [SB(c, "a_sb%d" % i, [128, 512], BF16) for i in range(2)]
                    hsb = SB(c, "hsb", [128, 2], F32)
                    bHsb = Buf()
                    bAsb = [Buf(), Buf()]
                    ycnt = [0]
                    bWo, bWd, bCw, bGp = Buf(), Buf(), Buf(), Buf()
                    bWu = [Buf() for _ in range(3)]
                    bScF = [Buf(), Buf()]
                    bGate, bGt = Buf(), Buf()
                    bXg = [Buf(), Buf()]
                    bOtg = [Buf(), Buf()]
                    bH2 = [Buf(), Buf()]
                    bEdge = [Buf(), Buf(), Buf()]
                    bHalo, bAct, bXn2, bJ, bTmp = Buf(), Buf(), Buf(), Buf(), Buf()
                    bActLo = Buf()
                    bGc = [Buf(), Buf()]
                    bGe = [Buf(), Buf()]
                    bSs, bLn, bRs = Buf(), Buf(), Buf()
                    bPYs = [Buf(), Buf()]
                    bPT1 = Buf()
                    bPTs = [bPT1, bPT1]
                    bPH = Buf()
                    bPA1 = Buf()
                    bPG = [Buf(), Buf()]

                    P.dmas("pool", [(w_out[:, k, :], D["w_out"][l, :, k, :]) for k in range(8)], writes=[bWo])
                    P.dmas("pool", [(w_dn[:, c0:c0 + 2, :], D["w_down"][l, :, c0:c0 + 2, :]) for c0 in range(0, NCH, 2)], writes=[bWd])
                    P.dmas("sp", [(cw[:], D["convwT"][l]), (cb[:], D["convbT"][l])], writes=[bCw])
                    P.dma("sp", gpre[:], D["gpre_ffnT"][l], writes=[bGp])
                    for s in range(2):
                        load_vec8(c, biF[s][:], l, s, 3, bScF[s])
                        load_vec8(c, scF[s][:], l, s, 4, bScF[s])
                        P.op("dve", lambda e, s=s: e.scalar_tensor_tensor(out=scF[s][:], in0=scF[s][:], scalar=1.0, in1=gpre[:], op0=ALU.add, op1=ALU.mult),
                             reads=[bScF[s], bGp], writes=[bScF[s]])

                    def load_gates(s):
                        for gt, jm, gp in ((gate1, 2, "gpost_mix"), (gate2, 5, "gpost_ffn")):
                            P.dma("sp", gt[:], D["modD"][l, s:s + 1, jm * 1024:(jm + 1) * 1024].partition_broadcast(128), writes=[bGate])
                            P.dma("sp", gtmp[:], D[gp][l:l + 1, :].partition_broadcast(128), writes=[bGt])
                            P.op("dve", lambda e, gt=gt: e.tensor_tensor(out=gt[:], in0=gt[:], in1=gtmp[:], op=ALU.mult), reads=[bGate, bGt], writes=[bGate])

                    def loadOT(g):
                        s, t0 = g // 8, (g % 8) * 512
                        j = g % 2
                        P.dma("sp", otg[j][:], D["OT"][s, :, t0:t0 + 512].rearrange("(k p) t -> p k t", p=128), writes=[bOtg[j]])

                    def loadX(g):
                        s, t0 = g // 8, (g % 8) * 512
                        j = g % 2
                        P.dmas("sp", [(xg[j][:, t, :], x_src[s, t0 + t * 128:t0 + (t + 1) * 128, :]) for t in range(4)], writes=[bXg[j]])

                    def norm_resid(X, t, gate, col, bX, psY, bPY):
                        P.op("act", lambda e: e.activation(out=junk[:], in_=psY[:], func=AF.Square, accum_out=ssv[:, col:col + 1]),
                             reads=[bPY], writes=[bJ, bSs])
                        P.op("act", lambda e: e.activation(out=lnc[:, col:col + 1], in_=ssv[:, col:col + 1], func=AF.Ln, scale=1.0 / DM, bias=EPS),
                             reads=[bSs], writes=[bLn])
                        P.op("act", lambda e: e.activation(out=rsc[:, col:col + 1], in_=lnc[:, col:col + 1], func=AF.Exp, scale=-0.5),
                             reads=[bLn], writes=[bRs])
                        P.op("dve", lambda e: e.scalar_tensor_tensor(out=tmp[:], in0=psY[:], scalar=rsc[:, col:col + 1], in1=gate[:], op0=ALU.mult, op1=ALU.mult),
                             reads=[bPY, bRs, bGate], writes=[bTmp])
                        P.op("pool", lambda e: e.tensor_tensor(out=X[:, t, :], in0=X[:, t, :], in1=tmp[:], op=ALU.add), reads=[bTmp, bX], writes=[bX])

                    def stage1(g):
                        s, t0 = g // 8, (g % 8) * 512
                        j = g % 2
                        X = xg[j]
                        for t in range(4):
                            yi = ycnt[0] % 2
                            ycnt[0] += 1
                            psY, bPY = psYs[yi], bPYs[yi]
                            fns = []
                            for hf in range(2):
                                for k in range(8):
                                    fns.append(lambda e, k=k, hf=hf, t=t, psY=psY: e.matmul(psY[:, hf * 512:(hf + 1) * 512], lhsT=otg[j][:, k, t * 128:(t + 1) * 128],
                                                                                   rhs=w_out[:, k, hf * 512:(hf + 1) * 512], start=(k == 0), stop=(k == 7)))
                            P.op("pe", fns, reads=[bOtg[j], bWo], writes=[bPY])
                            norm_resid(X, t, gate1, t, bXg[j], psY, bPY)
                            P.op("act", lambda e, t=t: e.activation(out=junk[:], in_=X[:, t, :], func=AF.Square, accum_out=ssv[:, 4 + t:5 + t]),
                                 reads=[bXg[j]], writes=[bJ, bSs])
                            P.op("act", lambda e, t=t: e.activation(out=lnc[:, 4 + t:5 + t], in_=ssv[:, 4 + t:5 + t], func=AF.Ln, scale=1.0 / DM, bias=EPS),
                                 reads=[bSs], writes=[bLn])
                            P.op("act", lambda e, t=t: e.activation(out=rsc[:, 4 + t:5 + t], in_=lnc[:, 4 + t:5 + t], func=AF.Exp, scale=-0.5), reads=[bLn], writes=[bRs])
                            P.op("dve", lambda e, t=t: e.tensor_scalar(out=xn2[:, t, :], in0=X[:, t, :], scalar1=rsc[:, 4 + t:5 + t], scalar2=None, op0=ALU.mult),
                                 reads=[bXg[j], bRs], writes=[bXn2])
                        for k in range(8):
                            psT, bPT = psTs[k % 2], bPTs[k % 2]
                            P.op("pe", [(lambda e, t=t, k=k, psT=psT: e.transpose(psT[:, t * 128:(t + 1) * 128], xn2[:, t, k * 128:(k + 1) * 128], ident[:]))
                                        for t in range(4)], reads=[bXn2, bId], writes=[bPT])
                            P.op("act", lambda e, k=k, psT=psT: e.activation(out=h2T[j][:, k, :], in_=psT[:], func=AF.Identity,
                                                                    scale=scF[s][:, k:k + 1], bias=biF[s][:, k:k + 1]),
                                 reads=[bPT, bScF[s]], writes=[bH2[j]])
                        ej3 = g % 3
                        P.op("pool", lambda e: e.tensor_copy(out=edge[ej3][:, :, 0:1], in_=h2T[j][:, :, 0:1]), reads=[bH2[j]], writes=[bEdge[ej3]])
                        P.op("pool", lambda e: e.tensor_copy(out=edge[ej3][:, :, 1:2], in_=h2T[j][:, :, 511:512]), reads=[bH2[j]], writes=[bEdge[ej3]])

                    wcount = [0]

                    def load_wu(ci):
                        wi = wcount[0] % 3
                        wcount[0] += 1
                        P.dma("pool", wu[wi][:], D["w_up"][l, ci], writes=[bWu[wi]])
                        return wi

                    wu_pref = []

                    def prefetch_wu():
                        wu_pref.extend([load_wu(0), load_wu(1)])

                    def stage2(g, have_next):
                        s, t0 = g // 8, (g % 8) * 512
                        j = g % 2
                        X = xg[j]
                        if t0 > 0:
                            P.op("pool", lambda e: e.tensor_copy(out=halo[:, :, 0:1], in_=edge[(g - 1) % 3][:, :, 1:2]), reads=[bEdge[(g - 1) % 3]], writes=[bHalo])
                        else:
                            P.op("pool", lambda e: e.memset(halo[:, :, 0:1], 0.0), writes=[bHalo])
                        if t0 + 512 < S:
                            assert have_next
                            P.op("pool", lambda e: e.tensor_copy(out=halo[:, :, 1:2], in_=edge[(g + 1) % 3][:, :, 0:1]), reads=[bEdge[(g + 1) % 3]], writes=[bHalo])
                        else:
                            P.op("pool", lambda e: e.memset(halo[:, :, 1:2], 0.0), writes=[bHalo])
                        def finish_chunk(ci):
                            pj = ci % 2
                            gc, ge = gcb[pj], geb[pj]
                            P.op("act", lambda e: e.activation(out=ge[:], in_=gc[:], func=AF.Gelu_apprx_tanh), reads=[bGc[pj]], writes=[bGe[pj]])
                            P.op("pool", lambda e: e.tensor_tensor(out=actT[:, ci, :], in0=a_sb[pj][:], in1=ge[:], op=ALU.mult),
                                 reads=[bAsb[pj], bGe[pj]], writes=[bActLo if ci < NCH - 2 else bAct])

                        pend = list(wu_pref)
                        del wu_pref[:]
                        for ci in range(NCH):
                            wi = pend.pop(0)
                            if ci + 2 < NCH:
                                pend.append(load_wu(ci + 2))
                            pj = ci % 2
                            W = wu[wi]
                            P.op("pe", [(lambda e, k=k, W=W, pj=pj: e.matmul(psA1[:], lhsT=W[:, k, 0:128], rhs=h2T[j][:, k, :], start=(k == 0), stop=(k == 7)))
                                        for k in range(8)], reads=[bWu[wi], bH2[j]], writes=[bPA1])
                            P.op("act", lambda e, pj=pj: e.copy(out=a_sb[pj][:], in_=psA1[:]), reads=[bPA1], writes=[bAsb[pj]])
                            P.op("pe", [(lambda e, k=k, W=W: e.matmul(psG1[:], lhsT=W[:, k, 128:256], rhs=h2T[j][:, k, :], start=(k == 0), stop=(k == 7)))
                                        for k in range(8)], reads=[bWu[wi], bH2[j]], writes=[bPG1])
                            P.op("pe", [(lambda e, k=k, W=W: e.matmul(psH, lhsT=W[:, k, 128:256], rhs=halo[:, k, :], start=(k == 0), stop=(k == 7)))
                                        for k in range(8)], reads=[bWu[wi], bHalo], writes=[bPH])
                            gc, ge, gs = gcb[pj], geb[pj], g_sb[pj]
                            P.op("act", lambda e, gs=gs: e.copy(out=gs[:], in_=psG1[:]), reads=[bPG1], writes=[bGsb[pj]])
                            P.op("dve", lambda e, ci=ci, gc=gc, gs=gs: e.tensor_scalar(out=gc[:], in0=gs[:], scalar1=cw[:, ci, 1:2], scalar2=cb[:, ci:ci + 1],
                                                                                       op0=ALU.mult, op1=ALU.add),
                                 reads=[bGsb[pj], bCw], writes=[bGc[pj]])
                            P.op("dve", lambda e, ci=ci, gc=gc, gs=gs: e.scalar_tensor_tensor(out=gc[:, 1:512], in0=gs[:, 0:511], scalar=cw[:, ci, 0:1],
                                                                                              in1=gc[:, 1:512], op0=ALU.mult, op1=ALU.add),
                                 reads=[bGsb[pj], bCw, bGc[pj]], writes=[bGc[pj]])
                            P.op("dve", lambda e, ci=ci, gc=gc, gs=gs: e.scalar_tensor_tensor(out=gc[:, 0:511], in0=gs[:, 1:512], scalar=cw[:, ci, 2:3],
                                                                                              in1=gc[:, 0:511], op0=ALU.mult, op1=ALU.add),
                                 reads=[bGsb[pj], bCw, bGc[pj]], writes=[bGc[pj]])
                            P.op("act", lambda e: e.copy(out=hsb[:], in_=psH), reads=[bPH], writes=[bHsb])
                            P.op("dve", lambda e, ci=ci, gc=gc: e.scalar_tensor_tensor(out=gc[:, 0:1], in0=hsb[:, 0:1], scalar=cw[:, ci, 0:1], in1=gc[:, 0:1],
                                                                                       op0=ALU.mult, op1=ALU.add), reads=[bHsb, bCw, bGc[pj]], writes=[bGc[pj]])
                            P.op("dve", lambda e, ci=ci, gc=gc: e.scalar_tensor_tensor(out=gc[:, 511:512], in0=hsb[:, 1:2], scalar=cw[:, ci, 2:3], in1=gc[:, 511:512],
                                                                                       op0=ALU.mult, op1=ALU.add), reads=[bHsb, bCw, bGc[pj]], writes=[bGc[pj]])
                            if ci >= 1:
                                finish_chunk(ci - 1)
                        finish_chunk(NCH - 1)
                        if g + 1 < 16:
                            prefetch_wu()
                        for t in range(4):
                            yi = ycnt[0] % 2
                            ycnt[0] += 1
                            psY, bPY = psYs[yi], bPYs[yi]
                            NLO = NCH - 2
                            fns = []
                            for hf in range(2):
                                for ci in range(NLO):
                                    fns.append(lambda e, ci=ci, hf=hf, t=t, psY=psY: e.matmul(psY[:, hf * 512:(hf + 1) * 512], lhsT=actT[:, ci, t * 128:(t + 1) * 128],
                                                                                     rhs=w_dn[:, ci, hf * 512:(hf + 1) * 512], start=(ci == 0), stop=False))
                            P.op("pe", fns, reads=[bActLo, bWd], writes=[bPY])
                            fns = []
                            for hf in range(2):
                                for ci in range(NLO, NCH):
                                    fns.append(lambda e, ci=ci, hf=hf, t=t, psY=psY: e.matmul(psY[:, hf * 512:(hf + 1) * 512], lhsT=actT[:, ci, t * 128:(t + 1) * 128],
                                                                                     rhs=w_dn[:, ci, hf * 512:(hf + 1) * 512], start=False, stop=(ci == NCH - 1)))
                            P.op("pe", fns, reads=[bAct, bWd], writes=[bPY])
                            norm_resid(X, t, gate2, 8 + t, bXg[j], psY, bPY)
                        P.dma("sp", x_dst[s, t0:t0 + 512, :].rearrange("(t p) d -> p t d", p=128), X[:], reads=[bXg[j]])

                    NG = 16
                    load_gates(0)
                    loadOT(0)
                    loadX(0)
                    loadOT(1)
                    stage1(0)
                    prefetch_wu()
                    for g in range(NG):
                        s = g // 8
                        nxt_same_seq = (g + 1 < NG) and ((g + 1) // 8 == s)
                        if nxt_same_seq:
                            loadX(g + 1)
                            stage1(g + 1)
                            if g + 2 < NG:
                                loadOT(g + 2)
                        stage2(g, nxt_same_seq)
                        if g + 1 < NG and not nxt_same_seq:
                            load_gates((g + 1) // 8)
                            loadX(g + 1)
                            stage1(g + 1)
                            if g + 2 < NG:
                                loadOT(g + 2)
                P.barrier()

        except _Stop:
            pass
        P.barrier()
    return nc


def _prep_shared(inp):
    f = lambda a: np.ascontiguousarray(np.asarray(a, dtype=np.float32))
    L = 4
    sh = {}
    sh["w_ada"] = f(np.asarray(inp["w_ada"]).reshape(L, 8, 128, 6144).transpose(0, 2, 1, 3))
    sh["b_ada"] = f(inp["b_ada"])
    sh["gpre_mixT"] = f(np.asarray(inp["g_pre_mix"]).reshape(L, 8, 128).transpose(0, 2, 1))
    sh["gpre_ffnT"] = f(np.asarray(inp["g_pre_ffn"]).reshape(L, 8, 128).transpose(0, 2, 1))
    sh["gpost_mix"] = f(inp["g_post_mix"])
    sh["gpost_ffn"] = f(inp["g_post_ffn"])
    sh["w_in"] = f(np.asarray(inp["w_in"]).reshape(L, 8, 128, D_IN).transpose(0, 2, 1, 3))
    wuq = np.asarray(inp["w_uq"]).reshape(L, 384, 4, 96)
    wuq = np.concatenate([wuq[..., :64].reshape(L, 384, 256), wuq[..., 64:].reshape(L, 384, 128)], axis=-1)
    sh["w_uq"] = f(wuq.reshape(L, 3, 128, 384).transpose(0, 2, 1, 3))
    wukv = np.asarray(inp["w_ukv"]).reshape(L, 256, 4, 128)
    wukv = np.concatenate([wukv[..., :64].reshape(L, 256, 256), wukv[..., 64:].reshape(L, 256, 256)], axis=-1)
    sh["w_ukv"] = f(wukv.reshape(L, 2, 128, 512).transpose(0, 2, 1, 3))
    sh["gqT"] = f(np.asarray(inp["mla_g_q"]).reshape(L, 3, 128).transpose(0, 2, 1))
    sh["gkvT"] = f(np.asarray(inp["mla_g_kv"]).reshape(L, 2, 128).transpose(0, 2, 1))
    sh["w_out"] = f(np.asarray(inp["w_out"]).reshape(L, 8, 128, DM).transpose(0, 2, 1, 3))
    wup = np.asarray(inp["w_up"]).reshape(L, 8, 128, 2, NCH, 128)
    sh["w_up"] = f(wup.transpose(0, 4, 2, 1, 3, 5).reshape(L, NCH, 128, 8, 256))
    sh["w_down"] = f(np.asarray(inp["w_down"]).reshape(L, NCH, 128, DM).transpose(0, 2, 1, 3))
    sh["convwT"] = f(np.asarray(inp["conv_w"]).reshape(L, 3, NCH, 128).transpose(0, 3, 2, 1))
    sh["convbT"] = f(np.asarray(inp["conv_b"]).reshape(L, NCH, 128).transpose(0, 2, 1))
    G, cm, T, tm = _host_tables(np.asarray(inp["na_rpb"], np.float32), np.asarray(inp["t5_table"], np.float32))
    sh["naG"] = f(G)
    sh["nacm"] = f(cm)
    sh["t5G"] = f(T)
    sh["t5m"] = f(tm)
    sh["pens"] = f(NA_PENS)
    A2 = np.zeros((2, 128), np.float32)
    A2[0, :64] = 1.0
    A2[1, 64:] = 1.0
    sh["A2"] = A2
    cosT, sinT = _rope_tables()
    sh["cosT"] = f(cosT)
    sh["sinT"] = f(sinT)
    return sh


def _in_maps(inp, ncores):
    sh = _prep_shared(inp)
    x = np.asarray(inp["x"], np.float32)
    cc = np.asarray(inp["c"], np.float32)
    maps = []
    for i in range(ncores):
        m = dict(sh)
        m["x"] = np.ascontiguousarray(x[2 * i:2 * i + 2])
        m["cT"] = np.ascontiguousarray(cc[2 * i:2 * i + 2].reshape(2, 8, 128).transpose(2, 1, 0))
        maps.append(m)
    return maps


_NC_CACHE = {}


def kernel(**inputs):
    if "nc" not in _NC_CACHE:
        _NC_CACHE["nc"] = build()
    nc = _NC_CACHE["nc"]
    maps = _in_maps(inputs, NCORES)
    res = run_bass_kernel_spmd(nc, maps, core_ids=list(range(NCORES)))
    out = np.concatenate([np.asarray(r["out"], dtype=np.float32) for r in res.results], axis=0)
    return out
```

```python
import math
from contextlib import ExitStack

import numpy as np
import ml_dtypes
import concourse.bass as bass
import concourse.mybir as mybir
from concourse.bass_utils import run_bass_kernel_spmd

F32 = mybir.dt.float32
BF16 = mybir.dt.bfloat16
AF = mybir.ActivationFunctionType
ALU = mybir.AluOpType

NCORES = 8
S = 4096
DM = 1024
DFF = 2816
NCH = 22
D_IN = 2976
EPS = 1e-6
NEG = -30000.0


class Buf:
    __slots__ = ("lw", "rd")

    def __init__(self):
        self.lw = []
        self.rd = {}


class Prog:
    def __init__(self, nc, ctx, n_dma_sems=40):
        self.nc = nc
        self.E = {"pe": nc.tensor, "act": nc.scalar, "dve": nc.vector, "pool": nc.gpsimd, "sp": nc.sync}
        self.psem = {k: ctx.enter_context(nc.semaphore("ps_" + k)) for k in self.E}
        self.pcnt = {k: 0 for k in self.E}
        self.seen = {k: {} for k in self.E}
        self.dsem = [ctx.enter_context(nc.semaphore("ds%d" % i)) for i in range(n_dma_sems)]
        self.dcnt = [0] * n_dma_sems
        self.dnext = 0
        self.nins = 0
        self.dead = False

    def _wait(self, eng, deps):
        if self.dead:
            return
        need = {}
        for d in deps:
            if d is None:
                continue
            s, v = d
            if need.get(s, 0) < v:
                need[s] = v
        seen = self.seen[eng]
        for s, v in need.items():
            if seen.get(s, 0) < v:
                self.E[eng].wait_ge(s, v)
                seen[s] = v
                self.nins += 1

    @staticmethod
    def _deps(reads, writes):
        deps = []
        for b in reads:
            deps.extend(b.lw)
        for b in writes:
            deps.extend(b.lw)
            deps.extend(b.rd.values())
        return deps

    def op(self, eng, fns, reads=(), writes=()):
        if self.dead:
            return None
        self._wait(eng, self._deps(reads, writes))
        if callable(fns):
            fns = [fns]
        ins = None
        e = self.E[eng]
        for f in fns:
            ins = f(e)
            self.nins += 1
        self.pcnt[eng] += 1
        ins.then_inc(self.psem[eng], 1)
        tok = (self.psem[eng], self.pcnt[eng])
        for b in reads:
            b.rd[eng] = tok
        for b in writes:
            b.lw = [tok]
            b.rd = {}
        return tok

    def dma(self, eng, out, in_, reads=(), writes=(), **kw):
        return self.dmas(eng, [(out, in_)], reads, writes, **kw)

    def dmas(self, eng, pairs, reads=(), writes=(), **kw):
        if self.dead:
            return []
        deps = self._deps(reads, writes)
        idx = []
        for _ in pairs:
            i = self.dnext
            self.dnext = (i + 1) % len(self.dsem)
            idx.append(i)
            if self.dcnt[i] > 0:
                deps.append((self.dsem[i], 16 * self.dcnt[i]))
        assert len(set(idx)) == len(idx)
        self._wait(eng, deps)
        toks = []
        for i, (out, in_) in zip(idx, pairs):
            self.dcnt[i] += 1
            self.E[eng].dma_start(out=out, in_=in_, **kw).then_inc(self.dsem[i], 16)
            self.nins += 1
            toks.append((self.dsem[i], 16 * self.dcnt[i]))
        for b in reads:
            for i, tok in zip(idx, toks):
                b.rd[("dma", i)] = tok
        for b in writes:
            b.lw = list(toks)
            b.rd = {}
        return toks

    def barrier(self):
        deps = [(self.psem[k], self.pcnt[k]) for k in self.E if self.pcnt[k] > 0]
        deps += [(s, 16 * c) for s, c in zip(self.dsem, self.dcnt) if c > 0]
        for k in self.E:
            self._wait(k, deps)


def _r_start(r):
    return min(max(r - 4, 0), 56)


def _c_start(c):
    return min(max(c - 8, 0), 48)


def _na_plan():
    pats = {}
    plan = []
    for J in range(16):
        klo = _r_start(4 * J)
        khi = _r_start(4 * J + 3) + 7
        units = []
        for i in range(klo // 2, khi // 2 + 1):
            key = []
            for a in range(2):
                for ap in range(4):
                    r = 4 * J + ap
                    kr = 2 * i + a
                    key.append(_r_start(r) <= kr <= _r_start(r) + 7)
            key = tuple(key)
            if all(key):
                pid = -1
            else:
                if key not in pats:
                    pats[key] = len(pats)
                pid = pats[key]
            di0 = 4 * J - 2 * i + 6
            assert 0 <= di0 <= 10
            units.append((i, di0, pid))
        plan.append(units)
    npat = len(pats)
    pens = np.zeros((2, npat, 4, 64), np.float32)
    for key, pid in pats.items():
        for a in range(2):
            for ap in range(4):
                if not key[a * 4 + ap]:
                    pens[a, pid, ap, :] = NEG
    return plan, pens.reshape(2, npat * 256)


NA_PLAN, NA_PENS = _na_plan()
NPAT = NA_PENS.shape[1] // 256


def _t5_bucket(rel):
    nb = 16
    max_exact = 8
    n = np.abs(rel)
    large = max_exact + (np.log(np.maximum(n, 1) / max_exact) / math.log(1024 / max_exact) * (nb - max_exact)).astype(np.int64)
    large = np.minimum(large, nb - 1)
    return (np.where(rel > 0, nb, 0) + np.where(n < max_exact, n, large)).astype(np.int32)


def _host_tables(na_rpb, t5_table):
    p = np.arange(128)
    a = p // 64
    kc = p % 64
    di = np.arange(14)
    c = np.arange(64)
    drow = a[:, None] - (di[None, :] - 6) + 7
    dcol = kc[:, None] - c[None, :] + 15
    ok = ((drow >= 0) & (drow <= 14))[:, :, None] & ((dcol >= 0) & (dcol <= 30))[:, None, :]
    drc = np.clip(drow, 0, 14)
    dcc = np.clip(dcol, 0, 30)
    G = na_rpb[:, :, drc[:, :, None], dcc[:, None, :]]
    G = np.where(ok[None, None], G, np.float32(0.0)).astype(np.float32)
    G = np.ascontiguousarray(np.transpose(G, (0, 2, 1, 3, 4))).reshape(na_rpb.shape[0], 128, 4 * 14 * 64)
    cs = np.array([_c_start(x) for x in range(64)])
    cm = ((kc[:, None] >= cs[None, :]) & (kc[:, None] < cs[None, :] + 16)).astype(np.float32)
    cm = np.ascontiguousarray(np.broadcast_to(cm[:, None, :], (128, 14, 64))).reshape(128, 14 * 64)
    q = np.arange(128)
    pc = np.arange(3)
    rel = 128 * (1 - pc[None, :, None]) + p[:, None, None] - q[None, None, :]
    valid = (np.abs(rel) <= 64)
    T = np.zeros((128, 8, 3, 3, 128), np.float32)
    for pi, d in enumerate((1, 4, 16)):
        b = _t5_bucket(rel * d)
        vals = t5_table[b]
        vals = np.where(valid[..., None], vals, np.float32(0.0))
        T[:, :, pi] = np.transpose(vals, (0, 3, 1, 2))
    T = T.reshape(128, 8 * 3 * 384)
    tm = valid.astype(np.float32).reshape(128, 384)
    return G, cm, T, tm


def _rope_tables():
    inv_freq = (10000.0 ** (-np.arange(0, 32, 2, dtype=np.float32) / 32)).astype(np.float32)
    ang = np.arange(S, dtype=np.float32)[:, None] * inv_freq[None, :]
    cos = np.cos(ang).astype(np.float32)
    sin = np.sin(ang).astype(np.float32)
    idx = (np.arange(128) % 32) % 16
    return np.ascontiguousarray(cos[:, idx].T), np.ascontiguousarray(sin[:, idx].T)


class _Stop(Exception):
    pass


def build(NL=4, dbg=()):
    nc = bass.Bass("TRN2", target_bir_lowering=False)
    stop_at = -1
    for dflag in dbg:
        if dflag.startswith("stage="):
            stop_at = int(dflag[6:])

    PP = []

    def ckpt(n):
        if n == stop_at and not PP[0].dead:
            PP[0].barrier()
            PP[0].dead = True
    D = {}

    def din(name, shape, dt=F32):
        D[name] = nc.dram_tensor(name, list(shape), dt, kind="ExternalInput").ap()

    def dscr(name, shape, dt):
        kind = "ExternalOutput" if name in dbg else "Internal"
        D[name] = nc.dram_tensor(name, list(shape), dt, kind=kind).ap()

    din("x", [2, S, DM])
    din("cT", [128, 8, 2])
    din("w_ada", [4, 128, 8, 6144])
    din("b_ada", [4, 6144])
    din("gpre_mixT", [4, 128, 8])
    din("gpre_ffnT", [4, 128, 8])
    din("gpost_mix", [4, DM])
    din("gpost_ffn", [4, DM])
    din("w_in", [4, 128, 8, D_IN])
    din("w_uq", [4, 128, 3, 384])
    din("w_ukv", [4, 128, 2, 512])
    din("gqT", [4, 128, 3])
    din("gkvT", [4, 128, 2])
    din("w_out", [4, 128, 8, DM])
    din("w_up", [4, NCH, 128, 8, 256])
    din("w_down", [4, 128, NCH, DM])
    din("convwT", [4, 128, NCH, 3])
    din("convbT", [4, 128, NCH])
    din("naG", [4, 128, 4 * 14 * 64])
    din("nacm", [128, 14 * 64])
    din("pens", [2, NPAT * 256])
    din("A2", [2, 128])
    din("t5G", [128, 8 * 3 * 384])
    din("t5m", [128, 384])
    din("cosT", [128, S])
    din("sinT", [128, S])
    D["out"] = nc.dram_tensor("out", [2, S, DM], F32, kind="ExternalOutput").ap()
    dscr("modD", [4, 2, 6144], F32)
    dscr("xs", [2, S, DM], F32)
    dscr("zT", [2, 12, 128, S], BF16)
    dscr("vtok", [2, S, 768], BF16)
    dscr("qn", [2, 2, 128, S], BF16)
    dscr("qr", [2, 128, S], BF16)
    dscr("kn", [2, 2, 128, S], BF16)
    dscr("kr", [2, 32, S], BF16)
    dscr("vc", [2, S, 256], BF16)
    dscr("OT", [2, DM, S], BF16)

    with ExitStack() as gctx:
        P = Prog(nc, gctx)
        PP.append(P)

        uid = [0]

        def SB(ctx, name, shape, dt):
            uid[0] += 1
            return ctx.enter_context(nc.sbuf_tensor("%s_u%d" % (name, uid[0]), list(shape), dt))

        def PS(ctx, name, shape, dt=F32):
            uid[0] += 1
            return ctx.enter_context(nc.psum_tensor("%s_u%d" % (name, uid[0]), list(shape), dt))

        zcol = SB(gctx, "zcol", [128, 1], F32)
        bZc = Buf()
        ident = SB(gctx, "ident", [128, 128], BF16)
        identf = SB(gctx, "identf", [128, 128], F32)
        bId = Buf()
        P.op("pool", lambda e: e.memset(zcol[:], 0.0), writes=[bZc])
        P.op("pool", lambda e: e.memset(identf[:], 1.0), writes=[bId])
        P.op("pool", lambda e: e.affine_select(out=identf[:], in_=identf[:], pattern=[[-1, 128]],
                                                compare_op=ALU.is_equal, fill=0.0, base=0, channel_multiplier=1),
             reads=[bId], writes=[bId])
        P.op("pool", lambda e: e.tensor_copy(out=ident[:], in_=identf[:]), reads=[bId], writes=[bId])

        def rstd_from_ss(ss_ap, out_ap, n, bss, bout, tmp_ap, btmp):
            P.op("act", lambda e: e.activation(out=tmp_ap, in_=ss_ap, func=AF.Ln, scale=1.0 / n, bias=EPS),
                 reads=[bss], writes=[btmp])
            P.op("act", lambda e: e.activation(out=out_ap, in_=tmp_ap, func=AF.Exp, scale=-0.5),
                 reads=[btmp], writes=[bout])

        try:
            with ExitStack() as c:
                cT = SB(c, "cT_sb", [128, 8, 2], F32)
                cact = SB(c, "cact", [128, 8, 2], F32)
                brow = SB(c, "brow", [2, 6144], F32)
                mrow = SB(c, "mrow", [2, 6144], F32)
                wt = [SB(c, "wada%d" % i, [128, 8, 512], F32) for i in range(2)]
                psM = [PS(c, "psM%d" % i, [2, 512]) for i in range(2)]
                bc, bb, bm = Buf(), Buf(), Buf()
                bw = [Buf(), Buf()]
                bp = [Buf(), Buf()]
                P.dma("sp", cT[:], D["cT"], writes=[bc])
                P.op("act", lambda e: e.activation(out=cact[:], in_=cT[:], func=AF.Silu), reads=[bc], writes=[bc])
                it = 0
                for l in range(NL):
                    P.dma("sp", brow[:], D["b_ada"][l:l + 1, :].partition_broadcast(2), writes=[bb])
                    for nb in range(12):
                        j = it % 2
                        it += 1
                        P.dma("sp", wt[j][:], D["w_ada"][l, :, :, nb * 512:(nb + 1) * 512], writes=[bw[j]])
                        P.op("pe", [(lambda e, k=k, j=j: e.matmul(psM[j][:], lhsT=cact[:, k, :], rhs=wt[j][:, k, :],
                                                                  start=(k == 0), stop=(k == 7))) for k in range(8)],
                             reads=[bc, bw[j]], writes=[bp[j]])
                        P.op("dve", lambda e, j=j, nb=nb: e.tensor_tensor(out=mrow[:, nb * 512:(nb + 1) * 512], in0=psM[j][:],
                                                                          in1=brow[:, nb * 512:(nb + 1) * 512], op=ALU.add),
                             reads=[bp[j], bb], writes=[bm])
                    P.dma("sp", D["modD"][l], mrow[:], reads=[bm])
            P.barrier()
            ckpt(1)

            def load_vec8(ctx_name, dst, l, s, j, bdst):
                src = D["modD"][l, s, j * 1024:(j + 1) * 1024].rearrange("(k p) -> p k", p=128)
                P.dma("sp", dst, src, writes=[bdst], allow_slow_non_contiguous=True)

            for l in range(NL):
                x_src = D["x"] if l == 0 else D["xs"]
                x_dst = D["out"] if l == NL - 1 else D["xs"]

                with ExitStack() as c:
                    w_in = SB(c, "w_in", [128, 8, D_IN], BF16)
                    w_rot = SB(c, "w_rot", [128, 8, 32], BF16)
                    w_uq = SB(c, "w_uq", [128, 3, 384], BF16)
                    w_uqr = SB(c, "w_uqr", [128, 3, 128], BF16)
                    w_ukv = SB(c, "w_ukv", [128, 2, 512], BF16)
                    gq = SB(c, "gq", [128, 3], F32)
                    gkv = SB(c, "gkv", [128, 2], F32)
                    gpre = SB(c, "gpre", [128, 8], F32)
                    scA = [SB(c, "scA%d" % s, [128, 8], F32) for s in range(2)]
                    biA = [SB(c, "biA%d" % s, [128, 8], F32) for s in range(2)]
                    xg = [SB(c, "xgA%d" % i, [128, 4, DM], F32) for i in range(2)]
                    cosg = [SB(c, "cosg%d" % i, [128, 512], F32) for i in range(2)]
                    sing = [SB(c, "sing%d" % i, [128, 512], F32) for i in range(2)]
                    junk = SB(c, "junkA", [128, DM], BF16)
                    ss = SB(c, "ssA", [128, 4], F32)
                    lnv = SB(c, "lnvA", [128, 4], F32)
                    rstd = SB(c, "rstdA", [128, 4], F32)
                    ssq = SB(c, "ssq", [128, 4], F32)
                    sskv = SB(c, "sskv", [128, 4], F32)
                    lnq = SB(c, "lnq", [128, 4], F32)
                    lnk = SB(c, "lnk", [128, 4], F32)
                    rsq = SB(c, "rsq", [128, 4], F32)
                    rskv = SB(c, "rskv", [128, 4], F32)
                    xn = SB(c, "xnA", [128, 4, DM], BF16)
                    hT = SB(c, "hT", [128, 8, 512], BF16)
                    zst = [SB(c, "zst%d" % i, [128, 12, 512], BF16) for i in range(2)]
                    vst = [SB(c, "vst%d" % i, [128, 4, 768], BF16) for i in range(2)]
                    cst = SB(c, "cst", [128, 4, 640], F32)
                    cqn = SB(c, "cqn", [128, 4, 384], BF16)
                    ckvn = SB(c, "ckvn", [128, 4, 256], BF16)
                    cqT = SB(c, "cqT", [128, 3, 512], BF16)
                    ckvT = SB(c, "ckvT", [128, 2, 512], BF16)
                    qnst = [SB(c, "qnst%d" % i, [128, 2, 512], BF16) for i in range(2)]
                    knst = [SB(c, "knst%d" % i, [128, 2, 512], BF16) for i in range(2)]
                    qrst = [SB(c, "qrst%d" % i, [128, 512], BF16) for i in range(2)]
                    krst = [SB(c, "krst%d" % i, [32, 512], BF16) for i in range(2)]
                    vcst = [SB(c, "vcst%d" % i, [128, 4, 256], BF16) for i in range(2)]
                    t1 = SB(c, "t1A", [128, 512], F32)
                    t2 = SB(c, "t2A", [128, 512], F32)
                    psT = [PS(c, "psTA%d" % i, [128, 512], BF16) for i in range(2)]
                    psZ = [PS(c, "psZA%d" % i, [128, 512]) for i in range(2)]
                    psW = [PS(c, "psWA%d" % i, [128, 1024]) for i in range(2)]

                    bW, bWr, bUq, bUqr, bUkv, bG = Buf(), Buf(), Buf(), Buf(), Buf(), Buf()
                    bSc = [Buf(), Buf()]
                    bXg = [Buf(), Buf()]
                    bCS = [Buf(), Buf()]
                    bJ, bSS, bLn, bRs, bXn, bHT = Buf(), Buf(), Buf(), Buf(), Buf(), Buf()
                    bSq, bLq, bRq = Buf(), Buf(), Buf()
                    bZst = [Buf(), Buf()]
                    bVst = [Buf(), Buf()]
                    bCst, bCqn, bCkvn, bCqT, bCkvT = Buf(), Buf(), Buf(), Buf(), Buf()
                    bQn = [Buf(), Buf()]
                    bKn = [Buf(), Buf()]
                    bQr = [Buf(), Buf()]
                    bKr = [Buf(), Buf()]
                    bVc = [Buf(), Buf()]
                    bT1, bT2 = Buf(), Buf()
                    bPT = [Buf(), Buf()]
                    bPZ = [Buf(), Buf()]
                    bPW = [Buf(), Buf()]

                    P.dmas("pool", [(w_in[:, k, :], D["w_in"][l, :, k, :]) for k in range(8)], writes=[bW])
                    P.dma("pool", w_uq[:], D["w_uq"][l], writes=[bUq])
                    P.dma("pool", w_ukv[:], D["w_ukv"][l], writes=[bUkv])
                    P.dmas("sp", [(gq[:], D["gqT"][l]), (gkv[:], D["gkvT"][l]), (gpre[:], D["gpre_mixT"][l])], writes=[bG])
                    P.op("act", lambda e: e.mul(out=w_rot[:, :, 0:16], in_=w_in[:, :, 2960:2976], mul=-1.0), reads=[bW], writes=[bWr])
                    P.op("act", lambda e: e.copy(out=w_rot[:, :, 16:32], in_=w_in[:, :, 2944:2960]), reads=[bW], writes=[bWr])
                    for k in range(3):
                        src = w_uq[:, k, 256:384].rearrange("p (h t j) -> p h t j", t=2, j=16)
                        dst = w_uqr[:, k, :].rearrange("p (h t j) -> p h t j", t=2, j=16)
                        P.op("act", lambda e, src=src, dst=dst: e.mul(out=dst[:, :, 0, :], in_=src[:, :, 1, :], mul=-1.0),
                             reads=[bUq], writes=[bUqr])
                        P.op("act", lambda e, src=src, dst=dst: e.copy(out=dst[:, :, 1, :], in_=src[:, :, 0, :]),
                             reads=[bUq], writes=[bUqr])
                    for s in range(2):
                        load_vec8(c, biA[s][:], l, s, 0, bSc[s])
                        load_vec8(c, scA[s][:], l, s, 1, bSc[s])
                        P.op("dve", lambda e, s=s: e.scalar_tensor_tensor(out=scA[s][:], in0=scA[s][:], scalar=1.0, in1=gpre[:],
                                                                          op0=ALU.add, op1=ALU.mult),
                             reads=[bSc[s], bG], writes=[bSc[s]])

                    def loadA(g):
                        s, t0 = g // 8, (g % 8) * 512
                        j = g % 2
                        P.dma("sp", xg[j][:], x_src[s, t0:t0 + 512, :].rearrange("(t p) d -> p t d", p=128), writes=[bXg[j]])
                        P.dmas("sp", [(cosg[j][:], D["cosT"][:, t0:t0 + 512]), (sing[j][:], D["sinT"][:, t0:t0 + 512])], writes=[bCS[j]])

                    evac_i = [0]

                    def evac(out_ap, in_ap, reads, writes):
                        evac_i[0] += 1
                        if evac_i[0] % 2:
                            P.op("dve", lambda e: e.tensor_copy(out=out_ap, in_=in_ap), reads=reads, writes=writes)
                        else:
                            P.op("act", lambda e: e.copy(out=out_ap, in_=in_ap), reads=reads, writes=writes)

                    NG = 16
                    ckpt(2)
                    loadA(0)
                    for g in range(NG):
                        s, t0 = g // 8, (g % 8) * 512
                        j = g % 2
                        if g + 1 < NG:
                            loadA(g + 1)
                        X = xg[j]
                        for t in range(4):
                            P.op("act", lambda e, t=t: e.activation(out=junk[:], in_=X[:, t, :], func=AF.Square, accum_out=ss[:, t:t + 1]),
                                 reads=[bXg[j]], writes=[bJ, bSS])
                        rstd_from_ss(ss[:], rstd[:], DM, bSS, bRs, lnv[:], bLn)
                        for t in range(4):
                            P.op("dve", lambda e, t=t: e.tensor_scalar(out=xn[:, t, :], in0=X[:, t, :], scalar1=rstd[:, t:t + 1], scalar2=None,
                                                                       op0=ALU.mult),
                                 reads=[bXg[j], bRs], writes=[bXn])
                        for k in range(8):
                            pj = k % 2
                            P.op("pe", [(lambda e, t=t, k=k, pj=pj: e.transpose(psT[pj][:, t * 128:(t + 1) * 128], xn[:, t, k * 128:(k + 1) * 128], ident[:]))
                                        for t in range(4)], reads=[bXn, bId], writes=[bPT[pj]])
                            P.op("act", lambda e, k=k, pj=pj: e.activation(out=hT[:, k, :], in_=psT[pj][:], func=AF.Identity,
                                                                           scale=scA[s][:, k:k + 1], bias=biA[s][:, k:k + 1]),
                                 reads=[bPT[pj], bSc[s]], writes=[bHT])
                        ckpt(3)
                        cbs = [0, 128, 256, 384] + [768 + 128 * i for i in range(8)]
                        for ci, cb in enumerate(cbs):
                            pj = ci % 2
                            P.op("pe", [(lambda e, k=k, cb=cb, pj=pj: e.matmul(psZ[pj][:], lhsT=w_in[:, k, cb:cb + 128], rhs=hT[:, k, :],
                                                                              start=(k == 0), stop=(k == 7))) for k in range(8)],
                                 reads=[bW, bHT], writes=[bPZ[pj]])
                            evac(zst[j][:, ci, :], psZ[pj][:], [bPZ[pj]], [bZst[j]])
                        P.dma("sp", D["zT"][s].rearrange("c p t -> p c t")[:, :, t0:t0 + 512], zst[j][:], reads=[bZst[j]])
                        ckpt(4)
                        for t in range(4):
                            pj = t % 2
                            fns = []
                            for k in range(8):
                                fns.append(lambda e, k=k, t=t, pj=pj: e.matmul(psW[pj][:, 0:256], lhsT=hT[:, k, t * 128:(t + 1) * 128],
                                                                               rhs=w_in[:, k, 512:768], start=(k == 0), stop=(k == 7)))
                            for k in range(8):
                                fns.append(lambda e, k=k, t=t, pj=pj: e.matmul(psW[pj][:, 512:1024], lhsT=hT[:, k, t * 128:(t + 1) * 128],
                                                                               rhs=w_in[:, k, 1792:2304], start=(k == 0), stop=(k == 7)))
                            P.op("pe", fns, reads=[bW, bHT], writes=[bPW[pj]])
                            evac(vst[j][:, t, 0:256], psW[pj][:, 0:256], [bPW[pj]], [bVst[j]])
                            evac(vst[j][:, t, 256:768], psW[pj][:, 512:1024], [bPW[pj]], [bVst[j]])
                        P.dma("sp", D["vtok"][s, t0:t0 + 512, :].rearrange("(t p) c -> p t c", p=128), vst[j][:], reads=[bVst[j]])
                        ckpt(5)
                        for t in range(4):
                            pj = t % 2
                            fns = []
                            for k in range(8):
                                fns.append(lambda e, k=k, t=t, pj=pj: e.matmul(psW[pj][:, 0:384], lhsT=hT[:, k, t * 128:(t + 1) * 128],
                                                                               rhs=w_in[:, k, 2304:2688], start=(k == 0), stop=(k == 7)))
                            for k in range(8):
                                fns.append(lambda e, k=k, t=t, pj=pj: e.matmul(psW[pj][:, 512:768], lhsT=hT[:, k, t * 128:(t + 1) * 128],
                                                                               rhs=w_in[:, k, 2688:2944], start=(k == 0), stop=(k == 7)))
                            if "noMm" not in dbg:
                                P.op("pe", fns, reads=[bW, bHT], writes=[bPW[pj]])
                            if "noCp" not in dbg:
                                P.op("dve", lambda e, t=t, pj=pj: e.tensor_copy(out=cst[:, t, 0:384], in_=psW[pj][:, 0:384]), reads=[bPW[pj]], writes=[bCst])
                                P.op("dve", lambda e, t=t, pj=pj: e.tensor_copy(out=cst[:, t, 384:640], in_=psW[pj][:, 512:768]), reads=[bPW[pj]], writes=[bCst])
                            if "noSq" not in dbg:
                              P.op("act", lambda e, t=t, pj=pj: e.activation(out=junk[:, 0:384], in_=cst[:, t, 0:384], func=AF.Square,
                                                                           accum_out=ssq[:, t:t + 1]), reads=[bCst], writes=[bJ, bSq])
                            if "noSq" not in dbg:
                              P.op("act", lambda e, t=t, pj=pj: e.activation(out=junk[:, 384:640], in_=cst[:, t, 384:640], func=AF.Square,
                                                                           accum_out=sskv[:, t:t + 1]), reads=[bCst], writes=[bJ, bSq])
                        ckpt(51)
                        rstd_from_ss(ssq[:], rsq[:], 384, bSq, bRq, lnq[:], bLq)
                        rstd_from_ss(sskv[:], rskv[:], 256, bSq, bRq, lnk[:], bLq)
                        for t in range(4):
                            P.op("dve", lambda e, t=t: e.tensor_scalar(out=cqn[:, t, :], in0=cst[:, t, 0:384], scalar1=rsq[:, t:t + 1], scalar2=None,
                                                                       op0=ALU.mult), reads=[bCst, bRq], writes=[bCqn])
                            P.op("dve", lambda e, t=t: e.tensor_scalar(out=ckvn[:, t, :], in0=cst[:, t, 384:640], scalar1=rskv[:, t:t + 1], scalar2=None,
                                                                       op0=ALU.mult), reads=[bCst, bRq], writes=[bCkvn])
                        ckpt(53)
                        for kk in range(3):
                            pj = kk % 2
                            P.op("pe", [(lambda e, t=t, kk=kk, pj=pj: e.transpose(psT[pj][:, t * 128:(t + 1) * 128], cqn[:, t, kk * 128:(kk + 1) * 128], ident[:]))
                                        for t in range(4)], reads=[bCqn, bId], writes=[bPT[pj]])
                            P.op("act", lambda e, kk=kk, pj=pj: e.activation(out=cqT[:, kk, :], in_=psT[pj][:], func=AF.Identity, scale=gq[:, kk:kk + 1], bias=zcol[:, 0:1]),
                                 reads=[bPT[pj], bG, bZc], writes=[bCqT])
                        ckpt(54)
                        for kk in range(2):
                            pj = (kk + 1) % 2
                            P.op("pe", [(lambda e, t=t, kk=kk, pj=pj: e.transpose(psT[pj][:, t * 128:(t + 1) * 128], ckvn[:, t, kk * 128:(kk + 1) * 128], ident[:]))
                                        for t in range(4)], reads=[bCkvn, bId], writes=[bPT[pj]])
                            P.op("act", lambda e, kk=kk, pj=pj: e.activation(out=ckvT[:, kk, :], in_=psT[pj][:], func=AF.Identity, scale=gkv[:, kk:kk + 1], bias=zcol[:, 0:1]),
                                 reads=[bPT[pj], bG, bZc], writes=[bCkvT])
                        ckpt(6)
                        for jj in range(2):
                            P.op("pe", [(lambda e, kk=kk, jj=jj: e.matmul(psZ[0][:], lhsT=w_uq[:, kk, jj * 128:(jj + 1) * 128], rhs=cqT[:, kk, :],
                                                                          start=(kk == 0), stop=(kk == 2))) for kk in range(3)],
                                 reads=[bUq, bCqT], writes=[bPZ[0]])
                            evac(qnst[j][:, jj, :], psZ[0][:], [bPZ[0]], [bQn[j]])
                            P.op("pe", [(lambda e, kk=kk, jj=jj: e.matmul(psZ[1][:], lhsT=w_ukv[:, kk, jj * 128:(jj + 1) * 128], rhs=ckvT[:, kk, :],
                                                                          start=(kk == 0), stop=(kk == 1))) for kk in range(2)],
                                 reads=[bUkv, bCkvT], writes=[bPZ[1]])
                            evac(knst[j][:, jj, :], psZ[1][:], [bPZ[1]], [bKn[j]])
                        P.op("pe", [(lambda e, kk=kk: e.matmul(psZ[0][:], lhsT=w_uq[:, kk, 256:384], rhs=cqT[:, kk, :], start=(kk == 0), stop=(kk == 2)))
                                    for kk in range(3)], reads=[bUq, bCqT], writes=[bPZ[0]])
                        P.op("pe", [(lambda e, kk=kk: e.matmul(psZ[1][:], lhsT=w_uqr[:, kk, :], rhs=cqT[:, kk, :], start=(kk == 0), stop=(kk == 2)))
                                    for kk in range(3)], reads=[bUqr, bCqT], writes=[bPZ[1]])
                        P.op("dve", lambda e: e.tensor_tensor(out=t1[:], in0=psZ[0][:], in1=cosg[j][:], op=ALU.mult), reads=[bPZ[0], bCS[j]], writes=[bT1])
                        P.op("dve", lambda e: e.tensor_tensor(out=t2[:], in0=psZ[1][:], in1=sing[j][:], op=ALU.mult), reads=[bPZ[1], bCS[j]], writes=[bT2])
                        P.op("pool", lambda e: e.tensor_tensor(out=qrst[j][:], in0=t1[:], in1=t2[:], op=ALU.add), reads=[bT1, bT2], writes=[bQr[j]])
                        ckpt(7)
                        P.op("pe", [(lambda e, k=k: e.matmul(psZ[0][0:32, :], lhsT=w_in[:, k, 2944:2976], rhs=hT[:, k, :], start=(k == 0), stop=(k == 7)))
                                    for k in range(8)], reads=[bW, bHT], writes=[bPZ[0]])
                        P.op("pe", [(lambda e, k=k: e.matmul(psZ[1][0:32, :], lhsT=w_rot[:, k, :], rhs=hT[:, k, :], start=(k == 0), stop=(k == 7)))
                                    for k in range(8)], reads=[bWr, bHT], writes=[bPZ[1]])
                        P.op("dve", lambda e: e.tensor_tensor(out=t1[0:32, :], in0=psZ[0][0:32, :], in1=cosg[j][0:32, :], op=ALU.mult),
                             reads=[bPZ[0], bCS[j]], writes=[bT1])
                        P.op("dve", lambda e: e.tensor_tensor(out=t2[0:32, :], in0=psZ[1][0:32, :], in1=sing[j][0:32, :], op=ALU.mult),
                             reads=[bPZ[1], bCS[j]], writes=[bT2])
                        P.op("pool", lambda e: e.tensor_tensor(out=krst[j][:], in0=t1[0:32, :], in1=t2[0:32, :], op=ALU.add),
                             reads=[bT1, bT2], writes=[bKr[j]])
                        ckpt(8)
                        for t in range(4):
                            pj = t % 2
                            P.op("pe", [(lambda e, kk=kk, t=t, pj=pj: e.matmul(psW[pj][:, 0:256], lhsT=ckvT[:, kk, t * 128:(t + 1) * 128],
                                                                               rhs=w_ukv[:, kk, 256:512], start=(kk == 0), stop=(kk == 1))) for kk in range(2)],
                                 reads=[bUkv, bCkvT], writes=[bPW[pj]])
                            evac(vcst[j][:, t, :], psW[pj][:, 0:256], [bPW[pj]], [bVc[j]])
                        P.dma("sp", D["qn"][s].rearrange("j p t -> p j t")[:, :, t0:t0 + 512], qnst[j][:], reads=[bQn[j]])
                        P.dma("sp", D["kn"][s].rearrange("j p t -> p j t")[:, :, t0:t0 + 512], knst[j][:], reads=[bKn[j]])
                        P.dma("sp", D["qr"][s, :, t0:t0 + 512], qrst[j][:], reads=[bQr[j]])
                        P.dma("sp", D["kr"][s, :, t0:t0 + 512], krst[j][:], reads=[bKr[j]])
                        P.dma("sp", D["vc"][s, t0:t0 + 512, :].rearrange("(t p) c -> p t c", p=128), vcst[j][:], reads=[bVc[j]])
                        ckpt(9)
                P.barrier()
                if "stopA" in dbg:
                    break

                SC_AB = 0.125
                SC_C = 96.0 ** -0.5
                def run_pipeline(units, nb):
                    n = len(units)
                    for i in range(min(nb, n)):
                        if "pre" in units[i]:
                            units[i]["pre"]()
                        units[i]["qk"]()
                    for i in range(n):
                        units[i]["mid"]()
                        units[i]["pv"]()
                        if "post" in units[i]:
                            units[i]["post"]()
                        if i + nb < n:
                            u2 = units[i + nb]
                            if "pre" in u2:
                                u2["pre"]()
                            u2["qk"]()

                with ExitStack() as c:
                    NB = 3
                    naX = SB(c, "naX", [128, 4 * 14 * 64], BF16)
                    t5X = SB(c, "t5X", [128, 8 * 3 * 384], BF16)
                    pens = SB(c, "pens", [2, NPAT * 256], BF16)
                    A2 = SB(c, "A2", [2, 128], BF16)
                    psS = [PS(c, "psS%d" % i, [128, 512]) for i in range(NB)]
                    psO = [PS(c, "psO%d" % i, [128, 512]) for i in range(2)]
                    E = [SB(c, "E%d" % i, [128, 384], BF16) for i in range(NB + 1)]
                    bE = [Buf() for _ in range(NB + 1)]
                    bPS = [Buf() for _ in range(NB)]
                    bPO = [Buf(), Buf()]
                    bNaX, bT5X, bPen = Buf(), Buf(), Buf()
                    P.dmas("pool", [(pens[:], D["pens"]), (A2[:], D["A2"])], writes=[bPen])
                    with ExitStack() as c2:
                        stg = SB(c2, "tstg", [128, 8 * 3 * 384], F32)
                        msk = SB(c2, "tmsk", [128, 14 * 64], F32)
                        bStg, bMsk = Buf(), Buf()
                        P.dma("sp", stg[:, 0:3584], D["naG"][l], writes=[bStg])
                        P.dma("sp", msk[:], D["nacm"], writes=[bMsk])
                        P.op("act", lambda e: e.activation(out=stg[:, 0:3584], in_=stg[:, 0:3584], func=AF.Exp), reads=[bStg], writes=[bStg])
                        for h in range(4):
                            P.op("dve", lambda e, h=h: e.tensor_tensor(out=naX[:, h * 896:(h + 1) * 896], in0=stg[:, h * 896:(h + 1) * 896], in1=msk[:],
                                                                       op=ALU.mult), reads=[bStg, bMsk], writes=[bNaX])
                        P.dma("sp", stg[:], D["t5G"], writes=[bStg])
                        P.dma("sp", msk[:, 0:384], D["t5m"], writes=[bMsk])
                        for hp in range(24):
                            P.op("act", lambda e, hp=hp: e.activation(out=stg[:, hp * 384:(hp + 1) * 384], in_=stg[:, hp * 384:(hp + 1) * 384], func=AF.Exp),
                                 reads=[bStg], writes=[bStg])
                            P.op("dve", lambda e, hp=hp: e.tensor_tensor(out=t5X[:, hp * 384:(hp + 1) * 384], in0=stg[:, hp * 384:(hp + 1) * 384],
                                                                         in1=msk[:, 0:384], op=ALU.mult), reads=[bStg, bMsk], writes=[bT5X])
                        P.barrier()
                    ucnt = [0]
                    bcnt = [0]

                    with ExitStack() as c2:
                        qT = [SB(c2, "naq%d" % i, [128, S], BF16) for i in range(2)]
                        kT = [SB(c2, "nak%d" % i, [128, S], BF16) for i in range(2)]
                        va = [SB(c2, "nav%d" % i, [128, 32, 2, 128], BF16) for i in range(2)]
                        ot = [SB(c2, "naot%d" % i, [128, S], BF16) for i in range(2)]
                        rec = SB(c2, "narec", [64, 256], F32)
                        bQ = [Buf(), Buf()]
                        bV = [Buf(), Buf()]
                        bOt = [Buf(), Buf()]
                        bRec = Buf()
                        for i in range(2):
                            P.op("pool", lambda e, i=i: e.memset(va[i][:, :, :, 64:128], 1.0), writes=[bV[i]])
                        items = [(s, hp) for s in range(2) for hp in range(2)]

                        def loadNA(n):
                            s, hp = items[n]
                            i = n % 2
                            P.dmas("sp", [(qT[i][:], D["zT"][s, hp]), (kT[i][:], D["zT"][s, 2 + hp])], writes=[bQ[i]])
                            src = D["vtok"][s].rearrange("(n p) c -> p n c", p=128)
                            P.dmas("sp", [(va[i][:, n0:n0 + 16, hh, 0:64], src[:, n0:n0 + 16, hp * 128 + hh * 64:hp * 128 + hh * 64 + 64])
                                          for n0 in range(0, 32, 16) for hh in range(2)], writes=[bV[i]])

                        units = []
                        for n, (s, hp) in enumerate(items):
                            i = n % 2
                            for hh in range(2):
                                h = 2 * hp + hh
                                pr = slice(hh * 64, hh * 64 + 64)
                                for J in range(16):
                                    ul = NA_PLAN[J]
                                    oj = bcnt[0] % 2
                                    bcnt[0] += 1
                                    for ui, (ci, di0, pid) in enumerate(ul):
                                        u = ucnt[0]
                                        ucnt[0] += 1
                                        sj, ej = u % NB, u % (NB + 1)
                                        first, last = (ui == 0), (ui == len(ul) - 1)

                                        def qk(i=i, pr=pr, ci=ci, J=J, pid=pid, sj=sj):
                                            fns = [lambda e: e.matmul(psS[sj][:, 0:256], lhsT=kT[i][pr, ci * 128:(ci + 1) * 128],
                                                                      rhs=qT[i][pr, J * 256:(J + 1) * 256], start=True, stop=(pid < 0))]
                                            if pid >= 0:
                                                fns.append(lambda e: e.matmul(psS[sj][:, 0:256], lhsT=A2[:, :], rhs=pens[:, pid * 256:(pid + 1) * 256],
                                                                              start=False, stop=True))
                                            P.op("pe", fns, reads=[bQ[i], bPen], writes=[bPS[sj]])

                                        def mid(h=h, di0=di0, sj=sj, ej=ej):
                                            P.op("act", lambda e: e.activation(out=E[ej][:, 0:256], in_=psS[sj][:, 0:256], func=AF.Exp, scale=SC_AB),
                                                 reads=[bPS[sj]], writes=[bE[ej]])
                                            xo = h * 896 + di0 * 64
                                            P.op("dve", lambda e: e.tensor_tensor(out=E[ej][:, 0:256], in0=E[ej][:, 0:256], in1=naX[:, xo:xo + 256], op=ALU.mult),
                                                 reads=[bE[ej], bNaX], writes=[bE[ej]])

                                        def pv(i=i, ci=ci, hh=hh, ej=ej, oj=oj, first=first, last=last, J=J, pr=pr):
                                            P.op("pe", lambda e: e.matmul(psO[oj][:, 0:256], lhsT=va[i][:, ci, hh, :], rhs=E[ej][:, 0:256], start=first, stop=last),
                                                 reads=[bV[i], bE[ej]], writes=[bPO[oj]])
                                            if last:
                                                P.op("dve", lambda e: e.reciprocal(out=rec[:], in_=psO[oj][64:128, 0:256]), reads=[bPO[oj]], writes=[bRec])
                                                P.op("dve", lambda e: e.tensor_tensor(out=ot[i][pr, J * 256:(J + 1) * 256], in0=psO[oj][0:64, 0:256], in1=rec[:],
                                                                                      op=ALU.mult), reads=[bPO[oj], bRec], writes=[bOt[i]])

                                        units.append(dict(qk=qk, mid=mid, pv=pv))

                            def post(n=n, s=s, hp=hp, i=i):
                                P.dma("sp", D["OT"][s, hp * 128:(hp + 1) * 128, :], ot[i][:], reads=[bOt[i]])
                                if n + 2 < len(items):
                                    loadNA(n + 2)
                            units[-1]["post"] = post
                        loadNA(0)
                        loadNA(1)
                        run_pipeline(units, NB)
                        P.barrier()

                    with ExitStack() as c2:
                        qT = [SB(c2, "dq%d" % i, [128, S], BF16) for i in range(2)]
                        kT = [SB(c2, "dk%d" % i, [128, S], BF16) for i in range(2)]
                        qo = [SB(c2, "dqo%d" % i, [128, S], BF16) for i in range(2)]
                        ko = [SB(c2, "dko%d" % i, [128, S], BF16) for i in range(2)]
                        va = [SB(c2, "dv%d" % i, [128, 32, 2, 128], BF16) for i in range(2)]
                        oacc = [SB(c2, "oacc%d" % i, [128, S], F32) for i in range(2)]
                        ot = [SB(c2, "dot%d" % i, [128, S], BF16) for i in range(2)]
                        rec = SB(c2, "drec", [64, 1024], F32)
                        bQ = [Buf(), Buf()]
                        bQo = [Buf(), Buf()]
                        bV = [Buf(), Buf()]
                        bAcc = [Buf(), Buf()]
                        bOt = [Buf(), Buf()]
                        bRec = Buf()
                        for i in range(2):
                            P.op("pool", lambda e, i=i: e.memset(va[i][:, :, :, 64:128], 1.0), writes=[bV[i]])
                        items = [(s, hp) for s in range(2) for hp in range(4)]
                        pats = (1, 4, 16)
                        NPI = len(items) * 3

                        def loadQK(n):
                            s, hp = items[n]
                            i = n % 2
                            P.dmas("sp", [(qT[i][:], D["zT"][s, 4 + hp]), (kT[i][:], D["zT"][s, 8 + hp])], writes=[bQ[i]])

                        def loadV(pidx):
                            n, pi = pidx // 3, pidx % 3
                            s, hp = items[n]
                            d = pats[pi]
                            vi = pidx % 2
                            nchunk = 32 // d
                            src = D["vtok"][s].rearrange("(n p r) c -> p r n c", p=128, r=d)
                            pairs = []
                            for r in range(d):
                                for n0 in range(0, nchunk, 16):
                                    n1 = min(n0 + 16, nchunk)
                                    for hh in range(2):
                                        cb0 = 256 + hp * 128 + hh * 64
                                        pairs.append((va[vi][:, r * nchunk + n0:r * nchunk + n1, hh, 0:64], src[:, r, n0:n1, cb0:cb0 + 64]))
                            P.dmas("sp", pairs, writes=[bV[vi]])

                        units = []
                        ocount = 0
                        for n, (s, hp) in enumerate(items):
                            i = n % 2
                            for pi, d in enumerate(pats):
                                pidx = n * 3 + pi
                                vi = pidx % 2
                                nchunk = 32 // d
                                pre = None
                                if d == 1:
                                    Q, K, bQQ = qT[i], kT[i], bQ[i]
                                else:
                                    oi = ocount % 2
                                    ocount += 1
                                    Q, K, bQQ = qo[oi], ko[oi], bQo[oi]

                                    def pre(Q=Q, K=K, d=d, i=i, bQQ=bQQ):
                                        P.op("act", lambda e: e.copy(out=Q[:].rearrange("p (r m) -> p r m", r=d),
                                                                     in_=qT[i][:].rearrange("p (m r) -> p r m", r=d)),
                                             reads=[bQ[i]], writes=[bQQ])
                                        P.op("act", lambda e: e.copy(out=K[:].rearrange("p (r m) -> p r m", r=d),
                                                                     in_=kT[i][:].rearrange("p (m r) -> p r m", r=d)),
                                             reads=[bQ[i]], writes=[bQQ])
                                firstu = True
                                for hh in range(2):
                                    h = 2 * hp + hh
                                    pr = slice(hh * 64, hh * 64 + 64)
                                    accv = oacc[hh][:].rearrange("p (m r) -> p r m", r=d)
                                    zero_first = (pi == 0)
                                    for r in range(d):
                                        for kc in range(nchunk):
                                            u = ucnt[0]
                                            ucnt[0] += 1
                                            sj, ej, oj = u % NB, u % (NB + 1), u % 2
                                            j0, j1 = max(kc - 1, 0), min(kc + 1, nchunk - 1)
                                            g0 = j0 - (kc - 1)
                                            ncol = (j1 - j0 + 1) * 128
                                            c0, c1 = g0 * 128, g0 * 128 + ncol
                                            qbase = (r * nchunk + j0) * 128
                                            kbase = (r * nchunk + kc) * 128
                                            xo = (h * 3 + pi) * 384
                                            dst = accv[:, r, j0 * 128:(j1 + 1) * 128]
                                            zf = zero_first
                                            zero_first = False

                                            def qk(Q=Q, K=K, bQQ=bQQ, pr=pr, qbase=qbase, kbase=kbase, ncol=ncol, c0=c0, c1=c1, sj=sj):
                                                P.op("pe", lambda e: e.matmul(psS[sj][:, c0:c1], lhsT=K[pr, kbase:kbase + 128], rhs=Q[pr, qbase:qbase + ncol],
                                                                              start=True, stop=True), reads=[bQQ], writes=[bPS[sj]])

                                            def mid(sj=sj, ej=ej, c0=c0, c1=c1, xo=xo):
                                                P.op("act", lambda e: e.activation(out=E[ej][:, c0:c1], in_=psS[sj][:, c0:c1], func=AF.Exp, scale=SC_AB),
                                                     reads=[bPS[sj]], writes=[bE[ej]])
                                                P.op("pool", lambda e: e.tensor_tensor(out=E[ej][:, c0:c1], in0=E[ej][:, c0:c1], in1=t5X[:, xo + c0:xo + c1],
                                                                                       op=ALU.mult), reads=[bE[ej], bT5X], writes=[bE[ej]])

                                            def pv(vi=vi, r=r, nchunk=nchunk, kc=kc, hh=hh, ej=ej, oj=oj, dst=dst, c0=c0, c1=c1, zf=zf):
                                                if zf:
                                                    P.op("pool", lambda e: e.memset(oacc[hh][:], 0.0), writes=[bAcc[hh]])
                                                P.op("pe", lambda e: e.matmul(psO[oj][:, c0:c1], lhsT=va[vi][:, r * nchunk + kc, hh, :], rhs=E[ej][:, c0:c1],
                                                                              start=True, stop=True), reads=[bV[vi], bE[ej]], writes=[bPO[oj]])
                                                P.op("dve", lambda e: e.tensor_tensor(out=dst, in0=psO[oj][:, c0:c1], in1=dst, op=ALU.add),
                                                     reads=[bPO[oj], bAcc[hh]], writes=[bAcc[hh]])

                                            ud = dict(qk=qk, mid=mid, pv=pv)
                                            if firstu and pre is not None:
                                                ud["pre"] = pre
                                            firstu = False
                                            units.append(ud)
                                    if pi == 2:
                                        def fin(hh=hh, pr=pr, i=i):
                                            for q4 in range(4):
                                                cs = slice(q4 * 1024, (q4 + 1) * 1024)
                                                P.op("dve", lambda e: e.reciprocal(out=rec[:], in_=oacc[hh][64:128, cs]), reads=[bAcc[hh]], writes=[bRec])
                                                P.op("dve", lambda e: e.tensor_tensor(out=ot[i][pr, cs], in0=oacc[hh][0:64, cs], in1=rec[:], op=ALU.mult),
                                                     reads=[bAcc[hh], bRec], writes=[bOt[i]])
                                        units[-1]["fin"] = fin

                                def postp(pidx=pidx):
                                    if pidx + 2 < NPI:
                                        loadV(pidx + 2)
                                units[-1]["postp"] = postp

                            def posti(n=n, s=s, hp=hp, i=i):
                                P.dma("sp", D["OT"][s, 256 + hp * 128:256 + (hp + 1) * 128, :], ot[i][:], reads=[bOt[i]])
                                if n + 2 < len(items):
                                    loadQK(n + 2)
                            units[-1]["posti"] = posti
                        for ud in units:
                            hooks = [ud[k] for k in ("fin", "postp", "posti") if k in ud]
                            if hooks:
                                ud["post"] = (lambda hooks=hooks: [hk() for hk in hooks])
                        loadQK(0)
                        loadQK(1)
                        loadV(0)
                        loadV(1)
                        run_pipeline(units, NB)
                        P.barrier()
                P.barrier()

                with ExitStack() as c:
                    psS = [PS(c, "psSc%d" % i, [128, 1024]) for i in range(2)]
                    psO = [PS(c, "psOc%d" % i, [128, 1024]) for i in range(2)]
                    E = [SB(c, "Ec%d" % i, [128, 1024], BF16) for i in range(3)]
                    qT = [SB(c, "cq%d" % i, [96, S], BF16) for i in range(2)]
                    kT = [SB(c, "ck%d" % i, [96, S], BF16) for i in range(2)]
                    va = [SB(c, "cv%d" % i, [128, 32, 128], BF16) for i in range(2)]
                    ot = [SB(c, "cot%d" % i, [128, S], BF16) for i in range(2)]
                    rec = SB(c, "crec", [64, 1024], F32)
                    bE = [Buf() for _ in range(3)]
                    bPS = [Buf(), Buf()]
                    bPO = [Buf(), Buf()]
                    bQ = [Buf(), Buf()]
                    bV = [Buf(), Buf()]
                    bOt = [Buf(), Buf()]
                    bRec = Buf()
                    for i in range(2):
                        P.op("pool", lambda e, i=i: e.memset(va[i][:, :, 64:128], 1.0), writes=[bV[i]])
                    items = [(s, h) for s in range(2) for h in range(4)]

                    def loadC(n):
                        s, h = items[n]
                        i = n % 2
                        P.dmas("sp", [(qT[i][0:64, :], D["qn"][s, h // 2, (h % 2) * 64:(h % 2) * 64 + 64, :]),
                                      (qT[i][64:96, :], D["qr"][s, h * 32:(h + 1) * 32, :]),
                                      (kT[i][0:64, :], D["kn"][s, h // 2, (h % 2) * 64:(h % 2) * 64 + 64, :]),
                                      (kT[i][64:96, :], D["kr"][s, :, :])], writes=[bQ[i]])
                        src = D["vc"][s].rearrange("(n p) c -> p n c", p=128)
                        P.dmas("sp", [(va[i][:, n0:n0 + 8, 0:64], src[:, n0:n0 + 8, h * 64:(h + 1) * 64]) for n0 in range(0, 32, 8)], writes=[bV[i]])

                    units = []
                    u = 0
                    blk = 0
                    for n, (s, h) in enumerate(items):
                        i = n % 2
                        oi = (n // 2) % 2
                        pr = slice((h % 2) * 64, (h % 2) * 64 + 64)
                        for qb in range(4):
                            oj = blk % 2
                            blk += 1
                            cs = slice(qb * 1024, (qb + 1) * 1024)
                            for kc in range(32):
                                sj, ej = u % 2, u % 3
                                u += 1

                                def qk(i=i, kc=kc, qb=qb, sj=sj):
                                    P.op("pe", [(lambda e, hf=hf: e.matmul(psS[sj][:, hf * 512:(hf + 1) * 512], lhsT=kT[i][0:96, kc * 128:(kc + 1) * 128],
                                                                           rhs=qT[i][0:96, qb * 1024 + hf * 512:qb * 1024 + (hf + 1) * 512],
                                                                           start=True, stop=True)) for hf in range(2)],
                                         reads=[bQ[i]], writes=[bPS[sj]])

                                def mid(sj=sj, ej=ej):
                                    P.op("act", lambda e: e.activation(out=E[ej][:], in_=psS[sj][:], func=AF.Exp, scale=SC_C),
                                         reads=[bPS[sj]], writes=[bE[ej]])

                                def pv(i=i, kc=kc, ej=ej, oj=oj, oi=oi, pr=pr, cs=cs):
                                    P.op("pe", [(lambda e, hf=hf: e.matmul(psO[oj][:, hf * 512:(hf + 1) * 512], lhsT=va[i][:, kc, :],
                                                                           rhs=E[ej][:, hf * 512:(hf + 1) * 512], start=(kc == 0), stop=(kc == 31)))
                                                for hf in range(2)], reads=[bV[i], bE[ej]], writes=[bPO[oj]])
                                    if kc == 31:
                                        P.op("dve", lambda e: e.reciprocal(out=rec[:], in_=psO[oj][64:128, :]), reads=[bPO[oj]], writes=[bRec])
                                        P.op("dve", lambda e: e.tensor_tensor(out=ot[oi][pr, cs], in0=psO[oj][0:64, :], in1=rec[:], op=ALU.mult),
                                             reads=[bPO[oj], bRec], writes=[bOt[oi]])

                                units.append(dict(qk=qk, mid=mid, pv=pv))

                        def post(n=n, s=s, h=h, oi=oi):
                            if h % 2 == 1:
                                P.dma("sp", D["OT"][s, 768 + (h // 2) * 128:768 + (h // 2 + 1) * 128, :], ot[oi][:], reads=[bOt[oi]])
                            if n + 2 < len(items):
                                loadC(n + 2)
                        units[-1]["post"] = post
                    loadC(0)
                    loadC(1)
                    run_pipeline(units, 2)
                P.barrier()
                if "stopB" in dbg:
                    break

                with ExitStack() as c:
                    w_out = SB(c, "w_out", [128, 8, DM], BF16)
                    w_dn = SB(c, "w_dn", [128, NCH, DM], BF16)
                    wu = [SB(c, "wu%d" % i, [128, 8, 256], BF16) for i in range(3)]
                    cw = SB(c, "convw", [128, NCH, 3], F32)
                    cb = SB(c, "convb", [128, NCH], F32)
                    gpre = SB(c, "gpreF", [128, 8], F32)
                    scF = [SB(c, "scF%d" % s, [128, 8], F32) for s in range(2)]
                    biF = [SB(c, "biF%d" % s, [128, 8], F32) for s in range(2)]
                    gate1 = SB(c, "gate1", [128, DM], F32)
                    gate2 = SB(c, "gate2", [128, DM], F32)
                    xm = [SB(c, "xm%d" % i, [128, DM], F32) for i in range(2)]
                    bXm = [Buf(), Buf()]
                    bXD = {}
                    xg = [SB(c, "xgC%d" % i, [128, 4, DM], F32) for i in range(2)]
                    otg = [SB(c, "otg%d" % i, [128, 8, 512], BF16) for i in range(2)]
                    h2T = [SB(c, "h2T%d" % i, [128, 8, 512], BF16) for i in range(2)]
                    halo = SB(c, "halo", [128, 8, 2], BF16)
                    edge = [SB(c, "edge%d" % i, [128, 8, 2], BF16) for i in range(3)]
                    actT = SB(c, "actT", [128, NCH, 512], BF16)
                    xn2 = SB(c, "xn2", [128, 4, DM], BF16)
                    junk = SB(c, "junkC", [128, DM], BF16)
                    tmp = SB(c, "tmpC", [128, DM], F32)
                    gcb = [SB(c, "gc%d" % i, [128, 512], F32) for i in range(2)]
                    geb = [SB(c, "ge%d" % i, [128, 512], F32) for i in range(2)]
                    ssv = SB(c, "ssC", [128, 16], F32)
                    lnc = SB(c, "lnC", [128, 16], F32)
                    rsc = SB(c, "rsC", [128, 16], F32)
                    psYs = [PS(c, "psY%d" % i, [128, 1024]) for i in range(2)]
                    psT1 = PS(c, "psTC", [128, 512], BF16)
                    psTs = [psT1[:], psT1[:]]
                    psHt = PS(c, "psH", [128, 16])
                    psH = psHt[:, 0:2]
                    psHT = psHt[:, 8:16]
                    bPHT = Buf()
                    hT8 = SB(c, "hT8", [128, 8], F32)
                    bHT8 = Buf()
                    psA1 = PS(c, "psA", [128, 512])
                    psG1 = PS(c, "psG", [128, 512])
                    g_sb = [SB(c, "g_sb%d" % i, [128, 512], F32) for i in range(2)]
                    bGsb = [Buf(), Buf()]
                    bPG1 = Buf()
                    a_sb = [SB(c, "a_sb%d" % i, [128, 512], BF16) for i in range(2)]
                    hsb = SB(c, "hsb", [128, 2], F32)
                    bHsb = Buf()
                    bAsb = [Buf(), Buf()]
                    ycnt = [0]
                    bWo, bWd, bCw, bGp = Buf(), Buf(), Buf(), Buf()
                    bWu = [Buf() for _ in range(3)]
                    bScF = [Buf(), Buf()]
                    bGate, bGt = Buf(), Buf()
                    bXg = [Buf(), Buf()]
                    bOtg = [Buf(), Buf()]
                    bH2 = [Buf(), Buf()]
                    bEdge = [Buf(), Buf(), Buf()]
                    bHalo, bAct, bXn2, bJ, bTmp = Buf(), Buf(), Buf(), Buf(), Buf()
                    bActLo = Buf()
                    bGc = [Buf(), Buf()]
                    bGe = [Buf(), Buf()]
                    bSs, bLn, bRs = Buf(), Buf(), Buf()
                    bPYs = [Buf(), Buf()]
                    bPT1 = Buf()
                    bPTs = [bPT1, bPT1]
                    bPH = Buf()
                    bPA1 = Buf()
                    bPG = [Buf(), Buf()]

                    P.dmas("pool", [(w_out[:, k, :], D["w_out"][l, :, k, :]) for k in range(8)], writes=[bWo])
                    P.dmas("pool", [(w_dn[:, c0:c0 + 2, :], D["w_down"][l, :, c0:c0 + 2, :]) for c0 in range(0, NCH, 2)], writes=[bWd])
                    P.dmas("sp", [(cw[:], D["convwT"][l]), (cb[:], D["convbT"][l])], writes=[bCw])
                    P.dma("sp", gpre[:], D["gpre_ffnT"][l], writes=[bGp])
                    for s in range(2):
                        load_vec8(c, biF[s][:], l, s, 3, bScF[s])
                        load_vec8(c, scF[s][:], l, s, 4, bScF[s])
                        P.op("dve", lambda e, s=s: e.scalar_tensor_tensor(out=scF[s][:], in0=scF[s][:], scalar=1.0, in1=gpre[:], op0=ALU.add, op1=ALU.mult),
                             reads=[bScF[s], bGp], writes=[bScF[s]])

                    def load_gates(s):
                        for gt, jm, gp in ((gate1, 2, "gpost_mix"), (gate2, 5, "gpost_ffn")):
                            P.dma("sp", gt[:], D["modD"][l, s:s + 1, jm * 1024:(jm + 1) * 1024].partition_broadcast(128), writes=[bGate])
                            P.dma("sp", tmp[:], D[gp][l:l + 1, :].partition_broadcast(128), writes=[bTmp])
                            P.op("dve", lambda e, gt=gt: e.tensor_tensor(out=gt[:], in0=gt[:], in1=tmp[:], op=ALU.mult), reads=[bGate, bTmp], writes=[bGate])

                    def loadOT(g):
                        s, t0 = g // 8, (g % 8) * 512
                        j = g % 2
                        P.dma("sp", otg[j][:], D["OT"][s, :, t0:t0 + 512].rearrange("(k p) t -> p k t", p=128), writes=[bOtg[j]])

                    def loadX(g):
                        s, t0 = g // 8, (g % 8) * 512
                        j = g % 2
                        P.dmas("sp", [(xg[j][:, t, :], x_src[s, t0 + t * 128:t0 + (t + 1) * 128, :]) for t in range(4)], writes=[bXg[j]])

                    def norm_resid(xap, gate, col, bX, psY, bPY):
                        P.op("act", lambda e: e.activation(out=junk[:], in_=psY[:], func=AF.Square, accum_out=ssv[:, col:col + 1]),
                             reads=[bPY], writes=[bJ, bSs])
                        P.op("act", lambda e: e.activation(out=lnc[:, col:col + 1], in_=ssv[:, col:col + 1], func=AF.Ln, scale=1.0 / DM, bias=EPS),
                             reads=[bSs], writes=[bLn])
                        P.op("act", lambda e: e.activation(out=rsc[:, col:col + 1], in_=lnc[:, col:col + 1], func=AF.Exp, scale=-0.5),
                             reads=[bLn], writes=[bRs])
                        P.op("dve", lambda e: e.scalar_tensor_tensor(out=tmp[:], in0=psY[:], scalar=rsc[:, col:col + 1], in1=gate[:], op0=ALU.mult, op1=ALU.mult),
                             reads=[bPY, bRs, bGate], writes=[bTmp])
                        P.op("pool", lambda e: e.tensor_tensor(out=xap, in0=xap, in1=tmp[:], op=ALU.add), reads=[bTmp, bX], writes=[bX])

                    def stage1a(g):
                        s, t0 = g // 8, (g % 8) * 512
                        j = g % 2
                        X = xg[j]

                        def second(t):
                            P.op("act", lambda e: e.activation(out=junk[:], in_=X[:, t, :], func=AF.Square, accum_out=ssv[:, 4 + t:5 + t]),
                                 reads=[bXg[j]], writes=[bJ, bSs])
                            P.op("act", lambda e: e.activation(out=lnc[:, 4 + t:5 + t], in_=ssv[:, 4 + t:5 + t], func=AF.Ln, scale=1.0 / DM, bias=EPS),
                                 reads=[bSs], writes=[bLn])
                            P.op("act", lambda e: e.activation(out=rsc[:, 4 + t:5 + t], in_=lnc[:, 4 + t:5 + t], func=AF.Exp, scale=-0.5), reads=[bLn], writes=[bRs])
                            P.op("dve", lambda e: e.tensor_scalar(out=xn2[:, t, :], in0=X[:, t, :], scalar1=rsc[:, 4 + t:5 + t], scalar2=None, op0=ALU.mult),
                                 reads=[bXg[j], bRs], writes=[bXn2])

                        for t in range(4):
                            yi = ycnt[0] % 2
                            ycnt[0] += 1
                            psY, bPY = psYs[yi], bPYs[yi]
                            fns = []
                            for hf in range(2):
                                for k in range(8):
                                    fns.append(lambda e, k=k, hf=hf, t=t, psY=psY: e.matmul(psY[:, hf * 512:(hf + 1) * 512], lhsT=otg[j][:, k, t * 128:(t + 1) * 128],
                                                                                   rhs=w_out[:, k, hf * 512:(hf + 1) * 512], start=(k == 0), stop=(k == 7)))
                            P.op("pe", fns, reads=[bOtg[j], bWo], writes=[bPY])
                            if t == 2:
                                P.op("pe", [(lambda e, k=k: e.matmul(psHT[:, k:k + 1], lhsT=xn2[0:1, 0, k * 128:(k + 1) * 128], rhs=ident[0:1, 0:1],
                                                                     start=True, stop=True)) for k in range(8)], reads=[bXn2, bId], writes=[bPHT])
                                P.op("act", lambda e: e.copy(out=hT8[:], in_=psHT), reads=[bPHT], writes=[bHT8])
                                P.op("dve", lambda e: e.tensor_tensor(out=hT8[:], in0=hT8[:], in1=scF[s][:], op=ALU.mult), reads=[bHT8, bScF[s]], writes=[bHT8])
                                ej3 = g % 3
                                P.op("dve", lambda e: e.tensor_tensor(out=edge[ej3][:, :, 0], in0=hT8[:], in1=biF[s][:], op=ALU.add),
                                     reads=[bHT8, bScF[s]], writes=[bEdge[ej3]])
                            norm_resid(X[:, t, :], gate1, t, bXg[j], psY, bPY)
                            if t >= 1:
                                second(t - 1)
                        second(3)
                        bXD[g] = Buf()
                        P.dma("sp", x_dst[s, t0:t0 + 512, :].rearrange("(t p) d -> p t d", p=128), X[:], reads=[bXg[j]], writes=[bXD[g]])

                    def tr(g, k):
                        s = g // 8
                        j = g % 2
                        psT, bPT = psTs[k % 2], bPTs[k % 2]
                        P.op("pe", [(lambda e, t=t: e.transpose(psT[:, t * 128:(t + 1) * 128], xn2[:, t, k * 128:(k + 1) * 128], ident[:]))
                                    for t in range(4)], reads=[bXn2, bId], writes=[bPT])
                        P.op("act", lambda e: e.activation(out=h2T[j][:, k, :], in_=psT[:], func=AF.Identity,
                                                           scale=scF[s][:, k:k + 1], bias=biF[s][:, k:k + 1]),
                             reads=[bPT, bScF[s]], writes=[bH2[j]])
                        if k == 7:
                            ej3 = g % 3
                            P.op("pool", lambda e: e.tensor_copy(out=edge[ej3][:, :, 1:2], in_=h2T[j][:, :, 511:512]), reads=[bH2[j]], writes=[bEdge[ej3]])

                    wcount = [0]

                    def load_wu(ci):
                        wi = wcount[0] % 3
                        wcount[0] += 1
                        P.dma("pool", wu[wi][:], D["w_up"][l, ci], writes=[bWu[wi]])
                        return wi

                    wu_pref = []

                    def prefetch_wu():
                        wu_pref.extend([load_wu(0), load_wu(1)])

                    def stage2_up(g, have_next):
                        s, t0 = g // 8, (g % 8) * 512
                        j = g % 2
                        X = xg[j]
                        if t0 > 0:
                            P.op("pool", lambda e: e.tensor_copy(out=halo[:, :, 0:1], in_=edge[(g - 1) % 3][:, :, 1:2]), reads=[bEdge[(g - 1) % 3]], writes=[bHalo])
                        else:
                            P.op("pool", lambda e: e.memset(halo[:, :, 0:1], 0.0), writes=[bHalo])
                        if t0 + 512 < S:
                            assert have_next
                            P.op("pool", lambda e: e.tensor_copy(out=halo[:, :, 1:2], in_=edge[(g + 1) % 3][:, :, 0:1]), reads=[bEdge[(g + 1) % 3]], writes=[bHalo])
                        else:
                            P.op("pool", lambda e: e.memset(halo[:, :, 1:2], 0.0), writes=[bHalo])
                        def finish_chunk(ci):
                            pj = ci % 2
                            gc, ge = gcb[pj], geb[pj]
                            P.op("act", lambda e: e.activation(out=ge[:], in_=gc[:], func=AF.Gelu_apprx_tanh), reads=[bGc[pj]], writes=[bGe[pj]])
                            P.op("pool", lambda e: e.tensor_tensor(out=actT[:, ci, :], in0=a_sb[pj][:], in1=ge[:], op=ALU.mult),
                                 reads=[bAsb[pj], bGe[pj]], writes=[bActLo if ci < NCH - 2 else bAct])

                        pend = list(wu_pref)
                        del wu_pref[:]
                        for ci in range(NCH):
                            wi = pend.pop(0)
                            if ci + 2 < NCH:
                                pend.append(load_wu(ci + 2))
                            pj = ci % 2
                            W = wu[wi]
                            P.op("pe", [(lambda e, k=k, W=W, pj=pj: e.matmul(psA1[:], lhsT=W[:, k, 0:128], rhs=h2T[j][:, k, :], start=(k == 0), stop=(k == 7)))
                                        for k in range(8)], reads=[bWu[wi], bH2[j]], writes=[bPA1])
                            P.op("act", lambda e, pj=pj: e.copy(out=a_sb[pj][:], in_=psA1[:]), reads=[bPA1], writes=[bAsb[pj]])
                            P.op("pe", [(lambda e, k=k, W=W: e.matmul(psG1[:], lhsT=W[:, k, 128:256], rhs=h2T[j][:, k, :], start=(k == 0), stop=(k == 7)))
                                        for k in range(8)], reads=[bWu[wi], bH2[j]], writes=[bPG1])
                            P.op("pe", [(lambda e, k=k, W=W: e.matmul(psH, lhsT=W[:, k, 128:256], rhs=halo[:, k, :], start=(k == 0), stop=(k == 7)))
                                        for k in range(8)], reads=[bWu[wi], bHalo], writes=[bPH])
                            gc, ge, gs = gcb[pj], geb[pj], g_sb[pj]
                            P.op("act", lambda e, gs=gs: e.copy(out=gs[:], in_=psG1[:]), reads=[bPG1], writes=[bGsb[pj]])
                            P.op("dve", lambda e, ci=ci, gc=gc, gs=gs: e.tensor_scalar(out=gc[:], in0=gs[:], scalar1=cw[:, ci, 1:2], scalar2=cb[:, ci:ci + 1],
                                                                                       op0=ALU.mult, op1=ALU.add),
                                 reads=[bGsb[pj], bCw], writes=[bGc[pj]])
                            P.op("dve", lambda e, ci=ci, gc=gc, gs=gs: e.scalar_tensor_tensor(out=gc[:, 1:512], in0=gs[:, 0:511], scalar=cw[:, ci, 0:1],
                                                                                              in1=gc[:, 1:512], op0=ALU.mult, op1=ALU.add),
                                 reads=[bGsb[pj], bCw, bGc[pj]], writes=[bGc[pj]])
                            P.op("dve", lambda e, ci=ci, gc=gc, gs=gs: e.scalar_tensor_tensor(out=gc[:, 0:511], in0=gs[:, 1:512], scalar=cw[:, ci, 2:3],
                                                                                              in1=gc[:, 0:511], op0=ALU.mult, op1=ALU.add),
                                 reads=[bGsb[pj], bCw, bGc[pj]], writes=[bGc[pj]])
                            P.op("act", lambda e: e.copy(out=hsb[:], in_=psH), reads=[bPH], writes=[bHsb])
                            P.op("dve", lambda e, ci=ci, gc=gc: e.scalar_tensor_tensor(out=gc[:, 0:1], in0=hsb[:, 0:1], scalar=cw[:, ci, 0:1], in1=gc[:, 0:1],
                                                                                       op0=ALU.mult, op1=ALU.add), reads=[bHsb, bCw, bGc[pj]], writes=[bGc[pj]])
                            P.op("dve", lambda e, ci=ci, gc=gc: e.scalar_tensor_tensor(out=gc[:, 511:512], in0=hsb[:, 1:2], scalar=cw[:, ci, 2:3], in1=gc[:, 511:512],
                                                                                       op0=ALU.mult, op1=ALU.add), reads=[bHsb, bCw, bGc[pj]], writes=[bGc[pj]])
                            if ci >= 1:
                                finish_chunk(ci - 1)
                        finish_chunk(NCH - 1)
                        if g + 1 < 16:
                            prefetch_wu()
                    def stage2_down(g, trg):
                        s, t0 = g // 8, (g % 8) * 512

                        def load_xm(t):
                            P.dma("sp", xm[t % 2][:], x_dst[s, t0 + t * 128:t0 + (t + 1) * 128, :], reads=[bXD[g]], writes=[bXm[t % 2]])

                        load_xm(0)
                        load_xm(1)
                        for t in range(4):
                            if trg is not None:
                                tr(trg, 2 * t)
                            yi = ycnt[0] % 2
                            ycnt[0] += 1
                            psY, bPY = psYs[yi], bPYs[yi]
                            NLO = NCH - 2
                            fns = []
                            for hf in range(2):
                                for ci in range(NLO):
                                    fns.append(lambda e, ci=ci, hf=hf, t=t, psY=psY: e.matmul(psY[:, hf * 512:(hf + 1) * 512], lhsT=actT[:, ci, t * 128:(t + 1) * 128],
                                                                                     rhs=w_dn[:, ci, hf * 512:(hf + 1) * 512], start=(ci == 0), stop=False))
                            P.op("pe", fns, reads=[bActLo, bWd], writes=[bPY])
                            if trg is not None:
                                tr(trg, 2 * t + 1)
                            fns = []
                            for hf in range(2):
                                for ci in range(NLO, NCH):
                                    fns.append(lambda e, ci=ci, hf=hf, t=t, psY=psY: e.matmul(psY[:, hf * 512:(hf + 1) * 512], lhsT=actT[:, ci, t * 128:(t + 1) * 128],
                                                                                     rhs=w_dn[:, ci, hf * 512:(hf + 1) * 512], start=False, stop=(ci == NCH - 1)))
                            P.op("pe", fns, reads=[bAct, bWd], writes=[bPY])
                            norm_resid(xm[t % 2][:], gate2, 8 + t, bXm[t % 2], psY, bPY)
                            P.dma("sp", x_dst[s, t0 + t * 128:t0 + (t + 1) * 128, :], xm[t % 2][:], reads=[bXm[t % 2]], writes=[bXD[g]])
                            if t + 2 < 4:
                                load_xm(t + 2)

                    NG = 16
                    load_gates(0)
                    loadOT(0)
                    loadX(0)
                    loadOT(1)
                    loadX(1)
                    stage1a(0)
                    for k in range(8):
                        tr(0, k)
                    prefetch_wu()
                    for g in range(NG):
                        s = g // 8
                        nxt_same_seq = (g + 1 < NG) and ((g + 1) // 8 == s)
                        if nxt_same_seq:
                            stage1a(g + 1)
                            if g + 2 < NG:
                                loadOT(g + 2)
                                loadX(g + 2)
                        stage2_up(g, nxt_same_seq)
                        stage2_down(g, (g + 1) if nxt_same_seq else None)
                        if g + 1 < NG and not nxt_same_seq:
                            load_gates((g + 1) // 8)
                            stage1a(g + 1)
                            for k in range(8):
                                tr(g + 1, k)
                            if g + 2 < NG:
                                loadOT(g + 2)
                                loadX(g + 2)
                P.barrier()

        except _Stop:
            pass
        P.barrier()
    return nc


def _prep_shared(inp):
    f = lambda a: np.ascontiguousarray(np.asarray(a, dtype=np.float32))
    L = 4
    sh = {}
    sh["w_ada"] = f(np.asarray(inp["w_ada"]).reshape(L, 8, 128, 6144).transpose(0, 2, 1, 3))
    sh["b_ada"] = f(inp["b_ada"])
    sh["gpre_mixT"] = f(np.asarray(inp["g_pre_mix"]).reshape(L, 8, 128).transpose(0, 2, 1))
    sh["gpre_ffnT"] = f(np.asarray(inp["g_pre_ffn"]).reshape(L, 8, 128).transpose(0, 2, 1))
    sh["gpost_mix"] = f(inp["g_post_mix"])
    sh["gpost_ffn"] = f(inp["g_post_ffn"])
    sh["w_in"] = f(np.asarray(inp["w_in"]).reshape(L, 8, 128, D_IN).transpose(0, 2, 1, 3))
    wuq = np.asarray(inp["w_uq"]).reshape(L, 384, 4, 96)
    wuq = np.concatenate([wuq[..., :64].reshape(L, 384, 256), wuq[..., 64:].reshape(L, 384, 128)], axis=-1)
    sh["w_uq"] = f(wuq.reshape(L, 3, 128, 384).transpose(0, 2, 1, 3))
    wukv = np.asarray(inp["w_ukv"]).reshape(L, 256, 4, 128)
    wukv = np.concatenate([wukv[..., :64].reshape(L, 256, 256), wukv[..., 64:].reshape(L, 256, 256)], axis=-1)
    sh["w_ukv"] = f(wukv.reshape(L, 2, 128, 512).transpose(0, 2, 1, 3))
    sh["gqT"] = f(np.asarray(inp["mla_g_q"]).reshape(L, 3, 128).transpose(0, 2, 1))
    sh["gkvT"] = f(np.asarray(inp["mla_g_kv"]).reshape(L, 2, 128).transpose(0, 2, 1))
    sh["w_out"] = f(np.asarray(inp["w_out"]).reshape(L, 8, 128, DM).transpose(0, 2, 1, 3))
    wup = np.asarray(inp["w_up"]).reshape(L, 8, 128, 2, NCH, 128)
    sh["w_up"] = f(wup.transpose(0, 4, 2, 1, 3, 5).reshape(L, NCH, 128, 8, 256))
    sh["w_down"] = f(np.asarray(inp["w_down"]).reshape(L, NCH, 128, DM).transpose(0, 2, 1, 3))
    sh["convwT"] = f(np.asarray(inp["conv_w"]).reshape(L, 3, NCH, 128).transpose(0, 3, 2, 1))
    sh["convbT"] = f(np.asarray(inp["conv_b"]).reshape(L, NCH, 128).transpose(0, 2, 1))
    G, cm, T, tm = _host_tables(np.asarray(inp["na_rpb"], np.float32), np.asarray(inp["t5_table"], np.float32))
    sh["naG"] = f(G)
    sh["nacm"] = f(cm)
    sh["t5G"] = f(T)
    sh["t5m"] = f(tm)
    sh["pens"] = f(NA_PENS)
    A2 = np.zeros((2, 128), np.float32)
    A2[0, :64] = 1.0
    A2[1, 64:] = 1.0
    sh["A2"] = A2
    cosT, sinT = _rope_tables()
    sh["cosT"] = f(cosT)
    sh["sinT"] = f(sinT)
    return sh


def _in_maps(inp, ncores):
    sh = _prep_shared(inp)
    x = np.asarray(inp["x"], np.float32)
    cc = np.asarray(inp["c"], np.float32)
    maps = []
    for i in range(ncores):
        m = dict(sh)
        m["x"] = np.ascontiguousarray(x[2 * i:2 * i + 2])
        m["cT"] = np.ascontiguousarray(cc[2 * i:2 * i + 2].reshape(2, 8, 128).transpose(2, 1, 0))
        maps.append(m)
    return maps


_NC_CACHE = {}


def kernel(**inputs):
    if "nc" not in _NC_CACHE:
        _NC_CACHE["nc"] = build()
    nc = _NC_CACHE["nc"]
    maps = _in_maps(inputs, NCORES)
    res = run_bass_kernel_spmd(nc, maps, core_ids=list(range(NCORES)))
    out = np.concatenate([np.asarray(r["out"], dtype=np.float32) for r in res.results], axis=0)
    return out
```

```python
import math
from contextlib import ExitStack

import numpy as np
import ml_dtypes
import concourse.bass as bass
import concourse.mybir as mybir
from concourse.bass_utils import run_bass_kernel_spmd

F32 = mybir.dt.float32
BF16 = mybir.dt.bfloat16
AF = mybir.ActivationFunctionType
ALU = mybir.AluOpType

NCORES = 8
S = 4096
DM = 1024
DFF = 2816
NCH = 22
D_IN = 2976
EPS = 1e-6
NEG = -30000.0


class Buf:
    __slots__ = ("lw", "rd")

    def __init__(self):
        self.lw = []
        self.rd = {}


class Prog:
    def __init__(self, nc, ctx, n_dma_sems=40):
        self.nc = nc
        self.E = {"pe": nc.tensor, "act": nc.scalar, "dve": nc.vector, "pool": nc.gpsimd, "sp": nc.sync}
        self.psem = {k: ctx.enter_context(nc.semaphore("ps_" + k)) for k in self.E}
        self.pcnt = {k: 0 for k in self.E}
        self.seen = {k: {} for k in self.E}
        self.dsem = [ctx.enter_context(nc.semaphore("ds%d" % i)) for i in range(n_dma_sems)]
        self.dcnt = [0] * n_dma_sems
        self.dnext = 0
        self.nins = 0
        self.dead = False

    def _wait(self, eng, deps):
        if self.dead:
            return
        need = {}
        for d in deps:
            if d is None:
                continue
            s, v = d
            if need.get(s, 0) < v:
                need[s] = v
        seen = self.seen[eng]
        for s, v in need.items():
            if seen.get(s, 0) < v:
                self.E[eng].wait_ge(s, v)
                seen[s] = v
                self.nins += 1

    @staticmethod
    def _deps(reads, writes):
        deps = []
        for b in reads:
            deps.extend(b.lw)
        for b in writes:
            deps.extend(b.lw)
            deps.extend(b.rd.values())
        return deps

    def op(self, eng, fns, reads=(), writes=()):
        if self.dead:
            return None
        self._wait(eng, self._deps(reads, writes))
        if callable(fns):
            fns = [fns]
        ins = None
        e = self.E[eng]
        for f in fns:
            ins = f(e)
            self.nins += 1
        self.pcnt[eng] += 1
        ins.then_inc(self.psem[eng], 1)
        tok = (self.psem[eng], self.pcnt[eng])
        for b in reads:
            b.rd[eng] = tok
        for b in writes:
            b.lw = [tok]
            b.rd = {}
        return tok

    def dma(self, eng, out, in_, reads=(), writes=(), **kw):
        return self.dmas(eng, [(out, in_)], reads, writes, **kw)

    def dmas(self, eng, pairs, reads=(), writes=(), **kw):
        if self.dead:
            return []
        deps = self._deps(reads, writes)
        idx = []
        for _ in pairs:
            i = self.dnext
            self.dnext = (i + 1) % len(self.dsem)
            idx.append(i)
            if self.dcnt[i] > 0:
                deps.append((self.dsem[i], 16 * self.dcnt[i]))
        assert len(set(idx)) == len(idx)
        self._wait(eng, deps)
        toks = []
        for i, (out, in_) in zip(idx, pairs):
            self.dcnt[i] += 1
            self.E[eng].dma_start(out=out, in_=in_, **kw).then_inc(self.dsem[i], 16)
            self.nins += 1
            toks.append((self.dsem[i], 16 * self.dcnt[i]))
        for b in reads:
            for i, tok in zip(idx, toks):
                b.rd[("dma", i)] = tok
        for b in writes:
            b.lw = list(toks)
            b.rd = {}
        return toks

    def barrier(self):
        deps = [(self.psem[k], self.pcnt[k]) for k in self.E if self.pcnt[k] > 0]
        deps += [(s, 16 * c) for s, c in zip(self.dsem, self.dcnt) if c > 0]
        for k in self.E:
            self._wait(k, deps)


def _r_start(r):
    return min(max(r - 4, 0), 56)


def _c_start(c):
    return min(max(c - 8, 0), 48)


def _na_plan():
    pats = {}
    plan = []
    for J in range(16):
        klo = _r_start(4 * J)
        khi = _r_start(4 * J + 3) + 7
        units = []
        for i in range(klo // 2, khi // 2 + 1):
            key = []
            for a in range(2):
                for ap in range(4):
                    r = 4 * J + ap
                    kr = 2 * i + a
                    key.append(_r_start(r) <= kr <= _r_start(r) + 7)
            key = tuple(key)
            if all(key):
                pid = -1
            else:
                if key not in pats:
                    pats[key] = len(pats)
                pid = pats[key]
            di0 = 4 * J - 2 * i + 6
            assert 0 <= di0 <= 10
            units.append((i, di0, pid))
        plan.append(units)
    npat = len(pats)
    pens = np.zeros((2, npat, 4, 64), np.float32)
    for key, pid in pats.items():
        for a in range(2):
            for ap in range(4):
                if not key[a * 4 + ap]:
                    pens[a, pid, ap, :] = NEG
    return plan, pens.reshape(2, npat * 256)


NA_PLAN, NA_PENS = _na_plan()
NPAT = NA_PENS.shape[1] // 256


def _t5_bucket(rel):
    nb = 16
    max_exact = 8
    n = np.abs(rel)
    large = max_exact + (np.log(np.maximum(n, 1) / max_exact) / math.log(1024 / max_exact) * (nb - max_exact)).astype(np.int64)
    large = np.minimum(large, nb - 1)
    return (np.where(rel > 0, nb, 0) + np.where(n < max_exact, n, large)).astype(np.int32)


def _host_tables(na_rpb, t5_table):
    p = np.arange(128)
    a = p // 64
    kc = p % 64
    di = np.arange(14)
    c = np.arange(64)
    drow = a[:, None] - (di[None, :] - 6) + 7
    dcol = kc[:, None] - c[None, :] + 15
    ok = ((drow >= 0) & (drow <= 14))[:, :, None] & ((dcol >= 0) & (dcol <= 30))[:, None, :]
    drc = np.clip(drow, 0, 14)
    dcc = np.clip(dcol, 0, 30)
    G = na_rpb[:, :, drc[:, :, None], dcc[:, None, :]]
    G = np.where(ok[None, None], G, np.float32(0.0)).astype(np.float32)
    G = np.ascontiguousarray(np.transpose(G, (0, 2, 1, 3, 4))).reshape(na_rpb.shape[0], 128, 4 * 14 * 64)
    cs = np.array([_c_start(x) for x in range(64)])
    cm = ((kc[:, None] >= cs[None, :]) & (kc[:, None] < cs[None, :] + 16)).astype(np.float32)
    cm = np.ascontiguousarray(np.broadcast_to(cm[:, None, :], (128, 14, 64))).reshape(128, 14 * 64)
    q = np.arange(128)
    pc = np.arange(3)
    rel = 128 * (1 - pc[None, :, None]) + p[:, None, None] - q[None, None, :]
    valid = (np.abs(rel) <= 64)
    T = np.zeros((128, 8, 3, 3, 128), np.float32)
    for pi, d in enumerate((1, 4, 16)):
        b = _t5_bucket(rel * d)
        vals = t5_table[b]
        vals = np.where(valid[..., None], vals, np.float32(0.0))
        T[:, :, pi] = np.transpose(vals, (0, 3, 1, 2))
    T = T.reshape(128, 8 * 3 * 384)
    tm = valid.astype(np.float32).reshape(128, 384)
    return G, cm, T, tm


def _rope_tables():
    inv_freq = (10000.0 ** (-np.arange(0, 32, 2, dtype=np.float32) / 32)).astype(np.float32)
    ang = np.arange(S, dtype=np.float32)[:, None] * inv_freq[None, :]
    cos = np.cos(ang).astype(np.float32)
    sin = np.sin(ang).astype(np.float32)
    idx = (np.arange(128) % 32) % 16
    return np.ascontiguousarray(cos[:, idx].T), np.ascontiguousarray(sin[:, idx].T)


class _Stop(Exception):
    pass


def build(NL=4, dbg=()):
    nc = bass.Bass("TRN2", target_bir_lowering=False)
    stop_at = -1
    for dflag in dbg:
        if dflag.startswith("stage="):
            stop_at = int(dflag[6:])

    PP = []

    def ckpt(n):
        if n == stop_at and not PP[0].dead:
            PP[0].barrier()
            PP[0].dead = True
    D = {}

    def din(name, shape, dt=F32):
        D[name] = nc.dram_tensor(name, list(shape), dt, kind="ExternalInput").ap()

    def dscr(name, shape, dt):
        kind = "ExternalOutput" if name in dbg else "Internal"
        D[name] = nc.dram_tensor(name, list(shape), dt, kind=kind).ap()

    din("x", [2, S, DM])
    din("cT", [128, 8, 2])
    din("w_ada", [4, 128, 8, 6144])
    din("b_ada", [4, 6144])
    din("gpre_mixT", [4, 128, 8])
    din("gpre_ffnT", [4, 128, 8])
    din("gpost_mix", [4, DM])
    din("gpost_ffn", [4, DM])
    din("w_in", [4, 128, 8, D_IN])
    din("w_uq", [4, 128, 3, 384])
    din("w_ukv", [4, 128, 2, 512])
    din("gqT", [4, 128, 3])
    din("gkvT", [4, 128, 2])
    din("w_out", [4, 128, 8, DM])
    din("w_up", [4, NCH, 128, 8, 256])
    din("w_down", [4, 128, NCH, DM])
    din("convwT", [4, 128, NCH, 3])
    din("convbT", [4, 128, NCH])
    din("naG", [4, 128, 4 * 14 * 64])
    din("nacm", [128, 14 * 64])
    din("pens", [2, NPAT * 256])
    din("navm", [128, NPAT * 256])
    din("A2", [2, 128])
    din("t5G", [128, 8 * 3 * 384])
    din("t5m", [128, 384])
    din("cosT", [128, S])
    din("sinT", [128, S])
    D["out"] = nc.dram_tensor("out", [2, S, DM], F32, kind="ExternalOutput").ap()
    dscr("modD", [4, 2, 6144], F32)
    dscr("xs", [2, S, DM], F32)
    dscr("zT", [2, 12, 128, S], BF16)
    dscr("vtok", [2, S, 768], BF16)
    dscr("qn", [2, 2, 128, S], BF16)
    dscr("qr", [2, 128, S], BF16)
    dscr("kn", [2, 2, 128, S], BF16)
    dscr("kr", [2, 32, S], BF16)
    dscr("vc", [2, S, 256], BF16)
    dscr("OT", [2, DM, S], BF16)

    with ExitStack() as gctx:
        P = Prog(nc, gctx)
        PP.append(P)

        uid = [0]

        def SB(ctx, name, shape, dt):
            uid[0] += 1
            return ctx.enter_context(nc.sbuf_tensor("%s_u%d" % (name, uid[0]), list(shape), dt))

        def PS(ctx, name, shape, dt=F32):
            uid[0] += 1
            return ctx.enter_context(nc.psum_tensor("%s_u%d" % (name, uid[0]), list(shape), dt))

        zcol = SB(gctx, "zcol", [128, 1], F32)
        bZc = Buf()
        ident = SB(gctx, "ident", [128, 128], BF16)
        identf = SB(gctx, "identf", [128, 128], F32)
        bId = Buf()
        P.op("pool", lambda e: e.memset(zcol[:], 0.0), writes=[bZc])
        P.op("pool", lambda e: e.memset(identf[:], 1.0), writes=[bId])
        P.op("pool", lambda e: e.affine_select(out=identf[:], in_=identf[:], pattern=[[-1, 128]],
                                                compare_op=ALU.is_equal, fill=0.0, base=0, channel_multiplier=1),
             reads=[bId], writes=[bId])
        P.op("pool", lambda e: e.tensor_copy(out=ident[:], in_=identf[:]), reads=[bId], writes=[bId])

        def rstd_from_ss(ss_ap, out_ap, n, bss, bout, tmp_ap, btmp):
            P.op("act", lambda e: e.activation(out=tmp_ap, in_=ss_ap, func=AF.Ln, scale=1.0 / n, bias=EPS),
                 reads=[bss], writes=[btmp])
            P.op("act", lambda e: e.activation(out=out_ap, in_=tmp_ap, func=AF.Exp, scale=-0.5),
                 reads=[btmp], writes=[bout])

        try:
            with ExitStack() as c:
                cT = SB(c, "cT_sb", [128, 8, 2], F32)
                cact = SB(c, "cact", [128, 8, 2], BF16)
                brow = SB(c, "brow", [2, 6144], F32)
                mrow = SB(c, "mrow", [2, 6144], F32)
                wt = [SB(c, "wada%d" % i, [128, 8, 1024], BF16) for i in range(3)]
                psM = [PS(c, "psM%d" % i, [2, 512]) for i in range(2)]
                bc, bb, bm = Buf(), Buf(), Buf()
                bw = [Buf(), Buf(), Buf()]
                bp = [Buf(), Buf()]
                P.dma("sp", cT[:], D["cT"], writes=[bc])
                P.op("act", lambda e: e.activation(out=cact[:], in_=cT[:], func=AF.Silu), reads=[bc], writes=[bc])
                it = 0
                ip = 0
                for l in range(NL):
                    P.dma("sp", brow[:], D["b_ada"][l:l + 1, :].partition_broadcast(2), writes=[bb])
                    for nb in range(6):
                        j = it % 3
                        it += 1
                        P.dmas("pool", [(wt[j][:, k0:k0 + 4, :], D["w_ada"][l, :, k0:k0 + 4, nb * 1024:(nb + 1) * 1024]) for k0 in (0, 4)], writes=[bw[j]])
                        for hf in range(2):
                            pj = ip % 2
                            ip += 1
                            cs = slice(nb * 1024 + hf * 512, nb * 1024 + (hf + 1) * 512)
                            P.op("pe", [(lambda e, k=k, j=j, pj=pj, hf=hf: e.matmul(psM[pj][:], lhsT=cact[:, k, :], rhs=wt[j][:, k, hf * 512:(hf + 1) * 512],
                                                                                    start=(k == 0), stop=(k == 7))) for k in range(8)],
                                 reads=[bc, bw[j]], writes=[bp[pj]])
                            P.op("dve", lambda e, pj=pj, cs=cs: e.tensor_tensor(out=mrow[:, cs], in0=psM[pj][:], in1=brow[:, cs], op=ALU.add),
                                 reads=[bp[pj], bb], writes=[bm])
                    P.dma("sp", D["modD"][l], mrow[:], reads=[bm])
            P.barrier()
            ckpt(1)

            def load_vec8(ctx_name, dst, l, s, j, bdst):
                src = D["modD"][l, s, j * 1024:(j + 1) * 1024].rearrange("(k p) -> p k", p=128)
                P.dma("sp", dst, src, writes=[bdst], allow_slow_non_contiguous=True)

            for l in range(NL):
                x_src = D["x"] if l == 0 else D["xs"]
                x_dst = D["out"] if l == NL - 1 else D["xs"]

                with ExitStack() as c:
                    w_in = SB(c, "w_in", [128, 8, D_IN], BF16)
                    w_rot = SB(c, "w_rot", [128, 8, 32], BF16)
                    w_uq = SB(c, "w_uq", [128, 3, 384], BF16)
                    w_uqr = SB(c, "w_uqr", [128, 3, 128], BF16)
                    w_ukv = SB(c, "w_ukv", [128, 2, 512], BF16)
                    gq = SB(c, "gq", [128, 3], F32)
                    gkv = SB(c, "gkv", [128, 2], F32)
                    gpre = SB(c, "gpre", [128, 8], F32)
                    scA = [SB(c, "scA%d" % s, [128, 8], F32) for s in range(2)]
                    biA = [SB(c, "biA%d" % s, [128, 8], F32) for s in range(2)]
                    xg = [SB(c, "xgA%d" % i, [128, 4, DM], F32) for i in range(2)]
                    cosg = [SB(c, "cosg%d" % i, [128, 512], F32) for i in range(2)]
                    sing = [SB(c, "sing%d" % i, [128, 512], F32) for i in range(2)]
                    junk = SB(c, "junkA", [128, DM], BF16)
                    ss = SB(c, "ssA", [128, 8], F32)
                    lnv = SB(c, "lnvA", [128, 8], F32)
                    rstd = SB(c, "rstdA", [128, 8], F32)
                    ssq = SB(c, "ssq", [128, 4], F32)
                    sskv = SB(c, "sskv", [128, 4], F32)
                    lnq = SB(c, "lnq", [128, 4], F32)
                    lnk = SB(c, "lnk", [128, 4], F32)
                    rsq = SB(c, "rsq", [128, 4], F32)
                    rskv = SB(c, "rskv", [128, 4], F32)
                    xns = [SB(c, "xnA%d" % i, [128, 4, DM], BF16) for i in range(2)]
                    bXns = [Buf(), Buf()]
                    hT = SB(c, "hT", [128, 8, 512], BF16)
                    zst = [SB(c, "zst%d" % i, [128, 12, 512], BF16) for i in range(2)]
                    vst = [SB(c, "vst%d" % i, [128, 4, 768], BF16) for i in range(2)]
                    cst = SB(c, "cst", [128, 4, 640], F32)
                    cqn = SB(c, "cqn", [128, 4, 384], BF16)
                    ckvn = SB(c, "ckvn", [128, 4, 256], BF16)
                    cqT = SB(c, "cqT", [128, 3, 512], BF16)
                    ckvT = SB(c, "ckvT", [128, 2, 512], BF16)
                    qnst = [SB(c, "qnst%d" % i, [128, 2, 512], BF16) for i in range(2)]
                    knst = [SB(c, "knst%d" % i, [128, 2, 512], BF16) for i in range(2)]
                    qrst = [SB(c, "qrst%d" % i, [128, 512], BF16) for i in range(2)]
                    krst = [SB(c, "krst%d" % i, [32, 512], BF16) for i in range(2)]
                    vcst = [SB(c, "vcst%d" % i, [128, 4, 256], BF16) for i in range(2)]
                    t1 = SB(c, "t1A", [128, 512], F32)
                    t2 = SB(c, "t2A", [128, 512], F32)
                    psT = [PS(c, "psTA%d" % i, [128, 512], BF16) for i in range(2)]
                    psZ = [PS(c, "psZA%d" % i, [128, 512]) for i in range(2)]
                    psW = [PS(c, "psWA%d" % i, [128, 1024]) for i in range(2)]

                    bW, bWr, bUq, bUqr, bUkv, bG = Buf(), Buf(), Buf(), Buf(), Buf(), Buf()
                    bSc = [Buf(), Buf()]
                    bXg = [Buf(), Buf()]
                    bCS = [Buf(), Buf()]
                    bJ, bSS, bLn, bRs, bXn, bHT = Buf(), Buf(), Buf(), Buf(), Buf(), Buf()
                    bSq, bLq, bRq = Buf(), Buf(), Buf()
                    bZst = [Buf(), Buf()]
                    bVst = [Buf(), Buf()]
                    bCst, bCqn, bCkvn, bCqT, bCkvT = Buf(), Buf(), Buf(), Buf(), Buf()
                    bQn = [Buf(), Buf()]
                    bKn = [Buf(), Buf()]
                    bQr = [Buf(), Buf()]
                    bKr = [Buf(), Buf()]
                    bVc = [Buf(), Buf()]
                    bT1, bT2 = Buf(), Buf()
                    bPT = [Buf(), Buf()]
                    bPZ = [Buf(), Buf()]
                    bPW = [Buf(), Buf()]

                    P.dmas("pool", [(w_in[:, k, :], D["w_in"][l, :, k, :]) for k in range(8)], writes=[bW])
                    P.dma("pool", w_uq[:], D["w_uq"][l], writes=[bUq])
                    P.dma("pool", w_ukv[:], D["w_ukv"][l], writes=[bUkv])
                    P.dmas("sp", [(gq[:], D["gqT"][l]), (gkv[:], D["gkvT"][l]), (gpre[:], D["gpre_mixT"][l])], writes=[bG])
                    P.op("act", lambda e: e.mul(out=w_rot[:, :, 0:16], in_=w_in[:, :, 2960:2976], mul=-1.0), reads=[bW], writes=[bWr])
                    P.op("act", lambda e: e.copy(out=w_rot[:, :, 16:32], in_=w_in[:, :, 2944:2960]), reads=[bW], writes=[bWr])
                    for k in range(3):
                        src = w_uq[:, k, 256:384].rearrange("p (h t j) -> p h t j", t=2, j=16)
                        dst = w_uqr[:, k, :].rearrange("p (h t j) -> p h t j", t=2, j=16)
                        P.op("act", lambda e, src=src, dst=dst: e.mul(out=dst[:, :, 0, :], in_=src[:, :, 1, :], mul=-1.0),
                             reads=[bUq], writes=[bUqr])
                        P.op("act", lambda e, src=src, dst=dst: e.copy(out=dst[:, :, 1, :], in_=src[:, :, 0, :]),
                             reads=[bUq], writes=[bUqr])
                    for s in range(2):
                        load_vec8(c, biA[s][:], l, s, 0, bSc[s])
                        load_vec8(c, scA[s][:], l, s, 1, bSc[s])
                        P.op("dve", lambda e, s=s: e.scalar_tensor_tensor(out=scA[s][:], in0=scA[s][:], scalar=1.0, in1=gpre[:],
                                                                          op0=ALU.add, op1=ALU.mult),
                             reads=[bSc[s], bG], writes=[bSc[s]])

                    def loadA(g):
                        s, t0 = g // 8, (g % 8) * 512
                        j = g % 2
                        P.dma("sp", xg[j][:], x_src[s, t0:t0 + 512, :].rearrange("(t p) d -> p t d", p=128), writes=[bXg[j]])
                        P.dmas("sp", [(cosg[j][:], D["cosT"][:, t0:t0 + 512]), (sing[j][:], D["sinT"][:, t0:t0 + 512])], writes=[bCS[j]])

                    evac_i = [0]

                    def evac(out_ap, in_ap, reads, writes):
                        evac_i[0] += 1
                        if evac_i[0] % 2:
                            P.op("dve", lambda e: e.tensor_copy(out=out_ap, in_=in_ap), reads=reads, writes=writes)
                        else:
                            P.op("act", lambda e: e.copy(out=out_ap, in_=in_ap), reads=reads, writes=writes)

                    def prep(g):
                        j = g % 2
                        X = xg[j]
                        c4 = slice(4 * j, 4 * j + 4)
                        for t in range(4):
                            P.op("act", lambda e, t=t: e.activation(out=junk[:], in_=X[:, t, :], func=AF.Square, accum_out=ss[:, 4 * j + t:4 * j + t + 1]),
                                 reads=[bXg[j]], writes=[bJ, bSS])
                        rstd_from_ss(ss[:, c4], rstd[:, c4], DM, bSS, bRs, lnv[:, c4], bLn)
                        for t in range(4):
                            P.op("dve", lambda e, t=t: e.tensor_scalar(out=xns[j][:, t, :], in0=X[:, t, :], scalar1=rstd[:, 4 * j + t:4 * j + t + 1], scalar2=None,
                                                                       op0=ALU.mult),
                                 reads=[bXg[j], bRs], writes=[bXns[j]])

                    NG = 16
                    ckpt(2)
                    loadA(0)
                    prep(0)
                    for g in range(NG):
                        s, t0 = g // 8, (g % 8) * 512
                        j = g % 2
                        if g + 1 < NG:
                            loadA(g + 1)
                        xn = xns[j]
                        bXn = bXns[j]
                        for k in range(8):
                            pj = k % 2
                            P.op("pe", [(lambda e, t=t, k=k, pj=pj: e.transpose(psT[pj][:, t * 128:(t + 1) * 128], xn[:, t, k * 128:(k + 1) * 128], ident[:]))
                                        for t in range(4)], reads=[bXn, bId], writes=[bPT[pj]])
                            P.op("act", lambda e, k=k, pj=pj: e.activation(out=hT[:, k, :], in_=psT[pj][:], func=AF.Identity,
                                                                           scale=scA[s][:, k:k + 1], bias=biA[s][:, k:k + 1]),
                                 reads=[bPT[pj], bSc[s]], writes=[bHT])
                        ckpt(3)
                        cbs = [0, 128, 256, 384] + [768 + 128 * i for i in range(8)]
                        for ci, cb in enumerate(cbs):
                            pj = ci % 2
                            P.op("pe", [(lambda e, k=k, cb=cb, pj=pj: e.matmul(psZ[pj][:], lhsT=w_in[:, k, cb:cb + 128], rhs=hT[:, k, :],
                                                                              start=(k == 0), stop=(k == 7))) for k in range(8)],
                                 reads=[bW, bHT], writes=[bPZ[pj]])
                            evac(zst[j][:, ci, :], psZ[pj][:], [bPZ[pj]], [bZst[j]])
                        P.dma("sp", D["zT"][s].rearrange("c p t -> p c t")[:, :, t0:t0 + 512], zst[j][:], reads=[bZst[j]])
                        ckpt(4)
                        ckpt(5)
                        for t in range(4):
                            pj = t % 2
                            fns = []
                            for k in range(8):
                                fns.append(lambda e, k=k, t=t, pj=pj: e.matmul(psW[pj][:, 0:384], lhsT=hT[:, k, t * 128:(t + 1) * 128],
                                                                               rhs=w_in[:, k, 2304:2688], start=(k == 0), stop=(k == 7)))
                            for k in range(8):
                                fns.append(lambda e, k=k, t=t, pj=pj: e.matmul(psW[pj][:, 512:768], lhsT=hT[:, k, t * 128:(t + 1) * 128],
                                                                               rhs=w_in[:, k, 2688:2944], start=(k == 0), stop=(k == 7)))
                            if "noMm" not in dbg:
                                P.op("pe", fns, reads=[bW, bHT], writes=[bPW[pj]])
                            if "noCp" not in dbg:
                                P.op("dve", lambda e, t=t, pj=pj: e.tensor_copy(out=cst[:, t, 0:384], in_=psW[pj][:, 0:384]), reads=[bPW[pj]], writes=[bCst])
                                P.op("dve", lambda e, t=t, pj=pj: e.tensor_copy(out=cst[:, t, 384:640], in_=psW[pj][:, 512:768]), reads=[bPW[pj]], writes=[bCst])
                            if "noSq" not in dbg:
                              P.op("act", lambda e, t=t, pj=pj: e.activation(out=junk[:, 0:384], in_=cst[:, t, 0:384], func=AF.Square,
                                                                           accum_out=ssq[:, t:t + 1]), reads=[bCst], writes=[bJ, bSq])
                            if "noSq" not in dbg:
                              P.op("act", lambda e, t=t, pj=pj: e.activation(out=junk[:, 384:640], in_=cst[:, t, 384:640], func=AF.Square,
                                                                           accum_out=sskv[:, t:t + 1]), reads=[bCst], writes=[bJ, bSq])
                        ckpt(51)
                        rstd_from_ss(ssq[:], rsq[:], 384, bSq, bRq, lnq[:], bLq)
                        rstd_from_ss(sskv[:], rskv[:], 256, bSq, bRq, lnk[:], bLq)
                        for t in range(4):
                            P.op("dve", lambda e, t=t: e.tensor_scalar(out=cqn[:, t, :], in0=cst[:, t, 0:384], scalar1=rsq[:, t:t + 1], scalar2=None,
                                                                       op0=ALU.mult), reads=[bCst, bRq], writes=[bCqn])
                            P.op("dve", lambda e, t=t: e.tensor_scalar(out=ckvn[:, t, :], in0=cst[:, t, 384:640], scalar1=rskv[:, t:t + 1], scalar2=None,
                                                                       op0=ALU.mult), reads=[bCst, bRq], writes=[bCkvn])
                        for t in range(4):
                            pj = t % 2
                            fns = []
                            for k in range(8):
                                fns.append(lambda e, k=k, t=t, pj=pj: e.matmul(psW[pj][:, 0:256], lhsT=hT[:, k, t * 128:(t + 1) * 128],
                                                                               rhs=w_in[:, k, 512:768], start=(k == 0), stop=(k == 7)))
                            for k in range(8):
                                fns.append(lambda e, k=k, t=t, pj=pj: e.matmul(psW[pj][:, 512:1024], lhsT=hT[:, k, t * 128:(t + 1) * 128],
                                                                               rhs=w_in[:, k, 1792:2304], start=(k == 0), stop=(k == 7)))
                            P.op("pe", fns, reads=[bW, bHT], writes=[bPW[pj]])
                            evac(vst[j][:, t, 0:256], psW[pj][:, 0:256], [bPW[pj]], [bVst[j]])
                            evac(vst[j][:, t, 256:768], psW[pj][:, 512:1024], [bPW[pj]], [bVst[j]])
                        P.dma("sp", D["vtok"][s, t0:t0 + 512, :].rearrange("(t p) c -> p t c", p=128), vst[j][:], reads=[bVst[j]])
                        ckpt(53)
                        for kk in range(3):
                            pj = kk % 2
                            P.op("pe", [(lambda e, t=t, kk=kk, pj=pj: e.transpose(psT[pj][:, t * 128:(t + 1) * 128], cqn[:, t, kk * 128:(kk + 1) * 128], ident[:]))
                                        for t in range(4)], reads=[bCqn, bId], writes=[bPT[pj]])
                            P.op("act", lambda e, kk=kk, pj=pj: e.activation(out=cqT[:, kk, :], in_=psT[pj][:], func=AF.Identity, scale=gq[:, kk:kk + 1], bias=zcol[:, 0:1]),
                                 reads=[bPT[pj], bG, bZc], writes=[bCqT])
                        ckpt(54)
                        for kk in range(2):
                            pj = (kk + 1) % 2
                            P.op("pe", [(lambda e, t=t, kk=kk, pj=pj: e.transpose(psT[pj][:, t * 128:(t + 1) * 128], ckvn[:, t, kk * 128:(kk + 1) * 128], ident[:]))
                                        for t in range(4)], reads=[bCkvn, bId], writes=[bPT[pj]])
                            P.op("act", lambda e, kk=kk, pj=pj: e.activation(out=ckvT[:, kk, :], in_=psT[pj][:], func=AF.Identity, scale=gkv[:, kk:kk + 1], bias=zcol[:, 0:1]),
                                 reads=[bPT[pj], bG, bZc], writes=[bCkvT])
                        if g + 1 < NG:
                            prep(g + 1)
                        ckpt(6)
                        for jj in range(2):
                            P.op("pe", [(lambda e, kk=kk, jj=jj: e.matmul(psZ[0][:], lhsT=w_uq[:, kk, jj * 128:(jj + 1) * 128], rhs=cqT[:, kk, :],
                                                                          start=(kk == 0), stop=(kk == 2))) for kk in range(3)],
                                 reads=[bUq, bCqT], writes=[bPZ[0]])
                            evac(qnst[j][:, jj, :], psZ[0][:], [bPZ[0]], [bQn[j]])
                            P.op("pe", [(lambda e, kk=kk, jj=jj: e.matmul(psZ[1][:], lhsT=w_ukv[:, kk, jj * 128:(jj + 1) * 128], rhs=ckvT[:, kk, :],
                                                                          start=(kk == 0), stop=(kk == 1))) for kk in range(2)],
                                 reads=[bUkv, bCkvT], writes=[bPZ[1]])
                            evac(knst[j][:, jj, :], psZ[1][:], [bPZ[1]], [bKn[j]])
                        P.op("pe", [(lambda e, kk=kk: e.matmul(psZ[0][:], lhsT=w_uq[:, kk, 256:384], rhs=cqT[:, kk, :], start=(kk == 0), stop=(kk == 2)))
                                    for kk in range(3)], reads=[bUq, bCqT], writes=[bPZ[0]])
                        P.op("pe", [(lambda e, kk=kk: e.matmul(psZ[1][:], lhsT=w_uqr[:, kk, :], rhs=cqT[:, kk, :], start=(kk == 0), stop=(kk == 2)))
                                    for kk in range(3)], reads=[bUqr, bCqT], writes=[bPZ[1]])
                        P.op("dve", lambda e: e.tensor_tensor(out=t1[:], in0=psZ[0][:], in1=cosg[j][:], op=ALU.mult), reads=[bPZ[0], bCS[j]], writes=[bT1])
                        P.op("dve", lambda e: e.tensor_tensor(out=t2[:], in0=psZ[1][:], in1=sing[j][:], op=ALU.mult), reads=[bPZ[1], bCS[j]], writes=[bT2])
                        P.op("pool", lambda e: e.tensor_tensor(out=qrst[j][:], in0=t1[:], in1=t2[:], op=ALU.add), reads=[bT1, bT2], writes=[bQr[j]])
                        ckpt(7)
                        P.op("pe", [(lambda e, k=k: e.matmul(psZ[0][0:32, :], lhsT=w_in[:, k, 2944:2976], rhs=hT[:, k, :], start=(k == 0), stop=(k == 7)))
                                    for k in range(8)], reads=[bW, bHT], writes=[bPZ[0]])
                        P.op("pe", [(lambda e, k=k: e.matmul(psZ[1][0:32, :], lhsT=w_rot[:, k, :], rhs=hT[:, k, :], start=(k == 0), stop=(k == 7)))
                                    for k in range(8)], reads=[bWr, bHT], writes=[bPZ[1]])
                        P.op("dve", lambda e: e.tensor_tensor(out=t1[0:32, :], in0=psZ[0][0:32, :], in1=cosg[j][0:32, :], op=ALU.mult),
                             reads=[bPZ[0], bCS[j]], writes=[bT1])
                        P.op("dve", lambda e: e.tensor_tensor(out=t2[0:32, :], in0=psZ[1][0:32, :], in1=sing[j][0:32, :], op=ALU.mult),
                             reads=[bPZ[1], bCS[j]], writes=[bT2])
                        P.op("pool", lambda e: e.tensor_tensor(out=krst[j][:], in0=t1[0:32, :], in1=t2[0:32, :], op=ALU.add),
                             reads=[bT1, bT2], writes=[bKr[j]])
                        ckpt(8)
                        for t in range(4):
                            pj = t % 2
                            P.op("pe", [(lambda e, kk=kk, t=t, pj=pj: e.matmul(psW[pj][:, 0:256], lhsT=ckvT[:, kk, t * 128:(t + 1) * 128],
                                                                               rhs=w_ukv[:, kk, 256:512], start=(kk == 0), stop=(kk == 1))) for kk in range(2)],
                                 reads=[bUkv, bCkvT], writes=[bPW[pj]])
                            evac(vcst[j][:, t, :], psW[pj][:, 0:256], [bPW[pj]], [bVc[j]])
                        P.dma("sp", D["qn"][s].rearrange("j p t -> p j t")[:, :, t0:t0 + 512], qnst[j][:], reads=[bQn[j]])
                        P.dma("sp", D["kn"][s].rearrange("j p t -> p j t")[:, :, t0:t0 + 512], knst[j][:], reads=[bKn[j]])
                        P.dma("sp", D["qr"][s, :, t0:t0 + 512], qrst[j][:], reads=[bQr[j]])
                        P.dma("sp", D["kr"][s, :, t0:t0 + 512], krst[j][:], reads=[bKr[j]])
                        P.dma("sp", D["vc"][s, t0:t0 + 512, :].rearrange("(t p) c -> p t c", p=128), vcst[j][:], reads=[bVc[j]])
                        ckpt(9)
                P.barrier()
                if "stopA" in dbg:
                    break

                SC_AB = 0.125
                SC_C = 96.0 ** -0.5
                def run_pipeline(units, nb):
                    n = len(units)
                    for i in range(min(nb, n)):
                        if "pre" in units[i]:
                            units[i]["pre"]()
                        units[i]["qk"]()
                    for i in range(n):
                        units[i]["mid"]()
                        units[i]["pv"]()
                        if "post" in units[i]:
                            units[i]["post"]()
                        if i + nb < n:
                            u2 = units[i + nb]
                            if "pre" in u2:
                                u2["pre"]()
                            u2["qk"]()

                with ExitStack() as c:
                    NB = 3
                    naX = SB(c, "naX", [128, 4 * 14 * 64], BF16)
                    t5X = SB(c, "t5X", [128, 8 * 3 * 384], BF16)
                    naXm = SB(c, "naXm", [128, NPAT * 4 * 256], BF16)
                    bNaXm = Buf()
                    pens = SB(c, "pens", [2, NPAT * 256], BF16)
                    A2 = SB(c, "A2", [2, 128], BF16)
                    psS = [PS(c, "psS%d" % i, [128, 512]) for i in range(NB)]
                    psO = [PS(c, "psO%d" % i, [128, 512]) for i in range(2)]
                    E = [SB(c, "E%d" % i, [128, 384], BF16) for i in range(NB + 1)]
                    bE = [Buf() for _ in range(NB + 1)]
                    bPS = [Buf() for _ in range(NB)]
                    bPO = [Buf(), Buf()]
                    bNaX, bT5X, bPen = Buf(), Buf(), Buf()
                    P.dmas("pool", [(pens[:], D["pens"]), (A2[:], D["A2"])], writes=[bPen])
                    with ExitStack() as c2:
                        stg = SB(c2, "tstg", [128, 8 * 3 * 384], F32)
                        msk = SB(c2, "tmsk", [128, 14 * 64], F32)
                        bStg, bMsk = Buf(), Buf()
                        P.dma("sp", stg[:, 0:3584], D["naG"][l], writes=[bStg])
                        P.dma("sp", msk[:], D["nacm"], writes=[bMsk])
                        P.op("act", lambda e: e.activation(out=stg[:, 0:3584], in_=stg[:, 0:3584], func=AF.Exp), reads=[bStg], writes=[bStg])
                        for h in range(4):
                            P.op("dve", lambda e, h=h: e.tensor_tensor(out=naX[:, h * 896:(h + 1) * 896], in0=stg[:, h * 896:(h + 1) * 896], in1=msk[:],
                                                                       op=ALU.mult), reads=[bStg, bMsk], writes=[bNaX])
                        vmst = SB(c2, "vmst", [128, NPAT * 256], F32)
                        bVm = Buf()
                        P.dma("sp", vmst[:], D["navm"], writes=[bVm])
                        for (pid_, di0_) in sorted({(u_[2], u_[1]) for ul_ in NA_PLAN for u_ in ul_ if u_[2] >= 0}):
                            for h in range(4):
                                P.op("dve", lambda e, h=h, pid_=pid_, di0_=di0_: e.tensor_tensor(
                                    out=naXm[:, (pid_ * 4 + h) * 256:(pid_ * 4 + h + 1) * 256],
                                    in0=naX[:, h * 896 + di0_ * 64:h * 896 + di0_ * 64 + 256],
                                    in1=vmst[:, pid_ * 256:(pid_ + 1) * 256], op=ALU.mult), reads=[bNaX, bVm], writes=[bNaXm])
                        P.dma("sp", stg[:], D["t5G"], writes=[bStg])
                        P.dma("sp", msk[:, 0:384], D["t5m"], writes=[bMsk])
                        for hp in range(24):
                            P.op("act", lambda e, hp=hp: e.activation(out=stg[:, hp * 384:(hp + 1) * 384], in_=stg[:, hp * 384:(hp + 1) * 384], func=AF.Exp),
                                 reads=[bStg], writes=[bStg])
                            P.op("dve", lambda e, hp=hp: e.tensor_tensor(out=t5X[:, hp * 384:(hp + 1) * 384], in0=stg[:, hp * 384:(hp + 1) * 384],
                                                                         in1=msk[:, 0:384], op=ALU.mult), reads=[bStg, bMsk], writes=[bT5X])
                        P.barrier()
                    ucnt = [0]
                    bcnt = [0]

                    with ExitStack() as c2:
                        qT = [SB(c2, "naq%d" % i, [128, S], BF16) for i in range(2)]
                        kT = [SB(c2, "nak%d" % i, [128, S], BF16) for i in range(2)]
                        va = [SB(c2, "nav%d" % i, [128, 32, 2, 128], BF16) for i in range(2)]
                        ot = [SB(c2, "naot%d" % i, [128, S], BF16) for i in range(2)]
                        rec = SB(c2, "narec", [64, 256], F32)
                        bQ = [Buf(), Buf()]
                        bV = [Buf(), Buf()]
                        bOt = [Buf(), Buf()]
                        bRec = Buf()
                        for i in range(2):
                            P.op("pool", lambda e, i=i: e.memset(va[i][:, :, :, 64:128], 1.0), writes=[bV[i]])
                        items = [(s, hp) for s in range(2) for hp in range(2)]

                        def loadNA(n):
                            s, hp = items[n]
                            i = n % 2
                            P.dmas("sp", [(qT[i][:], D["zT"][s, hp]), (kT[i][:], D["zT"][s, 2 + hp])], writes=[bQ[i]])
                            src = D["vtok"][s].rearrange("(n p) c -> p n c", p=128)
                            P.dmas("sp", [(va[i][:, n0:n0 + 16, hh, 0:64], src[:, n0:n0 + 16, hp * 128 + hh * 64:hp * 128 + hh * 64 + 64])
                                          for n0 in range(0, 32, 16) for hh in range(2)], writes=[bV[i]])

                        units = []
                        for n, (s, hp) in enumerate(items):
                            i = n % 2
                            for hh in range(2):
                                h = 2 * hp + hh
                                pr = slice(hh * 64, hh * 64 + 64)
                                for J in range(16):
                                    ul = NA_PLAN[J]
                                    oj = bcnt[0] % 2
                                    bcnt[0] += 1
                                    for ui, (ci, di0, pid) in enumerate(ul):
                                        u = ucnt[0]
                                        ucnt[0] += 1
                                        sj, ej = u % NB, u % (NB + 1)
                                        first, last = (ui == 0), (ui == len(ul) - 1)

                                        def qk(i=i, pr=pr, ci=ci, J=J, pid=pid, sj=sj):
                                            P.op("pe", lambda e: e.matmul(psS[sj][:, 0:256], lhsT=kT[i][pr, ci * 128:(ci + 1) * 128],
                                                                          rhs=qT[i][pr, J * 256:(J + 1) * 256], start=True, stop=True),
                                                 reads=[bQ[i]], writes=[bPS[sj]])

                                        def mid(h=h, di0=di0, sj=sj, ej=ej, pid=pid):
                                            P.op("act", lambda e: e.activation(out=E[ej][:, 0:256], in_=psS[sj][:, 0:256], func=AF.Exp, scale=SC_AB),
                                                 reads=[bPS[sj]], writes=[bE[ej]])
                                            if pid >= 0:
                                                tab = naXm[:, (pid * 4 + h) * 256:(pid * 4 + h + 1) * 256]
                                            else:
                                                tab = naX[:, h * 896 + di0 * 64:h * 896 + di0 * 64 + 256]
                                            P.op("dve", lambda e: e.tensor_tensor(out=E[ej][:, 0:256], in0=E[ej][:, 0:256], in1=tab, op=ALU.mult),
                                                 reads=[bE[ej], bNaX, bNaXm], writes=[bE[ej]])

                                        def pv(i=i, ci=ci, hh=hh, ej=ej, oj=oj, first=first, last=last, J=J, pr=pr):
                                            P.op("pe", lambda e: e.matmul(psO[oj][:, 0:256], lhsT=va[i][:, ci, hh, :], rhs=E[ej][:, 0:256], start=first, stop=last),
                                                 reads=[bV[i], bE[ej]], writes=[bPO[oj]])
                                            if last:
                                                P.op("dve", lambda e: e.reciprocal(out=rec[:], in_=psO[oj][64:128, 0:256]), reads=[bPO[oj]], writes=[bRec])
                                                P.op("dve", lambda e: e.tensor_tensor(out=ot[i][pr, J * 256:(J + 1) * 256], in0=psO[oj][0:64, 0:256], in1=rec[:],
                                                                                      op=ALU.mult), reads=[bPO[oj], bRec], writes=[bOt[i]])

                                        units.append(dict(qk=qk, mid=mid, pv=pv))

                            def post(n=n, s=s, hp=hp, i=i):
                                P.dma("sp", D["OT"][s, hp * 128:(hp + 1) * 128, :], ot[i][:], reads=[bOt[i]])
                                if n + 2 < len(items):
                                    loadNA(n + 2)
                            units[-1]["post"] = post
                        loadNA(0)
                        loadNA(1)
                        run_pipeline(units, NB)
                        P.barrier()

                    with ExitStack() as c2:
                        qT = [SB(c2, "dq%d" % i, [128, S], BF16) for i in range(2)]
                        kT = [SB(c2, "dk%d" % i, [128, S], BF16) for i in range(2)]
                        qo = [SB(c2, "dqo%d" % i, [128, S], BF16) for i in range(2)]
                        ko = [SB(c2, "dko%d" % i, [128, S], BF16) for i in range(2)]
                        va = [SB(c2, "dv%d" % i, [128, 32, 2, 128], BF16) for i in range(2)]
                        oacc = [SB(c2, "oacc%d" % i, [128, S], F32) for i in range(2)]
                        ot = [SB(c2, "dot%d" % i, [128, S], BF16) for i in range(2)]
                        rec = SB(c2, "drec", [64, 1024], F32)
                        bQ = [Buf(), Buf()]
                        bQo = [Buf(), Buf()]
                        bV = [Buf(), Buf()]
                        bAcc = [Buf(), Buf()]
                        bOt = [Buf(), Buf()]
                        bRec = Buf()
                        for i in range(2):
                            P.op("pool", lambda e, i=i: e.memset(va[i][:, :, :, 64:128], 1.0), writes=[bV[i]])
                        items = [(s, hp) for s in range(2) for hp in range(4)]
                        pats = (1, 4, 16)
                        NPI = len(items) * 3

                        def loadQK(n):
                            s, hp = items[n]
                            i = n % 2
                            P.dmas("sp", [(qT[i][:], D["zT"][s, 4 + hp]), (kT[i][:], D["zT"][s, 8 + hp])], writes=[bQ[i]])

                        def loadV(pidx):
                            n, pi = pidx // 3, pidx % 3
                            s, hp = items[n]
                            d = pats[pi]
                            vi = pidx % 2
                            nchunk = 32 // d
                            src = D["vtok"][s].rearrange("(n p r) c -> p r n c", p=128, r=d)
                            pairs = []
                            for r in range(d):
                                for n0 in range(0, nchunk, 16):
                                    n1 = min(n0 + 16, nchunk)
                                    for hh in range(2):
                                        cb0 = 256 + hp * 128 + hh * 64
                                        pairs.append((va[vi][:, r * nchunk + n0:r * nchunk + n1, hh, 0:64], src[:, r, n0:n1, cb0:cb0 + 64]))
                            P.dmas("sp", pairs, writes=[bV[vi]])

                        units = []
                        ocount = 0
                        for n, (s, hp) in enumerate(items):
                            i = n % 2
                            for pi, d in enumerate(pats):
                                pidx = n * 3 + pi
                                vi = pidx % 2
                                nchunk = 32 // d
                                pre = None
                                if d == 1:
                                    Q, K, bQQ = qT[i], kT[i], bQ[i]
                                else:
                                    oi = ocount % 2
                                    ocount += 1
                                    Q, K, bQQ = qo[oi], ko[oi], bQo[oi]

                                    def pre(Q=Q, K=K, d=d, i=i, bQQ=bQQ):
                                        P.op("act", lambda e: e.copy(out=Q[:].rearrange("p (r m) -> p r m", r=d),
                                                                     in_=qT[i][:].rearrange("p (m r) -> p r m", r=d)),
                                             reads=[bQ[i]], writes=[bQQ])
                                        P.op("act", lambda e: e.copy(out=K[:].rearrange("p (r m) -> p r m", r=d),
                                                                     in_=kT[i][:].rearrange("p (m r) -> p r m", r=d)),
                                             reads=[bQ[i]], writes=[bQQ])
                                firstu = True
                                for hh in range(2):
                                    h = 2 * hp + hh
                                    pr = slice(hh * 64, hh * 64 + 64)
                                    accv = oacc[hh][:].rearrange("p (m r) -> p r m", r=d)
                                    zero_first = (pi == 0)
                                    for r in range(d):
                                        for kc in range(nchunk):
                                            u = ucnt[0]
                                            ucnt[0] += 1
                                            sj, ej, oj = u % NB, u % (NB + 1), u % 2
                                            j0, j1 = max(kc - 1, 0), min(kc + 1, nchunk - 1)
                                            g0 = j0 - (kc - 1)
                                            ncol = (j1 - j0 + 1) * 128
                                            c0, c1 = g0 * 128, g0 * 128 + ncol
                                            qbase = (r * nchunk + j0) * 128
                                            kbase = (r * nchunk + kc) * 128
                                            xo = (h * 3 + pi) * 384
                                            dst = accv[:, r, j0 * 128:(j1 + 1) * 128]
                                            zf = zero_first
                                            zero_first = False

                                            def qk(Q=Q, K=K, bQQ=bQQ, pr=pr, qbase=qbase, kbase=kbase, ncol=ncol, c0=c0, c1=c1, sj=sj):
                                                P.op("pe", lambda e: e.matmul(psS[sj][:, c0:c1], lhsT=K[pr, kbase:kbase + 128], rhs=Q[pr, qbase:qbase + ncol],
                                                                              start=True, stop=True), reads=[bQQ], writes=[bPS[sj]])

                                            def mid(sj=sj, ej=ej, c0=c0, c1=c1, xo=xo):
                                                P.op("act", lambda e: e.activation(out=E[ej][:, c0:c1], in_=psS[sj][:, c0:c1], func=AF.Exp, scale=SC_AB),
                                                     reads=[bPS[sj]], writes=[bE[ej]])
                                                P.op("pool", lambda e: e.tensor_tensor(out=E[ej][:, c0:c1], in0=E[ej][:, c0:c1], in1=t5X[:, xo + c0:xo + c1],
                                                                                       op=ALU.mult), reads=[bE[ej], bT5X], writes=[bE[ej]])

                                            def pv(vi=vi, r=r, nchunk=nchunk, kc=kc, hh=hh, ej=ej, oj=oj, dst=dst, c0=c0, c1=c1, zf=zf):
                                                if zf:
                                                    P.op("pool", lambda e: e.memset(oacc[hh][:], 0.0), writes=[bAcc[hh]])
                                                P.op("pe", lambda e: e.matmul(psO[oj][:, c0:c1], lhsT=va[vi][:, r * nchunk + kc, hh, :], rhs=E[ej][:, c0:c1],
                                                                              start=True, stop=True), reads=[bV[vi], bE[ej]], writes=[bPO[oj]])
                                                P.op("dve", lambda e: e.tensor_tensor(out=dst, in0=psO[oj][:, c0:c1], in1=dst, op=ALU.add),
                                                     reads=[bPO[oj], bAcc[hh]], writes=[bAcc[hh]])

                                            ud = dict(qk=qk, mid=mid, pv=pv)
                                            if firstu and pre is not None:
                                                ud["pre"] = pre
                                            firstu = False
                                            units.append(ud)
                                    if pi == 2:
                                        def fin(hh=hh, pr=pr, i=i):
                                            for q4 in range(4):
                                                cs = slice(q4 * 1024, (q4 + 1) * 1024)
                                                P.op("dve", lambda e: e.reciprocal(out=rec[:], in_=oacc[hh][64:128, cs]), reads=[bAcc[hh]], writes=[bRec])
                                                P.op("dve", lambda e: e.tensor_tensor(out=ot[i][pr, cs], in0=oacc[hh][0:64, cs], in1=rec[:], op=ALU.mult),
                                                     reads=[bAcc[hh], bRec], writes=[bOt[i]])
                                        units[-1]["fin"] = fin

                                def postp(pidx=pidx):
                                    if pidx + 2 < NPI:
                                        loadV(pidx + 2)
                                units[-1]["postp"] = postp

                            def posti(n=n, s=s, hp=hp, i=i):
                                P.dma("sp", D["OT"][s, 256 + hp * 128:256 + (hp + 1) * 128, :], ot[i][:], reads=[bOt[i]])
                                if n + 2 < len(items):
                                    loadQK(n + 2)
                            units[-1]["posti"] = posti
                        for ud in units:
                            hooks = [ud[k] for k in ("fin", "postp", "posti") if k in ud]
                            if hooks:
                                ud["post"] = (lambda hooks=hooks: [hk() for hk in hooks])
                        loadQK(0)
                        loadQK(1)
                        loadV(0)
                        loadV(1)
                        run_pipeline(units, NB)
                        P.barrier()
                P.barrier()

                with ExitStack() as c:
                    psS = [PS(c, "psSc%d" % i, [128, 1024]) for i in range(2)]
                    psO = [PS(c, "psOc%d" % i, [128, 1024]) for i in range(2)]
                    E = [SB(c, "Ec%d" % i, [128, 1024], BF16) for i in range(3)]
                    qT = [SB(c, "cq%d" % i, [96, S], BF16) for i in range(2)]
                    kT = [SB(c, "ck%d" % i, [96, S], BF16) for i in range(2)]
                    va = [SB(c, "cv%d" % i, [128, 32, 128], BF16) for i in range(2)]
                    ot = [SB(c, "cot%d" % i, [128, S], BF16) for i in range(2)]
                    rec = SB(c, "crec", [64, 1024], F32)
                    bE = [Buf() for _ in range(3)]
                    bPS = [Buf(), Buf()]
                    bPO = [Buf(), Buf()]
                    bQ = [Buf(), Buf()]
                    bV = [Buf(), Buf()]
                    bOt = [Buf(), Buf()]
                    bRec = Buf()
                    for i in range(2):
                        P.op("pool", lambda e, i=i: e.memset(va[i][:, :, 64:128], 1.0), writes=[bV[i]])
                    items = [(s, h) for s in range(2) for h in range(4)]

                    def loadC(n):
                        s, h = items[n]
                        i = n % 2
                        P.dmas("sp", [(qT[i][0:64, :], D["qn"][s, h // 2, (h % 2) * 64:(h % 2) * 64 + 64, :]),
                                      (qT[i][64:96, :], D["qr"][s, h * 32:(h + 1) * 32, :]),
                                      (kT[i][0:64, :], D["kn"][s, h // 2, (h % 2) * 64:(h % 2) * 64 + 64, :]),
                                      (kT[i][64:96, :], D["kr"][s, :, :])], writes=[bQ[i]])
                        src = D["vc"][s].rearrange("(n p) c -> p n c", p=128)
                        P.dmas("sp", [(va[i][:, n0:n0 + 8, 0:64], src[:, n0:n0 + 8, h * 64:(h + 1) * 64]) for n0 in range(0, 32, 8)], writes=[bV[i]])

                    units = []
                    u = 0
                    blk = 0
                    for n, (s, h) in enumerate(items):
                        i = n % 2
                        oi = (n // 2) % 2
                        pr = slice((h % 2) * 64, (h % 2) * 64 + 64)
                        for qb in range(4):
                            oj = blk % 2
                            blk += 1
                            cs = slice(qb * 1024, (qb + 1) * 1024)
                            for kc in range(32):
                                sj, ej = u % 2, u % 3
                                u += 1

                                def qk(i=i, kc=kc, qb=qb, sj=sj):
                                    P.op("pe", [(lambda e, hf=hf: e.matmul(psS[sj][:, hf * 512:(hf + 1) * 512], lhsT=kT[i][0:96, kc * 128:(kc + 1) * 128],
                                                                           rhs=qT[i][0:96, qb * 1024 + hf * 512:qb * 1024 + (hf + 1) * 512],
                                                                           start=True, stop=True)) for hf in range(2)],
                                         reads=[bQ[i]], writes=[bPS[sj]])

                                def mid(sj=sj, ej=ej):
                                    P.op("act", lambda e: e.activation(out=E[ej][:], in_=psS[sj][:], func=AF.Exp, scale=SC_C),
                                         reads=[bPS[sj]], writes=[bE[ej]])

                                def pv(i=i, kc=kc, ej=ej, oj=oj, oi=oi, pr=pr, cs=cs):
                                    P.op("pe", [(lambda e, hf=hf: e.matmul(psO[oj][:, hf * 512:(hf + 1) * 512], lhsT=va[i][:, kc, :],
                                                                           rhs=E[ej][:, hf * 512:(hf + 1) * 512], start=(kc == 0), stop=(kc == 31)))
                                                for hf in range(2)], reads=[bV[i], bE[ej]], writes=[bPO[oj]])
                                    if kc == 31:
                                        P.op("dve", lambda e: e.reciprocal(out=rec[:], in_=psO[oj][64:128, :]), reads=[bPO[oj]], writes=[bRec])
                                        P.op("dve", lambda e: e.tensor_tensor(out=ot[oi][pr, cs], in0=psO[oj][0:64, :], in1=rec[:], op=ALU.mult),
                                             reads=[bPO[oj], bRec], writes=[bOt[oi]])

                                units.append(dict(qk=qk, mid=mid, pv=pv))

                        def post(n=n, s=s, h=h, oi=oi):
                            if h % 2 == 1:
                                P.dma("sp", D["OT"][s, 768 + (h // 2) * 128:768 + (h // 2 + 1) * 128, :], ot[oi][:], reads=[bOt[oi]])
                            if n + 2 < len(items):
                                loadC(n + 2)
                        units[-1]["post"] = post
                    loadC(0)
                    loadC(1)
                    run_pipeline(units, 2)
                P.barrier()
                if "stopB" in dbg:
                    break

                with ExitStack() as c:
                    w_out = SB(c, "w_out", [128, 8, DM], BF16)
                    w_dn = SB(c, "w_dn", [128, NCH, DM], BF16)
                    wu = [SB(c, "wu%d" % i, [128, 8, 256], BF16) for i in range(3)]
                    cw = SB(c, "convw", [128, NCH, 3], F32)
                    cb = SB(c, "convb", [128, NCH], F32)
                    gpre = SB(c, "gpreF", [128, 8], F32)
                    scF = [SB(c, "scF%d" % s, [128, 8], F32) for s in range(2)]
                    biF = [SB(c, "biF%d" % s, [128, 8], F32) for s in range(2)]
                    gate1 = SB(c, "gate1", [128, DM], F32)
                    gate2 = SB(c, "gate2", [128, DM], F32)
                    xm = [SB(c, "xm%d" % i, [128, DM], F32) for i in range(2)]
                    bXm = [Buf(), Buf()]
                    bXD = {}
                    xg = [SB(c, "xgC%d" % i, [128, 4, DM], F32) for i in range(2)]
                    otg = [SB(c, "otg%d" % i, [128, 8, 512], BF16) for i in range(2)]
                    h2T = [SB(c, "h2T%d" % i, [128, 8, 512], BF16) for i in range(2)]
                    halo = SB(c, "halo", [128, 8, 2], BF16)
                    edge = [SB(c, "edge%d" % i, [128, 8, 2], BF16) for i in range(3)]
                    actT = SB(c, "actT", [128, NCH, 512], BF16)
                    xn2 = SB(c, "xn2", [128, 4, DM], BF16)
                    junk = SB(c, "junkC", [128, DM], BF16)
                    tmp = SB(c, "tmpC", [128, DM], F32)
                    gcb = [SB(c, "gc%d" % i, [128, 512], F32) for i in range(2)]
                    geb = [SB(c, "ge%d" % i, [128, 512], F32) for i in range(2)]
                    ssv = SB(c, "ssC", [128, 16], F32)
                    lnc = SB(c, "lnC", [128, 16], F32)
                    rsc = SB(c, "rsC", [128, 16], F32)
                    psYs = [PS(c, "psY%d" % i, [128, 1024]) for i in range(2)]
                    psT1 = PS(c, "psTC", [128, 512], BF16)
                    psTs = [psT1[:], psT1[:]]
                    psHt = PS(c, "psH", [128, 16])
                    psH = psHt[:, 0:2]
                    psHT = psHt[:, 8:16]
                    bPHT = Buf()
                    hT8 = SB(c, "hT8", [128, 8], F32)
                    bHT8 = Buf()
                    psA1 = PS(c, "psA", [128, 512])
                    psG1 = PS(c, "psG", [128, 512])
                    g_sb = [SB(c, "g_sb%d" % i, [128, 512], F32) for i in range(2)]
                    bGsb = [Buf(), Buf()]
                    bPG1 = Buf()
                    a_sb = [SB(c, "a_sb%d" % i, [128, 512], BF16) for i in range(2)]
                    hsb = SB(c, "hsb", [128, 2], F32)
                    bHsb = Buf()
                    bAsb = [Buf(), Buf()]
                    ycnt = [0]
                    bWo, bWd, bCw, bGp = Buf(), Buf(), Buf(), Buf()
                    bWu = [Buf() for _ in range(3)]
                    bScF = [Buf(), Buf()]
                    bGate, bGt = Buf(), Buf()
                    bXg = [Buf(), Buf()]
                    bOtg = [Buf(), Buf()]
                    bH2 = [Buf(), Buf()]
                    bEdge = [Buf(), Buf(), Buf()]
                    bHalo, bAct, bXn2, bJ, bTmp = Buf(), Buf(), Buf(), Buf(), Buf()
                    bActLo = Buf()
                    bGc = [Buf(), Buf()]
                    bGe = [Buf(), Buf()]
                    bSs, bLn, bRs = Buf(), Buf(), Buf()
                    bPYs = [Buf(), Buf()]
                    bPT1 = Buf()
                    bPTs = [bPT1, bPT1]
                    bPH = Buf()
                    bPA1 = Buf()
                    bPG = [Buf(), Buf()]

                    P.dmas("pool", [(w_out[:, k, :], D["w_out"][l, :, k, :]) for k in range(8)], writes=[bWo])
                    P.dmas("pool", [(w_dn[:, c0:c0 + 2, :], D["w_down"][l, :, c0:c0 + 2, :]) for c0 in range(0, NCH, 2)], writes=[bWd])
                    P.dmas("sp", [(cw[:], D["convwT"][l]), (cb[:], D["convbT"][l])], writes=[bCw])
                    P.dma("sp", gpre[:], D["gpre_ffnT"][l], writes=[bGp])
                    for s in range(2):
                        load_vec8(c, biF[s][:], l, s, 3, bScF[s])
                        load_vec8(c, scF[s][:], l, s, 4, bScF[s])
                        P.op("dve", lambda e, s=s: e.scalar_tensor_tensor(out=scF[s][:], in0=scF[s][:], scalar=1.0, in1=gpre[:], op0=ALU.add, op1=ALU.mult),
                             reads=[bScF[s], bGp], writes=[bScF[s]])

                    def load_gates(s):
                        for gt, jm, gp in ((gate1, 2, "gpost_mix"), (gate2, 5, "gpost_ffn")):
                            P.dma("sp", gt[:], D["modD"][l, s:s + 1, jm * 1024:(jm + 1) * 1024].partition_broadcast(128), writes=[bGate])
                            P.dma("sp", tmp[:], D[gp][l:l + 1, :].partition_broadcast(128), writes=[bTmp])
                            P.op("dve", lambda e, gt=gt: e.tensor_tensor(out=gt[:], in0=gt[:], in1=tmp[:], op=ALU.mult), reads=[bGate, bTmp], writes=[bGate])

                    def loadOT(g):
                        s, t0 = g // 8, (g % 8) * 512
                        j = g % 2
                        P.dma("sp", otg[j][:], D["OT"][s, :, t0:t0 + 512].rearrange("(k p) t -> p k t", p=128), writes=[bOtg[j]])

                    def loadX(g):
                        s, t0 = g // 8, (g % 8) * 512
                        j = g % 2
                        P.dmas("sp", [(xg[j][:, t, :], x_src[s, t0 + t * 128:t0 + (t + 1) * 128, :]) for t in range(4)], writes=[bXg[j]])

                    def norm_resid(xap, gate, col, bX, psY, bPY):
                        P.op("act", lambda e: e.activation(out=junk[:], in_=psY[:], func=AF.Square, accum_out=ssv[:, col:col + 1]),
                             reads=[bPY], writes=[bJ, bSs])
                        P.op("act", lambda e: e.activation(out=lnc[:, col:col + 1], in_=ssv[:, col:col + 1], func=AF.Ln, scale=1.0 / DM, bias=EPS),
                             reads=[bSs], writes=[bLn])
                        P.op("act", lambda e: e.activation(out=rsc[:, col:col + 1], in_=lnc[:, col:col + 1], func=AF.Exp, scale=-0.5),
                             reads=[bLn], writes=[bRs])
                        P.op("dve", lambda e: e.scalar_tensor_tensor(out=tmp[:], in0=psY[:], scalar=rsc[:, col:col + 1], in1=gate[:], op0=ALU.mult, op1=ALU.mult),
                             reads=[bPY, bRs, bGate], writes=[bTmp])
                        P.op("pool", lambda e: e.tensor_tensor(out=xap, in0=xap, in1=tmp[:], op=ALU.add), reads=[bTmp, bX], writes=[bX])

                    def stage1a(g):
                        s, t0 = g // 8, (g % 8) * 512
                        j = g % 2
                        X = xg[j]

                        def second(t):
                            P.op("act", lambda e: e.activation(out=junk[:], in_=X[:, t, :], func=AF.Square, accum_out=ssv[:, 4 + t:5 + t]),
                                 reads=[bXg[j]], writes=[bJ, bSs])
                            P.op("act", lambda e: e.activation(out=lnc[:, 4 + t:5 + t], in_=ssv[:, 4 + t:5 + t], func=AF.Ln, scale=1.0 / DM, bias=EPS),
                                 reads=[bSs], writes=[bLn])
                            P.op("act", lambda e: e.activation(out=rsc[:, 4 + t:5 + t], in_=lnc[:, 4 + t:5 + t], func=AF.Exp, scale=-0.5), reads=[bLn], writes=[bRs])
                            P.op("dve", lambda e: e.tensor_scalar(out=xn2[:, t, :], in0=X[:, t, :], scalar1=rsc[:, 4 + t:5 + t], scalar2=None, op0=ALU.mult),
                                 reads=[bXg[j], bRs], writes=[bXn2])

                        for t in range(4):
                            yi = ycnt[0] % 2
                            ycnt[0] += 1
                            psY, bPY = psYs[yi], bPYs[yi]
                            fns = []
                            for hf in range(2):
                                for k in range(8):
                                    fns.append(lambda e, k=k, hf=hf, t=t, psY=psY: e.matmul(psY[:, hf * 512:(hf + 1) * 512], lhsT=otg[j][:, k, t * 128:(t + 1) * 128],
                                                                                   rhs=w_out[:, k, hf * 512:(hf + 1) * 512], start=(k == 0), stop=(k == 7)))
                            P.op("pe", fns, reads=[bOtg[j], bWo], writes=[bPY])
                            if t == 2:
                                P.op("pe", [(lambda e, k=k: e.matmul(psHT[:, k:k + 1], lhsT=xn2[0:1, 0, k * 128:(k + 1) * 128], rhs=ident[0:1, 0:1],
                                                                     start=True, stop=True)) for k in range(8)], reads=[bXn2, bId], writes=[bPHT])
                                P.op("act", lambda e: e.copy(out=hT8[:], in_=psHT), reads=[bPHT], writes=[bHT8])
                                P.op("dve", lambda e: e.tensor_tensor(out=hT8[:], in0=hT8[:], in1=scF[s][:], op=ALU.mult), reads=[bHT8, bScF[s]], writes=[bHT8])
                                ej3 = g % 3
                                P.op("dve", lambda e: e.tensor_tensor(out=edge[ej3][:, :, 0], in0=hT8[:], in1=biF[s][:], op=ALU.add),
                                     reads=[bHT8, bScF[s]], writes=[bEdge[ej3]])
                            norm_resid(X[:, t, :], gate1, t, bXg[j], psY, bPY)
                            if t >= 1:
                                second(t - 1)
                        second(3)
                        bXD[g] = Buf()
                        P.dma("sp", x_dst[s, t0:t0 + 512, :].rearrange("(t p) d -> p t d", p=128), X[:], reads=[bXg[j]], writes=[bXD[g]])

                    def tr(g, k):
                        s = g // 8
                        j = g % 2
                        psT, bPT = psTs[k % 2], bPTs[k % 2]
                        P.op("pe", [(lambda e, t=t: e.transpose(psT[:, t * 128:(t + 1) * 128], xn2[:, t, k * 128:(k + 1) * 128], ident[:]))
                                    for t in range(4)], reads=[bXn2, bId], writes=[bPT])
                        P.op("act", lambda e: e.activation(out=h2T[j][:, k, :], in_=psT[:], func=AF.Identity,
                                                           scale=scF[s][:, k:k + 1], bias=biF[s][:, k:k + 1]),
                             reads=[bPT, bScF[s]], writes=[bH2[j]])
                        if k == 7:
                            ej3 = g % 3
                            P.op("pool", lambda e: e.tensor_copy(out=edge[ej3][:, :, 1:2], in_=h2T[j][:, :, 511:512]), reads=[bH2[j]], writes=[bEdge[ej3]])

                    wcount = [0]

                    def load_wu(ci):
                        wi = wcount[0] % 3
                        wcount[0] += 1
                        P.dma("pool", wu[wi][:], D["w_up"][l, ci], writes=[bWu[wi]])
                        return wi

                    wu_pref = []

                    def prefetch_wu():
                        wu_pref.extend([load_wu(0), load_wu(1)])

                    def stage2_up(g, have_next):
                        s, t0 = g // 8, (g % 8) * 512
                        j = g % 2
                        X = xg[j]
                        if t0 > 0:
                            P.op("pool", lambda e: e.tensor_copy(out=halo[:, :, 0:1], in_=edge[(g - 1) % 3][:, :, 1:2]), reads=[bEdge[(g - 1) % 3]], writes=[bHalo])
                        else:
                            P.op("pool", lambda e: e.memset(halo[:, :, 0:1], 0.0), writes=[bHalo])
                        if t0 + 512 < S:
                            assert have_next
                            P.op("pool", lambda e: e.tensor_copy(out=halo[:, :, 1:2], in_=edge[(g + 1) % 3][:, :, 0:1]), reads=[bEdge[(g + 1) % 3]], writes=[bHalo])
                        else:
                            P.op("pool", lambda e: e.memset(halo[:, :, 1:2], 0.0), writes=[bHalo])
                        def finish_chunk(ci):
                            pj = ci % 2
                            gc, ge = gcb[pj], geb[pj]
                            P.op("act", lambda e: e.activation(out=ge[:], in_=gc[:], func=AF.Gelu_apprx_tanh), reads=[bGc[pj]], writes=[bGe[pj]])
                            P.op("pool", lambda e: e.tensor_tensor(out=actT[:, ci, :], in0=a_sb[pj][:], in1=ge[:], op=ALU.mult),
                                 reads=[bAsb[pj], bGe[pj]], writes=[bActLo if ci < NCH - 2 else bAct])

                        pend = list(wu_pref)
                        del wu_pref[:]
                        for ci in range(NCH):
                            wi = pend.pop(0)
                            if ci + 2 < NCH:
                                pend.append(load_wu(ci + 2))
                            pj = ci % 2
                            W = wu[wi]
                            P.op("pe", [(lambda e, k=k, W=W, pj=pj: e.matmul(psA1[:], lhsT=W[:, k, 0:128], rhs=h2T[j][:, k, :], start=(k == 0), stop=(k == 7)))
                                        for k in range(8)], reads=[bWu[wi], bH2[j]], writes=[bPA1])
                            P.op("act", lambda e, pj=pj: e.copy(out=a_sb[pj][:], in_=psA1[:]), reads=[bPA1], writes=[bAsb[pj]])
                            P.op("pe", [(lambda e, k=k, W=W: e.matmul(psG1[:], lhsT=W[:, k, 128:256], rhs=h2T[j][:, k, :], start=(k == 0), stop=(k == 7)))
                                        for k in range(8)], reads=[bWu[wi], bH2[j]], writes=[bPG1])
                            P.op("pe", [(lambda e, k=k, W=W: e.matmul(psH, lhsT=W[:, k, 128:256], rhs=halo[:, k, :], start=(k == 0), stop=(k == 7)))
                                        for k in range(8)], reads=[bWu[wi], bHalo], writes=[bPH])
                            gc, ge, gs = gcb[pj], geb[pj], g_sb[pj]
                            P.op("act", lambda e, gs=gs: e.copy(out=gs[:], in_=psG1[:]), reads=[bPG1], writes=[bGsb[pj]])
                            P.op("dve", lambda e, ci=ci, gc=gc, gs=gs: e.tensor_scalar(out=gc[:], in0=gs[:], scalar1=cw[:, ci, 1:2], scalar2=cb[:, ci:ci + 1],
                                                                                       op0=ALU.mult, op1=ALU.add),
                                 reads=[bGsb[pj], bCw], writes=[bGc[pj]])
                            P.op("dve", lambda e, ci=ci, gc=gc, gs=gs: e.scalar_tensor_tensor(out=gc[:, 1:512], in0=gs[:, 0:511], scalar=cw[:, ci, 0:1],
                                                                                              in1=gc[:, 1:512], op0=ALU.mult, op1=ALU.add),
                                 reads=[bGsb[pj], bCw, bGc[pj]], writes=[bGc[pj]])
                            P.op("dve", lambda e, ci=ci, gc=gc, gs=gs: e.scalar_tensor_tensor(out=gc[:, 0:511], in0=gs[:, 1:512], scalar=cw[:, ci, 2:3],
                                                                                              in1=gc[:, 0:511], op0=ALU.mult, op1=ALU.add),
                                 reads=[bGsb[pj], bCw, bGc[pj]], writes=[bGc[pj]])
                            P.op("act", lambda e: e.copy(out=hsb[:], in_=psH), reads=[bPH], writes=[bHsb])
                            P.op("dve", lambda e, ci=ci, gc=gc: e.scalar_tensor_tensor(out=gc[:, 0:1], in0=hsb[:, 0:1], scalar=cw[:, ci, 0:1], in1=gc[:, 0:1],
                                                                                       op0=ALU.mult, op1=ALU.add), reads=[bHsb, bCw, bGc[pj]], writes=[bGc[pj]])
                            P.op("dve", lambda e, ci=ci, gc=gc: e.scalar_tensor_tensor(out=gc[:, 511:512], in0=hsb[:, 1:2], scalar=cw[:, ci, 2:3], in1=gc[:, 511:512],
                                                                                       op0=ALU.mult, op1=ALU.add), reads=[bHsb, bCw, bGc[pj]], writes=[bGc[pj]])
                            if ci >= 1:
                                finish_chunk(ci - 1)
                        finish_chunk(NCH - 1)
                        if g + 1 < 16:
                            prefetch_wu()
                    def stage2_down(g, trg):
                        s, t0 = g // 8, (g % 8) * 512

                        def load_xm(t):
                            P.dma("sp", xm[t % 2][:], x_dst[s, t0 + t * 128:t0 + (t + 1) * 128, :], reads=[bXD[g]], writes=[bXm[t % 2]])

                        load_xm(0)
                        load_xm(1)
                        for t in range(4):
                            if trg is not None:
                                tr(trg, 2 * t)
                            yi = ycnt[0] % 2
                            ycnt[0] += 1
                            psY, bPY = psYs[yi], bPYs[yi]
                            NLO = NCH - 2
                            fns = []
                            for hf in range(2):
                                for ci in range(NLO):
                                    fns.append(lambda e, ci=ci, hf=hf, t=t, psY=psY: e.matmul(psY[:, hf * 512:(hf + 1) * 512], lhsT=actT[:, ci, t * 128:(t + 1) * 128],
                                                                                     rhs=w_dn[:, ci, hf * 512:(hf + 1) * 512], start=(ci == 0), stop=False))
                            P.op("pe", fns, reads=[bActLo, bWd], writes=[bPY])
                            if trg is not None:
                                tr(trg, 2 * t + 1)
                            fns = []
                            for hf in range(2):
                                for ci in range(NLO, NCH):
                                    fns.append(lambda e, ci=ci, hf=hf, t=t, psY=psY: e.matmul(psY[:, hf * 512:(hf + 1) * 512], lhsT=actT[:, ci, t * 128:(t + 1) * 128],
                                                                                     rhs=w_dn[:, ci, hf * 512:(hf + 1) * 512], start=False, stop=(ci == NCH - 1)))
                            P.op("pe", fns, reads=[bAct, bWd], writes=[bPY])
                            norm_resid(xm[t % 2][:], gate2, 8 + t, bXm[t % 2], psY, bPY)
                            P.dma("sp", x_dst[s, t0 + t * 128:t0 + (t + 1) * 128, :], xm[t % 2][:], reads=[bXm[t % 2]], writes=[bXD[g]])
                            if t + 2 < 4:
                                load_xm(t + 2)

                    NG = 16
                    load_gates(0)
                    loadOT(0)
                    loadX(0)
                    loadOT(1)
                    loadX(1)
                    stage1a(0)
                    for k in range(8):
                        tr(0, k)
                    prefetch_wu()
                    for g in range(NG):
                        s = g // 8
                        nxt_same_seq = (g + 1 < NG) and ((g + 1) // 8 == s)
                        if nxt_same_seq:
                            stage1a(g + 1)
                            if g + 2 < NG:
                                loadOT(g + 2)
                                loadX(g + 2)
                        stage2_up(g, nxt_same_seq)
                        stage2_down(g, (g + 1) if nxt_same_seq else None)
                        if g + 1 < NG and not nxt_same_seq:
                            load_gates((g + 1) // 8)
                            stage1a(g + 1)
                            for k in range(8):
                                tr(g + 1, k)
                            if g + 2 < NG:
                                loadOT(g + 2)
                                loadX(g + 2)
                P.barrier()

        except _Stop:
            pass
        P.barrier()
    return nc


def _prep_shared(inp):
    f = lambda a: np.ascontiguousarray(np.asarray(a, dtype=np.float32))
    L = 4
    sh = {}
    sh["w_ada"] = f(np.asarray(inp["w_ada"]).reshape(L, 8, 128, 6144).transpose(0, 2, 1, 3))
    sh["b_ada"] = f(inp["b_ada"])
    sh["gpre_mixT"] = f(np.asarray(inp["g_pre_mix"]).reshape(L, 8, 128).transpose(0, 2, 1))
    sh["gpre_ffnT"] = f(np.asarray(inp["g_pre_ffn"]).reshape(L, 8, 128).transpose(0, 2, 1))
    sh["gpost_mix"] = f(inp["g_post_mix"])
    sh["gpost_ffn"] = f(inp["g_post_ffn"])
    sh["w_in"] = f(np.asarray(inp["w_in"]).reshape(L, 8, 128, D_IN).transpose(0, 2, 1, 3))
    wuq = np.asarray(inp["w_uq"]).reshape(L, 384, 4, 96)
    wuq = np.concatenate([wuq[..., :64].reshape(L, 384, 256), wuq[..., 64:].reshape(L, 384, 128)], axis=-1)
    sh["w_uq"] = f(wuq.reshape(L, 3, 128, 384).transpose(0, 2, 1, 3))
    wukv = np.asarray(inp["w_ukv"]).reshape(L, 256, 4, 128)
    wukv = np.concatenate([wukv[..., :64].reshape(L, 256, 256), wukv[..., 64:].reshape(L, 256, 256)], axis=-1)
    sh["w_ukv"] = f(wukv.reshape(L, 2, 128, 512).transpose(0, 2, 1, 3))
    sh["gqT"] = f(np.asarray(inp["mla_g_q"]).reshape(L, 3, 128).transpose(0, 2, 1))
    sh["gkvT"] = f(np.asarray(inp["mla_g_kv"]).reshape(L, 2, 128).transpose(0, 2, 1))
    sh["w_out"] = f(np.asarray(inp["w_out"]).reshape(L, 8, 128, DM).transpose(0, 2, 1, 3))
    wup = np.asarray(inp["w_up"]).reshape(L, 8, 128, 2, NCH, 128)
    sh["w_up"] = f(wup.transpose(0, 4, 2, 1, 3, 5).reshape(L, NCH, 128, 8, 256))
    sh["w_down"] = f(np.asarray(inp["w_down"]).reshape(L, NCH, 128, DM).transpose(0, 2, 1, 3))
    sh["convwT"] = f(np.asarray(inp["conv_w"]).reshape(L, 3, NCH, 128).transpose(0, 3, 2, 1))
    sh["convbT"] = f(np.asarray(inp["conv_b"]).reshape(L, NCH, 128).transpose(0, 2, 1))
    G, cm, T, tm = _host_tables(np.asarray(inp["na_rpb"], np.float32), np.asarray(inp["t5_table"], np.float32))
    sh["naG"] = f(G)
    sh["nacm"] = f(cm)
    sh["t5G"] = f(T)
    sh["t5m"] = f(tm)
    sh["pens"] = f(NA_PENS)
    sh["navm"] = f(np.repeat((NA_PENS == 0).astype(np.float32), 64, axis=0))
    A2 = np.zeros((2, 128), np.float32)
    A2[0, :64] = 1.0
    A2[1, 64:] = 1.0
    sh["A2"] = A2
    cosT, sinT = _rope_tables()
    sh["cosT"] = f(cosT)
    sh["sinT"] = f(sinT)
    return sh


def _in_maps(inp, ncores):
    sh = _prep_shared(inp)
    x = np.asarray(inp["x"], np.float32)
    cc = np.asarray(inp["c"], np.float32)
    maps = []
    for i in range(ncores):
        m = dict(sh)
        m["x"] = np.ascontiguousarray(x[2 * i:2 * i + 2])
        m["cT"] = np.ascontiguousarray(cc[2 * i:2 * i + 2].reshape(2, 8, 128).transpose(2, 1, 0))
        maps.append(m)
    return maps


_NC_CACHE = {}


def kernel(**inputs):
    if "nc" not in _NC_CACHE:
        _NC_CACHE["nc"] = build()
    nc = _NC_CACHE["nc"]
    maps = _in_maps(inputs, NCORES)
    res = run_bass_kernel_spmd(nc, maps, core_ids=list(range(NCORES)))
    out = np.concatenate([np.asarray(r["out"], dtype=np.float32) for r in res.results], axis=0)
    return out
```

```python
import math
from contextlib import ExitStack

import numpy as np
import ml_dtypes
import concourse.bass as bass
import concourse.mybir as mybir
from concourse.bass_utils import run_bass_kernel_spmd

F32 = mybir.dt.float32
BF16 = mybir.dt.bfloat16
AF = mybir.ActivationFunctionType
ALU = mybir.AluOpType

NCORES = 8
S = 4096
DM = 1024
DFF = 2816
NCH = 22
D_IN = 2976
EPS = 1e-6
NEG = -30000.0


class Buf:
    __slots__ = ("lw", "rd")

    def __init__(self):
        self.lw = []
        self.rd = {}


class Prog:
    def __init__(self, nc, ctx, n_dma_sems=40):
        self.nc = nc
        self.E = {"pe": nc.tensor, "act": nc.scalar, "dve": nc.vector, "pool": nc.gpsimd, "sp": nc.sync}
        self.psem = {k: ctx.enter_context(nc.semaphore("ps_" + k)) for k in self.E}
        self.pcnt = {k: 0 for k in self.E}
        self.seen = {k: {} for k in self.E}
        self.dsem = [ctx.enter_context(nc.semaphore("ds%d" % i)) for i in range(n_dma_sems)]
        self.dcnt = [0] * n_dma_sems
        self.dnext = 0
        self.nins = 0
        self.dead = False

    def _wait(self, eng, deps):
        if self.dead:
            return
        need = {}
        for d in deps:
            if d is None:
                continue
            s, v = d
            if need.get(s, 0) < v:
                need[s] = v
        seen = self.seen[eng]
        for s, v in need.items():
            if seen.get(s, 0) < v:
                self.E[eng].wait_ge(s, v)
                seen[s] = v
                self.nins += 1

    @staticmethod
    def _deps(reads, writes):
        deps = []
        for b in reads:
            deps.extend(b.lw)
        for b in writes:
            deps.extend(b.lw)
            deps.extend(b.rd.values())
        return deps

    def op(self, eng, fns, reads=(), writes=()):
        if self.dead:
            return None
        self._wait(eng, self._deps(reads, writes))
        if callable(fns):
            fns = [fns]
        ins = None
        e = self.E[eng]
        for f in fns:
            ins = f(e)
            self.nins += 1
        self.pcnt[eng] += 1
        ins.then_inc(self.psem[eng], 1)
        tok = (self.psem[eng], self.pcnt[eng])
        for b in reads:
            b.rd[eng] = tok
        for b in writes:
            b.lw = [tok]
            b.rd = {}
        return tok

    def dma(self, eng, out, in_, reads=(), writes=(), **kw):
        return self.dmas(eng, [(out, in_)], reads, writes, **kw)

    def dmas(self, eng, pairs, reads=(), writes=(), **kw):
        if self.dead:
            return []
        deps = self._deps(reads, writes)
        idx = []
        for _ in pairs:
            i = self.dnext
            self.dnext = (i + 1) % len(self.dsem)
            idx.append(i)
            if self.dcnt[i] > 0:
                deps.append((self.dsem[i], 16 * self.dcnt[i]))
        assert len(set(idx)) == len(idx)
        self._wait(eng, deps)
        toks = []
        for i, (out, in_) in zip(idx, pairs):
            self.dcnt[i] += 1
            self.E[eng].dma_start(out=out, in_=in_, **kw).then_inc(self.dsem[i], 16)
            self.nins += 1
            toks.append((self.dsem[i], 16 * self.dcnt[i]))
        for b in reads:
            for i, tok in zip(idx, toks):
                b.rd[("dma", i)] = tok
        for b in writes:
            b.lw = list(toks)
            b.rd = {}
        return toks

    def barrier(self):
        deps = [(self.psem[k], self.pcnt[k]) for k in self.E if self.pcnt[k] > 0]
        deps += [(s, 16 * c) for s, c in zip(self.dsem, self.dcnt) if c > 0]
        for k in self.E:
            self._wait(k, deps)


def _r_start(r):
    return min(max(r - 4, 0), 56)


def _c_start(c):
    return min(max(c - 8, 0), 48)


def _na_plan():
    pats = {}
    plan = []
    for J in range(16):
        klo = _r_start(4 * J)
        khi = _r_start(4 * J + 3) + 7
        units = []
        for i in range(klo // 2, khi // 2 + 1):
            key = []
            for a in range(2):
                for ap in range(4):
                    r = 4 * J + ap
                    kr = 2 * i + a
                    key.append(_r_start(r) <= kr <= _r_start(r) + 7)
            key = tuple(key)
            if all(key):
                pid = -1
            else:
                if key not in pats:
                    pats[key] = len(pats)
                pid = pats[key]
            di0 = 4 * J - 2 * i + 6
            assert 0 <= di0 <= 10
            units.append((i, di0, pid))
        plan.append(units)
    npat = len(pats)
    pens = np.zeros((2, npat, 4, 64), np.float32)
    for key, pid in pats.items():
        for a in range(2):
            for ap in range(4):
                if not key[a * 4 + ap]:
                    pens[a, pid, ap, :] = NEG
    return plan, pens.reshape(2, npat * 256)


NA_PLAN, NA_PENS = _na_plan()
NPAT = NA_PENS.shape[1] // 256


def _t5_bucket(rel):
    nb = 16
    max_exact = 8
    n = np.abs(rel)
    large = max_exact + (np.log(np.maximum(n, 1) / max_exact) / math.log(1024 / max_exact) * (nb - max_exact)).astype(np.int64)
    large = np.minimum(large, nb - 1)
    return (np.where(rel > 0, nb, 0) + np.where(n < max_exact, n, large)).astype(np.int32)


def _host_tables(na_rpb, t5_table):
    p = np.arange(128)
    a = p // 64
    kc = p % 64
    di = np.arange(14)
    c = np.arange(64)
    drow = a[:, None] - (di[None, :] - 6) + 7
    dcol = kc[:, None] - c[None, :] + 15
    ok = ((drow >= 0) & (drow <= 14))[:, :, None] & ((dcol >= 0) & (dcol <= 30))[:, None, :]
    drc = np.clip(drow, 0, 14)
    dcc = np.clip(dcol, 0, 30)
    G = na_rpb[:, :, drc[:, :, None], dcc[:, None, :]]
    G = np.where(ok[None, None], G, np.float32(0.0)).astype(np.float32)
    G = np.ascontiguousarray(np.transpose(G, (0, 2, 1, 3, 4))).reshape(na_rpb.shape[0], 128, 4 * 14 * 64)
    cs = np.array([_c_start(x) for x in range(64)])
    cm = ((kc[:, None] >= cs[None, :]) & (kc[:, None] < cs[None, :] + 16)).astype(np.float32)
    cm = np.ascontiguousarray(np.broadcast_to(cm[:, None, :], (128, 14, 64))).reshape(128, 14 * 64)
    q = np.arange(128)
    pc = np.arange(3)
    rel = 128 * (1 - pc[None, :, None]) + p[:, None, None] - q[None, None, :]
    valid = (np.abs(rel) <= 64)
    T = np.zeros((128, 8, 3, 3, 128), np.float32)
    for pi, d in enumerate((1, 4, 16)):
        b = _t5_bucket(rel * d)
        vals = t5_table[b]
        vals = np.where(valid[..., None], vals, np.float32(0.0))
        T[:, :, pi] = np.transpose(vals, (0, 3, 1, 2))
    T = T.reshape(128, 8 * 3 * 384)
    tm = valid.astype(np.float32).reshape(128, 384)
    return G, cm, T, tm


def _rope_tables():
    inv_freq = (10000.0 ** (-np.arange(0, 32, 2, dtype=np.float32) / 32)).astype(np.float32)
    ang = np.arange(S, dtype=np.float32)[:, None] * inv_freq[None, :]
    cos = np.cos(ang).astype(np.float32)
    sin = np.sin(ang).astype(np.float32)
    idx = (np.arange(128) % 32) % 16
    return np.ascontiguousarray(cos[:, idx].T), np.ascontiguousarray(sin[:, idx].T)


class _Stop(Exception):
    pass


def build(NL=4, dbg=()):
    nc = bass.Bass("TRN2", target_bir_lowering=False)
    stop_at = -1
    for dflag in dbg:
        if dflag.startswith("stage="):
            stop_at = int(dflag[6:])

    PP = []

    def ckpt(n):
        if n == stop_at and not PP[0].dead:
            PP[0].barrier()
            PP[0].dead = True
    D = {}

    def din(name, shape, dt=F32):
        D[name] = nc.dram_tensor(name, list(shape), dt, kind="ExternalInput").ap()

    def dscr(name, shape, dt):
        kind = "ExternalOutput" if name in dbg else "Internal"
        D[name] = nc.dram_tensor(name, list(shape), dt, kind=kind).ap()

    din("x", [2, S, DM])
    din("cT", [128, 8, 2])
    din("w_ada", [4, 128, 8, 6144])
    din("b_ada", [4, 6144])
    din("gpre_mixT", [4, 128, 8])
    din("gpre_ffnT", [4, 128, 8])
    din("gpost_mix", [4, DM])
    din("gpost_ffn", [4, DM])
    din("w_in", [4, 128, 8, D_IN])
    din("w_uq", [4, 128, 3, 384])
    din("w_ukv", [4, 128, 2, 512])
    din("gqT", [4, 128, 3])
    din("gkvT", [4, 128, 2])
    din("w_out", [4, 128, 8, DM])
    din("w_up", [4, NCH, 128, 8, 256])
    din("w_down", [4, 128, NCH, DM])
    din("convwT", [4, 128, NCH, 3])
    din("convbT", [4, 128, NCH])
    din("naG", [4, 128, 4 * 14 * 64])
    din("nacm", [128, 14 * 64])
    din("pens", [2, NPAT * 256])
    din("navm", [128, NPAT * 256])
    din("A2", [2, 128])
    din("t5G", [128, 8 * 3 * 384])
    din("t5m", [128, 384])
    din("cosT", [128, S])
    din("sinT", [128, S])
    D["out"] = nc.dram_tensor("out", [2, S, DM], F32, kind="ExternalOutput").ap()
    dscr("modD", [4, 2, 6144], F32)
    dscr("xs", [2, S, DM], F32)
    dscr("zT", [2, 12, 128, S], BF16)
    dscr("vtok", [2, S, 768], BF16)
    dscr("qn", [2, 2, 128, S], BF16)
    dscr("qr", [2, 128, S], BF16)
    dscr("kn", [2, 2, 128, S], BF16)
    dscr("kr", [2, 32, S], BF16)
    dscr("vc", [2, S, 256], BF16)
    dscr("OT", [2, DM, S], BF16)

    with ExitStack() as gctx:
        P = Prog(nc, gctx)
        PP.append(P)

        uid = [0]

        def SB(ctx, name, shape, dt):
            uid[0] += 1
            return ctx.enter_context(nc.sbuf_tensor("%s_u%d" % (name, uid[0]), list(shape), dt))

        def PS(ctx, name, shape, dt=F32):
            uid[0] += 1
            return ctx.enter_context(nc.psum_tensor("%s_u%d" % (name, uid[0]), list(shape), dt))

        zcol = SB(gctx, "zcol", [128, 1], F32)
        bZc = Buf()
        ident = SB(gctx, "ident", [128, 128], BF16)
        identf = SB(gctx, "identf", [128, 128], F32)
        bId = Buf()
        P.op("pool", lambda e: e.memset(zcol[:], 0.0), writes=[bZc])
        P.op("pool", lambda e: e.memset(identf[:], 1.0), writes=[bId])
        P.op("pool", lambda e: e.affine_select(out=identf[:], in_=identf[:], pattern=[[-1, 128]],
                                                compare_op=ALU.is_equal, fill=0.0, base=0, channel_multiplier=1),
             reads=[bId], writes=[bId])
        P.op("pool", lambda e: e.tensor_copy(out=ident[:], in_=identf[:]), reads=[bId], writes=[bId])

        def rstd_from_ss(ss_ap, out_ap, n, bss, bout, tmp_ap, btmp):
            P.op("act", lambda e: e.activation(out=tmp_ap, in_=ss_ap, func=AF.Ln, scale=1.0 / n, bias=EPS),
                 reads=[bss], writes=[btmp])
            P.op("act", lambda e: e.activation(out=out_ap, in_=tmp_ap, func=AF.Exp, scale=-0.5),
                 reads=[btmp], writes=[bout])

        try:
            with ExitStack() as c:
                cT = SB(c, "cT_sb", [128, 8, 2], F32)
                cact = SB(c, "cact", [128, 8, 2], BF16)
                brow = SB(c, "brow", [2, 6144], F32)
                mrow = SB(c, "mrow", [2, 6144], F32)
                wt = [SB(c, "wada%d" % i, [128, 8, 1024], BF16) for i in range(3)]
                psM = [PS(c, "psM%d" % i, [2, 512]) for i in range(2)]
                bc, bb, bm = Buf(), Buf(), Buf()
                bw = [Buf(), Buf(), Buf()]
                bp = [Buf(), Buf()]
                P.dma("sp", cT[:], D["cT"], writes=[bc])
                P.op("act", lambda e: e.activation(out=cact[:], in_=cT[:], func=AF.Silu), reads=[bc], writes=[bc])
                it = 0
                ip = 0
                for l in range(NL):
                    P.dma("sp", brow[:], D["b_ada"][l:l + 1, :].partition_broadcast(2), writes=[bb])
                    for nb in range(6):
                        j = it % 3
                        it += 1
                        P.dmas("pool", [(wt[j][:, k0:k0 + 4, :], D["w_ada"][l, :, k0:k0 + 4, nb * 1024:(nb + 1) * 1024]) for k0 in (0, 4)], writes=[bw[j]])
                        for hf in range(2):
                            pj = ip % 2
                            ip += 1
                            cs = slice(nb * 1024 + hf * 512, nb * 1024 + (hf + 1) * 512)
                            P.op("pe", [(lambda e, k=k, j=j, pj=pj, hf=hf: e.matmul(psM[pj][:], lhsT=cact[:, k, :], rhs=wt[j][:, k, hf * 512:(hf + 1) * 512],
                                                                                    start=(k == 0), stop=(k == 7))) for k in range(8)],
                                 reads=[bc, bw[j]], writes=[bp[pj]])
                            P.op("dve", lambda e, pj=pj, cs=cs: e.tensor_tensor(out=mrow[:, cs], in0=psM[pj][:], in1=brow[:, cs], op=ALU.add),
                                 reads=[bp[pj], bb], writes=[bm])
                    P.dma("sp", D["modD"][l], mrow[:], reads=[bm])
            P.barrier()
            ckpt(1)

            def load_vec8(ctx_name, dst, l, s, j, bdst):
                src = D["modD"][l, s, j * 1024:(j + 1) * 1024].rearrange("(k p) -> p k", p=128)
                P.dma("sp", dst, src, writes=[bdst], allow_slow_non_contiguous=True)

            for l in range(NL):
                x_src = D["x"] if l == 0 else D["xs"]
                x_dst = D["out"] if l == NL - 1 else D["xs"]

                with ExitStack() as c:
                    w_in = SB(c, "w_in", [128, 8, D_IN], BF16)
                    w_rot = SB(c, "w_rot", [128, 8, 32], BF16)
                    w_uq = SB(c, "w_uq", [128, 3, 384], BF16)
                    w_uqr = SB(c, "w_uqr", [128, 3, 128], BF16)
                    w_ukv = SB(c, "w_ukv", [128, 2, 512], BF16)
                    gq = SB(c, "gq", [128, 3], F32)
                    gkv = SB(c, "gkv", [128, 2], F32)
                    gpre = SB(c, "gpre", [128, 8], F32)
                    scA = [SB(c, "scA%d" % s, [128, 8], F32) for s in range(2)]
                    biA = [SB(c, "biA%d" % s, [128, 8], F32) for s in range(2)]
                    xg = [SB(c, "xgA%d" % i, [128, 4, DM], F32) for i in range(2)]
                    cosg = [SB(c, "cosg%d" % i, [128, 512], F32) for i in range(2)]
                    sing = [SB(c, "sing%d" % i, [128, 512], F32) for i in range(2)]
                    junk = SB(c, "junkA", [128, DM], BF16)
                    ss = SB(c, "ssA", [128, 8], F32)
                    lnv = SB(c, "lnvA", [128, 8], F32)
                    rstd = SB(c, "rstdA", [128, 8], F32)
                    ssq = SB(c, "ssq", [128, 4], F32)
                    sskv = SB(c, "sskv", [128, 4], F32)
                    lnq = SB(c, "lnq", [128, 4], F32)
                    lnk = SB(c, "lnk", [128, 4], F32)
                    rsq = SB(c, "rsq", [128, 4], F32)
                    rskv = SB(c, "rskv", [128, 4], F32)
                    xns = [SB(c, "xnA%d" % i, [128, 4, DM], BF16) for i in range(2)]
                    bXns = [Buf(), Buf()]
                    hT = SB(c, "hT", [128, 8, 512], BF16)
                    zst = [SB(c, "zst%d" % i, [128, 12, 512], BF16) for i in range(2)]
                    vst = [SB(c, "vst%d" % i, [128, 4, 768], BF16) for i in range(2)]
                    cst = SB(c, "cst", [128, 4, 640], F32)
                    cqn = SB(c, "cqn", [128, 4, 384], BF16)
                    ckvn = SB(c, "ckvn", [128, 4, 256], BF16)
                    cqT = SB(c, "cqT", [128, 3, 512], BF16)
                    ckvT = SB(c, "ckvT", [128, 2, 512], BF16)
                    qnst = [SB(c, "qnst%d" % i, [128, 2, 512], BF16) for i in range(2)]
                    knst = [SB(c, "knst%d" % i, [128, 2, 512], BF16) for i in range(2)]
                    qrst = [SB(c, "qrst%d" % i, [128, 512], BF16) for i in range(2)]
                    krst = [SB(c, "krst%d" % i, [32, 512], BF16) for i in range(2)]
                    vcst = [SB(c, "vcst%d" % i, [128, 4, 256], BF16) for i in range(2)]
                    t1 = SB(c, "t1A", [128, 512], F32)
                    t2 = SB(c, "t2A", [128, 512], F32)
                    psT = [PS(c, "psTA%d" % i, [128, 512], BF16) for i in range(2)]
                    psZ = [PS(c, "psZA%d" % i, [128, 512]) for i in range(2)]
                    psW = [PS(c, "psWA%d" % i, [128, 1024]) for i in range(2)]

                    bW, bWr, bUq, bUqr, bUkv, bG = Buf(), Buf(), Buf(), Buf(), Buf(), Buf()
                    bSc = [Buf(), Buf()]
                    bXg = [Buf(), Buf()]
                    bCS = [Buf(), Buf()]
                    bJ, bSS, bLn, bRs, bXn, bHT = Buf(), Buf(), Buf(), Buf(), Buf(), Buf()
                    bSq, bLq, bRq = Buf(), Buf(), Buf()
                    bZst = [Buf(), Buf()]
                    bVst = [Buf(), Buf()]
                    bCst, bCqn, bCkvn, bCqT, bCkvT = Buf(), Buf(), Buf(), Buf(), Buf()
                    bQn = [Buf(), Buf()]
                    bKn = [Buf(), Buf()]
                    bQr = [Buf(), Buf()]
                    bKr = [Buf(), Buf()]
                    bVc = [Buf(), Buf()]
                    bT1, bT2 = Buf(), Buf()
                    bPT = [Buf(), Buf()]
                    bPZ = [Buf(), Buf()]
                    bPW = [Buf(), Buf()]

                    P.dmas("pool", [(w_in[:, k, :], D["w_in"][l, :, k, :]) for k in range(8)], writes=[bW])
                    P.dma("pool", w_uq[:], D["w_uq"][l], writes=[bUq])
                    P.dma("pool", w_ukv[:], D["w_ukv"][l], writes=[bUkv])
                    P.dmas("sp", [(gq[:], D["gqT"][l]), (gkv[:], D["gkvT"][l]), (gpre[:], D["gpre_mixT"][l])], writes=[bG])
                    P.op("act", lambda e: e.mul(out=w_rot[:, :, 0:16], in_=w_in[:, :, 2960:2976], mul=-1.0), reads=[bW], writes=[bWr])
                    P.op("act", lambda e: e.copy(out=w_rot[:, :, 16:32], in_=w_in[:, :, 2944:2960]), reads=[bW], writes=[bWr])
                    for k in range(3):
                        src = w_uq[:, k, 256:384].rearrange("p (h t j) -> p h t j", t=2, j=16)
                        dst = w_uqr[:, k, :].rearrange("p (h t j) -> p h t j", t=2, j=16)
                        P.op("act", lambda e, src=src, dst=dst: e.mul(out=dst[:, :, 0, :], in_=src[:, :, 1, :], mul=-1.0),
                             reads=[bUq], writes=[bUqr])
                        P.op("act", lambda e, src=src, dst=dst: e.copy(out=dst[:, :, 1, :], in_=src[:, :, 0, :]),
                             reads=[bUq], writes=[bUqr])
                    for s in range(2):
                        load_vec8(c, biA[s][:], l, s, 0, bSc[s])
                        load_vec8(c, scA[s][:], l, s, 1, bSc[s])
                        P.op("dve", lambda e, s=s: e.scalar_tensor_tensor(out=scA[s][:], in0=scA[s][:], scalar=1.0, in1=gpre[:],
                                                                          op0=ALU.add, op1=ALU.mult),
                             reads=[bSc[s], bG], writes=[bSc[s]])

                    def loadA(g):
                        s, t0 = g // 8, (g % 8) * 512
                        j = g % 2
                        P.dma("sp", xg[j][:], x_src[s, t0:t0 + 512, :].rearrange("(t p) d -> p t d", p=128), writes=[bXg[j]])
                        P.dmas("sp", [(cosg[j][:], D["cosT"][:, t0:t0 + 512]), (sing[j][:], D["sinT"][:, t0:t0 + 512])], writes=[bCS[j]])

                    evac_i = [0]

                    def evac(out_ap, in_ap, reads, writes):
                        evac_i[0] += 1
                        if evac_i[0] % 2:
                            P.op("dve", lambda e: e.tensor_copy(out=out_ap, in_=in_ap), reads=reads, writes=writes)
                        else:
                            P.op("act", lambda e: e.copy(out=out_ap, in_=in_ap), reads=reads, writes=writes)

                    def prep(g):
                        j = g % 2
                        X = xg[j]
                        c4 = slice(4 * j, 4 * j + 4)
                        for t in range(4):
                            P.op("act", lambda e, t=t: e.activation(out=junk[:], in_=X[:, t, :], func=AF.Square, accum_out=ss[:, 4 * j + t:4 * j + t + 1]),
                                 reads=[bXg[j]], writes=[bJ, bSS])
                        rstd_from_ss(ss[:, c4], rstd[:, c4], DM, bSS, bRs, lnv[:, c4], bLn)
                        for t in range(4):
                            P.op("dve", lambda e, t=t: e.tensor_scalar(out=xns[j][:, t, :], in0=X[:, t, :], scalar1=rstd[:, 4 * j + t:4 * j + t + 1], scalar2=None,
                                                                       op0=ALU.mult),
                                 reads=[bXg[j], bRs], writes=[bXns[j]])

                    NG = 16
                    ckpt(2)
                    loadA(0)
                    prep(0)
                    for g in range(NG):
                        s, t0 = g // 8, (g % 8) * 512
                        j = g % 2
                        if g + 1 < NG:
                            loadA(g + 1)
                        xn = xns[j]
                        bXn = bXns[j]
                        for k in range(8):
                            pj = k % 2
                            P.op("pe", [(lambda e, t=t, k=k, pj=pj: e.transpose(psT[pj][:, t * 128:(t + 1) * 128], xn[:, t, k * 128:(k + 1) * 128], ident[:]))
                                        for t in range(4)], reads=[bXn, bId], writes=[bPT[pj]])
                            P.op("act", lambda e, k=k, pj=pj: e.activation(out=hT[:, k, :], in_=psT[pj][:], func=AF.Identity,
                                                                           scale=scA[s][:, k:k + 1], bias=biA[s][:, k:k + 1]),
                                 reads=[bPT[pj], bSc[s]], writes=[bHT])
                        ckpt(3)
                        cbs = [0, 128, 256, 384] + [768 + 128 * i for i in range(8)]
                        for ci, cb in enumerate(cbs):
                            pj = ci % 2
                            P.op("pe", [(lambda e, k=k, cb=cb, pj=pj: e.matmul(psZ[pj][:], lhsT=w_in[:, k, cb:cb + 128], rhs=hT[:, k, :],
                                                                              start=(k == 0), stop=(k == 7))) for k in range(8)],
                                 reads=[bW, bHT], writes=[bPZ[pj]])
                            evac(zst[j][:, ci, :], psZ[pj][:], [bPZ[pj]], [bZst[j]])
                        P.dma("sp", D["zT"][s].rearrange("c p t -> p c t")[:, :, t0:t0 + 512], zst[j][:], reads=[bZst[j]])
                        ckpt(4)
                        ckpt(5)
                        for t in range(4):
                            pj = t % 2
                            fns = []
                            for k in range(8):
                                fns.append(lambda e, k=k, t=t, pj=pj: e.matmul(psW[pj][:, 0:384], lhsT=hT[:, k, t * 128:(t + 1) * 128],
                                                                               rhs=w_in[:, k, 2304:2688], start=(k == 0), stop=(k == 7)))
                            for k in range(8):
                                fns.append(lambda e, k=k, t=t, pj=pj: e.matmul(psW[pj][:, 512:768], lhsT=hT[:, k, t * 128:(t + 1) * 128],
                                                                               rhs=w_in[:, k, 2688:2944], start=(k == 0), stop=(k == 7)))
                            if "noMm" not in dbg:
                                P.op("pe", fns, reads=[bW, bHT], writes=[bPW[pj]])
                            if "noCp" not in dbg:
                                P.op("dve", lambda e, t=t, pj=pj: e.tensor_copy(out=cst[:, t, 0:384], in_=psW[pj][:, 0:384]), reads=[bPW[pj]], writes=[bCst])
                                P.op("dve", lambda e, t=t, pj=pj: e.tensor_copy(out=cst[:, t, 384:640], in_=psW[pj][:, 512:768]), reads=[bPW[pj]], writes=[bCst])
                            if "noSq" not in dbg:
                              P.op("act", lambda e, t=t, pj=pj: e.activation(out=junk[:, 0:384], in_=cst[:, t, 0:384], func=AF.Square,
                                                                           accum_out=ssq[:, t:t + 1]), reads=[bCst], writes=[bJ, bSq])
                            if "noSq" not in dbg:
                              P.op("act", lambda e, t=t, pj=pj: e.activation(out=junk[:, 384:640], in_=cst[:, t, 384:640], func=AF.Square,
                                                                           accum_out=sskv[:, t:t + 1]), reads=[bCst], writes=[bJ, bSq])
                        ckpt(51)
                        rstd_from_ss(ssq[:], rsq[:], 384, bSq, bRq, lnq[:], bLq)
                        rstd_from_ss(sskv[:], rskv[:], 256, bSq, bRq, lnk[:], bLq)
                        for t in range(4):
                            pj = t % 2
                            fns = []
                            for k in range(8):
                                fns.append(lambda e, k=k, t=t, pj=pj: e.matmul(psW[pj][:, 0:256], lhsT=hT[:, k, t * 128:(t + 1) * 128],
                                                                               rhs=w_in[:, k, 512:768], start=(k == 0), stop=(k == 7)))
                            for k in range(8):
                                fns.append(lambda e, k=k, t=t, pj=pj: e.matmul(psW[pj][:, 512:1024], lhsT=hT[:, k, t * 128:(t + 1) * 128],
                                                                               rhs=w_in[:, k, 1792:2304], start=(k == 0), stop=(k == 7)))
                            P.op("pe", fns, reads=[bW, bHT], writes=[bPW[pj]])
                            evac(vst[j][:, t, 0:256], psW[pj][:, 0:256], [bPW[pj]], [bVst[j]])
                            evac(vst[j][:, t, 256:768], psW[pj][:, 512:1024], [bPW[pj]], [bVst[j]])
                        P.dma("sp", D["vtok"][s, t0:t0 + 512, :].rearrange("(t p) c -> p t c", p=128), vst[j][:], reads=[bVst[j]])
                        for t in range(4):
                            P.op("dve", lambda e, t=t: e.tensor_scalar(out=cqn[:, t, :], in0=cst[:, t, 0:384], scalar1=rsq[:, t:t + 1], scalar2=None,
                                                                       op0=ALU.mult), reads=[bCst, bRq], writes=[bCqn])
                            P.op("dve", lambda e, t=t: e.tensor_scalar(out=ckvn[:, t, :], in0=cst[:, t, 384:640], scalar1=rskv[:, t:t + 1], scalar2=None,
                                                                       op0=ALU.mult), reads=[bCst, bRq], writes=[bCkvn])
                        ckpt(53)
                        for kk in range(3):
                            pj = kk % 2
                            P.op("pe", [(lambda e, t=t, kk=kk, pj=pj: e.transpose(psT[pj][:, t * 128:(t + 1) * 128], cqn[:, t, kk * 128:(kk + 1) * 128], ident[:]))
                                        for t in range(4)], reads=[bCqn, bId], writes=[bPT[pj]])
                            P.op("act", lambda e, kk=kk, pj=pj: e.activation(out=cqT[:, kk, :], in_=psT[pj][:], func=AF.Identity, scale=gq[:, kk:kk + 1], bias=zcol[:, 0:1]),
                                 reads=[bPT[pj], bG, bZc], writes=[bCqT])
                        ckpt(54)
                        for kk in range(2):
                            pj = (kk + 1) % 2
                            P.op("pe", [(lambda e, t=t, kk=kk, pj=pj: e.transpose(psT[pj][:, t * 128:(t + 1) * 128], ckvn[:, t, kk * 128:(kk + 1) * 128], ident[:]))
                                        for t in range(4)], reads=[bCkvn, bId], writes=[bPT[pj]])
                            P.op("act", lambda e, kk=kk, pj=pj: e.activation(out=ckvT[:, kk, :], in_=psT[pj][:], func=AF.Identity, scale=gkv[:, kk:kk + 1], bias=zcol[:, 0:1]),
                                 reads=[bPT[pj], bG, bZc], writes=[bCkvT])
                        if g + 1 < NG:
                            prep(g + 1)
                        ckpt(6)
                        for jj in range(2):
                            P.op("pe", [(lambda e, kk=kk, jj=jj: e.matmul(psZ[0][:], lhsT=w_uq[:, kk, jj * 128:(jj + 1) * 128], rhs=cqT[:, kk, :],
                                                                          start=(kk == 0), stop=(kk == 2))) for kk in range(3)],
                                 reads=[bUq, bCqT], writes=[bPZ[0]])
                            evac(qnst[j][:, jj, :], psZ[0][:], [bPZ[0]], [bQn[j]])
                            P.op("pe", [(lambda e, kk=kk, jj=jj: e.matmul(psZ[1][:], lhsT=w_ukv[:, kk, jj * 128:(jj + 1) * 128], rhs=ckvT[:, kk, :],
                                                                          start=(kk == 0), stop=(kk == 1))) for kk in range(2)],
                                 reads=[bUkv, bCkvT], writes=[bPZ[1]])
                            evac(knst[j][:, jj, :], psZ[1][:], [bPZ[1]], [bKn[j]])
                        P.op("pe", [(lambda e, kk=kk: e.matmul(psZ[0][:], lhsT=w_uq[:, kk, 256:384], rhs=cqT[:, kk, :], start=(kk == 0), stop=(kk == 2)))
                                    for kk in range(3)], reads=[bUq, bCqT], writes=[bPZ[0]])
                        P.op("pe", [(lambda e, kk=kk: e.matmul(psZ[1][:], lhsT=w_uqr[:, kk, :], rhs=cqT[:, kk, :], start=(kk == 0), stop=(kk == 2)))
                                    for kk in range(3)], reads=[bUqr, bCqT], writes=[bPZ[1]])
                        P.op("dve", lambda e: e.tensor_tensor(out=t1[:], in0=psZ[0][:], in1=cosg[j][:], op=ALU.mult), reads=[bPZ[0], bCS[j]], writes=[bT1])
                        P.op("dve", lambda e: e.tensor_tensor(out=t2[:], in0=psZ[1][:], in1=sing[j][:], op=ALU.mult), reads=[bPZ[1], bCS[j]], writes=[bT2])
                        P.op("pool", lambda e: e.tensor_tensor(out=qrst[j][:], in0=t1[:], in1=t2[:], op=ALU.add), reads=[bT1, bT2], writes=[bQr[j]])
                        ckpt(7)
                        P.op("pe", [(lambda e, k=k: e.matmul(psZ[0][0:32, :], lhsT=w_in[:, k, 2944:2976], rhs=hT[:, k, :], start=(k == 0), stop=(k == 7)))
                                    for k in range(8)], reads=[bW, bHT], writes=[bPZ[0]])
                        P.op("pe", [(lambda e, k=k: e.matmul(psZ[1][0:32, :], lhsT=w_rot[:, k, :], rhs=hT[:, k, :], start=(k == 0), stop=(k == 7)))
                                    for k in range(8)], reads=[bWr, bHT], writes=[bPZ[1]])
                        P.op("dve", lambda e: e.tensor_tensor(out=t1[0:32, :], in0=psZ[0][0:32, :], in1=cosg[j][0:32, :], op=ALU.mult),
                             reads=[bPZ[0], bCS[j]], writes=[bT1])
                        P.op("dve", lambda e: e.tensor_tensor(out=t2[0:32, :], in0=psZ[1][0:32, :], in1=sing[j][0:32, :], op=ALU.mult),
                             reads=[bPZ[1], bCS[j]], writes=[bT2])
                        P.op("pool", lambda e: e.tensor_tensor(out=krst[j][:], in0=t1[0:32, :], in1=t2[0:32, :], op=ALU.add),
                             reads=[bT1, bT2], writes=[bKr[j]])
                        ckpt(8)
                        for t in range(4):
                            pj = t % 2
                            P.op("pe", [(lambda e, kk=kk, t=t, pj=pj: e.matmul(psW[pj][:, 0:256], lhsT=ckvT[:, kk, t * 128:(t + 1) * 128],
                                                                               rhs=w_ukv[:, kk, 256:512], start=(kk == 0), stop=(kk == 1))) for kk in range(2)],
                                 reads=[bUkv, bCkvT], writes=[bPW[pj]])
                            evac(vcst[j][:, t, :], psW[pj][:, 0:256], [bPW[pj]], [bVc[j]])
                        P.dma("sp", D["qn"][s].rearrange("j p t -> p j t")[:, :, t0:t0 + 512], qnst[j][:], reads=[bQn[j]])
                        P.dma("sp", D["kn"][s].rearrange("j p t -> p j t")[:, :, t0:t0 + 512], knst[j][:], reads=[bKn[j]])
                        P.dma("sp", D["qr"][s, :, t0:t0 + 512], qrst[j][:], reads=[bQr[j]])
                        P.dma("sp", D["kr"][s, :, t0:t0 + 512], krst[j][:], reads=[bKr[j]])
                        P.dma("sp", D["vc"][s, t0:t0 + 512, :].rearrange("(t p) c -> p t c", p=128), vcst[j][:], reads=[bVc[j]])
                        ckpt(9)
                P.barrier()
                if "stopA" in dbg:
                    break

                SC_AB = 0.125
                SC_C = 96.0 ** -0.5
                def run_pipeline(units, nb):
                    n = len(units)
                    for i in range(min(nb, n)):
                        if "pre" in units[i]:
                            units[i]["pre"]()
                        units[i]["qk"]()
                    for i in range(n):
                        units[i]["mid"]()
                        units[i]["pv"]()
                        if "post" in units[i]:
                            units[i]["post"]()
                        if i + nb < n:
                            u2 = units[i + nb]
                            if "pre" in u2:
                                u2["pre"]()
                            u2["qk"]()

                with ExitStack() as c:
                    NB = 4
                    NO = 3
                    naX = SB(c, "naX", [128, 4 * 14 * 64], BF16)
                    t5X = SB(c, "t5X", [128, 8 * 3 * 384], BF16)
                    naXm = SB(c, "naXm", [128, NPAT * 4 * 256], BF16)
                    bNaXm = Buf()
                    pens = SB(c, "pens", [2, NPAT * 256], BF16)
                    A2 = SB(c, "A2", [2, 128], BF16)
                    psS = [PS(c, "psS%d" % i, [128, 512]) for i in range(NB)]
                    psO = [PS(c, "psO%d" % i, [128, 512]) for i in range(NO)]
                    E = [SB(c, "E%d" % i, [128, 384], BF16) for i in range(NB + 1)]
                    bE = [Buf() for _ in range(NB + 1)]
                    bPS = [Buf() for _ in range(NB)]
                    bPO = [Buf() for _ in range(NO)]
                    bNaX, bT5X, bPen = Buf(), Buf(), Buf()
                    P.dmas("pool", [(pens[:], D["pens"]), (A2[:], D["A2"])], writes=[bPen])
                    with ExitStack() as c2:
                        stg = SB(c2, "tstg", [128, 8 * 3 * 384], F32)
                        msk = SB(c2, "tmsk", [128, 14 * 64], F32)
                        bStg, bMsk = Buf(), Buf()
                        P.dma("sp", stg[:, 0:3584], D["naG"][l], writes=[bStg])
                        P.dma("sp", msk[:], D["nacm"], writes=[bMsk])
                        P.op("act", lambda e: e.activation(out=stg[:, 0:3584], in_=stg[:, 0:3584], func=AF.Exp), reads=[bStg], writes=[bStg])
                        for h in range(4):
                            P.op("dve", lambda e, h=h: e.tensor_tensor(out=naX[:, h * 896:(h + 1) * 896], in0=stg[:, h * 896:(h + 1) * 896], in1=msk[:],
                                                                       op=ALU.mult), reads=[bStg, bMsk], writes=[bNaX])
                        vmst = SB(c2, "vmst", [128, NPAT * 256], F32)
                        bVm = Buf()
                        P.dma("sp", vmst[:], D["navm"], writes=[bVm])
                        for (pid_, di0_) in sorted({(u_[2], u_[1]) for ul_ in NA_PLAN for u_ in ul_ if u_[2] >= 0}):
                            for h in range(4):
                                P.op("dve", lambda e, h=h, pid_=pid_, di0_=di0_: e.tensor_tensor(
                                    out=naXm[:, (pid_ * 4 + h) * 256:(pid_ * 4 + h + 1) * 256],
                                    in0=naX[:, h * 896 + di0_ * 64:h * 896 + di0_ * 64 + 256],
                                    in1=vmst[:, pid_ * 256:(pid_ + 1) * 256], op=ALU.mult), reads=[bNaX, bVm], writes=[bNaXm])
                        P.dma("sp", stg[:], D["t5G"], writes=[bStg])
                        P.dma("sp", msk[:, 0:384], D["t5m"], writes=[bMsk])
                        for hp in range(24):
                            P.op("act", lambda e, hp=hp: e.activation(out=stg[:, hp * 384:(hp + 1) * 384], in_=stg[:, hp * 384:(hp + 1) * 384], func=AF.Exp),
                                 reads=[bStg], writes=[bStg])
                            P.op("dve", lambda e, hp=hp: e.tensor_tensor(out=t5X[:, hp * 384:(hp + 1) * 384], in0=stg[:, hp * 384:(hp + 1) * 384],
                                                                         in1=msk[:, 0:384], op=ALU.mult), reads=[bStg, bMsk], writes=[bT5X])
                        P.barrier()
                    ucnt = [0]
                    bcnt = [0]

                    with ExitStack() as c2:
                        qT = [SB(c2, "naq%d" % i, [128, S], BF16) for i in range(2)]
                        kT = [SB(c2, "nak%d" % i, [128, S], BF16) for i in range(2)]
                        va = [SB(c2, "nav%d" % i, [128, 32, 2, 128], BF16) for i in range(2)]
                        ot = [SB(c2, "naot%d" % i, [128, S], BF16) for i in range(2)]
                        rec = SB(c2, "narec", [64, 256], F32)
                        bQ = [Buf(), Buf()]
                        bV = [Buf(), Buf()]
                        bOt = [Buf(), Buf()]
                        bRec = Buf()
                        for i in range(2):
                            P.op("pool", lambda e, i=i: e.memset(va[i][:, :, :, 64:128], 1.0), writes=[bV[i]])
                        items = [(s, hp) for s in range(2) for hp in range(2)]

                        def loadNA(n):
                            s, hp = items[n]
                            i = n % 2
                            P.dmas("sp", [(qT[i][:], D["zT"][s, hp]), (kT[i][:], D["zT"][s, 2 + hp])], writes=[bQ[i]])
                            src = D["vtok"][s].rearrange("(n p) c -> p n c", p=128)
                            P.dmas("sp", [(va[i][:, n0:n0 + 16, hh, 0:64], src[:, n0:n0 + 16, hp * 128 + hh * 64:hp * 128 + hh * 64 + 64])
                                          for n0 in range(0, 32, 16) for hh in range(2)], writes=[bV[i]])

                        units = []
                        for n, (s, hp) in enumerate(items):
                            i = n % 2
                            for hh in range(2):
                                h = 2 * hp + hh
                                pr = slice(hh * 64, hh * 64 + 64)
                                for J in range(16):
                                    ul = NA_PLAN[J]
                                    oj = bcnt[0] % NO
                                    bcnt[0] += 1
                                    for ui, (ci, di0, pid) in enumerate(ul):
                                        u = ucnt[0]
                                        ucnt[0] += 1
                                        sj, ej = u % NB, u % (NB + 1)
                                        first, last = (ui == 0), (ui == len(ul) - 1)

                                        def qk(i=i, pr=pr, ci=ci, J=J, pid=pid, sj=sj):
                                            P.op("pe", lambda e: e.matmul(psS[sj][:, 0:256], lhsT=kT[i][pr, ci * 128:(ci + 1) * 128],
                                                                          rhs=qT[i][pr, J * 256:(J + 1) * 256], start=True, stop=True),
                                                 reads=[bQ[i]], writes=[bPS[sj]])

                                        def mid(h=h, di0=di0, sj=sj, ej=ej, pid=pid):
                                            P.op("act", lambda e: e.activation(out=E[ej][:, 0:256], in_=psS[sj][:, 0:256], func=AF.Exp, scale=SC_AB),
                                                 reads=[bPS[sj]], writes=[bE[ej]])
                                            if pid >= 0:
                                                tab = naXm[:, (pid * 4 + h) * 256:(pid * 4 + h + 1) * 256]
                                            else:
                                                tab = naX[:, h * 896 + di0 * 64:h * 896 + di0 * 64 + 256]
                                            P.op("dve", lambda e: e.tensor_tensor(out=E[ej][:, 0:256], in0=E[ej][:, 0:256], in1=tab, op=ALU.mult),
                                                 reads=[bE[ej], bNaX, bNaXm], writes=[bE[ej]])

                                        def pv(i=i, ci=ci, hh=hh, ej=ej, oj=oj, first=first, last=last, J=J, pr=pr):
                                            P.op("pe", lambda e: e.matmul(psO[oj][:, 0:256], lhsT=va[i][:, ci, hh, :], rhs=E[ej][:, 0:256], start=first, stop=last),
                                                 reads=[bV[i], bE[ej]], writes=[bPO[oj]])
                                            if last:
                                                P.op("dve", lambda e: e.reciprocal(out=rec[:], in_=psO[oj][64:128, 0:256]), reads=[bPO[oj]], writes=[bRec])
                                                P.op("dve", lambda e: e.tensor_tensor(out=ot[i][pr, J * 256:(J + 1) * 256], in0=psO[oj][0:64, 0:256], in1=rec[:],
                                                                                      op=ALU.mult), reads=[bPO[oj], bRec], writes=[bOt[i]])

                                        units.append(dict(qk=qk, mid=mid, pv=pv))

                            def post(n=n, s=s, hp=hp, i=i):
                                P.dma("sp", D["OT"][s, hp * 128:(hp + 1) * 128, :], ot[i][:], reads=[bOt[i]])
                                if n + 2 < len(items):
                                    loadNA(n + 2)
                            units[-1]["post"] = post
                        loadNA(0)
                        loadNA(1)
                        run_pipeline(units, NB)
                        P.barrier()

                    with ExitStack() as c2:
                        qT = [SB(c2, "dq%d" % i, [128, S], BF16) for i in range(2)]
                        kT = [SB(c2, "dk%d" % i, [128, S], BF16) for i in range(2)]
                        qo = [SB(c2, "dqo%d" % i, [128, S], BF16) for i in range(2)]
                        ko = [SB(c2, "dko%d" % i, [128, S], BF16) for i in range(2)]
                        va = [SB(c2, "dv%d" % i, [128, 32, 2, 128], BF16) for i in range(2)]
                        oacc = [SB(c2, "oacc%d" % i, [128, S], F32) for i in range(2)]
                        ot = [SB(c2, "dot%d" % i, [128, S], BF16) for i in range(2)]
                        rec = SB(c2, "drec", [64, 1024], F32)
                        bQ = [Buf(), Buf()]
                        bQo = [Buf(), Buf()]
                        bV = [Buf(), Buf()]
                        bAcc = [Buf(), Buf()]
                        bOt = [Buf(), Buf()]
                        bRec = Buf()
                        for i in range(2):
                            P.op("pool", lambda e, i=i: e.memset(va[i][:, :, :, 64:128], 1.0), writes=[bV[i]])
                        items = [(s, hp) for s in range(2) for hp in range(4)]
                        pats = (1, 4, 16)
                        NPI = len(items) * 3

                        def loadQK(n):
                            s, hp = items[n]
                            i = n % 2
                            P.dmas("sp", [(qT[i][:], D["zT"][s, 4 + hp]), (kT[i][:], D["zT"][s, 8 + hp])], writes=[bQ[i]])

                        def loadV(pidx):
                            n, pi = pidx // 3, pidx % 3
                            s, hp = items[n]
                            d = pats[pi]
                            vi = pidx % 2
                            nchunk = 32 // d
                            src = D["vtok"][s].rearrange("(n p r) c -> p r n c", p=128, r=d)
                            pairs = []
                            for r in range(d):
                                for n0 in range(0, nchunk, 16):
                                    n1 = min(n0 + 16, nchunk)
                                    for hh in range(2):
                                        cb0 = 256 + hp * 128 + hh * 64
                                        pairs.append((va[vi][:, r * nchunk + n0:r * nchunk + n1, hh, 0:64], src[:, r, n0:n1, cb0:cb0 + 64]))
                            P.dmas("sp", pairs, writes=[bV[vi]])

                        units = []
                        ocount = 0
                        for n, (s, hp) in enumerate(items):
                            i = n % 2
                            for pi, d in enumerate(pats):
                                pidx = n * 3 + pi
                                vi = pidx % 2
                                nchunk = 32 // d
                                pre = None
                                if d == 1:
                                    Q, K, bQQ = qT[i], kT[i], bQ[i]
                                else:
                                    oi = ocount % 2
                                    ocount += 1
                                    Q, K, bQQ = qo[oi], ko[oi], bQo[oi]

                                    def pre(Q=Q, K=K, d=d, i=i, bQQ=bQQ):
                                        P.op("act", lambda e: e.copy(out=Q[:].rearrange("p (r m) -> p r m", r=d),
                                                                     in_=qT[i][:].rearrange("p (m r) -> p r m", r=d)),
                                             reads=[bQ[i]], writes=[bQQ])
                                        P.op("act", lambda e: e.copy(out=K[:].rearrange("p (r m) -> p r m", r=d),
                                                                     in_=kT[i][:].rearrange("p (m r) -> p r m", r=d)),
                                             reads=[bQ[i]], writes=[bQQ])
                                firstu = True
                                for hh in range(2):
                                    h = 2 * hp + hh
                                    pr = slice(hh * 64, hh * 64 + 64)
                                    accv = oacc[hh][:].rearrange("p (m r) -> p r m", r=d)
                                    zero_first = (pi == 0)
                                    for r in range(d):
                                        for kc in range(nchunk):
                                            u = ucnt[0]
                                            ucnt[0] += 1
                                            sj, ej, oj = u % NB, u % (NB + 1), u % NO
                                            j0, j1 = max(kc - 1, 0), min(kc + 1, nchunk - 1)
                                            g0 = j0 - (kc - 1)
                                            ncol = (j1 - j0 + 1) * 128
                                            c0, c1 = g0 * 128, g0 * 128 + ncol
                                            qbase = (r * nchunk + j0) * 128
                                            kbase = (r * nchunk + kc) * 128
                                            xo = (h * 3 + pi) * 384
                                            dst = accv[:, r, j0 * 128:(j1 + 1) * 128]
                                            zf = zero_first
                                            zero_first = False

                                            def qk(Q=Q, K=K, bQQ=bQQ, pr=pr, qbase=qbase, kbase=kbase, ncol=ncol, c0=c0, c1=c1, sj=sj):
                                                P.op("pe", lambda e: e.matmul(psS[sj][:, c0:c1], lhsT=K[pr, kbase:kbase + 128], rhs=Q[pr, qbase:qbase + ncol],
                                                                              start=True, stop=True), reads=[bQQ], writes=[bPS[sj]])

                                            def mid(sj=sj, ej=ej, c0=c0, c1=c1, xo=xo):
                                                P.op("act", lambda e: e.activation(out=E[ej][:, c0:c1], in_=psS[sj][:, c0:c1], func=AF.Exp, scale=SC_AB),
                                                     reads=[bPS[sj]], writes=[bE[ej]])
                                                P.op("pool", lambda e: e.tensor_tensor(out=E[ej][:, c0:c1], in0=E[ej][:, c0:c1], in1=t5X[:, xo + c0:xo + c1],
                                                                                       op=ALU.mult), reads=[bE[ej], bT5X], writes=[bE[ej]])

                                            def pv(vi=vi, r=r, nchunk=nchunk, kc=kc, hh=hh, ej=ej, oj=oj, dst=dst, c0=c0, c1=c1, zf=zf):
                                                if zf:
                                                    P.op("pool", lambda e: e.memset(oacc[hh][:], 0.0), writes=[bAcc[hh]])
                                                P.op("pe", lambda e: e.matmul(psO[oj][:, c0:c1], lhsT=va[vi][:, r * nchunk + kc, hh, :], rhs=E[ej][:, c0:c1],
                                                                              start=True, stop=True), reads=[bV[vi], bE[ej]], writes=[bPO[oj]])
                                                P.op("dve", lambda e: e.tensor_tensor(out=dst, in0=psO[oj][:, c0:c1], in1=dst, op=ALU.add),
                                                     reads=[bPO[oj], bAcc[hh]], writes=[bAcc[hh]])

                                            ud = dict(qk=qk, mid=mid, pv=pv)
                                            if firstu and pre is not None:
                                                ud["pre"] = pre
                                            firstu = False
                                            units.append(ud)
                                    if pi == 2:
                                        def fin(hh=hh, pr=pr, i=i):
                                            for q4 in range(4):
                                                cs = slice(q4 * 1024, (q4 + 1) * 1024)
                                                P.op("dve", lambda e: e.reciprocal(out=rec[:], in_=oacc[hh][64:128, cs]), reads=[bAcc[hh]], writes=[bRec])
                                                P.op("dve", lambda e: e.tensor_tensor(out=ot[i][pr, cs], in0=oacc[hh][0:64, cs], in1=rec[:], op=ALU.mult),
                                                     reads=[bAcc[hh], bRec], writes=[bOt[i]])
                                        units[-1]["fin"] = fin

                                def postp(pidx=pidx):
                                    if pidx + 2 < NPI:
                                        loadV(pidx + 2)
                                units[-1]["postp"] = postp

                            def posti(n=n, s=s, hp=hp, i=i):
                                P.dma("sp", D["OT"][s, 256 + hp * 128:256 + (hp + 1) * 128, :], ot[i][:], reads=[bOt[i]])
                                if n + 2 < len(items):
                                    loadQK(n + 2)
                            units[-1]["posti"] = posti
                        for ud in units:
                            hooks = [ud[k] for k in ("fin", "postp", "posti") if k in ud]
                            if hooks:
                                ud["post"] = (lambda hooks=hooks: [hk() for hk in hooks])
                        loadQK(0)
                        loadQK(1)
                        loadV(0)
                        loadV(1)
                        run_pipeline(units, NB)
                        P.barrier()
                P.barrier()

                with ExitStack() as c:
                    psS = [PS(c, "psSc%d" % i, [128, 1024]) for i in range(2)]
                    psO = [PS(c, "psOc%d" % i, [128, 1024]) for i in range(2)]
                    E = [SB(c, "Ec%d" % i, [128, 1024], BF16) for i in range(3)]
                    qT = [SB(c, "cq%d" % i, [96, S], BF16) for i in range(2)]
                    kT = [SB(c, "ck%d" % i, [96, S], BF16) for i in range(2)]
                    va = [SB(c, "cv%d" % i, [128, 32, 128], BF16) for i in range(2)]
                    ot = [SB(c, "cot%d" % i, [128, S], BF16) for i in range(2)]
                    rec = SB(c, "crec", [64, 1024], F32)
                    bE = [Buf() for _ in range(3)]
                    bPS = [Buf(), Buf()]
                    bPO = [Buf(), Buf()]
                    bQ = [Buf(), Buf()]
                    bV = [Buf(), Buf()]
                    bOt = [Buf(), Buf()]
                    bRec = Buf()
                    for i in range(2):
                        P.op("pool", lambda e, i=i: e.memset(va[i][:, :, 64:128], 1.0), writes=[bV[i]])
                    items = [(s, h) for s in range(2) for h in range(4)]

                    def loadC(n):
                        s, h = items[n]
                        i = n % 2
                        P.dmas("sp", [(qT[i][0:64, :], D["qn"][s, h // 2, (h % 2) * 64:(h % 2) * 64 + 64, :]),
                                      (qT[i][64:96, :], D["qr"][s, h * 32:(h + 1) * 32, :]),
                                      (kT[i][0:64, :], D["kn"][s, h // 2, (h % 2) * 64:(h % 2) * 64 + 64, :]),
                                      (kT[i][64:96, :], D["kr"][s, :, :])], writes=[bQ[i]])
                        src = D["vc"][s].rearrange("(n p) c -> p n c", p=128)
                        P.dmas("sp", [(va[i][:, n0:n0 + 8, 0:64], src[:, n0:n0 + 8, h * 64:(h + 1) * 64]) for n0 in range(0, 32, 8)], writes=[bV[i]])

                    units = []
                    u = 0
                    blk = 0
                    for n, (s, h) in enumerate(items):
                        i = n % 2
                        oi = (n // 2) % 2
                        pr = slice((h % 2) * 64, (h % 2) * 64 + 64)
                        for qb in range(4):
                            oj = blk % 2
                            blk += 1
                            cs = slice(qb * 1024, (qb + 1) * 1024)
                            for kc in range(32):
                                sj, ej = u % 2, u % 3
                                u += 1

                                def qk(i=i, kc=kc, qb=qb, sj=sj):
                                    P.op("pe", [(lambda e, hf=hf: e.matmul(psS[sj][:, hf * 512:(hf + 1) * 512], lhsT=kT[i][0:96, kc * 128:(kc + 1) * 128],
                                                                           rhs=qT[i][0:96, qb * 1024 + hf * 512:qb * 1024 + (hf + 1) * 512],
                                                                           start=True, stop=True)) for hf in range(2)],
                                         reads=[bQ[i]], writes=[bPS[sj]])

                                def mid(sj=sj, ej=ej):
                                    P.op("act", lambda e: e.activation(out=E[ej][:], in_=psS[sj][:], func=AF.Exp, scale=SC_C),
                                         reads=[bPS[sj]], writes=[bE[ej]])

                                def pv(i=i, kc=kc, ej=ej, oj=oj, oi=oi, pr=pr, cs=cs):
                                    P.op("pe", [(lambda e, hf=hf: e.matmul(psO[oj][:, hf * 512:(hf + 1) * 512], lhsT=va[i][:, kc, :],
                                                                           rhs=E[ej][:, hf * 512:(hf + 1) * 512], start=(kc == 0), stop=(kc == 31)))
                                                for hf in range(2)], reads=[bV[i], bE[ej]], writes=[bPO[oj]])
                                    if kc == 31:
                                        P.op("dve", lambda e: e.reciprocal(out=rec[:], in_=psO[oj][64:128, :]), reads=[bPO[oj]], writes=[bRec])
                                        P.op("dve", lambda e: e.tensor_tensor(out=ot[oi][pr, cs], in0=psO[oj][0:64, :], in1=rec[:], op=ALU.mult),
                                             reads=[bPO[oj], bRec], writes=[bOt[oi]])

                                units.append(dict(qk=qk, mid=mid, pv=pv))

                        def post(n=n, s=s, h=h, oi=oi):
                            if h % 2 == 1:
                                P.dma("sp", D["OT"][s, 768 + (h // 2) * 128:768 + (h // 2 + 1) * 128, :], ot[oi][:], reads=[bOt[oi]])
                            if n + 2 < len(items):
                                loadC(n + 2)
                        units[-1]["post"] = post
                    loadC(0)
                    loadC(1)
                    run_pipeline(units, 2)
                P.barrier()
                if "stopB" in dbg:
                    break

                with ExitStack() as c:
                    w_out = SB(c, "w_out", [128, 8, DM], BF16)
                    w_dn = SB(c, "w_dn", [128, NCH, DM], BF16)
                    wu = [SB(c, "wu%d" % i, [128, 8, 256], BF16) for i in range(3)]
                    cw = SB(c, "convw", [128, NCH, 3], F32)
                    cb = SB(c, "convb", [128, NCH], F32)
                    gpre = SB(c, "gpreF", [128, 8], F32)
                    scF = [SB(c, "scF%d" % s, [128, 8], F32) for s in range(2)]
                    biF = [SB(c, "biF%d" % s, [128, 8], F32) for s in range(2)]
                    gate1 = SB(c, "gate1", [128, DM], F32)
                    gate2 = SB(c, "gate2", [128, DM], F32)
                    xm = [SB(c, "xm%d" % i, [128, DM], F32) for i in range(2)]
                    bXm = [Buf(), Buf()]
                    bXD = {}
                    xg = [SB(c, "xgC%d" % i, [128, 4, DM], F32) for i in range(2)]
                    otg = [SB(c, "otg%d" % i, [128, 8, 512], BF16) for i in range(2)]
                    h2T = [SB(c, "h2T%d" % i, [128, 8, 512], BF16) for i in range(2)]
                    halo = SB(c, "halo", [128, 8, 2], BF16)
                    edge = [SB(c, "edge%d" % i, [128, 8, 2], BF16) for i in range(3)]
                    actT = SB(c, "actT", [128, NCH, 512], BF16)
                    xn2 = SB(c, "xn2", [128, 4, DM], BF16)
                    junk = SB(c, "junkC", [128, DM], BF16)
                    tmp = SB(c, "tmpC", [128, DM], F32)
                    gcb = [SB(c, "gc%d" % i, [128, 512], F32) for i in range(2)]
                    geb = [SB(c, "ge%d" % i, [128, 512], F32) for i in range(2)]
                    ssv = SB(c, "ssC", [128, 16], F32)
                    lnc = SB(c, "lnC", [128, 16], F32)
                    rsc = SB(c, "rsC", [128, 16], F32)
                    psYs = [PS(c, "psY%d" % i, [128, 1024]) for i in range(2)]
                    psT1 = PS(c, "psTC", [128, 512], BF16)
                    psTs = [psT1[:], psT1[:]]
                    psHt = PS(c, "psH", [128, 16])
                    psH = psHt[:, 0:2]
                    psHT = psHt[:, 8:16]
                    bPHT = Buf()
                    hT8 = SB(c, "hT8", [128, 8], F32)
                    bHT8 = Buf()
                    psA1 = PS(c, "psA", [128, 512])
                    psG1 = PS(c, "psG", [128, 512])
                    g_sb = [SB(c, "g_sb%d" % i, [128, 512], F32) for i in range(2)]
                    bGsb = [Buf(), Buf()]
                    bPG1 = Buf()
                    a_sb = [SB(c, "a_sb%d" % i, [128, 512], BF16) for i in range(2)]
                    hsb = SB(c, "hsb", [128, 2], F32)
                    bHsb = Buf()
                    bAsb = [Buf(), Buf()]
                    ycnt = [0]
                    bWo, bWd, bCw, bGp = Buf(), Buf(), Buf(), Buf()
                    bWu = [Buf() for _ in range(3)]
                    bScF = [Buf(), Buf()]
                    bGate, bGt = Buf(), Buf()
                    bXg = [Buf(), Buf()]
                    bOtg = [Buf(), Buf()]
                    bH2 = [Buf(), Buf()]
                    bEdge = [Buf(), Buf(), Buf()]
                    bHalo, bAct, bXn2, bJ, bTmp = Buf(), Buf(), Buf(), Buf(), Buf()
                    bActLo = Buf()
                    bGc = [Buf(), Buf()]
                    bGe = [Buf(), Buf()]
                    bSs, bLn, bRs = Buf(), Buf(), Buf()
                    bPYs = [Buf(), Buf()]
                    bPT1 = Buf()
                    bPTs = [bPT1, bPT1]
                    bPH = Buf()
                    bPA1 = Buf()
                    bPG = [Buf(), Buf()]

                    P.dmas("pool", [(w_out[:, k, :], D["w_out"][l, :, k, :]) for k in range(8)], writes=[bWo])
                    P.dmas("pool", [(w_dn[:, c0:c0 + 2, :], D["w_down"][l, :, c0:c0 + 2, :]) for c0 in range(0, NCH, 2)], writes=[bWd])
                    P.dmas("sp", [(cw[:], D["convwT"][l]), (cb[:], D["convbT"][l])], writes=[bCw])
                    P.dma("sp", gpre[:], D["gpre_ffnT"][l], writes=[bGp])
                    for s in range(2):
                        load_vec8(c, biF[s][:], l, s, 3, bScF[s])
                        load_vec8(c, scF[s][:], l, s, 4, bScF[s])
                        P.op("dve", lambda e, s=s: e.scalar_tensor_tensor(out=scF[s][:], in0=scF[s][:], scalar=1.0, in1=gpre[:], op0=ALU.add, op1=ALU.mult),
                             reads=[bScF[s], bGp], writes=[bScF[s]])

                    def load_gates(s):
                        for gt, jm, gp in ((gate1, 2, "gpost_mix"), (gate2, 5, "gpost_ffn")):
                            P.dma("sp", gt[:], D["modD"][l, s:s + 1, jm * 1024:(jm + 1) * 1024].partition_broadcast(128), writes=[bGate])
                            P.dma("sp", tmp[:], D[gp][l:l + 1, :].partition_broadcast(128), writes=[bTmp])
                            P.op("dve", lambda e, gt=gt: e.tensor_tensor(out=gt[:], in0=gt[:], in1=tmp[:], op=ALU.mult), reads=[bGate, bTmp], writes=[bGate])

                    def loadOT(g):
                        s, t0 = g // 8, (g % 8) * 512
                        j = g % 2
                        P.dma("sp", otg[j][:], D["OT"][s, :, t0:t0 + 512].rearrange("(k p) t -> p k t", p=128), writes=[bOtg[j]])

                    def loadX(g):
                        s, t0 = g // 8, (g % 8) * 512
                        j = g % 2
                        P.dmas("sp", [(xg[j][:, t, :], x_src[s, t0 + t * 128:t0 + (t + 1) * 128, :]) for t in range(4)], writes=[bXg[j]])

                    def norm_resid(xap, gate, col, bX, psY, bPY):
                        P.op("act", lambda e: e.activation(out=junk[:], in_=psY[:], func=AF.Square, accum_out=ssv[:, col:col + 1]),
                             reads=[bPY], writes=[bJ, bSs])
                        P.op("act", lambda e: e.activation(out=lnc[:, col:col + 1], in_=ssv[:, col:col + 1], func=AF.Ln, scale=1.0 / DM, bias=EPS),
                             reads=[bSs], writes=[bLn])
                        P.op("act", lambda e: e.activation(out=rsc[:, col:col + 1], in_=lnc[:, col:col + 1], func=AF.Exp, scale=-0.5),
                             reads=[bLn], writes=[bRs])
                        P.op("dve", lambda e: e.scalar_tensor_tensor(out=tmp[:], in0=psY[:], scalar=rsc[:, col:col + 1], in1=gate[:], op0=ALU.mult, op1=ALU.mult),
                             reads=[bPY, bRs, bGate], writes=[bTmp])
                        P.op("dve", lambda e: e.tensor_tensor(out=xap, in0=xap, in1=tmp[:], op=ALU.add), reads=[bTmp, bX], writes=[bX])

                    def stage1a(g):
                        s, t0 = g // 8, (g % 8) * 512
                        j = g % 2
                        X = xg[j]

                        def second(t):
                            P.op("act", lambda e: e.activation(out=junk[:], in_=X[:, t, :], func=AF.Square, accum_out=ssv[:, 4 + t:5 + t]),
                                 reads=[bXg[j]], writes=[bJ, bSs])
                            P.op("act", lambda e: e.activation(out=lnc[:, 4 + t:5 + t], in_=ssv[:, 4 + t:5 + t], func=AF.Ln, scale=1.0 / DM, bias=EPS),
                                 reads=[bSs], writes=[bLn])
                            P.op("act", lambda e: e.activation(out=rsc[:, 4 + t:5 + t], in_=lnc[:, 4 + t:5 + t], func=AF.Exp, scale=-0.5), reads=[bLn], writes=[bRs])
                            P.op("dve", lambda e: e.tensor_scalar(out=xn2[:, t, :], in0=X[:, t, :], scalar1=rsc[:, 4 + t:5 + t], scalar2=None, op0=ALU.mult),
                                 reads=[bXg[j], bRs], writes=[bXn2])

                        for t in range(4):
                            yi = ycnt[0] % 2
                            ycnt[0] += 1
                            psY, bPY = psYs[yi], bPYs[yi]
                            fns = []
                            for hf in range(2):
                                for k in range(8):
                                    fns.append(lambda e, k=k, hf=hf, t=t, psY=psY: e.matmul(psY[:, hf * 512:(hf + 1) * 512], lhsT=otg[j][:, k, t * 128:(t + 1) * 128],
                                                                                   rhs=w_out[:, k, hf * 512:(hf + 1) * 512], start=(k == 0), stop=(k == 7)))
                            P.op("pe", fns, reads=[bOtg[j], bWo], writes=[bPY])
                            if t == 2:
                                P.op("pe", [(lambda e, k=k: e.matmul(psHT[:, k:k + 1], lhsT=xn2[0:1, 0, k * 128:(k + 1) * 128], rhs=ident[0:1, 0:1],
                                                                     start=True, stop=True)) for k in range(8)], reads=[bXn2, bId], writes=[bPHT])
                                P.op("act", lambda e: e.copy(out=hT8[:], in_=psHT), reads=[bPHT], writes=[bHT8])
                                P.op("dve", lambda e: e.tensor_tensor(out=hT8[:], in0=hT8[:], in1=scF[s][:], op=ALU.mult), reads=[bHT8, bScF[s]], writes=[bHT8])
                                ej3 = g % 3
                                P.op("dve", lambda e: e.tensor_tensor(out=edge[ej3][:, :, 0], in0=hT8[:], in1=biF[s][:], op=ALU.add),
                                     reads=[bHT8, bScF[s]], writes=[bEdge[ej3]])
                            norm_resid(X[:, t, :], gate1, t, bXg[j], psY, bPY)
                            if t >= 1:
                                second(t - 1)
                        second(3)
                        bXD[g] = Buf()
                        P.dma("sp", x_dst[s, t0:t0 + 512, :].rearrange("(t p) d -> p t d", p=128), X[:], reads=[bXg[j]], writes=[bXD[g]])

                    def tr(g, k):
                        s = g // 8
                        j = g % 2
                        psT, bPT = psTs[k % 2], bPTs[k % 2]
                        P.op("pe", [(lambda e, t=t: e.transpose(psT[:, t * 128:(t + 1) * 128], xn2[:, t, k * 128:(k + 1) * 128], ident[:]))
                                    for t in range(4)], reads=[bXn2, bId], writes=[bPT])
                        P.op("act", lambda e: e.activation(out=h2T[j][:, k, :], in_=psT[:], func=AF.Identity,
                                                           scale=scF[s][:, k:k + 1], bias=biF[s][:, k:k + 1]),
                             reads=[bPT, bScF[s]], writes=[bH2[j]])
                        if k == 7:
                            ej3 = g % 3
                            P.op("pool", lambda e: e.tensor_copy(out=edge[ej3][:, :, 1:2], in_=h2T[j][:, :, 511:512]), reads=[bH2[j]], writes=[bEdge[ej3]])

                    wcount = [0]

                    def load_wu(ci):
                        wi = wcount[0] % 3
                        wcount[0] += 1
                        P.dma("pool", wu[wi][:], D["w_up"][l, ci], writes=[bWu[wi]])
                        return wi

                    wu_pref = []

                    def prefetch_wu():
                        wu_pref.extend([load_wu(0), load_wu(1)])

                    def stage2_up(g, have_next):
                        s, t0 = g // 8, (g % 8) * 512
                        j = g % 2
                        X = xg[j]
                        if t0 > 0:
                            P.op("pool", lambda e: e.tensor_copy(out=halo[:, :, 0:1], in_=edge[(g - 1) % 3][:, :, 1:2]), reads=[bEdge[(g - 1) % 3]], writes=[bHalo])
                        else:
                            P.op("pool", lambda e: e.memset(halo[:, :, 0:1], 0.0), writes=[bHalo])
                        if t0 + 512 < S:
                            assert have_next
                            P.op("pool", lambda e: e.tensor_copy(out=halo[:, :, 1:2], in_=edge[(g + 1) % 3][:, :, 0:1]), reads=[bEdge[(g + 1) % 3]], writes=[bHalo])
                        else:
                            P.op("pool", lambda e: e.memset(halo[:, :, 1:2], 0.0), writes=[bHalo])
                        def finish_chunk(ci):
                            pj = ci % 2
                            gc, ge = gcb[pj], geb[pj]
                            P.op("act", lambda e: e.activation(out=ge[:], in_=gc[:], func=AF.Gelu_apprx_tanh), reads=[bGc[pj]], writes=[bGe[pj]])
                            P.op("pool", lambda e: e.tensor_tensor(out=actT[:, ci, :], in0=a_sb[pj][:], in1=ge[:], op=ALU.mult),
                                 reads=[bAsb[pj], bGe[pj]], writes=[bActLo if ci < NCH - 2 else bAct])

                        pend = list(wu_pref)
                        del wu_pref[:]
                        for ci in range(NCH):
                            wi = pend.pop(0)
                            if ci + 2 < NCH:
                                pend.append(load_wu(ci + 2))
                            pj = ci % 2
                            W = wu[wi]
                            P.op("pe", [(lambda e, k=k, W=W, pj=pj: e.matmul(psA1[:], lhsT=W[:, k, 0:128], rhs=h2T[j][:, k, :], start=(k == 0), stop=(k == 7)))
                                        for k in range(8)], reads=[bWu[wi], bH2[j]], writes=[bPA1])
                            P.op("act", lambda e, pj=pj: e.copy(out=a_sb[pj][:], in_=psA1[:]), reads=[bPA1], writes=[bAsb[pj]])
                            P.op("pe", [(lambda e, k=k, W=W: e.matmul(psG1[:], lhsT=W[:, k, 128:256], rhs=h2T[j][:, k, :], start=(k == 0), stop=(k == 7)))
                                        for k in range(8)], reads=[bWu[wi], bH2[j]], writes=[bPG1])
                            P.op("pe", [(lambda e, k=k, W=W: e.matmul(psH, lhsT=W[:, k, 128:256], rhs=halo[:, k, :], start=(k == 0), stop=(k == 7)))
                                        for k in range(8)], reads=[bWu[wi], bHalo], writes=[bPH])
                            gc, ge, gs = gcb[pj], geb[pj], g_sb[pj]
                            P.op("act", lambda e, gs=gs: e.copy(out=gs[:], in_=psG1[:]), reads=[bPG1], writes=[bGsb[pj]])
                            P.op("dve", lambda e, ci=ci, gc=gc, gs=gs: e.tensor_scalar(out=gc[:], in0=gs[:], scalar1=cw[:, ci, 1:2], scalar2=cb[:, ci:ci + 1],
                                                                                       op0=ALU.mult, op1=ALU.add),
                                 reads=[bGsb[pj], bCw], writes=[bGc[pj]])
                            P.op("dve", lambda e, ci=ci, gc=gc, gs=gs: e.scalar_tensor_tensor(out=gc[:, 1:512], in0=gs[:, 0:511], scalar=cw[:, ci, 0:1],
                                                                                              in1=gc[:, 1:512], op0=ALU.mult, op1=ALU.add),
                                 reads=[bGsb[pj], bCw, bGc[pj]], writes=[bGc[pj]])
                            P.op("dve", lambda e, ci=ci, gc=gc, gs=gs: e.scalar_tensor_tensor(out=gc[:, 0:511], in0=gs[:, 1:512], scalar=cw[:, ci, 2:3],
                                                                                              in1=gc[:, 0:511], op0=ALU.mult, op1=ALU.add),
                                 reads=[bGsb[pj], bCw, bGc[pj]], writes=[bGc[pj]])
                            P.op("act", lambda e: e.copy(out=hsb[:], in_=psH), reads=[bPH], writes=[bHsb])
                            P.op("dve", lambda e, ci=ci, gc=gc: e.scalar_tensor_tensor(out=gc[:, 0:1], in0=hsb[:, 0:1], scalar=cw[:, ci, 0:1], in1=gc[:, 0:1],
                                                                                       op0=ALU.mult, op1=ALU.add), reads=[bHsb, bCw, bGc[pj]], writes=[bGc[pj]])
                            P.op("dve", lambda e, ci=ci, gc=gc: e.scalar_tensor_tensor(out=gc[:, 511:512], in0=hsb[:, 1:2], scalar=cw[:, ci, 2:3], in1=gc[:, 511:512],
                                                                                       op0=ALU.mult, op1=ALU.add), reads=[bHsb, bCw, bGc[pj]], writes=[bGc[pj]])
                            if ci >= 1:
                                finish_chunk(ci - 1)
                        finish_chunk(NCH - 1)
                        if g + 1 < 16:
                            prefetch_wu()
                    def stage2_down(g, trg):
                        s, t0 = g // 8, (g % 8) * 512

                        def load_xm(t):
                            P.dma("sp", xm[t % 2][:], x_dst[s, t0 + t * 128:t0 + (t + 1) * 128, :], reads=[bXD[g]], writes=[bXm[t % 2]])

                        load_xm(0)
                        load_xm(1)
                        for t in range(4):
                            if trg is not None:
                                tr(trg, 2 * t)
                            yi = ycnt[0] % 2
                            ycnt[0] += 1
                            psY, bPY = psYs[yi], bPYs[yi]
                            NLO = NCH - 2
                            fns = []
                            for hf in range(2):
                                for ci in range(NLO):
                                    fns.append(lambda e, ci=ci, hf=hf, t=t, psY=psY: e.matmul(psY[:, hf * 512:(hf + 1) * 512], lhsT=actT[:, ci, t * 128:(t + 1) * 128],
                                                                                     rhs=w_dn[:, ci, hf * 512:(hf + 1) * 512], start=(ci == 0), stop=False))
                            P.op("pe", fns, reads=[bActLo, bWd], writes=[bPY])
                            if trg is not None:
                                tr(trg, 2 * t + 1)
                            fns = []
                            for hf in range(2):
                                for ci in range(NLO, NCH):
                                    fns.append(lambda e, ci=ci, hf=hf, t=t, psY=psY: e.matmul(psY[:, hf * 512:(hf + 1) * 512], lhsT=actT[:, ci, t * 128:(t + 1) * 128],
                                                                                     rhs=w_dn[:, ci, hf * 512:(hf + 1) * 512], start=False, stop=(ci == NCH - 1)))
                            P.op("pe", fns, reads=[bAct, bWd], writes=[bPY])
                            norm_resid(xm[t % 2][:], gate2, 8 + t, bXm[t % 2], psY, bPY)
                            P.dma("sp", x_dst[s, t0 + t * 128:t0 + (t + 1) * 128, :], xm[t % 2][:], reads=[bXm[t % 2]], writes=[bXD[g]])
                            if t + 2 < 4:
                                load_xm(t + 2)

                    NG = 16
                    load_gates(0)
                    loadOT(0)
                    loadX(0)
                    loadOT(1)
                    loadX(1)
                    stage1a(0)
                    for k in range(8):
                        tr(0, k)
                    prefetch_wu()
                    for g in range(NG):
                        s = g // 8
                        nxt_same_seq = (g + 1 < NG) and ((g + 1) // 8 == s)
                        if nxt_same_seq:
                            stage1a(g + 1)
                            if g + 2 < NG:
                                loadOT(g + 2)
                                loadX(g + 2)
                        stage2_up(g, nxt_same_seq)
                        stage2_down(g, (g + 1) if nxt_same_seq else None)
                        if g + 1 < NG and not nxt_same_seq:
                            load_gates((g + 1) // 8)
                            stage1a(g + 1)
                            for k in range(8):
                                tr(g + 1, k)
                            if g + 2 < NG:
                                loadOT(g + 2)
                                loadX(g + 2)
                P.barrier()

        except _Stop:
            pass
        P.barrier()
    return nc


def _prep_shared(inp):
    f = lambda a: np.ascontiguousarray(np.asarray(a, dtype=np.float32))
    L = 4
    sh = {}
    sh["w_ada"] = f(np.asarray(inp["w_ada"]).reshape(L, 8, 128, 6144).transpose(0, 2, 1, 3))
    sh["b_ada"] = f(inp["b_ada"])
    sh["gpre_mixT"] = f(np.asarray(inp["g_pre_mix"]).reshape(L, 8, 128).transpose(0, 2, 1))
    sh["gpre_ffnT"] = f(np.asarray(inp["g_pre_ffn"]).reshape(L, 8, 128).transpose(0, 2, 1))
    sh["gpost_mix"] = f(inp["g_post_mix"])
    sh["gpost_ffn"] = f(inp["g_post_ffn"])
    sh["w_in"] = f(np.asarray(inp["w_in"]).reshape(L, 8, 128, D_IN).transpose(0, 2, 1, 3))
    wuq = np.asarray(inp["w_uq"]).reshape(L, 384, 4, 96)
    wuq = np.concatenate([wuq[..., :64].reshape(L, 384, 256), wuq[..., 64:].reshape(L, 384, 128)], axis=-1)
    sh["w_uq"] = f(wuq.reshape(L, 3, 128, 384).transpose(0, 2, 1, 3))
    wukv = np.asarray(inp["w_ukv"]).reshape(L, 256, 4, 128)
    wukv = np.concatenate([wukv[..., :64].reshape(L, 256, 256), wukv[..., 64:].reshape(L, 256, 256)], axis=-1)
    sh["w_ukv"] = f(wukv.reshape(L, 2, 128, 512).transpose(0, 2, 1, 3))
    sh["gqT"] = f(np.asarray(inp["mla_g_q"]).reshape(L, 3, 128).transpose(0, 2, 1))
    sh["gkvT"] = f(np.asarray(inp["mla_g_kv"]).reshape(L, 2, 128).transpose(0, 2, 1))
    sh["w_out"] = f(np.asarray(inp["w_out"]).reshape(L, 8, 128, DM).transpose(0, 2, 1, 3))
    wup = np.asarray(inp["w_up"]).reshape(L, 8, 128, 2, NCH, 128)
    sh["w_up"] = f(wup.transpose(0, 4, 2, 1, 3, 5).reshape(L, NCH, 128, 8, 256))
    sh["w_down"] = f(np.asarray(inp["w_down"]).reshape(L, NCH, 128, DM).transpose(0, 2, 1, 3))
    sh["convwT"] = f(np.asarray(inp["conv_w"]).reshape(L, 3, NCH, 128).transpose(0, 3, 2, 1))
    sh["convbT"] = f(np.asarray(inp["conv_b"]).reshape(L, NCH, 128).transpose(0, 2, 1))
    G, cm, T, tm = _host_tables(np.asarray(inp["na_rpb"], np.float32), np.asarray(inp["t5_table"], np.float32))
    sh["naG"] = f(G)
    sh["nacm"] = f(cm)
    sh["t5G"] = f(T)
    sh["t5m"] = f(tm)
    sh["pens"] = f(NA_PENS)
    sh["navm"] = f(np.repeat((NA_PENS == 0).astype(np.float32), 64, axis=0))
    A2 = np.zeros((2, 128), np.float32)
    A2[0, :64] = 1.0
    A2[1, 64:] = 1.0
    sh["A2"] = A2
    cosT, sinT = _rope_tables()
    sh["cosT"] = f(cosT)
    sh["sinT"] = f(sinT)
    return sh


def _in_maps(inp, ncores):
    sh = _prep_shared(inp)
    x = np.asarray(inp["x"], np.float32)
    cc = np.asarray(inp["c"], np.float32)
    maps = []
    for i in range(ncores):
        m = dict(sh)
        m["x"] = np.ascontiguousarray(x[2 * i:2 * i + 2])
        m["cT"] = np.ascontiguousarray(cc[2 * i:2 * i + 2].reshape(2, 8, 128).transpose(2, 1, 0))
        maps.append(m)
    return maps


_NC_CACHE = {}


def kernel(**inputs):
    if "nc" not in _NC_CACHE:
        _NC_CACHE["nc"] = build()
    nc = _NC_CACHE["nc"]
    maps = _in_maps(inputs, NCORES)
    res = run_bass_kernel_spmd(nc, maps, core_ids=list(range(NCORES)))
    out = np.concatenate([np.asarray(r["out"], dtype=np.float32) for r in res.results], axis=0)
    return out
```

```python
import math
from contextlib import ExitStack

import numpy as np
import ml_dtypes
import concourse.bass as bass
import concourse.mybir as mybir
from concourse.bass_utils import run_bass_kernel_spmd

F32 = mybir.dt.float32
BF16 = mybir.dt.bfloat16
AF = mybir.ActivationFunctionType
ALU = mybir.AluOpType

NCORES = 8
S = 4096
DM = 1024
DFF = 2816
NCH = 22
D_IN = 2976
EPS = 1e-6
NEG = -30000.0


class Buf:
    __slots__ = ("lw", "rd")

    def __init__(self):
        self.lw = []
        self.rd = {}


class Prog:
    def __init__(self, nc, ctx, n_dma_sems=40):
        self.nc = nc
        self.E = {"pe": nc.tensor, "act": nc.scalar, "dve": nc.vector, "pool": nc.gpsimd, "sp": nc.sync}
        self.psem = {k: ctx.enter_context(nc.semaphore("ps_" + k)) for k in self.E}
        self.pcnt = {k: 0 for k in self.E}
        self.seen = {k: {} for k in self.E}
        self.dsem = [ctx.enter_context(nc.semaphore("ds%d" % i)) for i in range(n_dma_sems)]
        self.dcnt = [0] * n_dma_sems
        self.dnext = 0
        self.nins = 0
        self.dead = False

    def _wait(self, eng, deps):
        if self.dead:
            return
        need = {}
        for d in deps:
            if d is None:
                continue
            s, v = d
            if need.get(s, 0) < v:
                need[s] = v
        seen = self.seen[eng]
        for s, v in need.items():
            if seen.get(s, 0) < v:
                self.E[eng].wait_ge(s, v)
                seen[s] = v
                self.nins += 1

    @staticmethod
    def _deps(reads, writes):
        deps = []
        for b in reads:
            deps.extend(b.lw)
        for b in writes:
            deps.extend(b.lw)
            deps.extend(b.rd.values())
        return deps

    def op(self, eng, fns, reads=(), writes=()):
        if self.dead:
            return None
        self._wait(eng, self._deps(reads, writes))
        if callable(fns):
            fns = [fns]
        ins = None
        e = self.E[eng]
        for f in fns:
            ins = f(e)
            self.nins += 1
        self.pcnt[eng] += 1
        ins.then_inc(self.psem[eng], 1)
        tok = (self.psem[eng], self.pcnt[eng])
        for b in reads:
            b.rd[eng] = tok
        for b in writes:
            b.lw = [tok]
            b.rd = {}
        return tok

    def dma(self, eng, out, in_, reads=(), writes=(), **kw):
        return self.dmas(eng, [(out, in_)], reads, writes, **kw)

    def dmas(self, eng, pairs, reads=(), writes=(), **kw):
        if self.dead:
            return []
        deps = self._deps(reads, writes)
        idx = []
        for _ in pairs:
            i = self.dnext
            self.dnext = (i + 1) % len(self.dsem)
            idx.append(i)
            if self.dcnt[i] > 0:
                deps.append((self.dsem[i], 16 * self.dcnt[i]))
        assert len(set(idx)) == len(idx)
        self._wait(eng, deps)
        toks = []
        for i, (out, in_) in zip(idx, pairs):
            self.dcnt[i] += 1
            self.E[eng].dma_start(out=out, in_=in_, **kw).then_inc(self.dsem[i], 16)
            self.nins += 1
            toks.append((self.dsem[i], 16 * self.dcnt[i]))
        for b in reads:
            for i, tok in zip(idx, toks):
                b.rd[("dma", i)] = tok
        for b in writes:
            b.lw = list(toks)
            b.rd = {}
        return toks

    def barrier(self):
        deps = [(self.psem[k], self.pcnt[k]) for k in self.E if self.pcnt[k] > 0]
        deps += [(s, 16 * c) for s, c in zip(self.dsem, self.dcnt) if c > 0]
        for k in self.E:
            self._wait(k, deps)


def _r_start(r):
    return min(max(r - 4, 0), 56)


def _c_start(c):
    return min(max(c - 8, 0), 48)


def _na_plan():
    pats = {}
    plan = []
    for J in range(16):
        klo = _r_start(4 * J)
        khi = _r_start(4 * J + 3) + 7
        units = []
        for i in range(klo // 2, khi // 2 + 1):
            key = []
            for a in range(2):
                for ap in range(4):
                    r = 4 * J + ap
                    kr = 2 * i + a
                    key.append(_r_start(r) <= kr <= _r_start(r) + 7)
            key = tuple(key)
            if all(key):
                pid = -1
            else:
                if key not in pats:
                    pats[key] = len(pats)
                pid = pats[key]
            di0 = 4 * J - 2 * i + 6
            assert 0 <= di0 <= 10
            units.append((i, di0, pid))
        plan.append(units)
    npat = len(pats)
    pens = np.zeros((2, npat, 4, 64), np.float32)
    for key, pid in pats.items():
        for a in range(2):
            for ap in range(4):
                if not key[a * 4 + ap]:
                    pens[a, pid, ap, :] = NEG
    return plan, pens.reshape(2, npat * 256)


NA_PLAN, NA_PENS = _na_plan()
NPAT = NA_PENS.shape[1] // 256


def _t5_bucket(rel):
    nb = 16
    max_exact = 8
    n = np.abs(rel)
    large = max_exact + (np.log(np.maximum(n, 1) / max_exact) / math.log(1024 / max_exact) * (nb - max_exact)).astype(np.int64)
    large = np.minimum(large, nb - 1)
    return (np.where(rel > 0, nb, 0) + np.where(n < max_exact, n, large)).astype(np.int32)


def _host_tables(na_rpb, t5_table):
    p = np.arange(128)
    a = p // 64
    kc = p % 64
    di = np.arange(14)
    c = np.arange(64)
    drow = a[:, None] - (di[None, :] - 6) + 7
    dcol = kc[:, None] - c[None, :] + 15
    ok = ((drow >= 0) & (drow <= 14))[:, :, None] & ((dcol >= 0) & (dcol <= 30))[:, None, :]
    drc = np.clip(drow, 0, 14)
    dcc = np.clip(dcol, 0, 30)
    G = na_rpb[:, :, drc[:, :, None], dcc[:, None, :]]
    G = np.where(ok[None, None], G, np.float32(0.0)).astype(np.float32)
    G = np.ascontiguousarray(np.transpose(G, (0, 2, 1, 3, 4))).reshape(na_rpb.shape[0], 128, 4 * 14 * 64)
    cs = np.array([_c_start(x) for x in range(64)])
    cm = ((kc[:, None] >= cs[None, :]) & (kc[:, None] < cs[None, :] + 16)).astype(np.float32)
    cm = np.ascontiguousarray(np.broadcast_to(cm[:, None, :], (128, 14, 64))).reshape(128, 14 * 64)
    q = np.arange(128)
    pc = np.arange(3)
    rel = 128 * (1 - pc[None, :, None]) + p[:, None, None] - q[None, None, :]
    valid = (np.abs(rel) <= 64)
    T = np.zeros((128, 8, 3, 3, 128), np.float32)
    for pi, d in enumerate((1, 4, 16)):
        b = _t5_bucket(rel * d)
        vals = t5_table[b]
        vals = np.where(valid[..., None], vals, np.float32(0.0))
        T[:, :, pi] = np.transpose(vals, (0, 3, 1, 2))
    T = T.reshape(128, 8 * 3 * 384)
    tm = valid.astype(np.float32).reshape(128, 384)
    return G, cm, T, tm


def _rope_tables():
    inv_freq = (10000.0 ** (-np.arange(0, 32, 2, dtype=np.float32) / 32)).astype(np.float32)
    ang = np.arange(S, dtype=np.float32)[:, None] * inv_freq[None, :]
    cos = np.cos(ang).astype(np.float32)
    sin = np.sin(ang).astype(np.float32)
    idx = (np.arange(128) % 32) % 16
    return np.ascontiguousarray(cos[:, idx].T), np.ascontiguousarray(sin[:, idx].T)


class _Stop(Exception):
    pass


def build(NL=4, dbg=()):
    nc = bass.Bass("TRN2", target_bir_lowering=False)
    stop_at = -1
    for dflag in dbg:
        if dflag.startswith("stage="):
            stop_at = int(dflag[6:])

    PP = []

    def ckpt(n):
        if n == stop_at and not PP[0].dead:
            PP[0].barrier()
            PP[0].dead = True
    D = {}

    def din(name, shape, dt=F32):
        D[name] = nc.dram_tensor(name, list(shape), dt, kind="ExternalInput").ap()

    def dscr(name, shape, dt):
        kind = "ExternalOutput" if name in dbg else "Internal"
        D[name] = nc.dram_tensor(name, list(shape), dt, kind=kind).ap()

    din("x", [2, S, DM])
    din("cT", [128, 8, 2])
    din("w_ada", [4, 128, 8, 6144])
    din("b_ada", [4, 6144])
    din("gpre_mixT", [4, 128, 8])
    din("gpre_ffnT", [4, 128, 8])
    din("gpost_mix", [4, DM])
    din("gpost_ffn", [4, DM])
    din("w_in", [4, 128, 8, D_IN])
    din("w_uq", [4, 128, 3, 384])
    din("w_ukv", [4, 128, 2, 512])
    din("gqT", [4, 128, 3])
    din("gkvT", [4, 128, 2])
    din("w_out", [4, 128, 8, DM])
    din("w_up", [4, NCH, 128, 8, 256])
    din("w_down", [4, 128, NCH, DM])
    din("convwT", [4, 128, NCH, 3])
    din("convbT", [4, 128, NCH])
    din("naG", [4, 128, 4 * 14 * 64])
    din("nacm", [128, 14 * 64])
    din("pens", [2, NPAT * 256])
    din("navm", [128, NPAT * 256])
    din("A2", [2, 128])
    din("t5G", [128, 8 * 3 * 384])
    din("t5m", [128, 384])
    din("cosT", [128, S])
    din("sinT", [128, S])
    D["out"] = nc.dram_tensor("out", [2, S, DM], F32, kind="ExternalOutput").ap()
    dscr("modD", [4, 2, 6144], F32)
    dscr("xs", [2, S, DM], F32)
    dscr("zT", [2, 12, 128, S], BF16)
    dscr("vtok", [2, S, 768], BF16)
    dscr("qn", [2, 2, 128, S], BF16)
    dscr("qr", [2, 128, S], BF16)
    dscr("kn", [2, 2, 128, S], BF16)
    dscr("kr", [2, 32, S], BF16)
    dscr("vc", [2, S, 256], BF16)
    dscr("OT", [2, DM, S], BF16)

    with ExitStack() as gctx:
        P = Prog(nc, gctx)
        PP.append(P)

        uid = [0]

        def SB(ctx, name, shape, dt):
            uid[0] += 1
            return ctx.enter_context(nc.sbuf_tensor("%s_u%d" % (name, uid[0]), list(shape), dt))

        def PS(ctx, name, shape, dt=F32):
            uid[0] += 1
            return ctx.enter_context(nc.psum_tensor("%s_u%d" % (name, uid[0]), list(shape), dt))

        zcol = SB(gctx, "zcol", [128, 1], F32)
        bZc = Buf()
        ident = SB(gctx, "ident", [128, 128], BF16)
        identf = SB(gctx, "identf", [128, 128], F32)
        bId = Buf()
        P.op("pool", lambda e: e.memset(zcol[:], 0.0), writes=[bZc])
        P.op("pool", lambda e: e.memset(identf[:], 1.0), writes=[bId])
        P.op("pool", lambda e: e.affine_select(out=identf[:], in_=identf[:], pattern=[[-1, 128]],
                                                compare_op=ALU.is_equal, fill=0.0, base=0, channel_multiplier=1),
             reads=[bId], writes=[bId])
        P.op("pool", lambda e: e.tensor_copy(out=ident[:], in_=identf[:]), reads=[bId], writes=[bId])

        def rstd_from_ss(ss_ap, out_ap, n, bss, bout, tmp_ap, btmp):
            P.op("act", lambda e: e.activation(out=tmp_ap, in_=ss_ap, func=AF.Ln, scale=1.0 / n, bias=EPS),
                 reads=[bss], writes=[btmp])
            P.op("act", lambda e: e.activation(out=out_ap, in_=tmp_ap, func=AF.Exp, scale=-0.5),
                 reads=[btmp], writes=[bout])

        try:
            with ExitStack() as c:
                cT = SB(c, "cT_sb", [128, 8, 2], F32)
                cact = SB(c, "cact", [128, 8, 2], BF16)
                brow = SB(c, "brow", [2, 6144], F32)
                mrow = SB(c, "mrow", [2, 6144], F32)
                wt = [SB(c, "wada%d" % i, [128, 8, 1024], BF16) for i in range(3)]
                psM = [PS(c, "psM%d" % i, [2, 512]) for i in range(2)]
                bc, bb, bm = Buf(), Buf(), Buf()
                bw = [Buf(), Buf(), Buf()]
                bp = [Buf(), Buf()]
                P.dma("sp", cT[:], D["cT"], writes=[bc])
                P.op("act", lambda e: e.activation(out=cact[:], in_=cT[:], func=AF.Silu), reads=[bc], writes=[bc])
                it = 0
                ip = 0
                for l in range(NL):
                    P.dma("sp", brow[:], D["b_ada"][l:l + 1, :].partition_broadcast(2), writes=[bb])
                    for nb in range(6):
                        j = it % 3
                        it += 1
                        P.dmas("pool", [(wt[j][:, k0:k0 + 4, :], D["w_ada"][l, :, k0:k0 + 4, nb * 1024:(nb + 1) * 1024]) for k0 in (0, 4)], writes=[bw[j]])
                        for hf in range(2):
                            pj = ip % 2
                            ip += 1
                            cs = slice(nb * 1024 + hf * 512, nb * 1024 + (hf + 1) * 512)
                            P.op("pe", [(lambda e, k=k, j=j, pj=pj, hf=hf: e.matmul(psM[pj][:], lhsT=cact[:, k, :], rhs=wt[j][:, k, hf * 512:(hf + 1) * 512],
                                                                                    start=(k == 0), stop=(k == 7))) for k in range(8)],
                                 reads=[bc, bw[j]], writes=[bp[pj]])
                            P.op("dve", lambda e, pj=pj, cs=cs: e.tensor_tensor(out=mrow[:, cs], in0=psM[pj][:], in1=brow[:, cs], op=ALU.add),
                                 reads=[bp[pj], bb], writes=[bm])
                    P.dma("sp", D["modD"][l], mrow[:], reads=[bm])
            P.barrier()
            ckpt(1)

            def load_vec8(ctx_name, dst, l, s, j, bdst):
                src = D["modD"][l, s, j * 1024:(j + 1) * 1024].rearrange("(k p) -> p k", p=128)
                P.dma("sp", dst, src, writes=[bdst], allow_slow_non_contiguous=True)

            for l in range(NL):
                x_src = D["x"] if l == 0 else D["xs"]
                x_dst = D["out"] if l == NL - 1 else D["xs"]

                with ExitStack() as c:
                    w_in = SB(c, "w_in", [128, 8, D_IN], BF16)
                    w_rot = SB(c, "w_rot", [128, 8, 32], BF16)
                    w_uq = SB(c, "w_uq", [128, 3, 384], BF16)
                    w_uqr = SB(c, "w_uqr", [128, 3, 128], BF16)
                    w_ukv = SB(c, "w_ukv", [128, 2, 512], BF16)
                    gq = SB(c, "gq", [128, 3], F32)
                    gkv = SB(c, "gkv", [128, 2], F32)
                    gpre = SB(c, "gpre", [128, 8], F32)
                    scA = [SB(c, "scA%d" % s, [128, 8], F32) for s in range(2)]
                    biA = [SB(c, "biA%d" % s, [128, 8], F32) for s in range(2)]
                    xg = [SB(c, "xgA%d" % i, [128, 4, DM], F32) for i in range(2)]
                    cosg = [SB(c, "cosg%d" % i, [128, 512], F32) for i in range(2)]
                    sing = [SB(c, "sing%d" % i, [128, 512], F32) for i in range(2)]
                    junk = SB(c, "junkA", [128, DM], BF16)
                    ss = SB(c, "ssA", [128, 8], F32)
                    lnv = SB(c, "lnvA", [128, 8], F32)
                    rstd = SB(c, "rstdA", [128, 8], F32)
                    ssq = SB(c, "ssq", [128, 4], F32)
                    sskv = SB(c, "sskv", [128, 4], F32)
                    lnq = SB(c, "lnq", [128, 4], F32)
                    lnk = SB(c, "lnk", [128, 4], F32)
                    rsq = SB(c, "rsq", [128, 4], F32)
                    rskv = SB(c, "rskv", [128, 4], F32)
                    xns = [SB(c, "xnA%d" % i, [128, 4, DM], BF16) for i in range(2)]
                    bXns = [Buf(), Buf()]
                    hT = SB(c, "hT", [128, 8, 512], BF16)
                    zst = [SB(c, "zst%d" % i, [128, 12, 512], BF16) for i in range(2)]
                    vst = [SB(c, "vst%d" % i, [128, 4, 768], BF16) for i in range(2)]
                    cst = SB(c, "cst", [128, 4, 640], F32)
                    cqn = SB(c, "cqn", [128, 4, 384], BF16)
                    ckvn = SB(c, "ckvn", [128, 4, 256], BF16)
                    cqT = SB(c, "cqT", [128, 3, 512], BF16)
                    ckvT = SB(c, "ckvT", [128, 2, 512], BF16)
                    qnst = [SB(c, "qnst%d" % i, [128, 2, 512], BF16) for i in range(2)]
                    knst = [SB(c, "knst%d" % i, [128, 2, 512], BF16) for i in range(2)]
                    qrst = [SB(c, "qrst%d" % i, [128, 512], BF16) for i in range(2)]
                    krst = [SB(c, "krst%d" % i, [32, 512], BF16) for i in range(2)]
                    vcst = [SB(c, "vcst%d" % i, [128, 4, 256], BF16) for i in range(2)]
                    t1 = SB(c, "t1A", [128, 512], F32)
                    t2 = SB(c, "t2A", [128, 512], F32)
                    psT = [PS(c, "psTA%d" % i, [128, 512], BF16) for i in range(2)]
                    psZ = [PS(c, "psZA%d" % i, [128, 512]) for i in range(2)]
                    psW = [PS(c, "psWA%d" % i, [128, 1024]) for i in range(2)]

                    bW, bWr, bUq, bUqr, bUkv, bG = Buf(), Buf(), Buf(), Buf(), Buf(), Buf()
                    bSc = [Buf(), Buf()]
                    bXg = [Buf(), Buf()]
                    bCS = [Buf(), Buf()]
                    bJ, bSS, bLn, bRs, bXn, bHT = Buf(), Buf(), Buf(), Buf(), Buf(), Buf()
                    bSq, bLq, bRq = Buf(), Buf(), Buf()
                    bZst = [Buf(), Buf()]
                    bVst = [Buf(), Buf()]
                    bCst, bCqn, bCkvn, bCqT, bCkvT = Buf(), Buf(), Buf(), Buf(), Buf()
                    bQn = [Buf(), Buf()]
                    bKn = [Buf(), Buf()]
                    bQr = [Buf(), Buf()]
                    bKr = [Buf(), Buf()]
                    bVc = [Buf(), Buf()]
                    bT1, bT2 = Buf(), Buf()
                    bPT = [Buf(), Buf()]
                    bPZ = [Buf(), Buf()]
                    bPW = [Buf(), Buf()]

                    P.dmas("pool", [(w_in[:, k, :], D["w_in"][l, :, k, :]) for k in range(8)], writes=[bW])
                    P.dma("pool", w_uq[:], D["w_uq"][l], writes=[bUq])
                    P.dma("pool", w_ukv[:], D["w_ukv"][l], writes=[bUkv])
                    P.dmas("sp", [(gq[:], D["gqT"][l]), (gkv[:], D["gkvT"][l]), (gpre[:], D["gpre_mixT"][l])], writes=[bG])
                    P.op("act", lambda e: e.mul(out=w_rot[:, :, 0:16], in_=w_in[:, :, 2960:2976], mul=-1.0), reads=[bW], writes=[bWr])
                    P.op("act", lambda e: e.copy(out=w_rot[:, :, 16:32], in_=w_in[:, :, 2944:2960]), reads=[bW], writes=[bWr])
                    for k in range(3):
                        src = w_uq[:, k, 256:384].rearrange("p (h t j) -> p h t j", t=2, j=16)
                        dst = w_uqr[:, k, :].rearrange("p (h t j) -> p h t j", t=2, j=16)
                        P.op("act", lambda e, src=src, dst=dst: e.mul(out=dst[:, :, 0, :], in_=src[:, :, 1, :], mul=-1.0),
                             reads=[bUq], writes=[bUqr])
                        P.op("act", lambda e, src=src, dst=dst: e.copy(out=dst[:, :, 1, :], in_=src[:, :, 0, :]),
                             reads=[bUq], writes=[bUqr])
                    for s in range(2):
                        load_vec8(c, biA[s][:], l, s, 0, bSc[s])
                        load_vec8(c, scA[s][:], l, s, 1, bSc[s])
                        P.op("dve", lambda e, s=s: e.scalar_tensor_tensor(out=scA[s][:], in0=scA[s][:], scalar=1.0, in1=gpre[:],
                                                                          op0=ALU.add, op1=ALU.mult),
                             reads=[bSc[s], bG], writes=[bSc[s]])

                    def loadA(g):
                        s, t0 = g // 8, (g % 8) * 512
                        j = g % 2
                        P.dma("sp", xg[j][:], x_src[s, t0:t0 + 512, :].rearrange("(t p) d -> p t d", p=128), writes=[bXg[j]])
                        P.dmas("sp", [(cosg[j][:], D["cosT"][:, t0:t0 + 512]), (sing[j][:], D["sinT"][:, t0:t0 + 512])], writes=[bCS[j]])

                    evac_i = [0]

                    def evac(out_ap, in_ap, reads, writes):
                        evac_i[0] += 1
                        if evac_i[0] % 2:
                            P.op("dve", lambda e: e.tensor_copy(out=out_ap, in_=in_ap), reads=reads, writes=writes)
                        else:
                            P.op("act", lambda e: e.copy(out=out_ap, in_=in_ap), reads=reads, writes=writes)

                    def prep(g):
                        j = g % 2
                        X = xg[j]
                        c4 = slice(4 * j, 4 * j + 4)
                        for t in range(4):
                            P.op("act", lambda e, t=t: e.activation(out=junk[:], in_=X[:, t, :], func=AF.Square, accum_out=ss[:, 4 * j + t:4 * j + t + 1]),
                                 reads=[bXg[j]], writes=[bJ, bSS])
                        rstd_from_ss(ss[:, c4], rstd[:, c4], DM, bSS, bRs, lnv[:, c4], bLn)
                        for t in range(4):
                            P.op("dve", lambda e, t=t: e.tensor_scalar(out=xns[j][:, t, :], in0=X[:, t, :], scalar1=rstd[:, 4 * j + t:4 * j + t + 1], scalar2=None,
                                                                       op0=ALU.mult),
                                 reads=[bXg[j], bRs], writes=[bXns[j]])

                    NG = 16
                    ckpt(2)
                    loadA(0)
                    prep(0)
                    for g in range(NG):
                        s, t0 = g // 8, (g % 8) * 512
                        j = g % 2
                        if g + 1 < NG:
                            loadA(g + 1)
                        xn = xns[j]
                        bXn = bXns[j]
                        for k in range(8):
                            pj = k % 2
                            P.op("pe", [(lambda e, t=t, k=k, pj=pj: e.transpose(psT[pj][:, t * 128:(t + 1) * 128], xn[:, t, k * 128:(k + 1) * 128], ident[:]))
                                        for t in range(4)], reads=[bXn, bId], writes=[bPT[pj]])
                            P.op("act", lambda e, k=k, pj=pj: e.activation(out=hT[:, k, :], in_=psT[pj][:], func=AF.Identity,
                                                                           scale=scA[s][:, k:k + 1], bias=biA[s][:, k:k + 1]),
                                 reads=[bPT[pj], bSc[s]], writes=[bHT])
                        ckpt(3)
                        cbs = [0, 128, 256, 384] + [768 + 128 * i for i in range(8)]
                        for ci, cb in enumerate(cbs):
                            pj = ci % 2
                            P.op("pe", [(lambda e, k=k, cb=cb, pj=pj: e.matmul(psZ[pj][:], lhsT=w_in[:, k, cb:cb + 128], rhs=hT[:, k, :],
                                                                              start=(k == 0), stop=(k == 7))) for k in range(8)],
                                 reads=[bW, bHT], writes=[bPZ[pj]])
                            evac(zst[j][:, ci, :], psZ[pj][:], [bPZ[pj]], [bZst[j]])
                        P.dma("sp", D["zT"][s].rearrange("c p t -> p c t")[:, :, t0:t0 + 512], zst[j][:], reads=[bZst[j]])
                        ckpt(4)
                        ckpt(5)
                        for t in range(4):
                            pj = t % 2
                            fns = []
                            for k in range(8):
                                fns.append(lambda e, k=k, t=t, pj=pj: e.matmul(psW[pj][:, 0:384], lhsT=hT[:, k, t * 128:(t + 1) * 128],
                                                                               rhs=w_in[:, k, 2304:2688], start=(k == 0), stop=(k == 7)))
                            for k in range(8):
                                fns.append(lambda e, k=k, t=t, pj=pj: e.matmul(psW[pj][:, 512:768], lhsT=hT[:, k, t * 128:(t + 1) * 128],
                                                                               rhs=w_in[:, k, 2688:2944], start=(k == 0), stop=(k == 7)))
                            if "noMm" not in dbg:
                                P.op("pe", fns, reads=[bW, bHT], writes=[bPW[pj]])
                            if "noCp" not in dbg:
                                P.op("dve", lambda e, t=t, pj=pj: e.tensor_copy(out=cst[:, t, 0:384], in_=psW[pj][:, 0:384]), reads=[bPW[pj]], writes=[bCst])
                                P.op("dve", lambda e, t=t, pj=pj: e.tensor_copy(out=cst[:, t, 384:640], in_=psW[pj][:, 512:768]), reads=[bPW[pj]], writes=[bCst])
                            if "noSq" not in dbg:
                              P.op("act", lambda e, t=t, pj=pj: e.activation(out=junk[:, 0:384], in_=cst[:, t, 0:384], func=AF.Square,
                                                                           accum_out=ssq[:, t:t + 1]), reads=[bCst], writes=[bJ, bSq])
                            if "noSq" not in dbg:
                              P.op("act", lambda e, t=t, pj=pj: e.activation(out=junk[:, 384:640], in_=cst[:, t, 384:640], func=AF.Square,
                                                                           accum_out=sskv[:, t:t + 1]), reads=[bCst], writes=[bJ, bSq])
                        ckpt(51)
                        rstd_from_ss(ssq[:], rsq[:], 384, bSq, bRq, lnq[:], bLq)
                        rstd_from_ss(sskv[:], rskv[:], 256, bSq, bRq, lnk[:], bLq)
                        for t in range(4):
                            pj = t % 2
                            fns = []
                            for k in range(8):
                                fns.append(lambda e, k=k, t=t, pj=pj: e.matmul(psW[pj][:, 0:256], lhsT=hT[:, k, t * 128:(t + 1) * 128],
                                                                               rhs=w_in[:, k, 512:768], start=(k == 0), stop=(k == 7)))
                            for k in range(8):
                                fns.append(lambda e, k=k, t=t, pj=pj: e.matmul(psW[pj][:, 512:1024], lhsT=hT[:, k, t * 128:(t + 1) * 128],
                                                                               rhs=w_in[:, k, 1792:2304], start=(k == 0), stop=(k == 7)))
                            P.op("pe", fns, reads=[bW, bHT], writes=[bPW[pj]])
                            evac(vst[j][:, t, 0:256], psW[pj][:, 0:256], [bPW[pj]], [bVst[j]])
                            evac(vst[j][:, t, 256:768], psW[pj][:, 512:1024], [bPW[pj]], [bVst[j]])
                        P.dma("sp", D["vtok"][s, t0:t0 + 512, :].rearrange("(t p) c -> p t c", p=128), vst[j][:], reads=[bVst[j]])
                        for t in range(4):
                            P.op("dve", lambda e, t=t: e.tensor_scalar(out=cqn[:, t, :], in0=cst[:, t, 0:384], scalar1=rsq[:, t:t + 1], scalar2=None,
                                                                       op0=ALU.mult), reads=[bCst, bRq], writes=[bCqn])
                            P.op("dve", lambda e, t=t: e.tensor_scalar(out=ckvn[:, t, :], in0=cst[:, t, 384:640], scalar1=rskv[:, t:t + 1], scalar2=None,
                                                                       op0=ALU.mult), reads=[bCst, bRq], writes=[bCkvn])
                        ckpt(53)
                        for kk in range(3):
                            pj = kk % 2
                            P.op("pe", [(lambda e, t=t, kk=kk, pj=pj: e.transpose(psT[pj][:, t * 128:(t + 1) * 128], cqn[:, t, kk * 128:(kk + 1) * 128], ident[:]))
                                        for t in range(4)], reads=[bCqn, bId], writes=[bPT[pj]])
                            P.op("act", lambda e, kk=kk, pj=pj: e.activation(out=cqT[:, kk, :], in_=psT[pj][:], func=AF.Identity, scale=gq[:, kk:kk + 1], bias=zcol[:, 0:1]),
                                 reads=[bPT[pj], bG, bZc], writes=[bCqT])
                        ckpt(54)
                        for kk in range(2):
                            pj = (kk + 1) % 2
                            P.op("pe", [(lambda e, t=t, kk=kk, pj=pj: e.transpose(psT[pj][:, t * 128:(t + 1) * 128], ckvn[:, t, kk * 128:(kk + 1) * 128], ident[:]))
                                        for t in range(4)], reads=[bCkvn, bId], writes=[bPT[pj]])
                            P.op("act", lambda e, kk=kk, pj=pj: e.activation(out=ckvT[:, kk, :], in_=psT[pj][:], func=AF.Identity, scale=gkv[:, kk:kk + 1], bias=zcol[:, 0:1]),
                                 reads=[bPT[pj], bG, bZc], writes=[bCkvT])
                        if g + 1 < NG:
                            prep(g + 1)
                        ckpt(6)
                        for jj in range(2):
                            P.op("pe", [(lambda e, kk=kk, jj=jj: e.matmul(psZ[0][:], lhsT=w_uq[:, kk, jj * 128:(jj + 1) * 128], rhs=cqT[:, kk, :],
                                                                          start=(kk == 0), stop=(kk == 2))) for kk in range(3)],
                                 reads=[bUq, bCqT], writes=[bPZ[0]])
                            evac(qnst[j][:, jj, :], psZ[0][:], [bPZ[0]], [bQn[j]])
                            P.op("pe", [(lambda e, kk=kk, jj=jj: e.matmul(psZ[1][:], lhsT=w_ukv[:, kk, jj * 128:(jj + 1) * 128], rhs=ckvT[:, kk, :],
                                                                          start=(kk == 0), stop=(kk == 1))) for kk in range(2)],
                                 reads=[bUkv, bCkvT], writes=[bPZ[1]])
                            evac(knst[j][:, jj, :], psZ[1][:], [bPZ[1]], [bKn[j]])
                        P.op("pe", [(lambda e, kk=kk: e.matmul(psZ[0][:], lhsT=w_uq[:, kk, 256:384], rhs=cqT[:, kk, :], start=(kk == 0), stop=(kk == 2)))
                                    for kk in range(3)], reads=[bUq, bCqT], writes=[bPZ[0]])
                        P.op("pe", [(lambda e, kk=kk: e.matmul(psZ[1][:], lhsT=w_uqr[:, kk, :], rhs=cqT[:, kk, :], start=(kk == 0), stop=(kk == 2)))
                                    for kk in range(3)], reads=[bUqr, bCqT], writes=[bPZ[1]])
                        P.op("dve", lambda e: e.tensor_tensor(out=t1[:], in0=psZ[0][:], in1=cosg[j][:], op=ALU.mult), reads=[bPZ[0], bCS[j]], writes=[bT1])
                        P.op("dve", lambda e: e.tensor_tensor(out=t2[:], in0=psZ[1][:], in1=sing[j][:], op=ALU.mult), reads=[bPZ[1], bCS[j]], writes=[bT2])
                        P.op("pool", lambda e: e.tensor_tensor(out=qrst[j][:], in0=t1[:], in1=t2[:], op=ALU.add), reads=[bT1, bT2], writes=[bQr[j]])
                        ckpt(7)
                        P.op("pe", [(lambda e, k=k: e.matmul(psZ[0][0:32, :], lhsT=w_in[:, k, 2944:2976], rhs=hT[:, k, :], start=(k == 0), stop=(k == 7)))
                                    for k in range(8)], reads=[bW, bHT], writes=[bPZ[0]])
                        P.op("pe", [(lambda e, k=k: e.matmul(psZ[1][0:32, :], lhsT=w_rot[:, k, :], rhs=hT[:, k, :], start=(k == 0), stop=(k == 7)))
                                    for k in range(8)], reads=[bWr, bHT], writes=[bPZ[1]])
                        P.op("dve", lambda e: e.tensor_tensor(out=t1[0:32, :], in0=psZ[0][0:32, :], in1=cosg[j][0:32, :], op=ALU.mult),
                             reads=[bPZ[0], bCS[j]], writes=[bT1])
                        P.op("dve", lambda e: e.tensor_tensor(out=t2[0:32, :], in0=psZ[1][0:32, :], in1=sing[j][0:32, :], op=ALU.mult),
                             reads=[bPZ[1], bCS[j]], writes=[bT2])
                        P.op("pool", lambda e: e.tensor_tensor(out=krst[j][:], in0=t1[0:32, :], in1=t2[0:32, :], op=ALU.add),
                             reads=[bT1, bT2], writes=[bKr[j]])
                        ckpt(8)
                        for t in range(4):
                            pj = t % 2
                            P.op("pe", [(lambda e, kk=kk, t=t, pj=pj: e.matmul(psW[pj][:, 0:256], lhsT=ckvT[:, kk, t * 128:(t + 1) * 128],
                                                                               rhs=w_ukv[:, kk, 256:512], start=(kk == 0), stop=(kk == 1))) for kk in range(2)],
                                 reads=[bUkv, bCkvT], writes=[bPW[pj]])
                            evac(vcst[j][:, t, :], psW[pj][:, 0:256], [bPW[pj]], [bVc[j]])
                        P.dma("sp", D["qn"][s].rearrange("j p t -> p j t")[:, :, t0:t0 + 512], qnst[j][:], reads=[bQn[j]])
                        P.dma("sp", D["kn"][s].rearrange("j p t -> p j t")[:, :, t0:t0 + 512], knst[j][:], reads=[bKn[j]])
                        P.dma("sp", D["qr"][s, :, t0:t0 + 512], qrst[j][:], reads=[bQr[j]])
                        P.dma("sp", D["kr"][s, :, t0:t0 + 512], krst[j][:], reads=[bKr[j]])
                        P.dma("sp", D["vc"][s, t0:t0 + 512, :].rearrange("(t p) c -> p t c", p=128), vcst[j][:], reads=[bVc[j]])
                        ckpt(9)
                P.barrier()
                if "stopA" in dbg:
                    break

                SC_AB = 0.125
                SC_C = 96.0 ** -0.5
                def run_pipeline(units, nb):
                    n = len(units)
                    for i in range(min(nb, n)):
                        if "pre" in units[i]:
                            units[i]["pre"]()
                        units[i]["qk"]()
                    for i in range(n):
                        units[i]["mid"]()
                        units[i]["pv"]()
                        if "post" in units[i]:
                            units[i]["post"]()
                        if i + nb < n:
                            u2 = units[i + nb]
                            if "pre" in u2:
                                u2["pre"]()
                            u2["qk"]()

                with ExitStack() as c:
                    NB = 4
                    NO = 3
                    naX = SB(c, "naX", [128, 4 * 14 * 64], BF16)
                    t5X = SB(c, "t5X", [128, 8 * 3 * 384], BF16)
                    naXm = SB(c, "naXm", [128, NPAT * 4 * 256], BF16)
                    bNaXm = Buf()
                    pens = SB(c, "pens", [2, NPAT * 256], BF16)
                    A2 = SB(c, "A2", [2, 128], BF16)
                    psS = [PS(c, "psS%d" % i, [128, 512]) for i in range(NB)]
                    psO = [PS(c, "psO%d" % i, [128, 512]) for i in range(NO)]
                    E = [SB(c, "E%d" % i, [128, 384], BF16) for i in range(NB + 1)]
                    bE = [Buf() for _ in range(NB + 1)]
                    bPS = [Buf() for _ in range(NB)]
                    bPO = [Buf() for _ in range(NO)]
                    bNaX, bT5X, bPen = Buf(), Buf(), Buf()
                    P.dmas("pool", [(pens[:], D["pens"]), (A2[:], D["A2"])], writes=[bPen])
                    with ExitStack() as c2:
                        stg = SB(c2, "tstg", [128, 8 * 3 * 384], F32)
                        msk = SB(c2, "tmsk", [128, 14 * 64], F32)
                        bStg, bMsk = Buf(), Buf()
                        P.dma("sp", stg[:, 0:3584], D["naG"][l], writes=[bStg])
                        P.dma("sp", msk[:], D["nacm"], writes=[bMsk])
                        P.op("act", lambda e: e.activation(out=stg[:, 0:3584], in_=stg[:, 0:3584], func=AF.Exp), reads=[bStg], writes=[bStg])
                        for h in range(4):
                            P.op("dve", lambda e, h=h: e.tensor_tensor(out=naX[:, h * 896:(h + 1) * 896], in0=stg[:, h * 896:(h + 1) * 896], in1=msk[:],
                                                                       op=ALU.mult), reads=[bStg, bMsk], writes=[bNaX])
                        vmst = SB(c2, "vmst", [128, NPAT * 256], F32)
                        bVm = Buf()
                        P.dma("sp", vmst[:], D["navm"], writes=[bVm])
                        for (pid_, di0_) in sorted({(u_[2], u_[1]) for ul_ in NA_PLAN for u_ in ul_ if u_[2] >= 0}):
                            for h in range(4):
                                P.op("dve", lambda e, h=h, pid_=pid_, di0_=di0_: e.tensor_tensor(
                                    out=naXm[:, (pid_ * 4 + h) * 256:(pid_ * 4 + h + 1) * 256],
                                    in0=naX[:, h * 896 + di0_ * 64:h * 896 + di0_ * 64 + 256],
                                    in1=vmst[:, pid_ * 256:(pid_ + 1) * 256], op=ALU.mult), reads=[bNaX, bVm], writes=[bNaXm])
                        P.dma("sp", stg[:], D["t5G"], writes=[bStg])
                        P.dma("sp", msk[:, 0:384], D["t5m"], writes=[bMsk])
                        for hp in range(24):
                            P.op("act", lambda e, hp=hp: e.activation(out=stg[:, hp * 384:(hp + 1) * 384], in_=stg[:, hp * 384:(hp + 1) * 384], func=AF.Exp),
                                 reads=[bStg], writes=[bStg])
                            P.op("dve", lambda e, hp=hp: e.tensor_tensor(out=t5X[:, hp * 384:(hp + 1) * 384], in0=stg[:, hp * 384:(hp + 1) * 384],
                                                                         in1=msk[:, 0:384], op=ALU.mult), reads=[bStg, bMsk], writes=[bT5X])
                        P.barrier()
                    ucnt = [0]
                    bcnt = [0]

                    with ExitStack() as c2:
                        qT = [SB(c2, "naq%d" % i, [128, S], BF16) for i in range(2)]
                        kT = [SB(c2, "nak%d" % i, [128, S], BF16) for i in range(2)]
                        va = [SB(c2, "nav%d" % i, [128, 32, 2, 128], BF16) for i in range(2)]
                        ot = [SB(c2, "naot%d" % i, [128, S], BF16) for i in range(2)]
                        rec = SB(c2, "narec", [64, 256], F32)
                        bQ = [Buf(), Buf()]
                        bV = [Buf(), Buf()]
                        bOt = [Buf(), Buf()]
                        bRec = Buf()
                        for i in range(2):
                            P.op("pool", lambda e, i=i: e.memset(va[i][:, :, :, 64:128], 1.0), writes=[bV[i]])
                        items = [(s, hp) for s in range(2) for hp in range(2)]

                        def loadNA(n):
                            s, hp = items[n]
                            i = n % 2
                            P.dmas("sp", [(qT[i][:], D["zT"][s, hp]), (kT[i][:], D["zT"][s, 2 + hp])], writes=[bQ[i]])
                            src = D["vtok"][s].rearrange("(n p) c -> p n c", p=128)
                            P.dmas("sp", [(va[i][:, n0:n0 + 16, hh, 0:64], src[:, n0:n0 + 16, hp * 128 + hh * 64:hp * 128 + hh * 64 + 64])
                                          for n0 in range(0, 32, 16) for hh in range(2)], writes=[bV[i]])

                        units = []
                        for n, (s, hp) in enumerate(items):
                            i = n % 2
                            for hh in range(2):
                                h = 2 * hp + hh
                                pr = slice(hh * 64, hh * 64 + 64)
                                for J in range(16):
                                    ul = NA_PLAN[J]
                                    oj = bcnt[0] % NO
                                    bcnt[0] += 1
                                    for ui, (ci, di0, pid) in enumerate(ul):
                                        u = ucnt[0]
                                        ucnt[0] += 1
                                        sj, ej = u % NB, u % (NB + 1)
                                        first, last = (ui == 0), (ui == len(ul) - 1)

                                        def qk(i=i, pr=pr, ci=ci, J=J, pid=pid, sj=sj):
                                            P.op("pe", lambda e: e.matmul(psS[sj][:, 0:256], lhsT=kT[i][pr, ci * 128:(ci + 1) * 128],
                                                                          rhs=qT[i][pr, J * 256:(J + 1) * 256], start=True, stop=True),
                                                 reads=[bQ[i]], writes=[bPS[sj]])

                                        def mid(h=h, di0=di0, sj=sj, ej=ej, pid=pid):
                                            P.op("act", lambda e: e.activation(out=E[ej][:, 0:256], in_=psS[sj][:, 0:256], func=AF.Exp, scale=SC_AB),
                                                 reads=[bPS[sj]], writes=[bE[ej]])
                                            if pid >= 0:
                                                tab = naXm[:, (pid * 4 + h) * 256:(pid * 4 + h + 1) * 256]
                                            else:
                                                tab = naX[:, h * 896 + di0 * 64:h * 896 + di0 * 64 + 256]
                                            P.op("dve", lambda e: e.tensor_tensor(out=E[ej][:, 0:256], in0=E[ej][:, 0:256], in1=tab, op=ALU.mult),
                                                 reads=[bE[ej], bNaX, bNaXm], writes=[bE[ej]])

                                        def pv(i=i, ci=ci, hh=hh, ej=ej, oj=oj, first=first, last=last, J=J, pr=pr):
                                            P.op("pe", lambda e: e.matmul(psO[oj][:, 0:256], lhsT=va[i][:, ci, hh, :], rhs=E[ej][:, 0:256], start=first, stop=last),
                                                 reads=[bV[i], bE[ej]], writes=[bPO[oj]])
                                            if last:
                                                P.op("dve", lambda e: e.reciprocal(out=rec[:], in_=psO[oj][64:128, 0:256]), reads=[bPO[oj]], writes=[bRec])
                                                P.op("dve", lambda e: e.tensor_tensor(out=ot[i][pr, J * 256:(J + 1) * 256], in0=psO[oj][0:64, 0:256], in1=rec[:],
                                                                                      op=ALU.mult), reads=[bPO[oj], bRec], writes=[bOt[i]])

                                        units.append(dict(qk=qk, mid=mid, pv=pv))

                            def post(n=n, s=s, hp=hp, i=i):
                                P.dma("sp", D["OT"][s, hp * 128:(hp + 1) * 128, :], ot[i][:], reads=[bOt[i]])
                                if n + 2 < len(items):
                                    loadNA(n + 2)
                            units[-1]["post"] = post
                        loadNA(0)
                        loadNA(1)
                        run_pipeline(units, NB)
                        P.barrier()

                    with ExitStack() as c2:
                        qT = [SB(c2, "dq%d" % i, [128, S], BF16) for i in range(2)]
                        kT = [SB(c2, "dk%d" % i, [128, S], BF16) for i in range(2)]
                        qo = [SB(c2, "dqo%d" % i, [128, S], BF16) for i in range(2)]
                        ko = [SB(c2, "dko%d" % i, [128, S], BF16) for i in range(2)]
                        va = [SB(c2, "dv%d" % i, [128, 32, 2, 128], BF16) for i in range(2)]
                        oacc = [SB(c2, "oacc%d" % i, [128, S], F32) for i in range(2)]
                        ot = [SB(c2, "dot%d" % i, [128, S], BF16) for i in range(2)]
                        rec = SB(c2, "drec", [128, 1024], F32)
                        bQ = [Buf(), Buf()]
                        bQo = [Buf(), Buf()]
                        bV = [Buf(), Buf()]
                        bAcc = [Buf(), Buf()]
                        bOt = [Buf(), Buf()]
                        bRec = Buf()
                        for i in range(2):
                            P.op("pool", lambda e, i=i: e.memset(va[i][:, :, :, 64:128], 1.0), writes=[bV[i]])
                        items = [(s, hp) for s in range(2) for hp in range(4)]
                        pats = (1, 4, 16)
                        NPI = len(items) * 3

                        def loadQK(n):
                            s, hp = items[n]
                            i = n % 2
                            P.dmas("sp", [(qT[i][:], D["zT"][s, 4 + hp]), (kT[i][:], D["zT"][s, 8 + hp])], writes=[bQ[i]])

                        def loadV(pidx):
                            n, pi = pidx // 3, pidx % 3
                            s, hp = items[n]
                            d = pats[pi]
                            vi = pidx % 2
                            nchunk = 32 // d
                            src = D["vtok"][s].rearrange("(n p r) c -> p r n c", p=128, r=d)
                            pairs = []
                            for r in range(d):
                                for n0 in range(0, nchunk, 16):
                                    n1 = min(n0 + 16, nchunk)
                                    for hh in range(2):
                                        cb0 = 256 + hp * 128 + hh * 64
                                        pairs.append((va[vi][:, r * nchunk + n0:r * nchunk + n1, hh, 0:64], src[:, r, n0:n1, cb0:cb0 + 64]))
                            P.dmas("sp", pairs, writes=[bV[vi]])

                        units = []
                        ocount = 0
                        for n, (s, hp) in enumerate(items):
                            i = n % 2
                            for pi, d in enumerate(pats):
                                pidx = n * 3 + pi
                                vi = pidx % 2
                                nchunk = 32 // d
                                pre = None
                                if d == 1:
                                    Q, K, bQQ = qT[i], kT[i], bQ[i]
                                else:
                                    oi = ocount % 2
                                    ocount += 1
                                    Q, K, bQQ = qo[oi], ko[oi], bQo[oi]

                                    def pre(Q=Q, K=K, d=d, i=i, bQQ=bQQ):
                                        P.op("act", lambda e: e.copy(out=Q[:].rearrange("p (r m) -> p r m", r=d),
                                                                     in_=qT[i][:].rearrange("p (m r) -> p r m", r=d)),
                                             reads=[bQ[i]], writes=[bQQ])
                                        P.op("act", lambda e: e.copy(out=K[:].rearrange("p (r m) -> p r m", r=d),
                                                                     in_=kT[i][:].rearrange("p (m r) -> p r m", r=d)),
                                             reads=[bQ[i]], writes=[bQQ])
                                firstu = True
                                for hh in range(2):
                                    h = 2 * hp + hh
                                    pr = slice(hh * 64, hh * 64 + 64)
                                    accv = oacc[hh][:].rearrange("p (m r) -> p r m", r=d)
                                    zero_first = (pi == 0)
                                    for r in range(d):
                                        for kc in range(nchunk):
                                            u = ucnt[0]
                                            ucnt[0] += 1
                                            sj, ej, oj = u % NB, u % (NB + 1), u % NO
                                            j0, j1 = max(kc - 1, 0), min(kc + 1, nchunk - 1)
                                            g0 = j0 - (kc - 1)
                                            ncol = (j1 - j0 + 1) * 128
                                            c0, c1 = g0 * 128, g0 * 128 + ncol
                                            qbase = (r * nchunk + j0) * 128
                                            kbase = (r * nchunk + kc) * 128
                                            xo = (h * 3 + pi) * 384
                                            dst = accv[:, r, j0 * 128:(j1 + 1) * 128]
                                            zf = zero_first
                                            zero_first = False

                                            def qk(Q=Q, K=K, bQQ=bQQ, pr=pr, qbase=qbase, kbase=kbase, ncol=ncol, c0=c0, c1=c1, sj=sj):
                                                P.op("pe", lambda e: e.matmul(psS[sj][:, c0:c1], lhsT=K[pr, kbase:kbase + 128], rhs=Q[pr, qbase:qbase + ncol],
                                                                              start=True, stop=True), reads=[bQQ], writes=[bPS[sj]])

                                            def mid(sj=sj, ej=ej, c0=c0, c1=c1, xo=xo):
                                                P.op("act", lambda e: e.activation(out=E[ej][:, c0:c1], in_=psS[sj][:, c0:c1], func=AF.Exp, scale=SC_AB),
                                                     reads=[bPS[sj]], writes=[bE[ej]])
                                                P.op("pool", lambda e: e.tensor_tensor(out=E[ej][:, c0:c1], in0=E[ej][:, c0:c1], in1=t5X[:, xo + c0:xo + c1],
                                                                                       op=ALU.mult), reads=[bE[ej], bT5X], writes=[bE[ej]])

                                            def pv(vi=vi, r=r, nchunk=nchunk, kc=kc, hh=hh, ej=ej, oj=oj, dst=dst, c0=c0, c1=c1, zf=zf):
                                                if zf:
                                                    P.op("pool", lambda e: e.memset(oacc[hh][:], 0.0), writes=[bAcc[hh]])
                                                P.op("pe", lambda e: e.matmul(psO[oj][:, c0:c1], lhsT=va[vi][:, r * nchunk + kc, hh, :], rhs=E[ej][:, c0:c1],
                                                                              start=True, stop=True), reads=[bV[vi], bE[ej]], writes=[bPO[oj]])
                                                P.op("dve", lambda e: e.tensor_tensor(out=dst, in0=psO[oj][:, c0:c1], in1=dst, op=ALU.add),
                                                     reads=[bPO[oj], bAcc[hh]], writes=[bAcc[hh]])

                                            ud = dict(qk=qk, mid=mid, pv=pv)
                                            if firstu and pre is not None:
                                                ud["pre"] = pre
                                            firstu = False
                                            units.append(ud)
                                    if pi == 2:
                                        def fin(hh=hh, pr=pr, i=i):
                                            for q4 in range(4):
                                                cs = slice(q4 * 1024, (q4 + 1) * 1024)
                                                P.op("act", lambda e: e.activation(out=rec[0:64, :], in_=oacc[hh][64:128, cs], func=AF.Ln),
                                                     reads=[bAcc[hh]], writes=[bRec])
                                                P.op("act", lambda e: e.activation(out=rec[0:64, :], in_=rec[0:64, :], func=AF.Exp, scale=-1.0),
                                                     reads=[bRec], writes=[bRec])
                                                P.op("dve", lambda e: e.tensor_tensor(out=ot[i][pr, cs], in0=oacc[hh][0:64, cs], in1=rec[0:64, :], op=ALU.mult),
                                                     reads=[bAcc[hh], bRec], writes=[bOt[i]])
                                        units[-1]["fin"] = fin

                                def postp(pidx=pidx):
                                    if pidx + 2 < NPI:
                                        loadV(pidx + 2)
                                units[-1]["postp"] = postp

                            def posti(n=n, s=s, hp=hp, i=i):
                                P.dma("sp", D["OT"][s, 256 + hp * 128:256 + (hp + 1) * 128, :], ot[i][:], reads=[bOt[i]])
                                if n + 2 < len(items):
                                    loadQK(n + 2)
                            units[-1]["posti"] = posti
                        for ud in units:
                            hooks = [ud[k] for k in ("fin", "postp", "posti") if k in ud]
                            if hooks:
                                ud["post"] = (lambda hooks=hooks: [hk() for hk in hooks])
                        loadQK(0)
                        loadQK(1)
                        loadV(0)
                        loadV(1)
                        run_pipeline(units, NB)
                        P.barrier()
                P.barrier()

                with ExitStack() as c:
                    psS = [PS(c, "psSc%d" % i, [128, 1024]) for i in range(2)]
                    psO = [PS(c, "psOc%d" % i, [128, 1024]) for i in range(2)]
                    E = [SB(c, "Ec%d" % i, [128, 1024], BF16) for i in range(3)]
                    qT = [SB(c, "cq%d" % i, [96, S], BF16) for i in range(2)]
                    kT = [SB(c, "ck%d" % i, [96, S], BF16) for i in range(2)]
                    va = [SB(c, "cv%d" % i, [128, 32, 128], BF16) for i in range(2)]
                    ot = [SB(c, "cot%d" % i, [128, S], BF16) for i in range(2)]
                    rec = SB(c, "crec", [64, 1024], F32)
                    bE = [Buf() for _ in range(3)]
                    bPS = [Buf(), Buf()]
                    bPO = [Buf(), Buf()]
                    bQ = [Buf(), Buf()]
                    bV = [Buf(), Buf()]
                    bOt = [Buf(), Buf()]
                    bRec = Buf()
                    for i in range(2):
                        P.op("pool", lambda e, i=i: e.memset(va[i][:, :, 64:128], 1.0), writes=[bV[i]])
                    items = [(s, h) for s in range(2) for h in range(4)]

                    def loadC(n):
                        s, h = items[n]
                        i = n % 2
                        P.dmas("sp", [(qT[i][0:64, :], D["qn"][s, h // 2, (h % 2) * 64:(h % 2) * 64 + 64, :]),
                                      (qT[i][64:96, :], D["qr"][s, h * 32:(h + 1) * 32, :]),
                                      (kT[i][0:64, :], D["kn"][s, h // 2, (h % 2) * 64:(h % 2) * 64 + 64, :]),
                                      (kT[i][64:96, :], D["kr"][s, :, :])], writes=[bQ[i]])
                        src = D["vc"][s].rearrange("(n p) c -> p n c", p=128)
                        P.dmas("sp", [(va[i][:, n0:n0 + 8, 0:64], src[:, n0:n0 + 8, h * 64:(h + 1) * 64]) for n0 in range(0, 32, 8)], writes=[bV[i]])

                    units = []
                    u = 0
                    blk = 0
                    for n, (s, h) in enumerate(items):
                        i = n % 2
                        oi = (n // 2) % 2
                        pr = slice((h % 2) * 64, (h % 2) * 64 + 64)
                        for qb in range(4):
                            oj = blk % 2
                            blk += 1
                            cs = slice(qb * 1024, (qb + 1) * 1024)
                            for kc in range(32):
                                sj, ej = u % 2, u % 3
                                u += 1

                                def qk(i=i, kc=kc, qb=qb, sj=sj):
                                    P.op("pe", [(lambda e, hf=hf: e.matmul(psS[sj][:, hf * 512:(hf + 1) * 512], lhsT=kT[i][0:96, kc * 128:(kc + 1) * 128],
                                                                           rhs=qT[i][0:96, qb * 1024 + hf * 512:qb * 1024 + (hf + 1) * 512],
                                                                           start=True, stop=True)) for hf in range(2)],
                                         reads=[bQ[i]], writes=[bPS[sj]])

                                def mid(sj=sj, ej=ej):
                                    P.op("act", lambda e: e.activation(out=E[ej][:], in_=psS[sj][:], func=AF.Exp, scale=SC_C),
                                         reads=[bPS[sj]], writes=[bE[ej]])

                                def pv(i=i, kc=kc, ej=ej, oj=oj, oi=oi, pr=pr, cs=cs):
                                    P.op("pe", [(lambda e, hf=hf: e.matmul(psO[oj][:, hf * 512:(hf + 1) * 512], lhsT=va[i][:, kc, :],
                                                                           rhs=E[ej][:, hf * 512:(hf + 1) * 512], start=(kc == 0), stop=(kc == 31)))
                                                for hf in range(2)], reads=[bV[i], bE[ej]], writes=[bPO[oj]])
                                    if kc == 31:
                                        P.op("dve", lambda e: e.reciprocal(out=rec[:], in_=psO[oj][64:128, :]), reads=[bPO[oj]], writes=[bRec])
                                        P.op("dve", lambda e: e.tensor_tensor(out=ot[oi][pr, cs], in0=psO[oj][0:64, :], in1=rec[:], op=ALU.mult),
                                             reads=[bPO[oj], bRec], writes=[bOt[oi]])

                                units.append(dict(qk=qk, mid=mid, pv=pv))

                        def post(n=n, s=s, h=h, oi=oi):
                            if h % 2 == 1:
                                P.dma("sp", D["OT"][s, 768 + (h // 2) * 128:768 + (h // 2 + 1) * 128, :], ot[oi][:], reads=[bOt[oi]])
                            if n + 2 < len(items):
                                loadC(n + 2)
                        units[-1]["post"] = post
                    loadC(0)
                    loadC(1)
                    run_pipeline(units, 2)
                P.barrier()
                if "stopB" in dbg:
                    break

                with ExitStack() as c:
                    w_out = SB(c, "w_out", [128, 8, DM], BF16)
                    w_dn = SB(c, "w_dn", [128, NCH, DM], BF16)
                    wu = [SB(c, "wu%d" % i, [128, 8, 256], BF16) for i in range(3)]
                    cw = SB(c, "convw", [128, NCH, 3], F32)
                    cb = SB(c, "convb", [128, NCH], F32)
                    gpre = SB(c, "gpreF", [128, 8], F32)
                    scF = [SB(c, "scF%d" % s, [128, 8], F32) for s in range(2)]
                    biF = [SB(c, "biF%d" % s, [128, 8], F32) for s in range(2)]
                    gate1 = SB(c, "gate1", [128, DM], F32)
                    gate2 = SB(c, "gate2", [128, DM], F32)
                    xm = [SB(c, "xm%d" % i, [128, DM], F32) for i in range(2)]
                    bXm = [Buf(), Buf()]
                    bXD = {}
                    xg = [SB(c, "xgC%d" % i, [128, 4, DM], F32) for i in range(2)]
                    otg = [SB(c, "otg%d" % i, [128, 8, 512], BF16) for i in range(2)]
                    h2T = [SB(c, "h2T%d" % i, [128, 8, 512], BF16) for i in range(2)]
                    halo = SB(c, "halo", [128, 8, 2], BF16)
                    edge = [SB(c, "edge%d" % i, [128, 8, 2], BF16) for i in range(3)]
                    actT = SB(c, "actT", [128, NCH, 512], BF16)
                    xn2 = SB(c, "xn2", [128, 4, DM], BF16)
                    junk = SB(c, "junkC", [128, DM], BF16)
                    tmp = SB(c, "tmpC", [128, DM], F32)
                    gcb = [SB(c, "gc%d" % i, [128, 512], F32) for i in range(2)]
                    geb = [SB(c, "ge%d" % i, [128, 512], F32) for i in range(2)]
                    ssv = SB(c, "ssC", [128, 16], F32)
                    lnc = SB(c, "lnC", [128, 16], F32)
                    rsc = SB(c, "rsC", [128, 16], F32)
                    psYs = [PS(c, "psY%d" % i, [128, 1024]) for i in range(2)]
                    psT1 = PS(c, "psTC", [128, 512], BF16)
                    psTs = [psT1[:], psT1[:]]
                    psHt = PS(c, "psH", [128, 16])
                    psH = psHt[:, 0:2]
                    psHT = psHt[:, 8:16]
                    bPHT = Buf()
                    hT8 = SB(c, "hT8", [128, 8], F32)
                    bHT8 = Buf()
                    psA1 = PS(c, "psA", [128, 512])
                    psG1 = PS(c, "psG", [128, 512])
                    g_sb = [SB(c, "g_sb%d" % i, [128, 512], F32) for i in range(2)]
                    bGsb = [Buf(), Buf()]
                    bPG1 = Buf()
                    a_sb = [SB(c, "a_sb%d" % i, [128, 512], BF16) for i in range(2)]
                    hsb = SB(c, "hsb", [128, 2], F32)
                    bHsb = Buf()
                    bAsb = [Buf(), Buf()]
                    ycnt = [0]
                    bWo, bWd, bCw, bGp = Buf(), Buf(), Buf(), Buf()
                    bWu = [Buf() for _ in range(3)]
                    bScF = [Buf(), Buf()]
                    bGate, bGt = Buf(), Buf()
                    bXg = [Buf(), Buf()]
                    bOtg = [Buf(), Buf()]
                    bH2 = [Buf(), Buf()]
                    bEdge = [Buf(), Buf(), Buf()]
                    bHalo, bAct, bXn2, bJ, bTmp = Buf(), Buf(), Buf(), Buf(), Buf()
                    bActLo = Buf()
                    bGc = [Buf(), Buf()]
                    bGe = [Buf(), Buf()]
                    bSs, bLn, bRs = Buf(), Buf(), Buf()
                    bPYs = [Buf(), Buf()]
                    bPT1 = Buf()
                    bPTs = [bPT1, bPT1]
                    bPH = Buf()
                    bPA1 = Buf()
                    bPG = [Buf(), Buf()]

                    P.dmas("pool", [(w_out[:, k, :], D["w_out"][l, :, k, :]) for k in range(8)], writes=[bWo])
                    P.dmas("pool", [(w_dn[:, c0:c0 + 2, :], D["w_down"][l, :, c0:c0 + 2, :]) for c0 in range(0, NCH, 2)], writes=[bWd])
                    P.dmas("sp", [(cw[:], D["convwT"][l]), (cb[:], D["convbT"][l])], writes=[bCw])
                    P.dma("sp", gpre[:], D["gpre_ffnT"][l], writes=[bGp])
                    for s in range(2):
                        load_vec8(c, biF[s][:], l, s, 3, bScF[s])
                        load_vec8(c, scF[s][:], l, s, 4, bScF[s])
                        P.op("dve", lambda e, s=s: e.scalar_tensor_tensor(out=scF[s][:], in0=scF[s][:], scalar=1.0, in1=gpre[:], op0=ALU.add, op1=ALU.mult),
                             reads=[bScF[s], bGp], writes=[bScF[s]])

                    def load_gates(s):
                        for gt, jm, gp in ((gate1, 2, "gpost_mix"), (gate2, 5, "gpost_ffn")):
                            P.dma("sp", gt[:], D["modD"][l, s:s + 1, jm * 1024:(jm + 1) * 1024].partition_broadcast(128), writes=[bGate])
                            P.dma("sp", tmp[:], D[gp][l:l + 1, :].partition_broadcast(128), writes=[bTmp])
                            P.op("dve", lambda e, gt=gt: e.tensor_tensor(out=gt[:], in0=gt[:], in1=tmp[:], op=ALU.mult), reads=[bGate, bTmp], writes=[bGate])

                    def loadOT(g):
                        s, t0 = g // 8, (g % 8) * 512
                        j = g % 2
                        P.dma("sp", otg[j][:], D["OT"][s, :, t0:t0 + 512].rearrange("(k p) t -> p k t", p=128), writes=[bOtg[j]])

                    def loadX(g):
                        s, t0 = g // 8, (g % 8) * 512
                        j = g % 2
                        P.dmas("sp", [(xg[j][:, t, :], x_src[s, t0 + t * 128:t0 + (t + 1) * 128, :]) for t in range(4)], writes=[bXg[j]])

                    def norm_resid(xap, gate, col, bX, psY, bPY):
                        P.op("act", lambda e: e.activation(out=junk[:], in_=psY[:], func=AF.Square, accum_out=ssv[:, col:col + 1]),
                             reads=[bPY], writes=[bJ, bSs])
                        P.op("act", lambda e: e.activation(out=lnc[:, col:col + 1], in_=ssv[:, col:col + 1], func=AF.Ln, scale=1.0 / DM, bias=EPS),
                             reads=[bSs], writes=[bLn])
                        P.op("act", lambda e: e.activation(out=rsc[:, col:col + 1], in_=lnc[:, col:col + 1], func=AF.Exp, scale=-0.5),
                             reads=[bLn], writes=[bRs])
                        P.op("dve", lambda e: e.scalar_tensor_tensor(out=tmp[:], in0=psY[:], scalar=rsc[:, col:col + 1], in1=gate[:], op0=ALU.mult, op1=ALU.mult),
                             reads=[bPY, bRs, bGate], writes=[bTmp])
                        P.op("dve", lambda e: e.tensor_tensor(out=xap, in0=xap, in1=tmp[:], op=ALU.add), reads=[bTmp, bX], writes=[bX])

                    def stage1a(g):
                        s, t0 = g // 8, (g % 8) * 512
                        j = g % 2
                        X = xg[j]

                        def second(t):
                            P.op("act", lambda e: e.activation(out=junk[:], in_=X[:, t, :], func=AF.Square, accum_out=ssv[:, 4 + t:5 + t]),
                                 reads=[bXg[j]], writes=[bJ, bSs])
                            P.op("act", lambda e: e.activation(out=lnc[:, 4 + t:5 + t], in_=ssv[:, 4 + t:5 + t], func=AF.Ln, scale=1.0 / DM, bias=EPS),
                                 reads=[bSs], writes=[bLn])
                            P.op("act", lambda e: e.activation(out=rsc[:, 4 + t:5 + t], in_=lnc[:, 4 + t:5 + t], func=AF.Exp, scale=-0.5), reads=[bLn], writes=[bRs])
                            P.op("dve", lambda e: e.tensor_scalar(out=xn2[:, t, :], in0=X[:, t, :], scalar1=rsc[:, 4 + t:5 + t], scalar2=None, op0=ALU.mult),
                                 reads=[bXg[j], bRs], writes=[bXn2])

                        for t in range(4):
                            yi = ycnt[0] % 2
                            ycnt[0] += 1
                            psY, bPY = psYs[yi], bPYs[yi]
                            fns = []
                            for hf in range(2):
                                for k in range(8):
                                    fns.append(lambda e, k=k, hf=hf, t=t, psY=psY: e.matmul(psY[:, hf * 512:(hf + 1) * 512], lhsT=otg[j][:, k, t * 128:(t + 1) * 128],
                                                                                   rhs=w_out[:, k, hf * 512:(hf + 1) * 512], start=(k == 0), stop=(k == 7)))
                            P.op("pe", fns, reads=[bOtg[j], bWo], writes=[bPY])
                            if t == 2:
                                P.op("pe", [(lambda e, k=k: e.matmul(psHT[:, k:k + 1], lhsT=xn2[0:1, 0, k * 128:(k + 1) * 128], rhs=ident[0:1, 0:1],
                                                                     start=True, stop=True)) for k in range(8)], reads=[bXn2, bId], writes=[bPHT])
                                P.op("act", lambda e: e.copy(out=hT8[:], in_=psHT), reads=[bPHT], writes=[bHT8])
                                P.op("dve", lambda e: e.tensor_tensor(out=hT8[:], in0=hT8[:], in1=scF[s][:], op=ALU.mult), reads=[bHT8, bScF[s]], writes=[bHT8])
                                ej3 = g % 3
                                P.op("dve", lambda e: e.tensor_tensor(out=edge[ej3][:, :, 0], in0=hT8[:], in1=biF[s][:], op=ALU.add),
                                     reads=[bHT8, bScF[s]], writes=[bEdge[ej3]])
                            norm_resid(X[:, t, :], gate1, t, bXg[j], psY, bPY)
                            if t >= 1:
                                second(t - 1)
                        second(3)
                        bXD[g] = Buf()
                        P.dma("sp", x_dst[s, t0:t0 + 512, :].rearrange("(t p) d -> p t d", p=128), X[:], reads=[bXg[j]], writes=[bXD[g]])

                    def tr(g, k):
                        s = g // 8
                        j = g % 2
                        psT, bPT = psTs[k % 2], bPTs[k % 2]
                        P.op("pe", [(lambda e, t=t: e.transpose(psT[:, t * 128:(t + 1) * 128], xn2[:, t, k * 128:(k + 1) * 128], ident[:]))
                                    for t in range(4)], reads=[bXn2, bId], writes=[bPT])
                        P.op("act", lambda e: e.activation(out=h2T[j][:, k, :], in_=psT[:], func=AF.Identity,
                                                           scale=scF[s][:, k:k + 1], bias=biF[s][:, k:k + 1]),
                             reads=[bPT, bScF[s]], writes=[bH2[j]])
                        if k == 7:
                            ej3 = g % 3
                            P.op("pool", lambda e: e.tensor_copy(out=edge[ej3][:, :, 1:2], in_=h2T[j][:, :, 511:512]), reads=[bH2[j]], writes=[bEdge[ej3]])

                    wcount = [0]

                    def load_wu(ci):
                        wi = wcount[0] % 3
                        wcount[0] += 1
                        P.dma("pool", wu[wi][:], D["w_up"][l, ci], writes=[bWu[wi]])
                        return wi

                    wu_pref = []

                    def prefetch_wu():
                        wu_pref.extend([load_wu(0), load_wu(1)])

                    def stage2_up(g, have_next):
                        s, t0 = g // 8, (g % 8) * 512
                        j = g % 2
                        X = xg[j]
                        if t0 > 0:
                            P.op("pool", lambda e: e.tensor_copy(out=halo[:, :, 0:1], in_=edge[(g - 1) % 3][:, :, 1:2]), reads=[bEdge[(g - 1) % 3]], writes=[bHalo])
                        else:
                            P.op("pool", lambda e: e.memset(halo[:, :, 0:1], 0.0), writes=[bHalo])
                        if t0 + 512 < S:
                            assert have_next
                            P.op("pool", lambda e: e.tensor_copy(out=halo[:, :, 1:2], in_=edge[(g + 1) % 3][:, :, 0:1]), reads=[bEdge[(g + 1) % 3]], writes=[bHalo])
                        else:
                            P.op("pool", lambda e: e.memset(halo[:, :, 1:2], 0.0), writes=[bHalo])
                        def finish_chunk(ci):
                            pj = ci % 2
                            gc, ge = gcb[pj], geb[pj]
                            P.op("act", lambda e: e.activation(out=ge[:], in_=gc[:], func=AF.Gelu_apprx_tanh), reads=[bGc[pj]], writes=[bGe[pj]])
                            P.op("pool", lambda e: e.tensor_tensor(out=actT[:, ci, :], in0=a_sb[pj][:], in1=ge[:], op=ALU.mult),
                                 reads=[bAsb[pj], bGe[pj]], writes=[bActLo if ci < NCH - 2 else bAct])

                        pend = list(wu_pref)
                        del wu_pref[:]
                        for ci in range(NCH):
                            wi = pend.pop(0)
                            if ci + 2 < NCH:
                                pend.append(load_wu(ci + 2))
                            pj = ci % 2
                            W = wu[wi]
                            P.op("pe", [(lambda e, k=k, W=W, pj=pj: e.matmul(psA1[:], lhsT=W[:, k, 0:128], rhs=h2T[j][:, k, :], start=(k == 0), stop=(k == 7)))
                                        for k in range(8)], reads=[bWu[wi], bH2[j]], writes=[bPA1])
                            P.op("act", lambda e, pj=pj: e.copy(out=a_sb[pj][:], in_=psA1[:]), reads=[bPA1], writes=[bAsb[pj]])
                            P.op("pe", [(lambda e, k=k, W=W: e.matmul(psG1[:], lhsT=W[:, k, 128:256], rhs=h2T[j][:, k, :], start=(k == 0), stop=(k == 7)))
                                        for k in range(8)], reads=[bWu[wi], bH2[j]], writes=[bPG1])
                            P.op("pe", [(lambda e, k=k, W=W: e.matmul(psH, lhsT=W[:, k, 128:256], rhs=halo[:, k, :], start=(k == 0), stop=(k == 7)))
                                        for k in range(8)], reads=[bWu[wi], bHalo], writes=[bPH])
                            gc, ge, gs = gcb[pj], geb[pj], g_sb[pj]
                            P.op("act", lambda e, gs=gs: e.copy(out=gs[:], in_=psG1[:]), reads=[bPG1], writes=[bGsb[pj]])
                            P.op("dve", lambda e, ci=ci, gc=gc, gs=gs: e.tensor_scalar(out=gc[:], in0=gs[:], scalar1=cw[:, ci, 1:2], scalar2=cb[:, ci:ci + 1],
                                                                                       op0=ALU.mult, op1=ALU.add),
                                 reads=[bGsb[pj], bCw], writes=[bGc[pj]])
                            P.op("dve", lambda e, ci=ci, gc=gc, gs=gs: e.scalar_tensor_tensor(out=gc[:, 1:512], in0=gs[:, 0:511], scalar=cw[:, ci, 0:1],
                                                                                              in1=gc[:, 1:512], op0=ALU.mult, op1=ALU.add),
                                 reads=[bGsb[pj], bCw, bGc[pj]], writes=[bGc[pj]])
                            P.op("dve", lambda e, ci=ci, gc=gc, gs=gs: e.scalar_tensor_tensor(out=gc[:, 0:511], in0=gs[:, 1:512], scalar=cw[:, ci, 2:3],
                                                                                              in1=gc[:, 0:511], op0=ALU.mult, op1=ALU.add),
                                 reads=[bGsb[pj], bCw, bGc[pj]], writes=[bGc[pj]])
                            P.op("act", lambda e: e.copy(out=hsb[:], in_=psH), reads=[bPH], writes=[bHsb])
                            P.op("dve", lambda e, ci=ci, gc=gc: e.scalar_tensor_tensor(out=gc[:, 0:1], in0=hsb[:, 0:1], scalar=cw[:, ci, 0:1], in1=gc[:, 0:1],
                                                                                       op0=ALU.mult, op1=ALU.add), reads=[bHsb, bCw, bGc[pj]], writes=[bGc[pj]])
                            P.op("dve", lambda e, ci=ci, gc=gc: e.scalar_tensor_tensor(out=gc[:, 511:512], in0=hsb[:, 1:2], scalar=cw[:, ci, 2:3], in1=gc[:, 511:512],
                                                                                       op0=ALU.mult, op1=ALU.add), reads=[bHsb, bCw, bGc[pj]], writes=[bGc[pj]])
                            if ci >= 1:
                                finish_chunk(ci - 1)
                        finish_chunk(NCH - 1)
                        if g + 1 < 16:
                            prefetch_wu()
                    def stage2_down(g, trg):
                        s, t0 = g // 8, (g % 8) * 512

                        def load_xm(t):
                            P.dma("sp", xm[t % 2][:], x_dst[s, t0 + t * 128:t0 + (t + 1) * 128, :], reads=[bXD[g]], writes=[bXm[t % 2]])

                        load_xm(0)
                        load_xm(1)
                        for t in range(4):
                            if trg is not None:
                                tr(trg, 2 * t)
                            yi = ycnt[0] % 2
                            ycnt[0] += 1
                            psY, bPY = psYs[yi], bPYs[yi]
                            NLO = NCH - 2
                            fns = []
                            for hf in range(2):
                                for ci in range(NLO):
                                    fns.append(lambda e, ci=ci, hf=hf, t=t, psY=psY: e.matmul(psY[:, hf * 512:(hf + 1) * 512], lhsT=actT[:, ci, t * 128:(t + 1) * 128],
                                                                                     rhs=w_dn[:, ci, hf * 512:(hf + 1) * 512], start=(ci == 0), stop=False))
                            P.op("pe", fns, reads=[bActLo, bWd], writes=[bPY])
                            if trg is not None:
                                tr(trg, 2 * t + 1)
                            fns = []
                            for hf in range(2):
                                for ci in range(NLO, NCH):
                                    fns.append(lambda e, ci=ci, hf=hf, t=t, psY=psY: e.matmul(psY[:, hf * 512:(hf + 1) * 512], lhsT=actT[:, ci, t * 128:(t + 1) * 128],
                                                                                     rhs=w_dn[:, ci, hf * 512:(hf + 1) * 512], start=False, stop=(ci == NCH - 1)))
                            P.op("pe", fns, reads=[bAct, bWd], writes=[bPY])
                            norm_resid(xm[t % 2][:], gate2, 8 + t, bXm[t % 2], psY, bPY)
                            P.dma("sp", x_dst[s, t0 + t * 128:t0 + (t + 1) * 128, :], xm[t % 2][:], reads=[bXm[t % 2]], writes=[bXD[g]])
                            if t + 2 < 4:
                                load_xm(t + 2)

                    NG = 16
                    load_gates(0)
                    loadOT(0)
                    loadX(0)
                    loadOT(1)
                    loadX(1)
                    stage1a(0)
                    for k in range(8):
                        tr(0, k)
                    prefetch_wu()
                    for g in range(NG):
                        s = g // 8
                        nxt_same_seq = (g + 1 < NG) and ((g + 1) // 8 == s)
                        if nxt_same_seq:
                            stage1a(g + 1)
                            if g + 2 < NG:
                                loadOT(g + 2)
                                loadX(g + 2)
                        stage2_up(g, nxt_same_seq)
                        stage2_down(g, (g + 1) if nxt_same_seq else None)
                        if g + 1 < NG and not nxt_same_seq:
                            load_gates((g + 1) // 8)
                            stage1a(g + 1)
                            for k in range(8):
                                tr(g + 1, k)
                            if g + 2 < NG:
                                loadOT(g + 2)
                                loadX(g + 2)
                P.barrier()

        except _Stop:
            pass
        P.barrier()
    return nc


def _prep_shared(inp):
    f = lambda a: np.ascontiguousarray(np.asarray(a, dtype=np.float32))
    L = 4
    sh = {}
    sh["w_ada"] = f(np.asarray(inp["w_ada"]).reshape(L, 8, 128, 6144).transpose(0, 2, 1, 3))
    sh["b_ada"] = f(inp["b_ada"])
    sh["gpre_mixT"] = f(np.asarray(inp["g_pre_mix"]).reshape(L, 8, 128).transpose(0, 2, 1))
    sh["gpre_ffnT"] = f(np.asarray(inp["g_pre_ffn"]).reshape(L, 8, 128).transpose(0, 2, 1))
    sh["gpost_mix"] = f(inp["g_post_mix"])
    sh["gpost_ffn"] = f(inp["g_post_ffn"])
    sh["w_in"] = f(np.asarray(inp["w_in"]).reshape(L, 8, 128, D_IN).transpose(0, 2, 1, 3))
    wuq = np.asarray(inp["w_uq"]).reshape(L, 384, 4, 96)
    wuq = np.concatenate([wuq[..., :64].reshape(L, 384, 256), wuq[..., 64:].reshape(L, 384, 128)], axis=-1)
    sh["w_uq"] = f(wuq.reshape(L, 3, 128, 384).transpose(0, 2, 1, 3))
    wukv = np.asarray(inp["w_ukv"]).reshape(L, 256, 4, 128)
    wukv = np.concatenate([wukv[..., :64].reshape(L, 256, 256), wukv[..., 64:].reshape(L, 256, 256)], axis=-1)
    sh["w_ukv"] = f(wukv.reshape(L, 2, 128, 512).transpose(0, 2, 1, 3))
    sh["gqT"] = f(np.asarray(inp["mla_g_q"]).reshape(L, 3, 128).transpose(0, 2, 1))
    sh["gkvT"] = f(np.asarray(inp["mla_g_kv"]).reshape(L, 2, 128).transpose(0, 2, 1))
    sh["w_out"] = f(np.asarray(inp["w_out"]).reshape(L, 8, 128, DM).transpose(0, 2, 1, 3))
    wup = np.asarray(inp["w_up"]).reshape(L, 8, 128, 2, NCH, 128)
    sh["w_up"] = f(wup.transpose(0, 4, 2, 1, 3, 5).reshape(L, NCH, 128, 8, 256))
    sh["w_down"] = f(np.asarray(inp["w_down"]).reshape(L, NCH, 128, DM).transpose(0, 2, 1, 3))
    sh["convwT"] = f(np.asarray(inp["conv_w"]).reshape(L, 3, NCH, 128).transpose(0, 3, 2, 1))
    sh["convbT"] = f(np.asarray(inp["conv_b"]).reshape(L, NCH, 128).transpose(0, 2, 1))
    G, cm, T, tm = _host_tables(np.asarray(inp["na_rpb"], np.float32), np.asarray(inp["t5_table"], np.float32))
    sh["naG"] = f(G)
    sh["nacm"] = f(cm)
    sh["t5G"] = f(T)
    sh["t5m"] = f(tm)
    sh["pens"] = f(NA_PENS)
    sh["navm"] = f(np.repeat((NA_PENS == 0).astype(np.float32), 64, axis=0))
    A2 = np.zeros((2, 128), np.float32)
    A2[0, :64] = 1.0
    A2[1, 64:] = 1.0
    sh["A2"] = A2
    cosT, sinT = _rope_tables()
    sh["cosT"] = f(cosT)
    sh["sinT"] = f(sinT)
    return sh


def _in_maps(inp, ncores):
    sh = _prep_shared(inp)
    x = np.asarray(inp["x"], np.float32)
    cc = np.asarray(inp["c"], np.float32)
    maps = []
    for i in range(ncores):
        m = dict(sh)
        m["x"] = np.ascontiguousarray(x[2 * i:2 * i + 2])
        m["cT"] = np.ascontiguousarray(cc[2 * i:2 * i + 2].reshape(2, 8, 128).transpose(2, 1, 0))
        maps.append(m)
    return maps


_NC_CACHE = {}


def kernel(**inputs):
    if "nc" not in _NC_CACHE:
        _NC_CACHE["nc"] = build()
    nc = _NC_CACHE["nc"]
    maps = _in_maps(inputs, NCORES)
    res = run_bass_kernel_spmd(nc, maps, core_ids=list(range(NCORES)))
    out = np.concatenate([np.asarray(r["out"], dtype=np.float32) for r in res.results], axis=0)
    return out
```
